# Optimizing a Trainium2 kernel written in Bass

```python
import jax, jax.numpy as jnp
from jax import lax
import numpy as np

D_MODEL = 1024
BATCH = 16
SEQ = 2048
DEPTH = 1
DEC_BATCH = 8
DEC_SEQ = 16
PAST_LEN = 2048

CHUNK = 64
D_A = 1024
D_B = 1024
GMLP_CHUNK = 128
A_GROUPS = 8
A_GROUP_DIM = D_A // A_GROUPS
CONV_WIDTH = 31
CONV_HIST = CONV_WIDTH - 1
EPS = 1e-6
SPLITS = (D_A, 2 * D_A, 3 * D_A, 3 * D_A + D_B, 3 * D_A + 2 * D_B, 3 * D_A + 3 * D_B, 3 * D_A + 3 * D_B + D_MODEL)
IN_COLS = 3 * D_A + 3 * D_B + 2 * D_MODEL

kernel_name = 'streaming_gmlp_conformer_hybrid'


def _rmsnorm(x, g):
    xf = x.astype(jnp.float32)
    y = xf * lax.rsqrt(jnp.mean(xf * xf, axis=-1, keepdims=True) + EPS)
    return (y * g.astype(jnp.float32)).astype(x.dtype)


def _layernorm(x, g, b):
    xf = x.astype(jnp.float32)
    mu = jnp.mean(xf, axis=-1, keepdims=True)
    xc = xf - mu
    var = jnp.mean(xc * xc, axis=-1, keepdims=True)
    y = xc * lax.rsqrt(var + EPS) * g.astype(jnp.float32) + b.astype(jnp.float32)
    return y.astype(x.dtype)


def _spatial_proj(v, w_spatial, b_spatial):
    bsz, t, _ = v.shape
    pad = (-t) % GMLP_CHUNK
    vp = jnp.pad(v, ((0, 0), (0, pad), (0, 0)))
    n = (t + pad) // GMLP_CHUNK
    vp = vp.reshape(bsz, n, GMLP_CHUNK, A_GROUPS, A_GROUP_DIM)
    mask = jnp.tril(jnp.ones((GMLP_CHUNK, GMLP_CHUNK), dtype=bool))
    wm = jnp.where(mask[None], w_spatial, jnp.zeros_like(w_spatial))
    s = jnp.einsum('hqk,bnkhc->bnqhc', wm, vp) + b_spatial.T[None, None, :, :, None]
    return s.reshape(bsz, n * GMLP_CHUNK, D_A)[:, :t]


def _layer(x, c, conv_hist, w_ada, b_ada, norm_g, w_in, b_in, ln_v_g, ln_v_b,
           w_spatial, b_spatial, conv_w, conv_b, ln_c_g, ln_c_b, w_o_a, w_o_b, w_out):
    mod = jax.nn.silu(c) @ w_ada + b_ada
    shift, scale, gate = jnp.split(mod[:, None, :], 3, axis=-1)
    h = _rmsnorm(x, norm_g) * (1 + scale) + shift
    z = h @ w_in + b_in
    u, v, gate_a, glu_a, glu_b, gate_b, merge_a, merge_b = jnp.split(z, SPLITS, axis=-1)
    v_n = _layernorm(jax.nn.gelu(v), ln_v_g, ln_v_b)
    y_a = jax.nn.gelu(u) * _spatial_proj(v_n, w_spatial, b_spatial)
    y_a = (y_a * jax.nn.silu(gate_a)) @ w_o_a
    g = glu_a * jax.nn.sigmoid(glu_b)
    g_cat = jnp.concatenate([conv_hist.astype(g.dtype), g], axis=1)
    dw = lax.conv_general_dilated(g_cat, conv_w[:, None, :].astype(g.dtype), (1,), 'VALID',
                                  dimension_numbers=('NWC', 'WIO', 'NWC'),
                                  feature_group_count=D_B) + conv_b
    y_b = jax.nn.silu(_layernorm(dw, ln_c_g, ln_c_b))
    y_b = (y_b * jax.nn.silu(gate_b)) @ w_o_b
    m = jax.nn.sigmoid(merge_a) * y_a + jax.nn.sigmoid(merge_b) * y_b
    x = x + gate * (m @ w_out)
    return x, g_cat[:, -CONV_HIST:], v_n


def setup_inputs(seed: int = 0) -> dict:
    key = jax.random.key(seed)
    ks = jax.random.split(key, 24)
    f32 = jnp.float32
    nrm = lambda k, shape, s: jax.random.normal(k, shape, f32) * s
    return {
        'x_prompt': nrm(ks[0], (BATCH, SEQ, D_MODEL), 1.0),
        'x_sample': nrm(ks[1], (DEC_BATCH, DEC_SEQ, D_MODEL), 1.0),
        'state_conv': nrm(ks[2], (DEPTH, DEC_BATCH, CONV_HIST, D_B), 0.5),
        'c_prompt': nrm(ks[3], (BATCH, D_MODEL), 1.0),
        'c_sample': nrm(ks[4], (DEC_BATCH, D_MODEL), 1.0),
        'w_ada': nrm(ks[5], (DEPTH, D_MODEL, 3 * D_MODEL), D_MODEL ** -0.5),
        'b_ada': nrm(ks[6], (DEPTH, 3 * D_MODEL), 0.02),
        'norm_g': 1.0 + nrm(ks[7], (DEPTH, D_MODEL), 0.02),
        'w_in': nrm(ks[8], (DEPTH, D_MODEL, IN_COLS), D_MODEL ** -0.5),
        'b_in': nrm(ks[9], (DEPTH, IN_COLS), 0.02),
        'ln_v_g': 1.0 + nrm(ks[10], (DEPTH, D_A), 0.02),
        'ln_v_b': nrm(ks[11], (DEPTH, D_A), 0.02),
        'w_spatial': nrm(ks[12], (DEPTH, A_GROUPS, GMLP_CHUNK, GMLP_CHUNK), GMLP_CHUNK ** -0.5),
        'b_spatial': 1.0 + nrm(ks[13], (DEPTH, A_GROUPS, GMLP_CHUNK), 0.02),
        'conv_w': nrm(ks[14], (DEPTH, CONV_WIDTH, D_B), CONV_WIDTH ** -0.5),
        'conv_b': nrm(ks[15], (DEPTH, D_B), 0.02),
        'ln_c_g': 1.0 + nrm(ks[16], (DEPTH, D_B), 0.02),
        'ln_c_b': nrm(ks[17], (DEPTH, D_B), 0.02),
        'w_o_a': nrm(ks[18], (DEPTH, D_A, D_MODEL), D_A ** -0.5),
        'w_o_b': nrm(ks[19], (DEPTH, D_B, D_MODEL), D_B ** -0.5),
        'w_out': nrm(ks[20], (DEPTH, D_MODEL, D_MODEL), D_MODEL ** -0.5),
        'final_g': 1.0 + nrm(ks[21], (D_MODEL,), 0.02),
    }


def reference(x_prompt, x_sample, state_conv, c_prompt, c_sample, w_ada, b_ada, norm_g,
              w_in, b_in, ln_v_g, ln_v_b, w_spatial, b_spatial, conv_w, conv_b,
              ln_c_g, ln_c_b, w_o_a, w_o_b, w_out, final_g):
    hp, hs = x_prompt, x_sample
    conv_p_out, conv_s_out, v_s_out = [], [], []
    for l in range(DEPTH):
        p = (w_ada[l], b_ada[l], norm_g[l], w_in[l], b_in[l], ln_v_g[l], ln_v_b[l],
             w_spatial[l], b_spatial[l], conv_w[l], conv_b[l], ln_c_g[l], ln_c_b[l],
             w_o_a[l], w_o_b[l], w_out[l])
        zero_hist = jnp.zeros((hp.shape[0], CONV_HIST, D_B), hp.dtype)
        hp, conv_p, _ = _layer(hp, c_prompt, zero_hist, *p)
        hs, conv_s, v_s = _layer(hs, c_sample, state_conv[l], *p)
        conv_p_out.append(conv_p)
        conv_s_out.append(conv_s)
        v_s_out.append(v_s)
    y_prompt = _rmsnorm(hp, final_g)
    y_sample = _rmsnorm(hs, final_g)
    new_conv_prompt = jnp.stack(conv_p_out)
    new_conv_sample = jnp.stack(conv_s_out)
    new_gmlp_v_sample = jnp.stack(v_s_out)
    return (y_prompt, y_sample, new_conv_prompt, new_conv_sample, new_gmlp_v_sample)
```

```python
import contextlib
import numpy as np
import concourse.bass as bass
import concourse.mybir as mybir
from concourse.bass_utils import run_bass_kernel_spmd

F32 = mybir.dt.float32
BF16 = mybir.dt.bfloat16
AF = mybir.ActivationFunctionType
ALU = mybir.AluOpType

D = 1024
KC = 8
SEQ = 2048
TB = 512
DEC = 16
HIST = 30
NTAP = 31
EPS = 1e-6
NSLOT = 22
SLOT_ELEMS = KC * 512

COMPUTE = ("pe", "act", "dve", "pool")


class _Tile:
    __slots__ = ("name", "last_w", "readers")

    def __init__(self, name):
        self.name = name
        self.last_w = None
        self.readers = []


class _Op:
    __slots__ = ("idx", "eng", "emit", "deps", "is_dma", "sem_key", "dma_val",
                 "needs_inc", "ticket", "n_dma", "tag", "w", "r", "nowaw")

    def __init__(self):
        self.deps = set()
        self.needs_inc = False
        self.ticket = None
        self.is_dma = False
        self.nowaw = False
        self.sem_key = None
        self.dma_val = None
        self.n_dma = 1


class Sched:
    def __init__(self, nc):
        self.nc = nc
        self.ops = []
        self.tiles = {}
        self.wait_all_keys = set()
        self.tag = ""

    def _tile(self, name):
        t = self.tiles.get(name)
        if t is None:
            t = _Tile(name)
            self.tiles[name] = t
        return t

    def _track(self, op, reads, writes, nowaw=False):
        for n in reads:
            t = self._tile(n)
            if t.last_w is not None:
                op.deps.add(t.last_w)
            t.readers.append(op.idx)
        for n in writes:
            t = self._tile(n)
            if t.last_w is not None:
                lw = self.ops[t.last_w]
                if not (nowaw and lw.eng == op.eng and getattr(lw, "nowaw", False) and not lw.is_dma):
                    op.deps.add(t.last_w)
            for r in t.readers:
                if r != op.idx:
                    op.deps.add(r)
            t.last_w = op.idx
            t.readers = []
        op.deps.discard(op.idx)

    def op(self, eng, emit, reads=(), writes=(), nowaw=False):
        o = _Op()
        o.idx = len(self.ops)
        o.eng = eng
        o.emit = emit
        o.tag = self.tag
        o.nowaw = nowaw
        o.w = tuple(writes); o.r = tuple(reads)
        self.ops.append(o)
        self._track(o, reads, writes, nowaw)
        return o

    def dma(self, queue, emit, reads=(), writes=(), sem_key=None, n_dma=1):
        o = _Op()
        o.idx = len(self.ops)
        o.eng = queue
        o.emit = emit
        o.is_dma = True
        o.sem_key = sem_key
        o.n_dma = n_dma
        o.tag = self.tag
        o.w = tuple(writes); o.r = tuple(reads)
        self.ops.append(o)
        self._track(o, reads, writes)
        return o

    def emit_all(self):
        nc = self.nc
        ops = self.ops
        for o in ops:
            latest = {}
            keep = set()
            for d in o.deps:
                p = ops[d]
                if p.is_dma:
                    keep.add(d)
                elif latest.get(p.eng, -1) < d:
                    latest[p.eng] = d
            keep.update(latest.values())
            o.deps = keep
        for o in ops:
            for d in o.deps:
                p = ops[d]
                if p.is_dma:
                    continue
                if p.eng == "pe" and o.eng == "pe" and not o.is_dma:
                    continue
                p.needs_inc = True
        cnt = {e: 0 for e in COMPUTE}
        for o in ops:
            if o.is_dma:
                continue
            if o.needs_inc:
                cnt[o.eng] += 1
                o.ticket = cnt[o.eng]
        dkeys = {}
        for o in ops:
            if o.is_dma:
                k = o.sem_key
                dkeys[k] = dkeys.get(k, 0) + 16 * o.n_dma
                o.dma_val = dkeys[k]
        for o in ops:
            if o.is_dma and o.sem_key in self.wait_all_keys:
                o.dma_val = dkeys[o.sem_key]
        self.stats = dict(cnt)
        self.stats["n_ops"] = len(ops)
        self.stats["dma_keys"] = len(dkeys)
        with contextlib.ExitStack() as st:
            esem = {e: st.enter_context(nc.semaphore("s_" + e)) for e in COMPUTE}
            dsem = {k: st.enter_context(nc.semaphore("d_%d" % i)) for i, k in enumerate(dkeys)}
            block = st.enter_context(nc.Block())
            streams = {}
            for o in ops:
                streams.setdefault(o.eng, []).append(o)

            def run(engname, e):
                waited = {}
                for o in streams.get(engname, []):
                    need = {}
                    for d in o.deps:
                        p = ops[d]
                        if p.is_dma:
                            if o.is_dma and o.sem_key == p.sem_key and p.sem_key in self.wait_all_keys:
                                continue
                            s = dsem[p.sem_key]
                            v = p.dma_val
                        else:
                            if p.eng == "pe" and engname == "pe" and not o.is_dma:
                                continue
                            s = esem[p.eng]
                            v = p.ticket
                        key = id(s)
                        if need.get(key, (None, 0))[1] < v:
                            need[key] = (s, v)
                    for key, (s, v) in need.items():
                        if waited.get(key, 0) >= v:
                            continue
                        e.wait_ge(s, v)
                        waited[key] = v
                    r = o.emit(e)
                    if o.is_dma:
                        s = dsem[o.sem_key]
                        assert len(r) == o.n_dma, (len(r), o.n_dma)
                        for ins in r:
                            ins.then_inc(s, 16)
                    elif o.needs_inc:
                        r.then_inc(esem[engname], 1)
                last = {}
                for o in streams.get(engname, []):
                    if o.is_dma:
                        last[o.sem_key] = max(last.get(o.sem_key, 0), o.dma_val)
                for k, v in last.items():
                    e.wait_ge(dsem[k], v)

            if "pe" in streams:
                @block.tensor
                def _(e):
                    run("pe", e)
            if "act" in streams:
                @block.scalar
                def _(e):
                    run("act", e)
            if "dve" in streams:
                @block.vector
                def _(e):
                    run("dve", e)
            if "pool" in streams:
                @block.gpsimd
                def _(e):
                    run("pool", e)
            if "sp" in streams:
                @block.sync
                def _(e):
                    run("sp", e)


def build_program(npb=4, nring=5, diag_act_every=2):
    seqlen = npb * TB
    nc = bass.Bass("TRN2", target_bir_lowering=False)

    def din(name, shape):
        return nc.dram_tensor(name, shape, F32, kind="ExternalInput").ap()

    def dout(name, shape):
        return nc.dram_tensor(name, shape, F32, kind="ExternalOutput").ap()

    xp = din("xp", [2, seqlen, D])
    xs = din("xs", [DEC, D])
    scv = din("sc", [HIST, D])
    cc = din("cc", [3, D])
    w_ada = din("w_ada", [D, 3 * D])
    b_ada = din("b_ada", [3 * D])
    norm_g = din("norm_g", [D])
    w_in = din("w_in", [D, 8 * D])
    b_in = din("b_in", [8 * D])
    ln_v_g = din("ln_v_g", [D])
    ln_v_b = din("ln_v_b", [D])
    w_sp = din("w_sp", [8, 128, 128])
    b_sp = din("b_sp", [8 * 128])
    conv_w = din("conv_w", [NTAP, D])
    conv_b = din("conv_b", [D])
    ln_c_g = din("ln_c_g", [D])
    ln_c_b = din("ln_c_b", [D])
    w_o_a = din("w_o_a", [D, D])
    w_o_b = din("w_o_b", [D, D])
    w_out = din("w_out", [D, D])
    final_g = din("final_g", [D])
    yp = dout("yp", [2, seqlen, D])
    ys = dout("ys", [DEC, D])
    ncp = dout("ncp", [2, HIST, D])
    ncs = dout("ncs", [HIST, D])
    nvs = dout("nvs", [DEC, D])
    wscr = nc.dram_tensor("wscr", [NSLOT, 128, SLOT_ELEMS], BF16).ap()

    S = Sched(nc)
    S.wait_all_keys.add("setup")
    st = contextlib.ExitStack()

    def sb(name, shape, dt):
        return st.enter_context(nc.sbuf_tensor(name, shape, dt))

    ring = sb("ring", [128, nring, KC, 512], BF16)
    xin = sb("xin", [128, 4, D], F32)
    junk = sb("junk", [128, D], BF16)
    xn = sb("xn", [128, 2, D], BF16)
    hT = sb("hT", [128, 2, KC, TB], BF16)
    gv = sb("gv", [128, D], F32)
    vhat = sb("vhat", [128, 4, D], BF16)
    gu = sb("gu", [128, KC, TB], BF16)
    gcat = sb("gcat", [128, KC, HIST + TB], BF16)
    ah = sb("ah", [128, 2, TB], F32)
    th = sb("th", [128, 2, TB], F32)
    gtl = sb("gtl", [128, KC, 32], F32)
    diag = sb("diag", [128, 2, NTAP, 128], BF16)
    dwb = sb("dwb", [128, KC, TB], BF16)
    dwsq = sb("dwsq", [128, 2, TB], BF16)
    mu_bc = sb("mu_bc", [128, TB], F32)
    rs_bc = sb("rs_bc", [128, TB], F32)
    nt1 = sb("nt1", [128, TB], F32)
    nt2 = sb("nt2", [128, TB], F32)
    sga = sb("sga", [128, 2, TB], BF16)
    pbuf = sb("pbuf", [128, 2, TB], BF16)
    sgb = sb("sgb", [128, KC, TB], BF16)
    nb = sb("nb", [128, 2, TB], BF16)
    tha = sb("tha", [128, 4, TB], BF16)
    thb = sb("thb", [128, 4, TB], BF16)
    xr = sb("xr", [128, D], F32)
    ybuf = sb("ybuf", [128, D], F32)
    ident_f = sb("ident_f", [128, 128], F32)
    ident_b = sb("ident_b", [128, 128], BF16)
    ones_f = sb("ones_f", [128, 128], F32)
    ones_b = sb("ones_b", [128, 128], BF16)
    negh = sb("negh", [128, 8], F32)
    R0 = sb("R0", [128, 128], F32)
    R1 = sb("R1", [128, 128], F32)
    RT0 = sb("RT0", [128, 128], F32)
    RT1 = sb("RT1", [128, 32], F32)
    bTh = sb("bTh", [128, 64], F32)
    cwT = sb("cwT", [128, NTAP * 8], F32)
    scT = sb("scT", [128, HIST * 8], F32)
    wmT = sb("wmT", [128, KC, 128], BF16)
    wmf = sb("wmf", [128, 128], F32)
    Bh = sb("Bh", [128, KC, 128], F32)
    siluT = sb("siluT", [128, 24], F32)
    modT = sb("modT", [128, 24, 3], F32)
    aT = sb("aT", [128, KC, 3], F32)
    gateh = sb("gateh", [128, KC, 3], F32)
    glh = sb("glh", [128, 128], F32)
    gate_bc = sb("gate_bc", [128, D], F32)
    fg_bc = sb("fg_bc", [128, D], F32)
    bv2 = sb("bv2", [2, D], BF16)
    lo_row = sb("lo_row", [1, D], BF16)
    ones2 = sb("ones2", [2, 128], BF16)
    st_s = sb("st_s", [128, 16], F32)
    st_v = sb("st_v", [128, 16], F32)
    print("sbuf bytes remaining:", nc.sbuf_bytes_remaining)

    banks = [st.enter_context(nc.psum_tensor("bank%d" % i, [128, 512], F32)) for i in range(8)]
    bank_ctr = [0]
    held = set()

    def newbank(hold=False):
        while True:
            i = bank_ctr[0] % 8
            bank_ctr[0] += 1
            if i not in held:
                break
        if hold:
            held.add(i)
        return banks[i], "bank%d" % i

    def release(name):
        held.discard(int(name[4:]))

    def mm(out, lhsT, rhs, start, stop, reads, writes):
        S.op("pe", lambda e: e.matmul(out, lhsT=lhsT, rhs=rhs, start=start, stop=stop),
             reads=reads, writes=writes)

    def tr(out, in_, ident, reads, writes):
        S.op("pe", lambda e: e.transpose(out, in_, ident), reads=reads, writes=writes)

    def act(out, in_, func, reads, writes, bias=None, scale=None, accum_out=None, nowaw=False):
        kw = {}
        if bias is not None:
            kw["bias"] = bias
        if scale is not None:
            kw["scale"] = scale
        if accum_out is not None:
            kw["accum_out"] = accum_out
        S.op("act", lambda e: e.activation(out=out, in_=in_, func=func, **kw), reads=reads, writes=writes, nowaw=nowaw)

    def ts(eng, out, in0, s1, s2, op0, op1, reads, writes, nowaw=False):
        if op1 is None:
            S.op(eng, lambda e: e.tensor_scalar(out=out, in0=in0, scalar1=s1, scalar2=None, op0=op0),
                 reads=reads, writes=writes, nowaw=nowaw)
        else:
            S.op(eng, lambda e: e.tensor_scalar(out=out, in0=in0, scalar1=s1, scalar2=s2, op0=op0, op1=op1),
                 reads=reads, writes=writes)

    def tt(eng, out, in0, in1, op, reads, writes):
        S.op(eng, lambda e: e.tensor_tensor(out=out, in0=in0, in1=in1, op=op), reads=reads, writes=writes)

    def stt(out, in0, scalar, in1, op0, op1, reads, writes):
        S.op("dve", lambda e: e.scalar_tensor_tensor(out=out, in0=in0, scalar=scalar, in1=in1, op0=op0, op1=op1),
             reads=reads, writes=writes)

    def cp(eng, out, in_, reads, writes):
        S.op(eng, lambda e: e.tensor_copy(out, in_), reads=reads, writes=writes)

    def dma(q, out, in_, reads, writes, key):
        S.dma(q, lambda e: [e.dma_start(out=out, in_=in_)], reads=reads, writes=writes, sem_key=key)

    def rows128(v):
        return v.rearrange("(r k) -> r k", k=128)

    S.op("pool", lambda e: e.memset(ident_f[:], 1.0), writes=["ident_f"])
    S.op("pool", lambda e: e.affine_select(out=ident_f[:], in_=ident_f[:], pattern=[[-1, 128]],
                                           compare_op=ALU.is_equal, fill=0.0, base=0, channel_multiplier=1),
         reads=["ident_f"], writes=["ident_f"])
    S.op("pool", lambda e: e.memset(ones_f[:], 1.0), writes=["ones_f"])
    S.op("pool", lambda e: e.memset(negh[:], -0.5), writes=["negh"])
    cp("dve", ident_b[:], ident_f[:], ["ident_f"], ["ident_b"])
    cp("dve", ones_b[:], ones_f[:], ["ones_f"], ["ones_b"])
    cp("dve", ones2[:], ones_f[0:2, :], ["ones_f"], ["ones2"])

    dma("sp", R0[0:64, :], rows128(b_in), [], ["R0"], "setup")
    dma("sp", R0[64:88, :], rows128(b_ada), [], ["R0"], "setup")
    dma("sp", R0[88:96, :], rows128(norm_g), [], ["R0"], "setup")
    dma("sp", R0[96:104, :], rows128(ln_v_g), [], ["R0"], "setup")
    dma("sp", R0[104:112, :], rows128(ln_v_b), [], ["R0"], "setup")
    dma("sp", R0[112:120, :], rows128(conv_b), [], ["R0"], "setup")
    dma("sp", R0[120:128, :], rows128(ln_c_g), [], ["R0"], "setup")
    dma("sp", R1[0:8, :], rows128(ln_c_b), [], ["R1"], "setup")
    dma("sp", R1[8:32, :], cc.rearrange("s (c k) -> (s c) k", k=128), [], ["R1"], "setup")
    dma("sp", fg_bc[:], final_g.rearrange("(o n) -> o n", o=1).partition_broadcast(128), [], ["fg_bc"], "setup")
    vh_f = vhat[:].rearrange("p a b -> p (a b)").bitcast(F32)
    gu_f = gu[:].rearrange("p a b -> p (a b)").bitcast(F32)
    lnvb_row = vh_f[0:1, 0:D]
    bs_row = vh_f[0:1, D:2 * D]
    rs_row = gu_f[0:1, 0:D]
    bv_row = gu_f[0:1, D:2 * D]
    dma("sp", lnvb_row, ln_v_b.rearrange("(o n) -> o n", o=1), [], ["vhat"], "setup")
    dma("sp", bs_row, b_sp.rearrange("(o n) -> o n", o=1), [], ["vhat"], "setup")
    dma("sp", bv_row, b_in[D:2 * D].rearrange("(o n) -> o n", o=1), [], ["gu"], "setup")

    def wsrc(slot):
        def cols(w, c0, n):
            return w[:, c0:c0 + n].rearrange("(k p) n -> p k n", p=128)
        if slot in (0, 1):
            return [(0, 512, cols(w_in, D + slot * 512, 512))]
        if slot in (2, 3):
            return [(0, 512, cols(w_in, (slot - 2) * 512, 512))]
        if 4 <= slot <= 7:
            i = slot - 4
            return [(0, 256, cols(w_in, 3 * D + i * 256, 256)), (256, 256, cols(w_in, 4 * D + i * 256, 256))]
        if slot in (8, 9):
            return [(0, 512, cols(w_in, 2 * D + (slot - 8) * 512, 512))]
        if slot in (10, 11):
            return [(0, 512, cols(w_in, 5 * D + (slot - 10) * 512, 512))]
        if 12 <= slot <= 15:
            i = slot - 12
            return [(0, 256, cols(w_in, 6 * D + i * 256, 256)), (256, 256, cols(w_in, 7 * D + i * 256, 256))]
        if 16 <= slot <= 19:
            i = slot - 16
            return [(0, 256, cols(w_o_a, i * 256, 256)), (256, 256, cols(w_o_b, i * 256, 256))]
        return [(0, 512, cols(w_out, (slot - 20) * 512, 512))]

    def emit_conversion():
        S.tag = "conv"
        stg = []
        for h in range(2):
            v = ring[:, nring - 2 + h].rearrange("p k n -> p (k n)").bitcast(F32)
            stg.append((v, "ring%d" % (nring - 2 + h)))
        cobs = [(ring[:, nring - 3], "ring%d" % (nring - 3)), (hT[:, 1], "hT1")]
        for slot in range(NSLOT):
            srcs = wsrc(slot)
            cob, cn = cobs[slot % 2]
            for h in range(2):
                sv, sn = stg[h]
                if len(srcs) == 2:
                    (c0, n, src) = srcs[h]
                    view = sv.rearrange("p (k n) -> p k n", k=KC)
                    dstv = cob[:, :, c0:c0 + n]
                    dma("sp", view, src, [], [sn], "stg%d" % h)
                else:
                    (c0, n, src) = srcs[0]
                    view = sv.rearrange("p (k n) -> p k n", k=KC // 2)
                    dstv = cob[:, h * 4:(h + 1) * 4, :]
                    dma("sp", view, src[:, h * 4:(h + 1) * 4, :], [], [sn], "stg%d" % h)
                if h == 0:
                    act(dstv, view, AF.Identity, [sn], [cn])
                else:
                    cp("dve", dstv, view, [sn], [cn])
            dma("act", wscr[slot], cob.rearrange("p k n -> p (k n)"), [cn], ["wscr%d" % slot], "wscr_w%d" % slot)

    b0, n0 = newbank()
    tr(b0[:, 0:128], R0[:], ident_f[:], ["R0", "ident_f"], [n0])
    act(RT0[:], b0[:, 0:128], AF.Identity, [n0], ["RT0"])
    b1, n1 = newbank()
    tr(b1[:, 0:32], R1[0:32, :], ident_f[0:32, 0:32], ["R1", "ident_f"], [n1])
    act(RT1[:], b1[:, 0:32], AF.Identity, [n1], ["RT1"])
    ts("dve", bTh[:], RT0[:, 0:64], 0.5, None, ALU.mult, None, ["RT0"], ["bTh"])
    bT = RT0
    b_adaT = lambda jc: RT0[:, 64 + jc:65 + jc]
    norm_gT = lambda c: RT0[:, 88 + c:89 + c]
    lnvgT = lambda c: RT0[:, 96 + c:97 + c]
    convbT = lambda c: RT0[:, 112 + c:113 + c]
    lncgT = lambda c: RT0[:, 120 + c:121 + c]
    lncbT = lambda c: RT1[:, c:c + 1]
    act(siluT[:], RT1[:, 8:32], AF.Silu, ["RT1"], ["siluT"])

    cw_rows = conv_w.rearrange("j (c k) -> (j c) k", k=128)
    dma("act", R0[:], cw_rows[0:128, :], ["RT0"], ["R0"], "cw0")
    dma("act", R1[0:120, :], cw_rows[128:248, :], ["RT1"], ["R1"], "cw1")
    b2, n2 = newbank()
    tr(b2[:, 0:128], R0[:], ident_f[:], ["R0", "ident_f"], [n2])
    tr(b2[:, 128:248], R1[0:120, :], ident_f[0:120, 0:120], ["R1", "ident_f"], [n2])
    act(cwT[:], b2[:, 0:248], AF.Identity, [n2], ["cwT"])
    sc_rows = scv.rearrange("r (c k) -> (r c) k", k=128)
    dma("act", R0[:], sc_rows[0:128, :], [], ["R0"], "sc0")
    dma("act", R1[0:112, :], sc_rows[128:240, :], [], ["R1"], "sc1")
    b3, n3 = newbank()
    tr(b3[:, 0:128], R0[:], ident_f[:], ["R0", "ident_f"], [n3])
    tr(b3[:, 128:240], R1[0:112, :], ident_f[0:112, 0:112], ["R1", "ident_f"], [n3])
    act(scT[:], b3[:, 0:240], AF.Identity, [n3], ["scT"])

    for h in range(8):
        dma("act", R0[:], w_sp[h], [], ["R0"], "wsp")
        bb, nn = newbank()
        tr(bb[:, 0:128], R0[:], ident_f[:], ["R0", "ident_f"], [nn])
        act(wmf[:], bb[:, 0:128], AF.Identity, [nn], ["wmf"])
        S.op("pool", lambda e: e.affine_select(out=wmf[:], in_=wmf[:], pattern=[[1, 128]],
                                               compare_op=ALU.is_ge, fill=0.0, base=0, channel_multiplier=-1),
             reads=["wmf"], writes=["wmf"])
        cp("dve", wmT[:, h, :], wmf[:], ["wmf"], ["wmT"])
        bb2, nn2 = newbank()
        mm(bb2[0:1, 0:128], ones_f[:, 0:1], wmf[:], True, True, ["ones_f", "wmf"], [nn2])
        act(rs_row[:, h * 128:(h + 1) * 128], bb2[0:1, 0:128], AF.Identity, [nn2], ["gu"])
        bb3, nn3 = newbank()
        mm(bb3[:, 0:128], lnvb_row[:, h * 128:(h + 1) * 128], rs_row[:, h * 128:(h + 1) * 128], True, False,
           ["vhat", "gu"], [nn3])
        mm(bb3[:, 0:128], ones_f[0:1, :], bs_row[:, h * 128:(h + 1) * 128], False, True,
           ["ones_f", "vhat"], [nn3])
        act(Bh[:, h, :], bb3[:, 0:128], AF.Identity, [nn3], ["Bh"])

    cp("dve", bv2[0:1, :], bv_row, ["gu"], ["bv2"])
    tt("dve", nt1[0:1, 0:512], bv_row[:, 0:512], bv2[0:1, 0:512], ALU.subtract, ["gu", "bv2"], ["nt1"])
    tt("dve", nt2[0:1, 0:512], bv_row[:, 512:1024], bv2[0:1, 512:1024], ALU.subtract, ["gu", "bv2"], ["nt2"])
    cp("dve", lo_row[:, 0:512], nt1[0:1, 0:512], ["nt1"], ["lo_row"])
    cp("dve", lo_row[:, 512:1024], nt2[0:1, 0:512], ["nt2"], ["lo_row"])
    dma("act", bv2[1:2, :], lo_row[:], ["lo_row"], ["bv2"], "bv2lo")

    sgb_names = ["sgb%d" % c for c in range(KC)]
    dwb_names = ["dwb%d" % c for c in range(KC)]
    ada_bufs = [
        (sgb[:].rearrange("p a b -> p (a b)").bitcast(F32).rearrange("p (k n) -> p k n", k=KC), sgb_names, "ada0"),
        (dwb[:].rearrange("p a b -> p (a b)").bitcast(F32).rearrange("p (k n) -> p k n", k=KC), dwb_names, "ada1"),
    ]
    for pc in range(12):
        ab, an, ak = ada_bufs[pc % 2]
        dma("act", ab, w_ada[:, pc * 256:(pc + 1) * 256].rearrange("(k p) n -> p k n", p=128), [], an, ak)
        for jl in range(2):
            jc = pc * 2 + jl
            bb, nn = newbank()
            for kc in range(KC):
                mm(bb[:, 0:3], ab[:, kc, jl * 128:(jl + 1) * 128], siluT[:, kc:24:8], kc == 0, kc == KC - 1,
                   an + ["siluT"], [nn])
            act(modT[:, jc, :], bb[:, 0:3], AF.Identity, [nn], ["modT"], bias=b_adaT(jc))
    for kc in range(KC):
        ts("dve", aT[:, kc, :], modT[:, 8 + kc, :], 1.0, norm_gT(kc), ALU.add, ALU.mult, ["modT", "RT0"], ["aT"])
        ts("dve", gateh[:, kc, :], modT[:, 16 + kc, :], 0.5, None, ALU.mult, None, ["modT"], ["gateh"])
    shT = lambda kc, s: modT[:, kc, s:s + 1]

    slot_ctr = [0]
    first_pass = [True]
    stg = []
    for h_ in range(2):
        v_ = ring[:, nring - 2 + h_].rearrange("p k n -> p (k n)").bitcast(F32)
        stg.append((v_, "ring%d" % (nring - 2 + h_), "stg%d" % h_))
    stg.append((hT[:, 1].rearrange("p k n -> p (k n)").bitcast(F32), "hT1", "stg2"))
    stg_ctr = [0]

    def load_slot(slot):
        nr = (nring - 2) if first_pass[0] else nring
        r = slot_ctr[0] % nr
        slot_ctr[0] += 1
        rn = "ring%d" % r
        dst = ring[:, r]
        if first_pass[0]:
            srcs = wsrc(slot)
            for h in range(2):
                sv, sn, sk = stg[stg_ctr[0] % len(stg)]
                stg_ctr[0] += 1
                if len(srcs) == 2:
                    (c0, n, src) = srcs[h]
                    view = sv.rearrange("p (k n) -> p k n", k=KC)
                    dstv = dst[:, :, c0:c0 + n]
                    dma("sp", view, src, [], [sn], sk)
                else:
                    (c0, n, src) = srcs[0]
                    view = sv.rearrange("p (k n) -> p k n", k=KC // 2)
                    dstv = dst[:, h * 4:(h + 1) * 4, :]
                    dma("sp", view, src[:, h * 4:(h + 1) * 4, :], [], [sn], sk)
                if h == 0:
                    act(dstv, view, AF.Identity, [sn], [rn])
                else:
                    cp("dve", dstv, view, [sn], [rn])
            dma("act", wscr[slot], dst.rearrange("p k n -> p (k n)"), [rn], ["wscr%d" % slot], "wscr_w%d" % slot)
        else:
            dma("sp", dst.rearrange("p k n -> p (k n)"), wscr[slot], ["wscr%d" % slot], [rn], "ringld%d" % r)
        return dst, rn

    class Blk:
        pass

    blocks = []
    for sq in range(2):
        for b in range(npb):
            k = Blk()
            k.seq, k.xsrc, k.ydst, k.t0, k.T = sq, xp[sq], yp[sq], b * TB, TB
            k.first, k.last, k.is_sample = (b == 0), (b == npb - 1), False
            blocks.append(k)
    k = Blk()
    k.seq, k.xsrc, k.ydst, k.t0, k.T = 2, xs, ys, 0, DEC
    k.first, k.last, k.is_sample = True, True, True
    blocks.append(k)
    for i, k in enumerate(blocks):
        k.idx = i
        k.TR = min(k.T, 128)
        k.NTT = max(1, k.T // 128)
        k.h = i % 2
        k.hn = "hT%d" % k.h

    def emit_xload(k):
        for t_ in range(k.NTT):
            dma("sp", xin[0:k.TR, t_, :], k.xsrc[k.t0 + t_ * 128:k.t0 + t_ * 128 + k.TR, :], [], ["xin%d" % t_],
                "xin%d" % t_)

    def emit_P0_A(k, t_):
        S.tag = "P0"
        TR = k.TR
        act(junk[0:TR, :], xin[0:TR, t_, :], AF.Square, ["xin%d" % t_], ["junk", "st_s0"], accum_out=st_s[0:TR, 0:1])
        ts("dve", st_v[0:TR, 0:1], st_s[0:TR, 0:1], 1.0 / D, EPS, ALU.mult, ALU.add, ["st_s0"], ["st_v0"])
        tt("pool", st_v[0:TR, 1:2], st_v[0:TR, 0:1], negh[0:TR, 0:1], ALU.pow, ["st_v0", "negh"], ["st_v0"])
        ts("dve", xn[0:TR, t_ % 2, :], xin[0:TR, t_, :], st_v[0:TR, 1:2], None, ALU.mult, None,
           ["xin%d" % t_, "st_v0"], ["xn%d" % (t_ % 2)])

    def emit_P0_B(k, t_):
        S.tag = "P0"
        TR, s = k.TR, k.seq
        hTb = hT[:, k.h]
        bb, nn = newbank()
        bbf = bb[:].bitcast(BF16)
        for kc in range(KC):
            tr(bbf[:, kc * 128:kc * 128 + TR], xn[0:TR, t_ % 2, kc * 128:(kc + 1) * 128], ident_b[0:TR, 0:TR],
               ["xn%d" % (t_ % 2), "ident_b"], [nn])
        for kc in range(KC):
            act(hTb[:, kc, t_ * 128:t_ * 128 + TR], bbf[:, kc * 128:kc * 128 + TR], AF.Identity, [nn, "aT", "modT"],
                [k.hn], bias=shT(kc, s), scale=aT[:, kc, s:s + 1], nowaw=True)

    def emit_P0_tile(k, t_):
        emit_P0_A(k, t_)
        emit_P0_B(k, t_)

    def emit_P1_pre(k):
        S.tag = "P1"
        k.slots_v = [load_slot(0), load_slot(1)]
        if k.is_sample:
            dma("sp", xr[0:DEC, :], ln_v_g.rearrange("(o n) -> o n", o=1).partition_broadcast(DEC), [], ["xr"], "xr")
            dma("sp", ybuf[0:DEC, :], ln_v_b.rearrange("(o n) -> o n", o=1).partition_broadcast(DEC), [], ["ybuf"],
                "ybuf_ld")

    def emit_P1_tile(k, t_):
        S.tag = "P1"
        TR = k.TR
        hTb = hT[:, k.h]
        for half in range(2):
            sl, sn = k.slots_v[half]
            bb, nn = newbank()
            for kc in range(KC):
                mm(bb[0:TR, :], hTb[:, kc, t_ * 128:t_ * 128 + TR], sl[:, kc, :], kc == 0, False, [k.hn, sn], [nn])
            mm(bb[0:TR, :], ones2[0:2, 0:TR], bv2[0:2, half * 512:(half + 1) * 512], False, True,
               ["ones2", "bv2"], [nn])
            act(gv[0:TR, half * 512:(half + 1) * 512], bb[0:TR, :], AF.Gelu_apprx_tanh, [nn], ["gv", "st_s1"],
                accum_out=st_s[0:TR, 2 + half:3 + half])
            act(junk[0:TR, 0:512], gv[0:TR, half * 512:(half + 1) * 512], AF.Square, ["gv"], ["junk", "st_s1"],
                accum_out=st_s[0:TR, 4 + half:5 + half])
        sv = ["st_v1"]
        tt("dve", st_v[0:TR, 2:3], st_s[0:TR, 2:3], st_s[0:TR, 3:4], ALU.add, ["st_s1"], sv)
        tt("dve", st_v[0:TR, 3:4], st_s[0:TR, 4:5], st_s[0:TR, 5:6], ALU.add, ["st_s1"], sv)
        ts("dve", st_v[0:TR, 2:4], st_v[0:TR, 2:4], 1.0 / D, None, ALU.mult, None, sv, sv)
        tt("dve", st_v[0:TR, 4:5], st_v[0:TR, 2:3], st_v[0:TR, 2:3], ALU.mult, sv, sv)
        tt("dve", st_v[0:TR, 5:6], st_v[0:TR, 3:4], st_v[0:TR, 4:5], ALU.subtract, sv, sv)
        ts("dve", st_v[0:TR, 5:6], st_v[0:TR, 5:6], EPS, None, ALU.add, None, sv, sv)
        tt("pool", st_v[0:TR, 6:7], st_v[0:TR, 5:6], negh[0:TR, 0:1], ALU.pow, sv + ["negh"], sv)
        ts("dve", st_v[0:TR, 7:8], st_v[0:TR, 2:3], st_v[0:TR, 6:7], -1.0, ALU.mult, ALU.mult, sv, sv)
        ts("dve", vhat[0:TR, t_, :], gv[0:TR, :], st_v[0:TR, 6:7], st_v[0:TR, 7:8], ALU.mult, ALU.add,
           ["gv"] + sv, ["vhat"])
        if k.is_sample:
            ts("dve", gv[0:TR, :], gv[0:TR, :], st_v[0:TR, 6:7], st_v[0:TR, 7:8], ALU.mult, ALU.add,
               ["gv"] + sv, ["gv"])
            tt("dve", gv[0:TR, :], gv[0:TR, :], xr[0:TR, :], ALU.mult, ["gv", "xr"], ["gv"])
            tt("dve", gv[0:TR, :], gv[0:TR, :], ybuf[0:TR, :], ALU.add, ["gv", "ybuf"], ["gv"])
            dma("act", nvs, gv[0:DEC, :], ["gv"], [], "nvs")

    def emit_front(k):
        emit_xload(k)
        for t_ in range(k.NTT):
            emit_P0_tile(k, t_)
        emit_P1_pre(k)
        for t_ in range(k.NTT):
            emit_P1_tile(k, t_)

    def emit_body(k, nxt, hoist):
        seq, xsrc, ydst, t0, T, first, last, is_sample = k.seq, k.xsrc, k.ydst, k.t0, k.T, k.first, k.last, k.is_sample
        TR, NTT, s = k.TR, k.NTT, k.seq
        hTb = hT[:, k.h]
        hn = k.hn
        S.tag = "P2"
        for sidx in (2, 3):
            sl, sn = load_slot(sidx)
            for cl in range(4):
                c = (sidx - 2) * 4 + cl
                bb, nn = newbank()
                for kc in range(KC):
                    mm(bb[:, 0:T], sl[:, kc, cl * 128:(cl + 1) * 128], hTb[:, kc, 0:T], kc == 0, kc == KC - 1,
                       [sn, hn], [nn])
                act(gu[:, c, 0:T], bb[:, 0:T], AF.Gelu_apprx_tanh, [nn], ["gu%d" % c], bias=bT[:, c:c + 1])

        def build_diag(c):
            for j in range(NTAP):
                if diag_act_every and (j % diag_act_every == diag_act_every - 1):
                    act(diag[:, c % 2, j, :], ident_b[:], AF.Identity, ["ident_b", "cwT"], ["diagA%d" % (c % 2)],
                        scale=cwT[:, j * 8 + c:j * 8 + c + 1], nowaw=True)
                else:
                    ts("dve", diag[:, c % 2, j, :], ident_b[:], cwT[:, j * 8 + c:j * 8 + c + 1], None, ALU.mult, None,
                       ["ident_b", "cwT"], ["diagD%d" % (c % 2)], nowaw=True)
        S.tag = "P3"
        build_diag(0)
        gnames = ["gcat%d" % c for c in range(KC)]
        if first and not is_sample:
            S.op("dve", lambda e: e.memset(gcat[:, :, 0:HIST], 0.0), writes=gnames)
        elif is_sample:
            cp("dve", gcat[:, :, 0:HIST], scT[:].rearrange("p (r c) -> p c r", c=8), ["scT"], gnames)
        NTL = min(HIST, T)
        for i in range(4):
            sl, sn = load_slot(4 + i)
            for cl in range(2):
                c = i * 2 + cl
                ba, na = newbank()
                bbk, nbk = newbank()
                for kc in range(KC):
                    mm(ba[:, 0:T], sl[:, kc, cl * 128:(cl + 1) * 128], hTb[:, kc, 0:T], kc == 0, kc == KC - 1,
                       [sn, hn], [na])
                for kc in range(KC):
                    mm(bbk[:, 0:T], sl[:, kc, 256 + cl * 128:256 + (cl + 1) * 128], hTb[:, kc, 0:T], kc == 0,
                       kc == KC - 1, [sn, hn], [nbk])
                act(ah[:, c % 2, 0:T], ba[:, 0:T], AF.Identity, [na, "bTh"], ["ah%d" % (c % 2)],
                    bias=bTh[:, 24 + c:25 + c], scale=0.5)
                act(th[:, c % 2, 0:T], bbk[:, 0:T], AF.Tanh, [nbk, "bTh"], ["th%d" % (c % 2)],
                    bias=bTh[:, 32 + c:33 + c], scale=0.5)
                stt(gcat[:, c, HIST:HIST + T], th[:, c % 2, 0:T], 1.0, ah[:, c % 2, 0:T], ALU.add, ALU.mult,
                    ["ah%d" % (c % 2), "th%d" % (c % 2)], ["gcat%d" % c])
                if last:
                    stt(gtl[:, c, 0:NTL], th[:, c % 2, T - NTL:T], 1.0, ah[:, c % 2, T - NTL:T], ALU.add, ALU.mult,
                        ["ah%d" % (c % 2), "th%d" % (c % 2)], ["gtl"])
        if last:
            bb, nn = newbank()
            bb2, nn2 = newbank()
            for c in range(KC):
                tgt = bb if c < 4 else bb2
                tr(tgt[0:NTL, (c % 4) * 128:(c % 4 + 1) * 128], gtl[:, c, 0:NTL], ident_f[:], ["gtl", "ident_f"],
                   [nn if c < 4 else nn2])
            act(ybuf[0:NTL, 0:512], bb[0:NTL, :], AF.Identity, [nn], ["ybuf"])
            act(ybuf[0:NTL, 512:1024], bb2[0:NTL, :], AF.Identity, [nn2], ["ybuf"])
            if is_sample:
                dma("act", ncs[HIST - NTL:HIST, :], ybuf[0:NTL, :], ["ybuf"], [], "ybuf_st")
                dma("sp", ncs[0:HIST - NTL, :], scv[NTL:HIST, :], [], [], "ncs_cp")
            else:
                dma("act", ncp[s], ybuf[0:NTL, :], ["ybuf"], [], "ybuf_st")
        if nxt is not None:
            emit_xload(nxt)
        S.tag = "P4P5"
        sl_ga = [None, None]
        sl_gb = [None, None]
        bsum, nsum = newbank(hold=True)
        bsq, nsq = newbank(hold=True)

        def stats_mm(c):
            mm(bsum[:, 0:T], ones_b[:], dwb[:, c, 0:T], c == 0, c == KC - 1, ["ones_b", "dwb%d" % c], [nsum])
            mm(bsq[:, 0:T], ones_b[:], dwsq[:, c % 2, 0:T], c == 0, c == KC - 1, ["ones_b", "dwsq%d" % (c % 2)], [nsq])

        for c in range(KC):
            if c + 1 < KC:
                build_diag(c + 1)
            bb, nn = newbank()
            for j in range(NTAP):
                mm(bb[:, 0:T], diag[:, c % 2, j, :], gcat[:, c, j:j + T], j == 0, j == NTAP - 1,
                   ["diagA%d" % (c % 2), "diagD%d" % (c % 2), "gcat%d" % c], [nn])
            act(dwb[:, c, 0:T], bb[:, 0:T], AF.Identity, [nn], ["dwb%d" % c], bias=convbT(c))
            act(dwsq[:, c % 2, 0:T], bb[:, 0:T], AF.Square, [nn], ["dwsq%d" % (c % 2)], bias=convbT(c))
            if c % 4 == 0:
                sl_ga[0], sl_ga[1] = load_slot(8 + c // 4)
                sl_gb[0], sl_gb[1] = load_slot(10 + c // 4)
            cl = c % 4
            bg, ng = newbank()
            for kc in range(KC):
                mm(bg[:, 0:T], sl_ga[0][:, kc, cl * 128:(cl + 1) * 128], hTb[:, kc, 0:T], kc == 0, kc == KC - 1,
                   [sl_ga[1], hn], [ng])
            bs_, ns_ = newbank()
            for n_ in range(NTT):
                mm(bs_[:, n_ * 128:n_ * 128 + TR], vhat[0:TR, n_, c * 128:(c + 1) * 128], wmT[0:TR, c, 0:TR], True, True,
                   ["vhat", "wmT"], [ns_])
            act(sga[:, c % 2, 0:T], bg[:, 0:T], AF.Silu, [ng], ["sga%d" % (c % 2)], bias=bT[:, 16 + c:17 + c])
            tt("dve", pbuf[:, c % 2, 0:T], gu[:, c, 0:T], sga[:, c % 2, 0:T], ALU.mult, ["gu%d" % c, "sga%d" % (c % 2)],
               ["pbuf%d" % (c % 2)])
            if T == TB:
                in1 = Bh[:, c, :].unsqueeze(1).to_broadcast([128, 4, 128])
                in0 = bs_[:, :].rearrange("p (a b) -> p a b", a=4)
                o_ = nt1[:, :].rearrange("p (a b) -> p a b", a=4)
            else:
                in1 = Bh[:, c, 0:T]
                in0 = bs_[:, 0:T]
                o_ = nt1[:, 0:T]
            stt(o_, in0, lnvgT(c), in1, ALU.mult, ALU.add, [ns_, "Bh", "RT0"], ["nt1"])
            tt("dve", gu[:, c, 0:T], nt1[:, 0:T], pbuf[:, c % 2, 0:T], ALU.mult, ["nt1", "pbuf%d" % (c % 2)],
               ["gu%d" % c])
            bgb, ngb = newbank()
            for kc in range(KC):
                mm(bgb[:, 0:T], sl_gb[0][:, kc, cl * 128:(cl + 1) * 128], hTb[:, kc, 0:T], kc == 0, kc == KC - 1,
                   [sl_gb[1], hn], [ngb])
            act(sgb[:, c, 0:T], bgb[:, 0:T], AF.Silu, [ngb], ["sgb%d" % c], bias=bT[:, 40 + c:41 + c])
            if c >= 1:
                stats_mm(c - 1)
        stats_mm(KC - 1)
        if not last:
            cp("dve", gcat[:, :, 0:HIST], gcat[:, :, T:T + HIST], gnames, gnames)
        S.tag = "P7"
        mT = gcat
        p7 = {}

        def p7_slots(i):
            if ("m", i) not in p7:
                p7[("m", i)] = load_slot(12 + i)
                p7[("o", i)] = load_slot(16 + i)
            return p7[("m", i)], p7[("o", i)]

        def p7_part1(o):
            S.tag = "P7"
            i, ol = o // 2, o % 2
            (sl_m, sn_m), (sl_o, sn_o) = p7_slots(i)
            bma, nma = newbank()
            bmb, nmb = newbank()
            bpa, npa = newbank(hold=True)
            p7[("pa", o)] = (bpa, npa)
            for kc in range(KC):
                mm(bma[:, 0:T], sl_m[:, kc, ol * 128:(ol + 1) * 128], hTb[:, kc, 0:T], kc == 0, kc == KC - 1,
                   [sn_m, hn], [nma])
            for kc in range(KC):
                mm(bmb[:, 0:T], sl_m[:, kc, 256 + ol * 128:256 + (ol + 1) * 128], hTb[:, kc, 0:T], kc == 0,
                   kc == KC - 1, [sn_m, hn], [nmb])
            for c in range(KC):
                mm(bpa[:, 0:T], sl_o[:, c, ol * 128:(ol + 1) * 128], gu[:, c, 0:T], c == 0, c == KC - 1,
                   [sn_o, "gu%d" % c], [npa])
            act(tha[:, o % 4, 0:T], bma[:, 0:T], AF.Tanh, [nma, "bTh"], ["tha%d" % (o % 4)],
                bias=bTh[:, 48 + o:49 + o], scale=0.5)
            act(thb[:, o % 4, 0:T], bmb[:, 0:T], AF.Tanh, [nmb, "bTh"], ["thb%d" % (o % 4)],
                bias=bTh[:, 56 + o:57 + o], scale=0.5)

        def p7_part2(o):
            S.tag = "P7"
            i, ol = o // 2, o % 2
            (sl_m, sn_m), (sl_o, sn_o) = p7_slots(i)
            bpa, npa = p7[("pa", o)]
            bpb, npb_ = newbank()
            for c in range(KC):
                mm(bpb[:, 0:T], sl_o[:, c, 256 + ol * 128:256 + (ol + 1) * 128], sgb[:, c, 0:T], c == 0, c == KC - 1,
                   [sn_o, "sgb%d" % c], [npb_])
            stt(nt1[:, 0:T], tha[:, o % 4, 0:T], 1.0, bpa[:, 0:T], ALU.add, ALU.mult, ["tha%d" % (o % 4), npa], ["nt1"])
            stt(nt2[:, 0:T], thb[:, o % 4, 0:T], 1.0, bpb[:, 0:T], ALU.add, ALU.mult, ["thb%d" % (o % 4), npb_], ["nt2"])
            tt("dve", mT[:, o, HIST:HIST + T], nt1[:, 0:T], nt2[:, 0:T], ALU.add, ["nt1", "nt2"], ["gcat%d" % o])
            release(npa)

        p7_part1(0)
        p7_part1(1)
        p7_part1(2)
        S.tag = "LN_c"
        release(nsum)
        release(nsq)
        ts("dve", mu_bc[:, 0:T], bsum[:, 0:T], 1.0 / D, None, ALU.mult, None, [nsum], ["mu_bc"])
        tt("dve", nt1[:, 0:T], mu_bc[:, 0:T], mu_bc[:, 0:T], ALU.mult, ["mu_bc"], ["nt1"])
        stt(nt2[:, 0:T], bsq[:, 0:T], 1.0 / D, nt1[:, 0:T], ALU.mult, ALU.subtract, [nsq, "nt1"], ["nt2"])
        ts("dve", nt2[:, 0:T], nt2[:, 0:T], EPS, None, ALU.add, None, ["nt2"], ["nt2"])
        I32 = mybir.dt.int32
        rsi = rs_bc[:, 0:T].bitcast(I32)
        S.op("dve", lambda e: e.tensor_single_scalar(out=rsi, in_=nt2[:, 0:T].bitcast(I32), scalar=1,
                                                     op=ALU.arith_shift_right), reads=["nt2"], writes=["rs_bc"])
        ts("dve", rsi, rsi, -1, 0x5f3759df, ALU.mult, ALU.add, ["rs_bc"], ["rs_bc"])
        for it in range(3):
            tt("dve", nt1[:, 0:T], rs_bc[:, 0:T], rs_bc[:, 0:T], ALU.mult, ["rs_bc"], ["nt1"])
            tt("dve", nt1[:, 0:T], nt1[:, 0:T], nt2[:, 0:T], ALU.mult, ["nt1", "nt2"], ["nt1"])
            ts("dve", nt1[:, 0:T], nt1[:, 0:T], -0.5, 1.5, ALU.mult, ALU.add, ["nt1"], ["nt1"])
            tt("dve", rs_bc[:, 0:T], rs_bc[:, 0:T], nt1[:, 0:T], ALU.mult, ["rs_bc", "nt1"], ["rs_bc"])
        S.tag = "P6"
        for c in range(KC):
            tt("dve", nt1[:, 0:T], dwb[:, c, 0:T], mu_bc[:, 0:T], ALU.subtract, ["dwb%d" % c, "mu_bc"], ["nt1"])
            tt("dve", nt1[:, 0:T], nt1[:, 0:T], rs_bc[:, 0:T], ALU.mult, ["nt1", "rs_bc"], ["nt1"])
            act(nb[:, c % 2, 0:T], nt1[:, 0:T], AF.Silu, ["nt1", "RT0", "RT1"], ["nb%d" % (c % 2)], bias=lncbT(c),
                scale=lncgT(c))
            tt("dve", sgb[:, c, 0:T], nb[:, c % 2, 0:T], sgb[:, c, 0:T], ALU.mult, ["nb%d" % (c % 2), "sgb%d" % c],
               ["sgb%d" % c])
        for o in range(KC):
            p7_part2(o)
            if o + 3 < KC:
                p7_part1(o + 3)
            if nxt is not None and hoist:
                if o < 4:
                    if o == 0:
                        emit_P0_A(nxt, 0)
                    if o + 1 < nxt.NTT:
                        emit_P0_A(nxt, o + 1)
                    if o < nxt.NTT:
                        emit_P0_B(nxt, o)
                    if o == 3:
                        emit_P1_pre(nxt)
                else:
                    if o - 4 < nxt.NTT:
                        emit_P1_tile(nxt, o - 4)
        S.tag = "P8"
        slots_o = [load_slot(20), load_slot(21)]
        xrb = [(xr, "xr"), (gv, "gv")]

        def p8_A(t_):
            xb, xn_ = xrb[t_ % 2]
            dma("act", xb[0:TR, :], xsrc[t0 + t_ * 128:t0 + t_ * 128 + TR, :], [], [xn_], "xr_ld%d" % (t_ % 2))
            for half in range(2):
                sl, sn = slots_o[half]
                bb, nn = newbank()
                for kc in range(KC):
                    mm(bb[0:TR, :], mT[:, kc, HIST + t_ * 128:HIST + t_ * 128 + TR], sl[:, kc, :], kc == 0, kc == KC - 1,
                       ["gcat%d" % kc, sn], [nn])
                tmp, tmpn = (nt1, "nt1") if half == 0 else (nt2, "nt2")
                tt("dve", tmp[0:TR, :], bb[0:TR, :], gate_bc[0:TR, half * 512:(half + 1) * 512],
                   ALU.mult, [nn, "gate_bc"], [tmpn])
                tt("dve", xb[0:TR, half * 512:(half + 1) * 512], xb[0:TR, half * 512:(half + 1) * 512], tmp[0:TR, :],
                   ALU.add, [xn_, tmpn], [xn_])
            c8 = 8 + (t_ % 2)
            v8 = 8 + 2 * (t_ % 2)
            sn8, vn8 = "st_s8_%d" % (t_ % 2), "st_v8_%d" % (t_ % 2)
            act(junk[0:TR, :], xb[0:TR, :], AF.Square, [xn_], ["junk", sn8], accum_out=st_s[0:TR, c8:c8 + 1])

        def p8_A2(t_):
            c8 = 8 + (t_ % 2)
            v8 = 8 + 2 * (t_ % 2)
            sn8, vn8 = "st_s8_%d" % (t_ % 2), "st_v8_%d" % (t_ % 2)
            ts("dve", st_v[0:TR, v8:v8 + 1], st_s[0:TR, c8:c8 + 1], 1.0 / D, EPS, ALU.mult, ALU.add, [sn8], [vn8])
            tt("pool", st_v[0:TR, v8 + 1:v8 + 2], st_v[0:TR, v8:v8 + 1], negh[0:TR, 0:1], ALU.pow, [vn8, "negh"], [vn8])

        def p8_B(t_):
            xb, xn_ = xrb[t_ % 2]
            v8 = 8 + 2 * (t_ % 2)
            vn8 = "st_v8_%d" % (t_ % 2)
            stt(ybuf[0:TR, :], xb[0:TR, :], st_v[0:TR, v8 + 1:v8 + 2], fg_bc[0:TR, :], ALU.mult, ALU.mult,
                [xn_, vn8, "fg_bc"], ["ybuf"])
            dma("act", ydst[t0 + t_ * 128:t0 + t_ * 128 + TR, :], ybuf[0:TR, :], ["ybuf"], [], "ybuf_st")

        for t_ in range(NTT):
            p8_A(t_)
            if t_ >= 1:
                p8_B(t_ - 1)
            p8_A2(t_)
        p8_B(NTT - 1)
        if nxt is not None and not hoist:
            first_pass[0] = False
            for t_ in range(nxt.NTT):
                emit_P0_tile(nxt, t_)
            emit_P1_pre(nxt)
            for t_ in range(nxt.NTT):
                emit_P1_tile(nxt, t_)

    def emit_gate(seq):
        S.tag = "gate"
        for half in range(2):
            bb, nn = newbank()
            for k4 in range(4):
                kc = half * 4 + k4
                mm(bb[:, k4 * 128:(k4 + 1) * 128], gateh[:, kc, seq:seq + 1].to_broadcast([128, 128]), ident_f[:],
                   True, True, ["gateh", "ident_f"], [nn])
            act(gate_bc[:, half * 512:(half + 1) * 512], bb[:, :], AF.Identity, [nn], ["gate_bc"])

    emit_gate(blocks[0].seq)
    emit_front(blocks[0])
    cur_seq = blocks[0].seq
    for i, k in enumerate(blocks):
        nxt = blocks[i + 1] if i + 1 < len(blocks) else None
        if k.seq != cur_seq:
            emit_gate(k.seq)
            cur_seq = k.seq
        emit_body(k, nxt, hoist=(i > 0))
        first_pass[0] = False

    S.emit_all()
    print("sched stats:", S.stats)
    import os
    if os.environ.get("KDBG_TAGS"):
        import pickle
        tg = {}
        for o in S.ops:
            tg.setdefault(o.eng, []).append((o.tag, o.is_dma, o.ticket, o.w, o.r, o.idx, sorted(o.deps)))
        pickle.dump(tg, open(os.environ["KDBG_TAGS"], "wb"))
    st.close()
    return nc


_CACHE = {}


def _get_program(npb):
    if npb not in _CACHE:
        _CACHE[npb] = build_program(npb)
    return _CACHE[npb]


def kernel(x_prompt, x_sample, state_conv, c_prompt, c_sample, w_ada, b_ada, norm_g,
           w_in, b_in, ln_v_g, ln_v_b, w_spatial, b_spatial, conv_w, conv_b,
           ln_c_g, ln_c_b, w_o_a, w_o_b, w_out, final_g):
    f = lambda a: np.ascontiguousarray(np.asarray(a, dtype=np.float32))
    x_prompt = f(x_prompt)
    seqlen = x_prompt.shape[1]
    npb = seqlen // TB
    nc = _get_program(npb)
    ncores = 8
    shared = {
        "w_ada": f(w_ada)[0], "b_ada": f(b_ada)[0], "norm_g": f(norm_g)[0], "w_in": f(w_in)[0],
        "b_in": f(b_in)[0], "ln_v_g": f(ln_v_g)[0], "ln_v_b": f(ln_v_b)[0], "w_sp": f(w_spatial)[0],
        "b_sp": f(b_spatial)[0].reshape(-1), "conv_w": f(conv_w)[0], "conv_b": f(conv_b)[0],
        "ln_c_g": f(ln_c_g)[0], "ln_c_b": f(ln_c_b)[0], "w_o_a": f(w_o_a)[0], "w_o_b": f(w_o_b)[0],
        "w_out": f(w_out)[0], "final_g": f(final_g),
    }
    x_sample = f(x_sample)
    state_conv = f(state_conv)
    c_prompt = f(c_prompt)
    c_sample = f(c_sample)
    in_maps = []
    for i in range(ncores):
        m = dict(shared)
        m["xp"] = np.ascontiguousarray(x_prompt[2 * i:2 * i + 2])
        m["xs"] = np.ascontiguousarray(x_sample[i])
        m["sc"] = np.ascontiguousarray(state_conv[0, i])
        m["cc"] = np.ascontiguousarray(np.concatenate([c_prompt[2 * i:2 * i + 2], c_sample[i:i + 1]], axis=0))
        in_maps.append(m)
    res = run_bass_kernel_spmd(nc, in_maps, core_ids=list(range(ncores)))
    r = res.results
    y_prompt = np.concatenate([r[i]["yp"] for i in range(ncores)], axis=0)
    y_sample = np.stack([r[i]["ys"] for i in range(ncores)], axis=0)
    ncp = np.concatenate([r[i]["ncp"] for i in range(ncores)], axis=0)[None]
    ncs = np.stack([r[i]["ncs"] for i in range(ncores)], axis=0)[None]
    nvs = np.stack([r[i]["nvs"] for i in range(ncores)], axis=0)[None]
    return (y_prompt.astype(np.float32), y_sample.astype(np.float32), ncp.astype(np.float32),
            ncs.astype(np.float32), nvs.astype(np.float32))
```

```python
import contextlib
import numpy as np
import concourse.bass as bass
import concourse.mybir as mybir
from concourse.bass_utils import run_bass_kernel_spmd

F32 = mybir.dt.float32
BF16 = mybir.dt.bfloat16
AF = mybir.ActivationFunctionType
ALU = mybir.AluOpType

D = 1024
KC = 8
SEQ = 2048
TB = 512
DEC = 16
HIST = 30
NTAP = 31
EPS = 1e-6
NSLOT = 22
SLOT_ELEMS = KC * 512

COMPUTE = ("pe", "act", "dve", "pool")


class _Tile:
    __slots__ = ("name", "last_w", "readers")

    def __init__(self, name):
        self.name = name
        self.last_w = None
        self.readers = []


class _Op:
    __slots__ = ("idx", "eng", "emit", "deps", "is_dma", "sem_key", "dma_val",
                 "needs_inc", "ticket", "n_dma", "tag", "w", "r", "nowaw")

    def __init__(self):
        self.deps = set()
        self.needs_inc = False
        self.ticket = None
        self.is_dma = False
        self.nowaw = False
        self.sem_key = None
        self.dma_val = None
        self.n_dma = 1


class Sched:
    def __init__(self, nc):
        self.nc = nc
        self.ops = []
        self.tiles = {}
        self.wait_all_keys = set()
        self.tag = ""

    def _tile(self, name):
        t = self.tiles.get(name)
        if t is None:
            t = _Tile(name)
            self.tiles[name] = t
        return t

    def _track(self, op, reads, writes, nowaw=False):
        for n in reads:
            t = self._tile(n)
            if t.last_w is not None:
                op.deps.add(t.last_w)
            t.readers.append(op.idx)
        for n in writes:
            t = self._tile(n)
            if t.last_w is not None:
                lw = self.ops[t.last_w]
                if not (nowaw and lw.eng == op.eng and getattr(lw, "nowaw", False) and not lw.is_dma):
                    op.deps.add(t.last_w)
            for r in t.readers:
                if r != op.idx:
                    op.deps.add(r)
            t.last_w = op.idx
            t.readers = []
        op.deps.discard(op.idx)

    def op(self, eng, emit, reads=(), writes=(), nowaw=False):
        o = _Op()
        o.idx = len(self.ops)
        o.eng = eng
        o.emit = emit
        o.tag = self.tag
        o.nowaw = nowaw
        o.w = tuple(writes); o.r = tuple(reads)
        self.ops.append(o)
        self._track(o, reads, writes, nowaw)
        return o

    def dma(self, queue, emit, reads=(), writes=(), sem_key=None, n_dma=1):
        o = _Op()
        o.idx = len(self.ops)
        o.eng = queue
        o.emit = emit
        o.is_dma = True
        o.sem_key = sem_key
        o.n_dma = n_dma
        o.tag = self.tag
        o.w = tuple(writes); o.r = tuple(reads)
        self.ops.append(o)
        self._track(o, reads, writes)
        return o

    def emit_all(self):
        nc = self.nc
        ops = self.ops
        for o in ops:
            latest = {}
            keep = set()
            for d in o.deps:
                p = ops[d]
                if p.is_dma:
                    keep.add(d)
                elif latest.get(p.eng, -1) < d:
                    latest[p.eng] = d
            keep.update(latest.values())
            o.deps = keep
        for o in ops:
            for d in o.deps:
                p = ops[d]
                if p.is_dma:
                    continue
                if p.eng == "pe" and o.eng == "pe" and not o.is_dma:
                    continue
                p.needs_inc = True
        cnt = {e: 0 for e in COMPUTE}
        for o in ops:
            if o.is_dma:
                continue
            if o.needs_inc:
                cnt[o.eng] += 1
                o.ticket = cnt[o.eng]
        dkeys = {}
        for o in ops:
            if o.is_dma:
                k = o.sem_key
                dkeys[k] = dkeys.get(k, 0) + 16 * o.n_dma
                o.dma_val = dkeys[k]
        for o in ops:
            if o.is_dma and o.sem_key in self.wait_all_keys:
                o.dma_val = dkeys[o.sem_key]
        self.stats = dict(cnt)
        self.stats["n_ops"] = len(ops)
        self.stats["dma_keys"] = len(dkeys)
        with contextlib.ExitStack() as st:
            esem = {e: st.enter_context(nc.semaphore("s_" + e)) for e in COMPUTE}
            dsem = {k: st.enter_context(nc.semaphore("d_%d" % i)) for i, k in enumerate(dkeys)}
            block = st.enter_context(nc.Block())
            streams = {}
            for o in ops:
                streams.setdefault(o.eng, []).append(o)

            def run(engname, e):
                waited = {}
                for o in streams.get(engname, []):
                    need = {}
                    for d in o.deps:
                        p = ops[d]
                        if p.is_dma:
                            if o.is_dma and o.sem_key == p.sem_key and p.sem_key in self.wait_all_keys:
                                continue
                            s = dsem[p.sem_key]
                            v = p.dma_val
                        else:
                            if p.eng == "pe" and engname == "pe" and not o.is_dma:
                                continue
                            s = esem[p.eng]
                            v = p.ticket
                        key = id(s)
                        if need.get(key, (None, 0))[1] < v:
                            need[key] = (s, v)
                    for key, (s, v) in need.items():
                        if waited.get(key, 0) >= v:
                            continue
                        e.wait_ge(s, v)
                        waited[key] = v
                    r = o.emit(e)
                    if o.is_dma:
                        s = dsem[o.sem_key]
                        assert len(r) == o.n_dma, (len(r), o.n_dma)
                        for ins in r:
                            ins.then_inc(s, 16)
                    elif o.needs_inc:
                        r.then_inc(esem[engname], 1)
                last = {}
                for o in streams.get(engname, []):
                    if o.is_dma:
                        last[o.sem_key] = max(last.get(o.sem_key, 0), o.dma_val)
                for k, v in last.items():
                    e.wait_ge(dsem[k], v)

            if "pe" in streams:
                @block.tensor
                def _(e):
                    run("pe", e)
            if "act" in streams:
                @block.scalar
                def _(e):
                    run("act", e)
            if "dve" in streams:
                @block.vector
                def _(e):
                    run("dve", e)
            if "pool" in streams:
                @block.gpsimd
                def _(e):
                    run("pool", e)
            if "sp" in streams:
                @block.sync
                def _(e):
                    run("sp", e)


def build_program(npb=4, nring=5, diag_act_every=2):
    seqlen = npb * TB
    nc = bass.Bass("TRN2", target_bir_lowering=False)

    def din(name, shape):
        return nc.dram_tensor(name, shape, F32, kind="ExternalInput").ap()

    def dout(name, shape):
        return nc.dram_tensor(name, shape, F32, kind="ExternalOutput").ap()

    xp = din("xp", [2, seqlen, D])
    xs = din("xs", [DEC, D])
    scv = din("sc", [HIST, D])
    cc = din("cc", [3, D])
    w_ada = din("w_ada", [D, 3 * D])
    b_ada = din("b_ada", [3 * D])
    norm_g = din("norm_g", [D])
    w_in = din("w_in", [D, 8 * D])
    b_in = din("b_in", [8 * D])
    ln_v_g = din("ln_v_g", [D])
    ln_v_b = din("ln_v_b", [D])
    w_sp = din("w_sp", [8, 128, 128])
    b_sp = din("b_sp", [8 * 128])
    conv_w = din("conv_w", [NTAP, D])
    conv_b = din("conv_b", [D])
    ln_c_g = din("ln_c_g", [D])
    ln_c_b = din("ln_c_b", [D])
    w_o_a = din("w_o_a", [D, D])
    w_o_b = din("w_o_b", [D, D])
    w_out = din("w_out", [D, D])
    final_g = din("final_g", [D])
    yp = dout("yp", [2, seqlen, D])
    ys = dout("ys", [DEC, D])
    ncp = dout("ncp", [2, HIST, D])
    ncs = dout("ncs", [HIST, D])
    nvs = dout("nvs", [DEC, D])
    wscr = nc.dram_tensor("wscr", [NSLOT, 128, SLOT_ELEMS], BF16).ap()

    S = Sched(nc)
    S.wait_all_keys.add("setup")
    st = contextlib.ExitStack()

    def sb(name, shape, dt):
        return st.enter_context(nc.sbuf_tensor(name, shape, dt))

    ring = sb("ring", [128, nring, KC, 512], BF16)
    xin = sb("xin", [128, 4, D], F32)
    junk = sb("junk", [128, D], BF16)
    xn = sb("xn", [128, 2, D], BF16)
    hT = sb("hT", [128, 2, KC, TB], BF16)
    gv = sb("gv", [128, D], F32)
    vhat = sb("vhat", [128, 4, D], BF16)
    gu = sb("gu", [128, KC, TB], BF16)
    gcat = sb("gcat", [128, KC, HIST + TB], BF16)
    ah = sb("ah", [128, 2, TB], F32)
    th = sb("th", [128, 2, TB], F32)
    gtl = sb("gtl", [128, KC, 32], F32)
    diag = sb("diag", [128, 2, NTAP, 128], BF16)
    dwb = sb("dwb", [128, KC, TB], BF16)
    dwsq = sb("dwsq", [128, 2, TB], BF16)
    mu_bc = sb("mu_bc", [128, TB], F32)
    rs_bc = sb("rs_bc", [128, TB], F32)
    nt1 = sb("nt1", [128, TB], F32)
    nt2 = sb("nt2", [128, TB], F32)
    sga = sb("sga", [128, 2, TB], BF16)
    pbuf = sb("pbuf", [128, 2, TB], BF16)
    sgb = sb("sgb", [128, KC, TB], BF16)
    nb = sb("nb", [128, 2, TB], BF16)
    tha = sb("tha", [128, 4, TB], BF16)
    thb = sb("thb", [128, 4, TB], BF16)
    xr = sb("xr", [128, D], F32)
    ybuf = sb("ybuf", [128, D], F32)
    ident_f = sb("ident_f", [128, 128], F32)
    ident_b = sb("ident_b", [128, 128], BF16)
    ones_f = sb("ones_f", [128, 128], F32)
    ones_b = sb("ones_b", [128, 128], BF16)
    negh = sb("negh", [128, 8], F32)
    R0 = sb("R0", [128, 128], F32)
    R1 = sb("R1", [128, 128], F32)
    RT0 = sb("RT0", [128, 128], F32)
    RT1 = sb("RT1", [128, 32], F32)
    bTh = sb("bTh", [128, 64], F32)
    cwT = sb("cwT", [128, NTAP * 8], F32)
    scT = sb("scT", [128, HIST * 8], F32)
    wmT = sb("wmT", [128, KC, 128], BF16)
    wmf = sb("wmf", [128, 128], F32)
    Bh = sb("Bh", [128, KC, 128], F32)
    siluT = sb("siluT", [128, 24], F32)
    modT = sb("modT", [128, 24, 3], F32)
    aT = sb("aT", [128, KC, 3], F32)
    gateh = sb("gateh", [128, KC, 3], F32)
    glh = sb("glh", [128, 128], F32)
    gate_bc = sb("gate_bc", [128, D], F32)
    fg_bc = sb("fg_bc", [128, D], F32)
    bv2 = sb("bv2", [2, D], BF16)
    lo_row = sb("lo_row", [1, D], BF16)
    ones2 = sb("ones2", [2, 128], BF16)
    st_s = sb("st_s", [128, 16], F32)
    st_v = sb("st_v", [128, 16], F32)
    print("sbuf bytes remaining:", nc.sbuf_bytes_remaining)

    banks = [st.enter_context(nc.psum_tensor("bank%d" % i, [128, 512], F32)) for i in range(8)]
    bank_ctr = [0]
    held = set()

    def newbank(hold=False):
        while True:
            i = bank_ctr[0] % 8
            bank_ctr[0] += 1
            if i not in held:
                break
        if hold:
            held.add(i)
        return banks[i], "bank%d" % i

    def release(name):
        held.discard(int(name[4:]))

    def mm(out, lhsT, rhs, start, stop, reads, writes):
        S.op("pe", lambda e: e.matmul(out, lhsT=lhsT, rhs=rhs, start=start, stop=stop),
             reads=reads, writes=writes)

    def tr(out, in_, ident, reads, writes):
        S.op("pe", lambda e: e.transpose(out, in_, ident), reads=reads, writes=writes)

    def act(out, in_, func, reads, writes, bias=None, scale=None, accum_out=None, nowaw=False):
        kw = {}
        if bias is not None:
            kw["bias"] = bias
        if scale is not None:
            kw["scale"] = scale
        if accum_out is not None:
            kw["accum_out"] = accum_out
        S.op("act", lambda e: e.activation(out=out, in_=in_, func=func, **kw), reads=reads, writes=writes, nowaw=nowaw)

    def ts(eng, out, in0, s1, s2, op0, op1, reads, writes, nowaw=False):
        if op1 is None:
            S.op(eng, lambda e: e.tensor_scalar(out=out, in0=in0, scalar1=s1, scalar2=None, op0=op0),
                 reads=reads, writes=writes, nowaw=nowaw)
        else:
            S.op(eng, lambda e: e.tensor_scalar(out=out, in0=in0, scalar1=s1, scalar2=s2, op0=op0, op1=op1),
                 reads=reads, writes=writes)

    def tt(eng, out, in0, in1, op, reads, writes):
        S.op(eng, lambda e: e.tensor_tensor(out=out, in0=in0, in1=in1, op=op), reads=reads, writes=writes)

    def stt(out, in0, scalar, in1, op0, op1, reads, writes):
        S.op("dve", lambda e: e.scalar_tensor_tensor(out=out, in0=in0, scalar=scalar, in1=in1, op0=op0, op1=op1),
             reads=reads, writes=writes)

    def cp(eng, out, in_, reads, writes):
        S.op(eng, lambda e: e.tensor_copy(out, in_), reads=reads, writes=writes)

    def dma(q, out, in_, reads, writes, key):
        S.dma(q, lambda e: [e.dma_start(out=out, in_=in_)], reads=reads, writes=writes, sem_key=key)

    def rows128(v):
        return v.rearrange("(r k) -> r k", k=128)

    S.op("pool", lambda e: e.memset(ident_f[:], 1.0), writes=["ident_f"])
    S.op("pool", lambda e: e.affine_select(out=ident_f[:], in_=ident_f[:], pattern=[[-1, 128]],
                                           compare_op=ALU.is_equal, fill=0.0, base=0, channel_multiplier=1),
         reads=["ident_f"], writes=["ident_f"])
    S.op("pool", lambda e: e.memset(ones_f[:], 1.0), writes=["ones_f"])
    S.op("pool", lambda e: e.memset(negh[:], -0.5), writes=["negh"])
    cp("dve", ident_b[:], ident_f[:], ["ident_f"], ["ident_b"])
    cp("dve", ones_b[:], ones_f[:], ["ones_f"], ["ones_b"])
    cp("dve", ones2[:], ones_f[0:2, :], ["ones_f"], ["ones2"])

    dma("sp", R0[0:64, :], rows128(b_in), [], ["R0"], "setup")
    dma("sp", R0[64:88, :], rows128(b_ada), [], ["R0"], "setup")
    dma("sp", R0[88:96, :], rows128(norm_g), [], ["R0"], "setup")
    dma("sp", R0[96:104, :], rows128(ln_v_g), [], ["R0"], "setup")
    dma("sp", R0[104:112, :], rows128(ln_v_b), [], ["R0"], "setup")
    dma("sp", R0[112:120, :], rows128(conv_b), [], ["R0"], "setup")
    dma("sp", R0[120:128, :], rows128(ln_c_g), [], ["R0"], "setup")
    dma("sp", R1[0:8, :], rows128(ln_c_b), [], ["R1"], "setup")
    dma("sp", R1[8:32, :], cc.rearrange("s (c k) -> (s c) k", k=128), [], ["R1"], "setup")
    dma("sp", fg_bc[:], final_g.rearrange("(o n) -> o n", o=1).partition_broadcast(128), [], ["fg_bc"], "setup")
    vh_f = vhat[:].rearrange("p a b -> p (a b)").bitcast(F32)
    gu_f = gu[:].rearrange("p a b -> p (a b)").bitcast(F32)
    lnvb_row = vh_f[0:1, 0:D]
    bs_row = vh_f[0:1, D:2 * D]
    rs_row = gu_f[0:1, 0:D]
    bv_row = gu_f[0:1, D:2 * D]
    dma("sp", lnvb_row, ln_v_b.rearrange("(o n) -> o n", o=1), [], ["vhat"], "setup")
    dma("sp", bs_row, b_sp.rearrange("(o n) -> o n", o=1), [], ["vhat"], "setup")
    dma("sp", bv_row, b_in[D:2 * D].rearrange("(o n) -> o n", o=1), [], ["gu"], "setup")

    def wsrc(slot):
        def cols(w, c0, n):
            return w[:, c0:c0 + n].rearrange("(k p) n -> p k n", p=128)
        if slot in (0, 1):
            return [(0, 512, cols(w_in, D + slot * 512, 512))]
        if slot in (2, 3):
            return [(0, 512, cols(w_in, (slot - 2) * 512, 512))]
        if 4 <= slot <= 7:
            i = slot - 4
            return [(0, 256, cols(w_in, 3 * D + i * 256, 256)), (256, 256, cols(w_in, 4 * D + i * 256, 256))]
        if slot in (8, 9):
            return [(0, 512, cols(w_in, 2 * D + (slot - 8) * 512, 512))]
        if slot in (10, 11):
            return [(0, 512, cols(w_in, 5 * D + (slot - 10) * 512, 512))]
        if 12 <= slot <= 15:
            i = slot - 12
            return [(0, 256, cols(w_in, 6 * D + i * 256, 256)), (256, 256, cols(w_in, 7 * D + i * 256, 256))]
        if 16 <= slot <= 19:
            i = slot - 16
            return [(0, 256, cols(w_o_a, i * 256, 256)), (256, 256, cols(w_o_b, i * 256, 256))]
        return [(0, 512, cols(w_out, (slot - 20) * 512, 512))]

    def emit_conversion():
        S.tag = "conv"
        stg = []
        for h in range(2):
            v = ring[:, nring - 2 + h].rearrange("p k n -> p (k n)").bitcast(F32)
            stg.append((v, "ring%d" % (nring - 2 + h)))
        cobs = [(ring[:, nring - 3], "ring%d" % (nring - 3)), (hT[:, 1], "hT1")]
        for slot in range(NSLOT):
            srcs = wsrc(slot)
            cob, cn = cobs[slot % 2]
            for h in range(2):
                sv, sn = stg[h]
                if len(srcs) == 2:
                    (c0, n, src) = srcs[h]
                    view = sv.rearrange("p (k n) -> p k n", k=KC)
                    dstv = cob[:, :, c0:c0 + n]
                    dma("sp", view, src, [], [sn], "stg%d" % h)
                else:
                    (c0, n, src) = srcs[0]
                    view = sv.rearrange("p (k n) -> p k n", k=KC // 2)
                    dstv = cob[:, h * 4:(h + 1) * 4, :]
                    dma("sp", view, src[:, h * 4:(h + 1) * 4, :], [], [sn], "stg%d" % h)
                if h == 0:
                    act(dstv, view, AF.Identity, [sn], [cn])
                else:
                    cp("dve", dstv, view, [sn], [cn])
            dma("act", wscr[slot], cob.rearrange("p k n -> p (k n)"), [cn], ["wscr%d" % slot], "wscr_w%d" % slot)

    b0, n0 = newbank()
    tr(b0[:, 0:128], R0[:], ident_f[:], ["R0", "ident_f"], [n0])
    act(RT0[:], b0[:, 0:128], AF.Identity, [n0], ["RT0"])
    b1, n1 = newbank()
    tr(b1[:, 0:32], R1[0:32, :], ident_f[0:32, 0:32], ["R1", "ident_f"], [n1])
    act(RT1[:], b1[:, 0:32], AF.Identity, [n1], ["RT1"])
    ts("dve", bTh[:], RT0[:, 0:64], 0.5, None, ALU.mult, None, ["RT0"], ["bTh"])
    bT = RT0
    b_adaT = lambda jc: RT0[:, 64 + jc:65 + jc]
    norm_gT = lambda c: RT0[:, 88 + c:89 + c]
    lnvgT = lambda c: RT0[:, 96 + c:97 + c]
    convbT = lambda c: RT0[:, 112 + c:113 + c]
    lncgT = lambda c: RT0[:, 120 + c:121 + c]
    lncbT = lambda c: RT1[:, c:c + 1]
    act(siluT[:], RT1[:, 8:32], AF.Silu, ["RT1"], ["siluT"])

    cw_rows = conv_w.rearrange("j (c k) -> (j c) k", k=128)
    dma("act", R0[:], cw_rows[0:128, :], ["RT0"], ["R0"], "cw0")
    dma("act", R1[0:120, :], cw_rows[128:248, :], ["RT1"], ["R1"], "cw1")
    b2, n2 = newbank()
    tr(b2[:, 0:128], R0[:], ident_f[:], ["R0", "ident_f"], [n2])
    tr(b2[:, 128:248], R1[0:120, :], ident_f[0:120, 0:120], ["R1", "ident_f"], [n2])
    act(cwT[:], b2[:, 0:248], AF.Identity, [n2], ["cwT"])
    sc_rows = scv.rearrange("r (c k) -> (r c) k", k=128)
    dma("act", R0[:], sc_rows[0:128, :], [], ["R0"], "sc0")
    dma("act", R1[0:112, :], sc_rows[128:240, :], [], ["R1"], "sc1")
    b3, n3 = newbank()
    tr(b3[:, 0:128], R0[:], ident_f[:], ["R0", "ident_f"], [n3])
    tr(b3[:, 128:240], R1[0:112, :], ident_f[0:112, 0:112], ["R1", "ident_f"], [n3])
    act(scT[:], b3[:, 0:240], AF.Identity, [n3], ["scT"])

    for h in range(8):
        dma("act", R0[:], w_sp[h], [], ["R0"], "wsp")
        bb, nn = newbank()
        tr(bb[:, 0:128], R0[:], ident_f[:], ["R0", "ident_f"], [nn])
        act(wmf[:], bb[:, 0:128], AF.Identity, [nn], ["wmf"])
        S.op("pool", lambda e: e.affine_select(out=wmf[:], in_=wmf[:], pattern=[[1, 128]],
                                               compare_op=ALU.is_ge, fill=0.0, base=0, channel_multiplier=-1),
             reads=["wmf"], writes=["wmf"])
        cp("dve", wmT[:, h, :], wmf[:], ["wmf"], ["wmT"])
        bb2, nn2 = newbank()
        mm(bb2[0:1, 0:128], ones_f[:, 0:1], wmf[:], True, True, ["ones_f", "wmf"], [nn2])
        act(rs_row[:, h * 128:(h + 1) * 128], bb2[0:1, 0:128], AF.Identity, [nn2], ["gu"])
        bb3, nn3 = newbank()
        mm(bb3[:, 0:128], lnvb_row[:, h * 128:(h + 1) * 128], rs_row[:, h * 128:(h + 1) * 128], True, False,
           ["vhat", "gu"], [nn3])
        mm(bb3[:, 0:128], ones_f[0:1, :], bs_row[:, h * 128:(h + 1) * 128], False, True,
           ["ones_f", "vhat"], [nn3])
        act(Bh[:, h, :], bb3[:, 0:128], AF.Identity, [nn3], ["Bh"])

    cp("dve", bv2[0:1, :], bv_row, ["gu"], ["bv2"])
    tt("dve", nt1[0:1, 0:512], bv_row[:, 0:512], bv2[0:1, 0:512], ALU.subtract, ["gu", "bv2"], ["nt1"])
    tt("dve", nt2[0:1, 0:512], bv_row[:, 512:1024], bv2[0:1, 512:1024], ALU.subtract, ["gu", "bv2"], ["nt2"])
    cp("dve", lo_row[:, 0:512], nt1[0:1, 0:512], ["nt1"], ["lo_row"])
    cp("dve", lo_row[:, 512:1024], nt2[0:1, 0:512], ["nt2"], ["lo_row"])
    dma("act", bv2[1:2, :], lo_row[:], ["lo_row"], ["bv2"], "bv2lo")

    sgb_names = ["sgb%d" % c for c in range(KC)]
    dwb_names = ["dwb%d" % c for c in range(KC)]
    ada_bufs = [
        (sgb[:].rearrange("p a b -> p (a b)").bitcast(F32).rearrange("p (k n) -> p k n", k=KC), sgb_names, "ada0"),
        (dwb[:].rearrange("p a b -> p (a b)").bitcast(F32).rearrange("p (k n) -> p k n", k=KC), dwb_names, "ada1"),
    ]
    for pc in range(12):
        ab, an, ak = ada_bufs[pc % 2]
        dma("act", ab, w_ada[:, pc * 256:(pc + 1) * 256].rearrange("(k p) n -> p k n", p=128), [], an, ak)
        for jl in range(2):
            jc = pc * 2 + jl
            bb, nn = newbank()
            for kc in range(KC):
                mm(bb[:, 0:3], ab[:, kc, jl * 128:(jl + 1) * 128], siluT[:, kc:24:8], kc == 0, kc == KC - 1,
                   an + ["siluT"], [nn])
            act(modT[:, jc, :], bb[:, 0:3], AF.Identity, [nn], ["modT"], bias=b_adaT(jc))
    for kc in range(KC):
        ts("dve", aT[:, kc, :], modT[:, 8 + kc, :], 1.0, norm_gT(kc), ALU.add, ALU.mult, ["modT", "RT0"], ["aT"])
        ts("dve", gateh[:, kc, :], modT[:, 16 + kc, :], 0.5, None, ALU.mult, None, ["modT"], ["gateh"])
    shT = lambda kc, s: modT[:, kc, s:s + 1]

    slot_ctr = [0]
    first_pass = [True]
    stg = []
    for h_ in range(2):
        v_ = ring[:, nring - 2 + h_].rearrange("p k n -> p (k n)").bitcast(F32)
        stg.append((v_, "ring%d" % (nring - 2 + h_), "stg%d" % h_))
    stg.append((hT[:, 1].rearrange("p k n -> p (k n)").bitcast(F32), "hT1", "stg2"))
    stg_ctr = [0]

    def load_slot(slot):
        nr = (nring - 2) if first_pass[0] else nring
        r = slot_ctr[0] % nr
        slot_ctr[0] += 1
        rn = "ring%d" % r
        dst = ring[:, r]
        if first_pass[0]:
            srcs = wsrc(slot)
            for h in range(2):
                sv, sn, sk = stg[stg_ctr[0] % len(stg)]
                stg_ctr[0] += 1
                if len(srcs) == 2:
                    (c0, n, src) = srcs[h]
                    view = sv.rearrange("p (k n) -> p k n", k=KC)
                    dstv = dst[:, :, c0:c0 + n]
                    dma("sp", view, src, [], [sn], sk)
                else:
                    (c0, n, src) = srcs[0]
                    view = sv.rearrange("p (k n) -> p k n", k=KC // 2)
                    dstv = dst[:, h * 4:(h + 1) * 4, :]
                    dma("sp", view, src[:, h * 4:(h + 1) * 4, :], [], [sn], sk)
                if h == 0:
                    act(dstv, view, AF.Identity, [sn], [rn])
                else:
                    cp("dve", dstv, view, [sn], [rn])
            dma("act", wscr[slot], dst.rearrange("p k n -> p (k n)"), [rn], ["wscr%d" % slot], "wscr_w%d" % slot)
        else:
            dma("sp", dst.rearrange("p k n -> p (k n)"), wscr[slot], ["wscr%d" % slot], [rn], "ringld%d" % r)
        return dst, rn

    class Blk:
        pass

    blocks = []
    for sq in range(2):
        for b in range(npb):
            k = Blk()
            k.seq, k.xsrc, k.ydst, k.t0, k.T = sq, xp[sq], yp[sq], b * TB, TB
            k.first, k.last, k.is_sample = (b == 0), (b == npb - 1), False
            blocks.append(k)
    k = Blk()
    k.seq, k.xsrc, k.ydst, k.t0, k.T = 2, xs, ys, 0, DEC
    k.first, k.last, k.is_sample = True, True, True
    blocks.append(k)
    for i, k in enumerate(blocks):
        k.idx = i
        k.TR = min(k.T, 128)
        k.NTT = max(1, k.T // 128)
        k.h = i % 2
        k.hn = "hT%d" % k.h

    def emit_xload(k):
        for t_ in range(k.NTT):
            dma("sp", xin[0:k.TR, t_, :], k.xsrc[k.t0 + t_ * 128:k.t0 + t_ * 128 + k.TR, :], [], ["xin%d" % t_],
                "xin%d" % t_)

    def emit_P0_A(k, t_):
        S.tag = "P0"
        TR = k.TR
        act(junk[0:TR, :], xin[0:TR, t_, :], AF.Square, ["xin%d" % t_], ["junk", "st_s0"], accum_out=st_s[0:TR, 0:1])
        ts("dve", st_v[0:TR, 0:1], st_s[0:TR, 0:1], 1.0 / D, EPS, ALU.mult, ALU.add, ["st_s0"], ["st_v0"])
        tt("pool", st_v[0:TR, 1:2], st_v[0:TR, 0:1], negh[0:TR, 0:1], ALU.pow, ["st_v0", "negh"], ["st_v0"])
        ts("dve", xn[0:TR, t_ % 2, :], xin[0:TR, t_, :], st_v[0:TR, 1:2], None, ALU.mult, None,
           ["xin%d" % t_, "st_v0"], ["xn%d" % (t_ % 2)])

    def emit_P0_B(k, t_):
        S.tag = "P0"
        TR, s = k.TR, k.seq
        hTb = hT[:, k.h]
        bb, nn = newbank()
        bbf = bb[:].bitcast(BF16)
        for kc in range(KC):
            tr(bbf[:, kc * 128:kc * 128 + TR], xn[0:TR, t_ % 2, kc * 128:(kc + 1) * 128], ident_b[0:TR, 0:TR],
               ["xn%d" % (t_ % 2), "ident_b"], [nn])
        for kc in range(KC):
            act(hTb[:, kc, t_ * 128:t_ * 128 + TR], bbf[:, kc * 128:kc * 128 + TR], AF.Identity, [nn, "aT", "modT"],
                [k.hn], bias=shT(kc, s), scale=aT[:, kc, s:s + 1], nowaw=True)

    def emit_P0_tile(k, t_):
        emit_P0_A(k, t_)
        emit_P0_B(k, t_)

    def emit_P1_pre(k):
        S.tag = "P1"
        k.slots_v = [load_slot(0), load_slot(1)]
        if k.is_sample:
            dma("sp", xr[0:DEC, :], ln_v_g.rearrange("(o n) -> o n", o=1).partition_broadcast(DEC), [], ["xr"], "xr")
            dma("sp", ybuf[0:DEC, :], ln_v_b.rearrange("(o n) -> o n", o=1).partition_broadcast(DEC), [], ["ybuf"],
                "ybuf_ld")

    def emit_P1_tile(k, t_):
        S.tag = "P1"
        TR = k.TR
        hTb = hT[:, k.h]
        for half in range(2):
            sl, sn = k.slots_v[half]
            bb, nn = newbank()
            for kc in range(KC):
                mm(bb[0:TR, :], hTb[:, kc, t_ * 128:t_ * 128 + TR], sl[:, kc, :], kc == 0, False, [k.hn, sn], [nn])
            mm(bb[0:TR, :], ones2[0:2, 0:TR], bv2[0:2, half * 512:(half + 1) * 512], False, True,
               ["ones2", "bv2"], [nn])
            act(gv[0:TR, half * 512:(half + 1) * 512], bb[0:TR, :], AF.Gelu_apprx_tanh, [nn], ["gv", "st_s1"],
                accum_out=st_s[0:TR, 2 + half:3 + half])
            act(junk[0:TR, 0:512], gv[0:TR, half * 512:(half + 1) * 512], AF.Square, ["gv"], ["junk", "st_s1"],
                accum_out=st_s[0:TR, 4 + half:5 + half])
        sv = ["st_v1"]
        tt("dve", st_v[0:TR, 2:3], st_s[0:TR, 2:3], st_s[0:TR, 3:4], ALU.add, ["st_s1"], sv)
        tt("dve", st_v[0:TR, 3:4], st_s[0:TR, 4:5], st_s[0:TR, 5:6], ALU.add, ["st_s1"], sv)
        ts("dve", st_v[0:TR, 2:4], st_v[0:TR, 2:4], 1.0 / D, None, ALU.mult, None, sv, sv)
        tt("dve", st_v[0:TR, 4:5], st_v[0:TR, 2:3], st_v[0:TR, 2:3], ALU.mult, sv, sv)
        tt("dve", st_v[0:TR, 5:6], st_v[0:TR, 3:4], st_v[0:TR, 4:5], ALU.subtract, sv, sv)
        ts("dve", st_v[0:TR, 5:6], st_v[0:TR, 5:6], EPS, None, ALU.add, None, sv, sv)
        tt("pool", st_v[0:TR, 6:7], st_v[0:TR, 5:6], negh[0:TR, 0:1], ALU.pow, sv + ["negh"], sv)
        ts("dve", st_v[0:TR, 7:8], st_v[0:TR, 2:3], st_v[0:TR, 6:7], -1.0, ALU.mult, ALU.mult, sv, sv)
        ts("dve", vhat[0:TR, t_, :], gv[0:TR, :], st_v[0:TR, 6:7], st_v[0:TR, 7:8], ALU.mult, ALU.add,
           ["gv"] + sv, ["vhat"])
        if k.is_sample:
            ts("dve", gv[0:TR, :], gv[0:TR, :], st_v[0:TR, 6:7], st_v[0:TR, 7:8], ALU.mult, ALU.add,
               ["gv"] + sv, ["gv"])
            tt("dve", gv[0:TR, :], gv[0:TR, :], xr[0:TR, :], ALU.mult, ["gv", "xr"], ["gv"])
            tt("dve", gv[0:TR, :], gv[0:TR, :], ybuf[0:TR, :], ALU.add, ["gv", "ybuf"], ["gv"])
            dma("act", nvs, gv[0:DEC, :], ["gv"], [], "nvs")

    def emit_front(k):
        emit_xload(k)
        for t_ in range(k.NTT):
            emit_P0_tile(k, t_)
        emit_P1_pre(k)
        for t_ in range(k.NTT):
            emit_P1_tile(k, t_)

    def emit_body(k, nxt, hoist):
        seq, xsrc, ydst, t0, T, first, last, is_sample = k.seq, k.xsrc, k.ydst, k.t0, k.T, k.first, k.last, k.is_sample
        TR, NTT, s = k.TR, k.NTT, k.seq
        hTb = hT[:, k.h]
        hn = k.hn
        S.tag = "P2"
        for sidx in (2, 3):
            sl, sn = load_slot(sidx)
            for cl in range(4):
                c = (sidx - 2) * 4 + cl
                bb, nn = newbank()
                for kc in range(KC):
                    mm(bb[:, 0:T], sl[:, kc, cl * 128:(cl + 1) * 128], hTb[:, kc, 0:T], kc == 0, kc == KC - 1,
                       [sn, hn], [nn])
                act(gu[:, c, 0:T], bb[:, 0:T], AF.Gelu_apprx_tanh, [nn], ["gu%d" % c], bias=bT[:, c:c + 1])

        def build_diag(c):
            for j in range(NTAP):
                if diag_act_every and (j % diag_act_every == diag_act_every - 1):
                    act(diag[:, c % 2, j, :], ident_b[:], AF.Identity, ["ident_b", "cwT"], ["diagA%d" % (c % 2)],
                        scale=cwT[:, j * 8 + c:j * 8 + c + 1], nowaw=True)
                else:
                    ts("dve", diag[:, c % 2, j, :], ident_b[:], cwT[:, j * 8 + c:j * 8 + c + 1], None, ALU.mult, None,
                       ["ident_b", "cwT"], ["diagD%d" % (c % 2)], nowaw=True)
        S.tag = "P3"
        build_diag(0)
        gnames = ["gcat%d" % c for c in range(KC)]
        if first and not is_sample:
            S.op("dve", lambda e: e.memset(gcat[:, :, 0:HIST], 0.0), writes=gnames)
        elif is_sample:
            cp("dve", gcat[:, :, 0:HIST], scT[:].rearrange("p (r c) -> p c r", c=8), ["scT"], gnames)
        NTL = min(HIST, T)
        for i in range(4):
            sl, sn = load_slot(4 + i)
            for cl in range(2):
                c = i * 2 + cl
                ba, na = newbank()
                bbk, nbk = newbank()
                for kc in range(KC):
                    mm(ba[:, 0:T], sl[:, kc, cl * 128:(cl + 1) * 128], hTb[:, kc, 0:T], kc == 0, kc == KC - 1,
                       [sn, hn], [na])
                for kc in range(KC):
                    mm(bbk[:, 0:T], sl[:, kc, 256 + cl * 128:256 + (cl + 1) * 128], hTb[:, kc, 0:T], kc == 0,
                       kc == KC - 1, [sn, hn], [nbk])
                act(ah[:, c % 2, 0:T], ba[:, 0:T], AF.Identity, [na, "bTh"], ["ah%d" % (c % 2)],
                    bias=bTh[:, 24 + c:25 + c], scale=0.5)
                act(th[:, c % 2, 0:T], bbk[:, 0:T], AF.Tanh, [nbk, "bTh"], ["th%d" % (c % 2)],
                    bias=bTh[:, 32 + c:33 + c], scale=0.5)
                stt(gcat[:, c, HIST:HIST + T], th[:, c % 2, 0:T], 1.0, ah[:, c % 2, 0:T], ALU.add, ALU.mult,
                    ["ah%d" % (c % 2), "th%d" % (c % 2)], ["gcat%d" % c])
                if last:
                    stt(gtl[:, c, 0:NTL], th[:, c % 2, T - NTL:T], 1.0, ah[:, c % 2, T - NTL:T], ALU.add, ALU.mult,
                        ["ah%d" % (c % 2), "th%d" % (c % 2)], ["gtl"])
        if last:
            bb, nn = newbank()
            bb2, nn2 = newbank()
            for c in range(KC):
                tgt = bb if c < 4 else bb2
                tr(tgt[0:NTL, (c % 4) * 128:(c % 4 + 1) * 128], gtl[:, c, 0:NTL], ident_f[:], ["gtl", "ident_f"],
                   [nn if c < 4 else nn2])
            act(ybuf[0:NTL, 0:512], bb[0:NTL, :], AF.Identity, [nn], ["ybuf"])
            act(ybuf[0:NTL, 512:1024], bb2[0:NTL, :], AF.Identity, [nn2], ["ybuf"])
            if is_sample:
                dma("act", ncs[HIST - NTL:HIST, :], ybuf[0:NTL, :], ["ybuf"], [], "ybuf_st")
                dma("sp", ncs[0:HIST - NTL, :], scv[NTL:HIST, :], [], [], "ncs_cp")
            else:
                dma("act", ncp[s], ybuf[0:NTL, :], ["ybuf"], [], "ybuf_st")
        if nxt is not None:
            emit_xload(nxt)
        S.tag = "P4P5"
        sl_ga = [None, None]
        sl_gb = [None, None]
        bsum, nsum = newbank(hold=True)
        bsq, nsq = newbank(hold=True)

        def stats_mm(c):
            mm(bsum[:, 0:T], ones_b[:], dwb[:, c, 0:T], c == 0, c == KC - 1, ["ones_b", "dwb%d" % c], [nsum])
            mm(bsq[:, 0:T], ones_b[:], dwsq[:, c % 2, 0:T], c == 0, c == KC - 1, ["ones_b", "dwsq%d" % (c % 2)], [nsq])

        for c in range(KC):
            if c + 1 < KC:
                build_diag(c + 1)
            bb, nn = newbank()
            for j in range(NTAP):
                mm(bb[:, 0:T], diag[:, c % 2, j, :], gcat[:, c, j:j + T], j == 0, j == NTAP - 1,
                   ["diagA%d" % (c % 2), "diagD%d" % (c % 2), "gcat%d" % c], [nn])
            act(dwb[:, c, 0:T], bb[:, 0:T], AF.Identity, [nn], ["dwb%d" % c], bias=convbT(c))
            act(dwsq[:, c % 2, 0:T], bb[:, 0:T], AF.Square, [nn], ["dwsq%d" % (c % 2)], bias=convbT(c))
            if c % 4 == 0:
                sl_ga[0], sl_ga[1] = load_slot(8 + c // 4)
                sl_gb[0], sl_gb[1] = load_slot(10 + c // 4)
            cl = c % 4
            bg, ng = newbank()
            for kc in range(KC):
                mm(bg[:, 0:T], sl_ga[0][:, kc, cl * 128:(cl + 1) * 128], hTb[:, kc, 0:T], kc == 0, kc == KC - 1,
                   [sl_ga[1], hn], [ng])
            bs_, ns_ = newbank()
            for n_ in range(NTT):
                mm(bs_[:, n_ * 128:n_ * 128 + TR], vhat[0:TR, n_, c * 128:(c + 1) * 128], wmT[0:TR, c, 0:TR], True, True,
                   ["vhat", "wmT"], [ns_])
            act(sga[:, c % 2, 0:T], bg[:, 0:T], AF.Silu, [ng], ["sga%d" % (c % 2)], bias=bT[:, 16 + c:17 + c])
            tt("dve", pbuf[:, c % 2, 0:T], gu[:, c, 0:T], sga[:, c % 2, 0:T], ALU.mult, ["gu%d" % c, "sga%d" % (c % 2)],
               ["pbuf%d" % (c % 2)])
            if T == TB:
                in1 = Bh[:, c, :].unsqueeze(1).to_broadcast([128, 4, 128])
                in0 = bs_[:, :].rearrange("p (a b) -> p a b", a=4)
                o_ = nt1[:, :].rearrange("p (a b) -> p a b", a=4)
            else:
                in1 = Bh[:, c, 0:T]
                in0 = bs_[:, 0:T]
                o_ = nt1[:, 0:T]
            stt(o_, in0, lnvgT(c), in1, ALU.mult, ALU.add, [ns_, "Bh", "RT0"], ["nt1"])
            tt("dve", gu[:, c, 0:T], nt1[:, 0:T], pbuf[:, c % 2, 0:T], ALU.mult, ["nt1", "pbuf%d" % (c % 2)],
               ["gu%d" % c])
            bgb, ngb = newbank()
            for kc in range(KC):
                mm(bgb[:, 0:T], sl_gb[0][:, kc, cl * 128:(cl + 1) * 128], hTb[:, kc, 0:T], kc == 0, kc == KC - 1,
                   [sl_gb[1], hn], [ngb])
            act(sgb[:, c, 0:T], bgb[:, 0:T], AF.Silu, [ngb], ["sgb%d" % c], bias=bT[:, 40 + c:41 + c])
            if c >= 1:
                stats_mm(c - 1)
        stats_mm(KC - 1)
        if not last:
            cp("dve", gcat[:, :, 0:HIST], gcat[:, :, T:T + HIST], gnames, gnames)
        S.tag = "P7"
        mT = gcat
        p7 = {}

        def p7_slots(i):
            if ("m", i) not in p7:
                p7[("m", i)] = load_slot(12 + i)
                p7[("o", i)] = load_slot(16 + i)
            return p7[("m", i)], p7[("o", i)]

        def p7_part1(o):
            S.tag = "P7"
            i, ol = o // 2, o % 2
            (sl_m, sn_m), (sl_o, sn_o) = p7_slots(i)
            bma, nma = newbank()
            bmb, nmb = newbank()
            bpa, npa = newbank(hold=True)
            p7[("pa", o)] = (bpa, npa)
            for kc in range(KC):
                mm(bma[:, 0:T], sl_m[:, kc, ol * 128:(ol + 1) * 128], hTb[:, kc, 0:T], kc == 0, kc == KC - 1,
                   [sn_m, hn], [nma])
            for kc in range(KC):
                mm(bmb[:, 0:T], sl_m[:, kc, 256 + ol * 128:256 + (ol + 1) * 128], hTb[:, kc, 0:T], kc == 0,
                   kc == KC - 1, [sn_m, hn], [nmb])
            for c in range(KC):
                mm(bpa[:, 0:T], sl_o[:, c, ol * 128:(ol + 1) * 128], gu[:, c, 0:T], c == 0, c == KC - 1,
                   [sn_o, "gu%d" % c], [npa])
            act(tha[:, o % 4, 0:T], bma[:, 0:T], AF.Tanh, [nma, "bTh"], ["tha%d" % (o % 4)],
                bias=bTh[:, 48 + o:49 + o], scale=0.5)
            act(thb[:, o % 4, 0:T], bmb[:, 0:T], AF.Tanh, [nmb, "bTh"], ["thb%d" % (o % 4)],
                bias=bTh[:, 56 + o:57 + o], scale=0.5)

        def p7_part2(o):
            S.tag = "P7"
            i, ol = o // 2, o % 2
            (sl_m, sn_m), (sl_o, sn_o) = p7_slots(i)
            bpa, npa = p7[("pa", o)]
            bpb, npb_ = newbank()
            for c in range(KC):
                mm(bpb[:, 0:T], sl_o[:, c, 256 + ol * 128:256 + (ol + 1) * 128], sgb[:, c, 0:T], c == 0, c == KC - 1,
                   [sn_o, "sgb%d" % c], [npb_])
            stt(nt1[:, 0:T], tha[:, o % 4, 0:T], 1.0, bpa[:, 0:T], ALU.add, ALU.mult, ["tha%d" % (o % 4), npa], ["nt1"])
            stt(nt2[:, 0:T], thb[:, o % 4, 0:T], 1.0, bpb[:, 0:T], ALU.add, ALU.mult, ["thb%d" % (o % 4), npb_], ["nt2"])
            tt("dve", mT[:, o, HIST:HIST + T], nt1[:, 0:T], nt2[:, 0:T], ALU.add, ["nt1", "nt2"], ["gcat%d" % o])
            release(npa)

        p7_part1(0)
        p7_part1(1)
        p7_part1(2)
        S.tag = "LN_c"
        release(nsum)
        release(nsq)
        ts("dve", mu_bc[:, 0:T], bsum[:, 0:T], 1.0 / D, None, ALU.mult, None, [nsum], ["mu_bc"])
        tt("dve", nt1[:, 0:T], mu_bc[:, 0:T], mu_bc[:, 0:T], ALU.mult, ["mu_bc"], ["nt1"])
        stt(nt2[:, 0:T], bsq[:, 0:T], 1.0 / D, nt1[:, 0:T], ALU.mult, ALU.subtract, [nsq, "nt1"], ["nt2"])
        ts("dve", nt2[:, 0:T], nt2[:, 0:T], EPS, None, ALU.add, None, ["nt2"], ["nt2"])
        act(rs_bc[:, 0:T], nt2[:, 0:T], AF.Sqrt, ["nt2"], ["rs_bc"])
        S.op("dve", lambda e: e.reciprocal(rs_bc[:, 0:T], rs_bc[:, 0:T]), reads=["rs_bc"], writes=["rs_bc"])
        S.tag = "P6"
        for c in range(KC):
            tt("dve", nt1[:, 0:T], dwb[:, c, 0:T], mu_bc[:, 0:T], ALU.subtract, ["dwb%d" % c, "mu_bc"], ["nt1"])
            tt("dve", nt1[:, 0:T], nt1[:, 0:T], rs_bc[:, 0:T], ALU.mult, ["nt1", "rs_bc"], ["nt1"])
            act(nb[:, c % 2, 0:T], nt1[:, 0:T], AF.Silu, ["nt1", "RT0", "RT1"], ["nb%d" % (c % 2)], bias=lncbT(c),
                scale=lncgT(c))
            tt("dve", sgb[:, c, 0:T], nb[:, c % 2, 0:T], sgb[:, c, 0:T], ALU.mult, ["nb%d" % (c % 2), "sgb%d" % c],
               ["sgb%d" % c])
        for o in range(KC):
            p7_part2(o)
            if o + 3 < KC:
                p7_part1(o + 3)
            if nxt is not None and hoist:
                if o < 4:
                    if o == 0:
                        emit_P0_A(nxt, 0)
                    if o + 1 < nxt.NTT:
                        emit_P0_A(nxt, o + 1)
                    if o < nxt.NTT:
                        emit_P0_B(nxt, o)
                    if o == 3:
                        emit_P1_pre(nxt)
                else:
                    if o - 4 < nxt.NTT:
                        emit_P1_tile(nxt, o - 4)
        S.tag = "P8"
        slots_o = [load_slot(20), load_slot(21)]
        xrb = [(xr, "xr"), (gv, "gv")]

        def p8_A(t_):
            xb, xn_ = xrb[t_ % 2]
            dma("act", xb[0:TR, :], xsrc[t0 + t_ * 128:t0 + t_ * 128 + TR, :], [], [xn_], "xr_ld%d" % (t_ % 2))
            for half in range(2):
                sl, sn = slots_o[half]
                bb, nn = newbank()
                for kc in range(KC):
                    mm(bb[0:TR, :], mT[:, kc, HIST + t_ * 128:HIST + t_ * 128 + TR], sl[:, kc, :], kc == 0, kc == KC - 1,
                       ["gcat%d" % kc, sn], [nn])
                tmp, tmpn = (nt1, "nt1") if half == 0 else (nt2, "nt2")
                tt("dve", tmp[0:TR, :], bb[0:TR, :], gate_bc[0:TR, half * 512:(half + 1) * 512],
                   ALU.mult, [nn, "gate_bc"], [tmpn])
                tt("dve", xb[0:TR, half * 512:(half + 1) * 512], xb[0:TR, half * 512:(half + 1) * 512], tmp[0:TR, :],
                   ALU.add, [xn_, tmpn], [xn_])
            c8 = 8 + (t_ % 2)
            v8 = 8 + 2 * (t_ % 2)
            sn8, vn8 = "st_s8_%d" % (t_ % 2), "st_v8_%d" % (t_ % 2)
            act(junk[0:TR, :], xb[0:TR, :], AF.Square, [xn_], ["junk", sn8], accum_out=st_s[0:TR, c8:c8 + 1])

        def p8_A2(t_):
            c8 = 8 + (t_ % 2)
            v8 = 8 + 2 * (t_ % 2)
            sn8, vn8 = "st_s8_%d" % (t_ % 2), "st_v8_%d" % (t_ % 2)
            ts("dve", st_v[0:TR, v8:v8 + 1], st_s[0:TR, c8:c8 + 1], 1.0 / D, EPS, ALU.mult, ALU.add, [sn8], [vn8])
            tt("pool", st_v[0:TR, v8 + 1:v8 + 2], st_v[0:TR, v8:v8 + 1], negh[0:TR, 0:1], ALU.pow, [vn8, "negh"], [vn8])

        def p8_B(t_):
            xb, xn_ = xrb[t_ % 2]
            v8 = 8 + 2 * (t_ % 2)
            vn8 = "st_v8_%d" % (t_ % 2)
            stt(ybuf[0:TR, :], xb[0:TR, :], st_v[0:TR, v8 + 1:v8 + 2], fg_bc[0:TR, :], ALU.mult, ALU.mult,
                [xn_, vn8, "fg_bc"], ["ybuf"])
            dma("act", ydst[t0 + t_ * 128:t0 + t_ * 128 + TR, :], ybuf[0:TR, :], ["ybuf"], [], "ybuf_st")

        for t_ in range(NTT):
            p8_A(t_)
            if t_ >= 1:
                p8_B(t_ - 1)
            p8_A2(t_)
        p8_B(NTT - 1)
        if nxt is not None and not hoist:
            first_pass[0] = False
            for t_ in range(nxt.NTT):
                emit_P0_tile(nxt, t_)
            emit_P1_pre(nxt)
            for t_ in range(nxt.NTT):
                emit_P1_tile(nxt, t_)

    def emit_gate(seq):
        S.tag = "gate"
        for half in range(2):
            bb, nn = newbank()
            for k4 in range(4):
                kc = half * 4 + k4
                mm(bb[:, k4 * 128:(k4 + 1) * 128], gateh[:, kc, seq:seq + 1].to_broadcast([128, 128]), ident_f[:],
                   True, True, ["gateh", "ident_f"], [nn])
            act(gate_bc[:, half * 512:(half + 1) * 512], bb[:, :], AF.Identity, [nn], ["gate_bc"])

    emit_gate(blocks[0].seq)
    emit_front(blocks[0])
    cur_seq = blocks[0].seq
    for i, k in enumerate(blocks):
        nxt = blocks[i + 1] if i + 1 < len(blocks) else None
        if k.seq != cur_seq:
            emit_gate(k.seq)
            cur_seq = k.seq
        emit_body(k, nxt, hoist=(i > 0))
        first_pass[0] = False

    S.emit_all()
    print("sched stats:", S.stats)
    import os
    if os.environ.get("KDBG_TAGS"):
        import pickle
        tg = {}
        for o in S.ops:
            tg.setdefault(o.eng, []).append((o.tag, o.is_dma, o.ticket, o.w, o.r, o.idx, sorted(o.deps)))
        pickle.dump(tg, open(os.environ["KDBG_TAGS"], "wb"))
    st.close()
    return nc


_CACHE = {}


def _get_program(npb):
    if npb not in _CACHE:
        _CACHE[npb] = build_program(npb)
    return _CACHE[npb]


def kernel(x_prompt, x_sample, state_conv, c_prompt, c_sample, w_ada, b_ada, norm_g,
           w_in, b_in, ln_v_g, ln_v_b, w_spatial, b_spatial, conv_w, conv_b,
           ln_c_g, ln_c_b, w_o_a, w_o_b, w_out, final_g):
    f = lambda a: np.ascontiguousarray(np.asarray(a, dtype=np.float32))
    x_prompt = f(x_prompt)
    seqlen = x_prompt.shape[1]
    npb = seqlen // TB
    nc = _get_program(npb)
    ncores = 8
    shared = {
        "w_ada": f(w_ada)[0], "b_ada": f(b_ada)[0], "norm_g": f(norm_g)[0], "w_in": f(w_in)[0],
        "b_in": f(b_in)[0], "ln_v_g": f(ln_v_g)[0], "ln_v_b": f(ln_v_b)[0], "w_sp": f(w_spatial)[0],
        "b_sp": f(b_spatial)[0].reshape(-1), "conv_w": f(conv_w)[0], "conv_b": f(conv_b)[0],
        "ln_c_g": f(ln_c_g)[0], "ln_c_b": f(ln_c_b)[0], "w_o_a": f(w_o_a)[0], "w_o_b": f(w_o_b)[0],
        "w_out": f(w_out)[0], "final_g": f(final_g),
    }
    x_sample = f(x_sample)
    state_conv = f(state_conv)
    c_prompt = f(c_prompt)
    c_sample = f(c_sample)
    in_maps = []
    for i in range(ncores):
        m = dict(shared)
        m["xp"] = np.ascontiguousarray(x_prompt[2 * i:2 * i + 2])
        m["xs"] = np.ascontiguousarray(x_sample[i])
        m["sc"] = np.ascontiguousarray(state_conv[0, i])
        m["cc"] = np.ascontiguousarray(np.concatenate([c_prompt[2 * i:2 * i + 2], c_sample[i:i + 1]], axis=0))
        in_maps.append(m)
    res = run_bass_kernel_spmd(nc, in_maps, core_ids=list(range(ncores)))
    r = res.results
    y_prompt = np.concatenate([r[i]["yp"] for i in range(ncores)], axis=0)
    y_sample = np.stack([r[i]["ys"] for i in range(ncores)], axis=0)
    ncp = np.concatenate([r[i]["ncp"] for i in range(ncores)], axis=0)[None]
    ncs = np.stack([r[i]["ncs"] for i in range(ncores)], axis=0)[None]
    nvs = np.stack([r[i]["nvs"] for i in range(ncores)], axis=0)[None]
    return (y_prompt.astype(np.float32), y_sample.astype(np.float32), ncp.astype(np.float32),
            ncs.astype(np.float32), nvs.astype(np.float32))
```

```python
import contextlib
import numpy as np
import concourse.bass as bass
import concourse.mybir as mybir
from concourse.bass_utils import run_bass_kernel_spmd

F32 = mybir.dt.float32
BF16 = mybir.dt.bfloat16
AF = mybir.ActivationFunctionType
ALU = mybir.AluOpType

D = 1024
KC = 8
SEQ = 2048
TB = 512
DEC = 16
HIST = 30
NTAP = 31
EPS = 1e-6
NSLOT = 22
SLOT_ELEMS = KC * 512

COMPUTE = ("pe", "act", "dve", "pool")


class _Tile:
    __slots__ = ("name", "last_w", "readers")

    def __init__(self, name):
        self.name = name
        self.last_w = None
        self.readers = []


class _Op:
    __slots__ = ("idx", "eng", "emit", "deps", "is_dma", "sem_key", "dma_val",
                 "needs_inc", "ticket", "n_dma", "tag", "w", "r", "nowaw")

    def __init__(self):
        self.deps = set()
        self.needs_inc = False
        self.ticket = None
        self.is_dma = False
        self.nowaw = False
        self.sem_key = None
        self.dma_val = None
        self.n_dma = 1


class Sched:
    def __init__(self, nc):
        self.nc = nc
        self.ops = []
        self.tiles = {}
        self.wait_all_keys = set()
        self.tag = ""

    def _tile(self, name):
        t = self.tiles.get(name)
        if t is None:
            t = _Tile(name)
            self.tiles[name] = t
        return t

    def _track(self, op, reads, writes, nowaw=False):
        for n in reads:
            t = self._tile(n)
            if t.last_w is not None:
                op.deps.add(t.last_w)
            t.readers.append(op.idx)
        for n in writes:
            t = self._tile(n)
            if t.last_w is not None:
                lw = self.ops[t.last_w]
                if not (nowaw and lw.eng == op.eng and getattr(lw, "nowaw", False) and not lw.is_dma):
                    op.deps.add(t.last_w)
            for r in t.readers:
                if r != op.idx:
                    op.deps.add(r)
            t.last_w = op.idx
            t.readers = []
        op.deps.discard(op.idx)

    def op(self, eng, emit, reads=(), writes=(), nowaw=False):
        o = _Op()
        o.idx = len(self.ops)
        o.eng = eng
        o.emit = emit
        o.tag = self.tag
        o.nowaw = nowaw
        o.w = tuple(writes); o.r = tuple(reads)
        self.ops.append(o)
        self._track(o, reads, writes, nowaw)
        return o

    def dma(self, queue, emit, reads=(), writes=(), sem_key=None, n_dma=1):
        o = _Op()
        o.idx = len(self.ops)
        o.eng = queue
        o.emit = emit
        o.is_dma = True
        o.sem_key = sem_key
        o.n_dma = n_dma
        o.tag = self.tag
        o.w = tuple(writes); o.r = tuple(reads)
        self.ops.append(o)
        self._track(o, reads, writes)
        return o

    def emit_all(self):
        nc = self.nc
        ops = self.ops
        for o in ops:
            latest = {}
            keep = set()
            for d in o.deps:
                p = ops[d]
                if p.is_dma:
                    keep.add(d)
                elif latest.get(p.eng, -1) < d:
                    latest[p.eng] = d
            keep.update(latest.values())
            o.deps = keep
        for o in ops:
            for d in o.deps:
                p = ops[d]
                if p.is_dma:
                    continue
                if p.eng == "pe" and o.eng == "pe" and not o.is_dma:
                    continue
                p.needs_inc = True
        cnt = {e: 0 for e in COMPUTE}
        for o in ops:
            if o.is_dma:
                continue
            if o.needs_inc:
                cnt[o.eng] += 1
                o.ticket = cnt[o.eng]
        dkeys = {}
        for o in ops:
            if o.is_dma:
                k = o.sem_key
                dkeys[k] = dkeys.get(k, 0) + 16 * o.n_dma
                o.dma_val = dkeys[k]
        for o in ops:
            if o.is_dma and o.sem_key in self.wait_all_keys:
                o.dma_val = dkeys[o.sem_key]
        self.stats = dict(cnt)
        self.stats["n_ops"] = len(ops)
        self.stats["dma_keys"] = len(dkeys)
        with contextlib.ExitStack() as st:
            esem = {e: st.enter_context(nc.semaphore("s_" + e)) for e in COMPUTE}
            dsem = {k: st.enter_context(nc.semaphore("d_%d" % i)) for i, k in enumerate(dkeys)}
            block = st.enter_context(nc.Block())
            streams = {}
            for o in ops:
                streams.setdefault(o.eng, []).append(o)

            def run(engname, e):
                waited = {}
                for o in streams.get(engname, []):
                    need = {}
                    for d in o.deps:
                        p = ops[d]
                        if p.is_dma:
                            if o.is_dma and o.sem_key == p.sem_key and p.sem_key in self.wait_all_keys:
                                continue
                            s = dsem[p.sem_key]
                            v = p.dma_val
                        else:
                            if p.eng == "pe" and engname == "pe" and not o.is_dma:
                                continue
                            s = esem[p.eng]
                            v = p.ticket
                        key = id(s)
                        if need.get(key, (None, 0))[1] < v:
                            need[key] = (s, v)
                    for key, (s, v) in need.items():
                        if waited.get(key, 0) >= v:
                            continue
                        e.wait_ge(s, v)
                        waited[key] = v
                    r = o.emit(e)
                    if o.is_dma:
                        s = dsem[o.sem_key]
                        assert len(r) == o.n_dma, (len(r), o.n_dma)
                        for ins in r:
                            ins.then_inc(s, 16)
                    elif o.needs_inc:
                        r.then_inc(esem[engname], 1)
                last = {}
                for o in streams.get(engname, []):
                    if o.is_dma:
                        last[o.sem_key] = max(last.get(o.sem_key, 0), o.dma_val)
                for k, v in last.items():
                    e.wait_ge(dsem[k], v)

            if "pe" in streams:
                @block.tensor
                def _(e):
                    run("pe", e)
            if "act" in streams:
                @block.scalar
                def _(e):
                    run("act", e)
            if "dve" in streams:
                @block.vector
                def _(e):
                    run("dve", e)
            if "pool" in streams:
                @block.gpsimd
                def _(e):
                    run("pool", e)
            if "sp" in streams:
                @block.sync
                def _(e):
                    run("sp", e)


def build_program(npb=4, nring=5, diag_act_every=2):
    seqlen = npb * TB
    nc = bass.Bass("TRN2", target_bir_lowering=False)

    def din(name, shape):
        return nc.dram_tensor(name, shape, F32, kind="ExternalInput").ap()

    def dout(name, shape):
        return nc.dram_tensor(name, shape, F32, kind="ExternalOutput").ap()

    xp = din("xp", [2, seqlen, D])
    xs = din("xs", [DEC, D])
    scv = din("sc", [HIST, D])
    cc = din("cc", [3, D])
    w_ada = din("w_ada", [D, 3 * D])
    b_ada = din("b_ada", [3 * D])
    norm_g = din("norm_g", [D])
    w_in = din("w_in", [D, 8 * D])
    b_in = din("b_in", [8 * D])
    ln_v_g = din("ln_v_g", [D])
    ln_v_b = din("ln_v_b", [D])
    w_sp = din("w_sp", [8, 128, 128])
    b_sp = din("b_sp", [8 * 128])
    conv_w = din("conv_w", [NTAP, D])
    conv_b = din("conv_b", [D])
    ln_c_g = din("ln_c_g", [D])
    ln_c_b = din("ln_c_b", [D])
    w_o_a = din("w_o_a", [D, D])
    w_o_b = din("w_o_b", [D, D])
    w_out = din("w_out", [D, D])
    final_g = din("final_g", [D])
    yp = dout("yp", [2, seqlen, D])
    ys = dout("ys", [DEC, D])
    ncp = dout("ncp", [2, HIST, D])
    ncs = dout("ncs", [HIST, D])
    nvs = dout("nvs", [DEC, D])
    wscr = nc.dram_tensor("wscr", [NSLOT, 128, SLOT_ELEMS], BF16).ap()

    S = Sched(nc)
    S.wait_all_keys.add("setup")
    st = contextlib.ExitStack()

    def sb(name, shape, dt):
        return st.enter_context(nc.sbuf_tensor(name, shape, dt))

    ring = sb("ring", [128, nring, KC, 512], BF16)
    xin = sb("xin", [128, 4, D], F32)
    junk = sb("junk", [128, D], BF16)
    xn = sb("xn", [128, 2, D], BF16)
    hT = sb("hT", [128, 2, KC, TB], BF16)
    gv = sb("gv", [128, D], F32)
    vhat = sb("vhat", [128, 4, D], BF16)
    gu = sb("gu", [128, KC, TB], BF16)
    gcat = sb("gcat", [128, KC, HIST + TB], BF16)
    ah = sb("ah", [128, 2, TB], F32)
    th = sb("th", [128, 2, TB], F32)
    gtl = sb("gtl", [128, KC, 32], F32)
    diag = sb("diag", [128, 2, NTAP, 128], BF16)
    dwb = sb("dwb", [128, KC, TB], BF16)
    dwsq = sb("dwsq", [128, 2, TB], BF16)
    mu_bc = sb("mu_bc", [128, TB], F32)
    rs_bc = sb("rs_bc", [128, TB], F32)
    nt1 = sb("nt1", [128, TB], F32)
    nt2 = sb("nt2", [128, TB], F32)
    sga = sb("sga", [128, 2, TB], BF16)
    pbuf = sb("pbuf", [128, 2, TB], BF16)
    sgb = sb("sgb", [128, KC, TB], BF16)
    nb = sb("nb", [128, 2, TB], BF16)
    tha = sb("tha", [128, 4, TB], BF16)
    thb = sb("thb", [128, 4, TB], BF16)
    xr = sb("xr", [128, D], F32)
    ybuf = sb("ybuf", [128, D], F32)
    ident_f = sb("ident_f", [128, 128], F32)
    ident_b = sb("ident_b", [128, 128], BF16)
    ones_f = sb("ones_f", [128, 128], F32)
    ones_b = sb("ones_b", [128, 128], BF16)
    negh = sb("negh", [128, 8], F32)
    R0 = sb("R0", [128, 128], F32)
    R1 = sb("R1", [128, 128], F32)
    RT0 = sb("RT0", [128, 128], F32)
    RT1 = sb("RT1", [128, 32], F32)
    bTh = sb("bTh", [128, 64], F32)
    cwT = sb("cwT", [128, NTAP * 8], F32)
    scT = sb("scT", [128, HIST * 8], F32)
    wmT = sb("wmT", [128, KC, 128], BF16)
    wmf = sb("wmf", [128, 128], F32)
    Bh = sb("Bh", [128, KC, 128], F32)
    siluT = sb("siluT", [128, 24], F32)
    modT = sb("modT", [128, 24, 3], F32)
    aT = sb("aT", [128, KC, 3], F32)
    gateh = sb("gateh", [128, KC, 3], F32)
    glh = sb("glh", [128, 128], F32)
    gate_bc = sb("gate_bc", [128, D], F32)
    fg_bc = sb("fg_bc", [128, D], F32)
    bv2 = sb("bv2", [2, D], BF16)
    lo_row = sb("lo_row", [1, D], BF16)
    ones2 = sb("ones2", [2, 128], BF16)
    st_s = sb("st_s", [128, 16], F32)
    st_v = sb("st_v", [128, 16], F32)
    print("sbuf bytes remaining:", nc.sbuf_bytes_remaining)

    banks = [st.enter_context(nc.psum_tensor("bank%d" % i, [128, 512], F32)) for i in range(8)]
    bank_ctr = [0]
    held = set()

    def newbank(hold=False):
        while True:
            i = bank_ctr[0] % 8
            bank_ctr[0] += 1
            if i not in held:
                break
        if hold:
            held.add(i)
        return banks[i], "bank%d" % i

    def release(name):
        held.discard(int(name[4:]))

    def mm(out, lhsT, rhs, start, stop, reads, writes):
        S.op("pe", lambda e: e.matmul(out, lhsT=lhsT, rhs=rhs, start=start, stop=stop),
             reads=reads, writes=writes)

    def tr(out, in_, ident, reads, writes):
        S.op("pe", lambda e: e.transpose(out, in_, ident), reads=reads, writes=writes)

    def act(out, in_, func, reads, writes, bias=None, scale=None, accum_out=None, nowaw=False):
        kw = {}
        if bias is not None:
            kw["bias"] = bias
        if scale is not None:
            kw["scale"] = scale
        if accum_out is not None:
            kw["accum_out"] = accum_out
        S.op("act", lambda e: e.activation(out=out, in_=in_, func=func, **kw), reads=reads, writes=writes, nowaw=nowaw)

    def ts(eng, out, in0, s1, s2, op0, op1, reads, writes, nowaw=False):
        if op1 is None:
            S.op(eng, lambda e: e.tensor_scalar(out=out, in0=in0, scalar1=s1, scalar2=None, op0=op0),
                 reads=reads, writes=writes, nowaw=nowaw)
        else:
            S.op(eng, lambda e: e.tensor_scalar(out=out, in0=in0, scalar1=s1, scalar2=s2, op0=op0, op1=op1),
                 reads=reads, writes=writes)

    def tt(eng, out, in0, in1, op, reads, writes):
        S.op(eng, lambda e: e.tensor_tensor(out=out, in0=in0, in1=in1, op=op), reads=reads, writes=writes)

    def stt(out, in0, scalar, in1, op0, op1, reads, writes):
        S.op("dve", lambda e: e.scalar_tensor_tensor(out=out, in0=in0, scalar=scalar, in1=in1, op0=op0, op1=op1),
             reads=reads, writes=writes)

    def cp(eng, out, in_, reads, writes):
        S.op(eng, lambda e: e.tensor_copy(out, in_), reads=reads, writes=writes)

    def dma(q, out, in_, reads, writes, key):
        S.dma(q, lambda e: [e.dma_start(out=out, in_=in_)], reads=reads, writes=writes, sem_key=key)

    def rows128(v):
        return v.rearrange("(r k) -> r k", k=128)

    S.op("pool", lambda e: e.memset(ident_f[:], 1.0), writes=["ident_f"])
    S.op("pool", lambda e: e.affine_select(out=ident_f[:], in_=ident_f[:], pattern=[[-1, 128]],
                                           compare_op=ALU.is_equal, fill=0.0, base=0, channel_multiplier=1),
         reads=["ident_f"], writes=["ident_f"])
    S.op("pool", lambda e: e.memset(ones_f[:], 1.0), writes=["ones_f"])
    S.op("pool", lambda e: e.memset(negh[:], -0.5), writes=["negh"])
    cp("dve", ident_b[:], ident_f[:], ["ident_f"], ["ident_b"])
    cp("dve", ones_b[:], ones_f[:], ["ones_f"], ["ones_b"])
    cp("dve", ones2[:], ones_f[0:2, :], ["ones_f"], ["ones2"])

    dma("sp", R0[0:64, :], rows128(b_in), [], ["R0"], "setup")
    dma("sp", R0[64:88, :], rows128(b_ada), [], ["R0"], "setup")
    dma("sp", R0[88:96, :], rows128(norm_g), [], ["R0"], "setup")
    dma("sp", R0[96:104, :], rows128(ln_v_g), [], ["R0"], "setup")
    dma("sp", R0[104:112, :], rows128(ln_v_b), [], ["R0"], "setup")
    dma("sp", R0[112:120, :], rows128(conv_b), [], ["R0"], "setup")
    dma("sp", R0[120:128, :], rows128(ln_c_g), [], ["R0"], "setup")
    dma("sp", R1[0:8, :], rows128(ln_c_b), [], ["R1"], "setup")
    dma("sp", R1[8:32, :], cc.rearrange("s (c k) -> (s c) k", k=128), [], ["R1"], "setup")
    dma("sp", fg_bc[:], final_g.rearrange("(o n) -> o n", o=1).partition_broadcast(128), [], ["fg_bc"], "setup")
    vh_f = vhat[:].rearrange("p a b -> p (a b)").bitcast(F32)
    gu_f = gu[:].rearrange("p a b -> p (a b)").bitcast(F32)
    lnvb_row = vh_f[0:1, 0:D]
    bs_row = vh_f[0:1, D:2 * D]
    rs_row = gu_f[0:1, 0:D]
    bv_row = gu_f[0:1, D:2 * D]
    dma("sp", lnvb_row, ln_v_b.rearrange("(o n) -> o n", o=1), [], ["vhat"], "setup")
    dma("sp", bs_row, b_sp.rearrange("(o n) -> o n", o=1), [], ["vhat"], "setup")
    dma("sp", bv_row, b_in[D:2 * D].rearrange("(o n) -> o n", o=1), [], ["gu"], "setup")

    def wsrc(slot):
        def cols(w, c0, n):
            return w[:, c0:c0 + n].rearrange("(k p) n -> p k n", p=128)
        if slot in (0, 1):
            return [(0, 512, cols(w_in, D + slot * 512, 512))]
        if slot in (2, 3):
            return [(0, 512, cols(w_in, (slot - 2) * 512, 512))]
        if 4 <= slot <= 7:
            i = slot - 4
            return [(0, 256, cols(w_in, 3 * D + i * 256, 256)), (256, 256, cols(w_in, 4 * D + i * 256, 256))]
        if slot in (8, 9):
            return [(0, 512, cols(w_in, 2 * D + (slot - 8) * 512, 512))]
        if slot in (10, 11):
            return [(0, 512, cols(w_in, 5 * D + (slot - 10) * 512, 512))]
        if 12 <= slot <= 15:
            i = slot - 12
            return [(0, 256, cols(w_in, 6 * D + i * 256, 256)), (256, 256, cols(w_in, 7 * D + i * 256, 256))]
        if 16 <= slot <= 19:
            i = slot - 16
            return [(0, 256, cols(w_o_a, i * 256, 256)), (256, 256, cols(w_o_b, i * 256, 256))]
        return [(0, 512, cols(w_out, (slot - 20) * 512, 512))]

    def emit_conversion():
        S.tag = "conv"
        stg = []
        for h in range(2):
            v = ring[:, nring - 2 + h].rearrange("p k n -> p (k n)").bitcast(F32)
            stg.append((v, "ring%d" % (nring - 2 + h)))
        cobs = [(ring[:, nring - 3], "ring%d" % (nring - 3)), (hT[:, 1], "hT1")]
        for slot in range(NSLOT):
            srcs = wsrc(slot)
            cob, cn = cobs[slot % 2]
            for h in range(2):
                sv, sn = stg[h]
                if len(srcs) == 2:
                    (c0, n, src) = srcs[h]
                    view = sv.rearrange("p (k n) -> p k n", k=KC)
                    dstv = cob[:, :, c0:c0 + n]
                    dma("sp", view, src, [], [sn], "stg%d" % h)
                else:
                    (c0, n, src) = srcs[0]
                    view = sv.rearrange("p (k n) -> p k n", k=KC // 2)
                    dstv = cob[:, h * 4:(h + 1) * 4, :]
                    dma("sp", view, src[:, h * 4:(h + 1) * 4, :], [], [sn], "stg%d" % h)
                if h == 0:
                    act(dstv, view, AF.Identity, [sn], [cn])
                else:
                    cp("dve", dstv, view, [sn], [cn])
            dma("act", wscr[slot], cob.rearrange("p k n -> p (k n)"), [cn], ["wscr%d" % slot], "wscr_w%d" % slot)

    b0, n0 = newbank()
    tr(b0[:, 0:128], R0[:], ident_f[:], ["R0", "ident_f"], [n0])
    act(RT0[:], b0[:, 0:128], AF.Identity, [n0], ["RT0"])
    b1, n1 = newbank()
    tr(b1[:, 0:32], R1[0:32, :], ident_f[0:32, 0:32], ["R1", "ident_f"], [n1])
    act(RT1[:], b1[:, 0:32], AF.Identity, [n1], ["RT1"])
    ts("dve", bTh[:], RT0[:, 0:64], 0.5, None, ALU.mult, None, ["RT0"], ["bTh"])
    bT = RT0
    b_adaT = lambda jc: RT0[:, 64 + jc:65 + jc]
    norm_gT = lambda c: RT0[:, 88 + c:89 + c]
    lnvgT = lambda c: RT0[:, 96 + c:97 + c]
    convbT = lambda c: RT0[:, 112 + c:113 + c]
    lncgT = lambda c: RT0[:, 120 + c:121 + c]
    lncbT = lambda c: RT1[:, c:c + 1]
    act(siluT[:], RT1[:, 8:32], AF.Silu, ["RT1"], ["siluT"])

    cw_rows = conv_w.rearrange("j (c k) -> (j c) k", k=128)
    dma("act", R0[:], cw_rows[0:128, :], ["RT0"], ["R0"], "cw0")
    dma("act", R1[0:120, :], cw_rows[128:248, :], ["RT1"], ["R1"], "cw1")
    b2, n2 = newbank()
    tr(b2[:, 0:128], R0[:], ident_f[:], ["R0", "ident_f"], [n2])
    tr(b2[:, 128:248], R1[0:120, :], ident_f[0:120, 0:120], ["R1", "ident_f"], [n2])
    act(cwT[:], b2[:, 0:248], AF.Identity, [n2], ["cwT"])
    sc_rows = scv.rearrange("r (c k) -> (r c) k", k=128)
    dma("act", R0[:], sc_rows[0:128, :], [], ["R0"], "sc0")
    dma("act", R1[0:112, :], sc_rows[128:240, :], [], ["R1"], "sc1")
    b3, n3 = newbank()
    tr(b3[:, 0:128], R0[:], ident_f[:], ["R0", "ident_f"], [n3])
    tr(b3[:, 128:240], R1[0:112, :], ident_f[0:112, 0:112], ["R1", "ident_f"], [n3])
    act(scT[:], b3[:, 0:240], AF.Identity, [n3], ["scT"])

    def wsp_head(h):
        Rb, Rn = (R0, "R0") if h % 2 == 0 else (R1, "R1")
        dma("sp", Rb[:], w_sp[h], [], [Rn], "wsp%d" % (h % 2))
        bb, nn = newbank()
        tr(bb[:, 0:128], Rb[:], ident_f[:], [Rn, "ident_f"], [nn])
        act(wmf[:], bb[:, 0:128], AF.Identity, [nn], ["wmf"])
        S.op("pool", lambda e: e.affine_select(out=wmf[:], in_=wmf[:], pattern=[[1, 128]],
                                               compare_op=ALU.is_ge, fill=0.0, base=0, channel_multiplier=-1),
             reads=["wmf"], writes=["wmf"])
        cp("dve", wmT[:, h, :], wmf[:], ["wmf"], ["wmT"])
        bb2, nn2 = newbank()
        mm(bb2[0:1, 0:128], ones_f[:, 0:1], wmf[:], True, True, ["ones_f", "wmf"], [nn2])
        act(rs_row[:, h * 128:(h + 1) * 128], bb2[0:1, 0:128], AF.Identity, [nn2], ["gu"])
        bb3, nn3 = newbank()
        mm(bb3[:, 0:128], lnvb_row[:, h * 128:(h + 1) * 128], rs_row[:, h * 128:(h + 1) * 128], True, False,
           ["vhat", "gu"], [nn3])
        mm(bb3[:, 0:128], ones_f[0:1, :], bs_row[:, h * 128:(h + 1) * 128], False, True,
           ["ones_f", "vhat"], [nn3])
        act(Bh[:, h, :], bb3[:, 0:128], AF.Identity, [nn3], ["Bh"])

    sgb_names = ["sgb%d" % c for c in range(KC)]
    dwb_names = ["dwb%d" % c for c in range(KC)]
    ada_bufs = [
        (sgb[:].rearrange("p a b -> p (a b)").bitcast(F32).rearrange("p (k n) -> p k n", k=KC), sgb_names, "ada0"),
        (dwb[:].rearrange("p a b -> p (a b)").bitcast(F32).rearrange("p (k n) -> p k n", k=KC), dwb_names, "ada1"),
    ]
    for pc in range(12):
        ab, an, ak = ada_bufs[pc % 2]
        dma("sp", ab, w_ada[:, pc * 256:(pc + 1) * 256].rearrange("(k p) n -> p k n", p=128), [], an, ak)
        for jl in range(2):
            jc = pc * 2 + jl
            bb, nn = newbank()
            for kc in range(KC):
                mm(bb[:, 0:3], ab[:, kc, jl * 128:(jl + 1) * 128], siluT[:, kc:24:8], kc == 0, kc == KC - 1,
                   an + ["siluT"], [nn])
            act(modT[:, jc, :], bb[:, 0:3], AF.Identity, [nn], ["modT"], bias=b_adaT(jc))
        if pc < 8:
            wsp_head(pc)
    cp("dve", bv2[0:1, :], bv_row, ["gu"], ["bv2"])
    tt("dve", nt1[0:1, 0:512], bv_row[:, 0:512], bv2[0:1, 0:512], ALU.subtract, ["gu", "bv2"], ["nt1"])
    tt("dve", nt2[0:1, 0:512], bv_row[:, 512:1024], bv2[0:1, 512:1024], ALU.subtract, ["gu", "bv2"], ["nt2"])
    cp("dve", lo_row[:, 0:512], nt1[0:1, 0:512], ["nt1"], ["lo_row"])
    cp("dve", lo_row[:, 512:1024], nt2[0:1, 0:512], ["nt2"], ["lo_row"])
    dma("act", bv2[1:2, :], lo_row[:], ["lo_row"], ["bv2"], "bv2lo")

    for kc in range(KC):
        ts("dve", aT[:, kc, :], modT[:, 8 + kc, :], 1.0, norm_gT(kc), ALU.add, ALU.mult, ["modT", "RT0"], ["aT"])
        ts("dve", gateh[:, kc, :], modT[:, 16 + kc, :], 0.5, None, ALU.mult, None, ["modT"], ["gateh"])
    shT = lambda kc, s: modT[:, kc, s:s + 1]

    slot_ctr = [0]
    first_pass = [True]
    stg = []
    for h_ in range(2):
        v_ = ring[:, nring - 2 + h_].rearrange("p k n -> p (k n)").bitcast(F32)
        stg.append((v_, "ring%d" % (nring - 2 + h_), "stg%d" % h_))
    stg.append((hT[:, 1].rearrange("p k n -> p (k n)").bitcast(F32), "hT1", "stg2"))
    stg_ctr = [0]

    def load_slot(slot):
        nr = (nring - 2) if first_pass[0] else nring
        r = slot_ctr[0] % nr
        slot_ctr[0] += 1
        rn = "ring%d" % r
        dst = ring[:, r]
        if first_pass[0]:
            srcs = wsrc(slot)
            for h in range(2):
                sv, sn, sk = stg[stg_ctr[0] % len(stg)]
                stg_ctr[0] += 1
                if len(srcs) == 2:
                    (c0, n, src) = srcs[h]
                    view = sv.rearrange("p (k n) -> p k n", k=KC)
                    dstv = dst[:, :, c0:c0 + n]
                    dma("sp", view, src, [], [sn], sk)
                else:
                    (c0, n, src) = srcs[0]
                    view = sv.rearrange("p (k n) -> p k n", k=KC // 2)
                    dstv = dst[:, h * 4:(h + 1) * 4, :]
                    dma("sp", view, src[:, h * 4:(h + 1) * 4, :], [], [sn], sk)
                if h == 0:
                    act(dstv, view, AF.Identity, [sn], [rn])
                else:
                    cp("dve", dstv, view, [sn], [rn])
            dma("act", wscr[slot], dst.rearrange("p k n -> p (k n)"), [rn], ["wscr%d" % slot], "wscr_w%d" % slot)
        else:
            dma("sp", dst.rearrange("p k n -> p (k n)"), wscr[slot], ["wscr%d" % slot], [rn], "ringld%d" % r)
        return dst, rn

    class Blk:
        pass

    blocks = []
    for sq in range(2):
        for b in range(npb):
            k = Blk()
            k.seq, k.xsrc, k.ydst, k.t0, k.T = sq, xp[sq], yp[sq], b * TB, TB
            k.first, k.last, k.is_sample = (b == 0), (b == npb - 1), False
            blocks.append(k)
    k = Blk()
    k.seq, k.xsrc, k.ydst, k.t0, k.T = 2, xs, ys, 0, DEC
    k.first, k.last, k.is_sample = True, True, True
    blocks.append(k)
    for i, k in enumerate(blocks):
        k.idx = i
        k.TR = min(k.T, 128)
        k.NTT = max(1, k.T // 128)
        k.h = i % 2
        k.hn = "hT%d" % k.h

    def emit_xload(k):
        for t_ in range(k.NTT):
            dma("sp", xin[0:k.TR, t_, :], k.xsrc[k.t0 + t_ * 128:k.t0 + t_ * 128 + k.TR, :], [], ["xin%d" % t_],
                "xin%d" % t_)

    def emit_P0_A(k, t_):
        S.tag = "P0"
        TR = k.TR
        act(junk[0:TR, :], xin[0:TR, t_, :], AF.Square, ["xin%d" % t_], ["junk", "st_s0"], accum_out=st_s[0:TR, 0:1])
        ts("dve", st_v[0:TR, 0:1], st_s[0:TR, 0:1], 1.0 / D, EPS, ALU.mult, ALU.add, ["st_s0"], ["st_v0"])
        tt("pool", st_v[0:TR, 1:2], st_v[0:TR, 0:1], negh[0:TR, 0:1], ALU.pow, ["st_v0", "negh"], ["st_v0"])
        ts("dve", xn[0:TR, t_ % 2, :], xin[0:TR, t_, :], st_v[0:TR, 1:2], None, ALU.mult, None,
           ["xin%d" % t_, "st_v0"], ["xn%d" % (t_ % 2)])

    def emit_P0_B(k, t_):
        S.tag = "P0"
        TR, s = k.TR, k.seq
        hTb = hT[:, k.h]
        bb, nn = newbank()
        bbf = bb[:].bitcast(BF16)
        for kc in range(KC):
            tr(bbf[:, kc * 128:kc * 128 + TR], xn[0:TR, t_ % 2, kc * 128:(kc + 1) * 128], ident_b[0:TR, 0:TR],
               ["xn%d" % (t_ % 2), "ident_b"], [nn])
        for kc in range(KC):
            act(hTb[:, kc, t_ * 128:t_ * 128 + TR], bbf[:, kc * 128:kc * 128 + TR], AF.Identity, [nn, "aT", "modT"],
                [k.hn], bias=shT(kc, s), scale=aT[:, kc, s:s + 1], nowaw=True)

    def emit_P0_tile(k, t_):
        emit_P0_A(k, t_)
        emit_P0_B(k, t_)

    def emit_P1_pre(k):
        S.tag = "P1"
        k.slots_v = [load_slot(0), load_slot(1)]
        if k.is_sample:
            dma("sp", xr[0:DEC, :], ln_v_g.rearrange("(o n) -> o n", o=1).partition_broadcast(DEC), [], ["xr"], "xr")
            dma("sp", ybuf[0:DEC, :], ln_v_b.rearrange("(o n) -> o n", o=1).partition_broadcast(DEC), [], ["ybuf"],
                "ybuf_ld")

    def emit_P1_tile(k, t_):
        S.tag = "P1"
        TR = k.TR
        hTb = hT[:, k.h]
        for half in range(2):
            sl, sn = k.slots_v[half]
            bb, nn = newbank()
            for kc in range(KC):
                mm(bb[0:TR, :], hTb[:, kc, t_ * 128:t_ * 128 + TR], sl[:, kc, :], kc == 0, False, [k.hn, sn], [nn])
            mm(bb[0:TR, :], ones2[0:2, 0:TR], bv2[0:2, half * 512:(half + 1) * 512], False, True,
               ["ones2", "bv2"], [nn])
            act(gv[0:TR, half * 512:(half + 1) * 512], bb[0:TR, :], AF.Gelu_apprx_tanh, [nn], ["gv", "st_s1"],
                accum_out=st_s[0:TR, 2 + half:3 + half])
            act(junk[0:TR, 0:512], gv[0:TR, half * 512:(half + 1) * 512], AF.Square, ["gv"], ["junk", "st_s1"],
                accum_out=st_s[0:TR, 4 + half:5 + half])
        sv = ["st_v1"]
        tt("dve", st_v[0:TR, 2:3], st_s[0:TR, 2:3], st_s[0:TR, 3:4], ALU.add, ["st_s1"], sv)
        tt("dve", st_v[0:TR, 3:4], st_s[0:TR, 4:5], st_s[0:TR, 5:6], ALU.add, ["st_s1"], sv)
        ts("dve", st_v[0:TR, 2:4], st_v[0:TR, 2:4], 1.0 / D, None, ALU.mult, None, sv, sv)
        tt("dve", st_v[0:TR, 4:5], st_v[0:TR, 2:3], st_v[0:TR, 2:3], ALU.mult, sv, sv)
        tt("dve", st_v[0:TR, 5:6], st_v[0:TR, 3:4], st_v[0:TR, 4:5], ALU.subtract, sv, sv)
        ts("dve", st_v[0:TR, 5:6], st_v[0:TR, 5:6], EPS, None, ALU.add, None, sv, sv)
        tt("pool", st_v[0:TR, 6:7], st_v[0:TR, 5:6], negh[0:TR, 0:1], ALU.pow, sv + ["negh"], sv)
        ts("dve", st_v[0:TR, 7:8], st_v[0:TR, 2:3], st_v[0:TR, 6:7], -1.0, ALU.mult, ALU.mult, sv, sv)
        ts("dve", vhat[0:TR, t_, :], gv[0:TR, :], st_v[0:TR, 6:7], st_v[0:TR, 7:8], ALU.mult, ALU.add,
           ["gv"] + sv, ["vhat"])
        if k.is_sample:
            ts("dve", gv[0:TR, :], gv[0:TR, :], st_v[0:TR, 6:7], st_v[0:TR, 7:8], ALU.mult, ALU.add,
               ["gv"] + sv, ["gv"])
            tt("dve", gv[0:TR, :], gv[0:TR, :], xr[0:TR, :], ALU.mult, ["gv", "xr"], ["gv"])
            tt("dve", gv[0:TR, :], gv[0:TR, :], ybuf[0:TR, :], ALU.add, ["gv", "ybuf"], ["gv"])
            dma("act", nvs, gv[0:DEC, :], ["gv"], [], "nvs")

    def emit_front(k):
        emit_xload(k)
        for t_ in range(k.NTT):
            emit_P0_tile(k, t_)
        emit_P1_pre(k)
        for t_ in range(k.NTT):
            emit_P1_tile(k, t_)

    def emit_body(k, nxt, hoist):
        seq, xsrc, ydst, t0, T, first, last, is_sample = k.seq, k.xsrc, k.ydst, k.t0, k.T, k.first, k.last, k.is_sample
        TR, NTT, s = k.TR, k.NTT, k.seq
        hTb = hT[:, k.h]
        hn = k.hn
        S.tag = "P2"
        for sidx in (2, 3):
            sl, sn = load_slot(sidx)
            for cl in range(4):
                c = (sidx - 2) * 4 + cl
                bb, nn = newbank()
                for kc in range(KC):
                    mm(bb[:, 0:T], sl[:, kc, cl * 128:(cl + 1) * 128], hTb[:, kc, 0:T], kc == 0, kc == KC - 1,
                       [sn, hn], [nn])
                act(gu[:, c, 0:T], bb[:, 0:T], AF.Gelu_apprx_tanh, [nn], ["gu%d" % c], bias=bT[:, c:c + 1])

        def build_diag(c):
            for j in range(NTAP):
                if diag_act_every and (j % diag_act_every == diag_act_every - 1):
                    act(diag[:, c % 2, j, :], ident_b[:], AF.Identity, ["ident_b", "cwT"], ["diagA%d" % (c % 2)],
                        scale=cwT[:, j * 8 + c:j * 8 + c + 1], nowaw=True)
                else:
                    ts("dve", diag[:, c % 2, j, :], ident_b[:], cwT[:, j * 8 + c:j * 8 + c + 1], None, ALU.mult, None,
                       ["ident_b", "cwT"], ["diagD%d" % (c % 2)], nowaw=True)
        S.tag = "P3"
        build_diag(0)
        gnames = ["gcat%d" % c for c in range(KC)]
        if first and not is_sample:
            S.op("dve", lambda e: e.memset(gcat[:, :, 0:HIST], 0.0), writes=gnames)
        elif is_sample:
            cp("dve", gcat[:, :, 0:HIST], scT[:].rearrange("p (r c) -> p c r", c=8), ["scT"], gnames)
        NTL = min(HIST, T)
        for i in range(4):
            sl, sn = load_slot(4 + i)
            for cl in range(2):
                c = i * 2 + cl
                ba, na = newbank()
                bbk, nbk = newbank()
                for kc in range(KC):
                    mm(ba[:, 0:T], sl[:, kc, cl * 128:(cl + 1) * 128], hTb[:, kc, 0:T], kc == 0, kc == KC - 1,
                       [sn, hn], [na])
                for kc in range(KC):
                    mm(bbk[:, 0:T], sl[:, kc, 256 + cl * 128:256 + (cl + 1) * 128], hTb[:, kc, 0:T], kc == 0,
                       kc == KC - 1, [sn, hn], [nbk])
                act(ah[:, c % 2, 0:T], ba[:, 0:T], AF.Identity, [na, "bTh"], ["ah%d" % (c % 2)],
                    bias=bTh[:, 24 + c:25 + c], scale=0.5)
                act(th[:, c % 2, 0:T], bbk[:, 0:T], AF.Tanh, [nbk, "bTh"], ["th%d" % (c % 2)],
                    bias=bTh[:, 32 + c:33 + c], scale=0.5)
                stt(gcat[:, c, HIST:HIST + T], th[:, c % 2, 0:T], 1.0, ah[:, c % 2, 0:T], ALU.add, ALU.mult,
                    ["ah%d" % (c % 2), "th%d" % (c % 2)], ["gcat%d" % c])
                if last:
                    stt(gtl[:, c, 0:NTL], th[:, c % 2, T - NTL:T], 1.0, ah[:, c % 2, T - NTL:T], ALU.add, ALU.mult,
                        ["ah%d" % (c % 2), "th%d" % (c % 2)], ["gtl"])
        if last:
            bb, nn = newbank()
            bb2, nn2 = newbank()
            for c in range(KC):
                tgt = bb if c < 4 else bb2
                tr(tgt[0:NTL, (c % 4) * 128:(c % 4 + 1) * 128], gtl[:, c, 0:NTL], ident_f[:], ["gtl", "ident_f"],
                   [nn if c < 4 else nn2])
            act(ybuf[0:NTL, 0:512], bb[0:NTL, :], AF.Identity, [nn], ["ybuf"])
            act(ybuf[0:NTL, 512:1024], bb2[0:NTL, :], AF.Identity, [nn2], ["ybuf"])
            if is_sample:
                dma("act", ncs[HIST - NTL:HIST, :], ybuf[0:NTL, :], ["ybuf"], [], "ybuf_st")
                dma("sp", ncs[0:HIST - NTL, :], scv[NTL:HIST, :], [], [], "ncs_cp")
            else:
                dma("act", ncp[s], ybuf[0:NTL, :], ["ybuf"], [], "ybuf_st")
        if nxt is not None:
            emit_xload(nxt)
        S.tag = "P4P5"
        sl_ga = [None, None]
        sl_gb = [None, None]
        bsum, nsum = newbank(hold=True)
        bsq, nsq = newbank(hold=True)

        def stats_mm(c):
            mm(bsum[:, 0:T], ones_b[:], dwb[:, c, 0:T], c == 0, c == KC - 1, ["ones_b", "dwb%d" % c], [nsum])
            mm(bsq[:, 0:T], ones_b[:], dwsq[:, c % 2, 0:T], c == 0, c == KC - 1, ["ones_b", "dwsq%d" % (c % 2)], [nsq])

        for c in range(KC):
            if c + 1 < KC:
                build_diag(c + 1)
            bb, nn = newbank()
            for j in range(NTAP):
                mm(bb[:, 0:T], diag[:, c % 2, j, :], gcat[:, c, j:j + T], j == 0, j == NTAP - 1,
                   ["diagA%d" % (c % 2), "diagD%d" % (c % 2), "gcat%d" % c], [nn])
            act(dwb[:, c, 0:T], bb[:, 0:T], AF.Identity, [nn], ["dwb%d" % c], bias=convbT(c))
            act(dwsq[:, c % 2, 0:T], bb[:, 0:T], AF.Square, [nn], ["dwsq%d" % (c % 2)], bias=convbT(c))
            if c % 4 == 0:
                sl_ga[0], sl_ga[1] = load_slot(8 + c // 4)
                sl_gb[0], sl_gb[1] = load_slot(10 + c // 4)
            cl = c % 4
            bg, ng = newbank()
            for kc in range(KC):
                mm(bg[:, 0:T], sl_ga[0][:, kc, cl * 128:(cl + 1) * 128], hTb[:, kc, 0:T], kc == 0, kc == KC - 1,
                   [sl_ga[1], hn], [ng])
            bs_, ns_ = newbank()
            for n_ in range(NTT):
                mm(bs_[:, n_ * 128:n_ * 128 + TR], vhat[0:TR, n_, c * 128:(c + 1) * 128], wmT[0:TR, c, 0:TR], True, True,
                   ["vhat", "wmT"], [ns_])
            act(sga[:, c % 2, 0:T], bg[:, 0:T], AF.Silu, [ng], ["sga%d" % (c % 2)], bias=bT[:, 16 + c:17 + c])
            tt("dve", pbuf[:, c % 2, 0:T], gu[:, c, 0:T], sga[:, c % 2, 0:T], ALU.mult, ["gu%d" % c, "sga%d" % (c % 2)],
               ["pbuf%d" % (c % 2)])
            if T == TB:
                in1 = Bh[:, c, :].unsqueeze(1).to_broadcast([128, 4, 128])
                in0 = bs_[:, :].rearrange("p (a b) -> p a b", a=4)
                o_ = nt1[:, :].rearrange("p (a b) -> p a b", a=4)
            else:
                in1 = Bh[:, c, 0:T]
                in0 = bs_[:, 0:T]
                o_ = nt1[:, 0:T]
            stt(o_, in0, lnvgT(c), in1, ALU.mult, ALU.add, [ns_, "Bh", "RT0"], ["nt1"])
            tt("dve", gu[:, c, 0:T], nt1[:, 0:T], pbuf[:, c % 2, 0:T], ALU.mult, ["nt1", "pbuf%d" % (c % 2)],
               ["gu%d" % c])
            bgb, ngb = newbank()
            for kc in range(KC):
                mm(bgb[:, 0:T], sl_gb[0][:, kc, cl * 128:(cl + 1) * 128], hTb[:, kc, 0:T], kc == 0, kc == KC - 1,
                   [sl_gb[1], hn], [ngb])
            act(sgb[:, c, 0:T], bgb[:, 0:T], AF.Silu, [ngb], ["sgb%d" % c], bias=bT[:, 40 + c:41 + c])
            if c >= 1:
                stats_mm(c - 1)
        stats_mm(KC - 1)
        if not last:
            cp("dve", gcat[:, :, 0:HIST], gcat[:, :, T:T + HIST], gnames, gnames)
        S.tag = "P7"
        mT = gcat
        p7 = {}

        def p7_slots(i):
            if ("m", i) not in p7:
                p7[("m", i)] = load_slot(12 + i)
                p7[("o", i)] = load_slot(16 + i)
            return p7[("m", i)], p7[("o", i)]

        def p7_part1(o):
            S.tag = "P7"
            i, ol = o // 2, o % 2
            (sl_m, sn_m), (sl_o, sn_o) = p7_slots(i)
            bma, nma = newbank()
            bmb, nmb = newbank()
            bpa, npa = newbank(hold=True)
            p7[("pa", o)] = (bpa, npa)
            for kc in range(KC):
                mm(bma[:, 0:T], sl_m[:, kc, ol * 128:(ol + 1) * 128], hTb[:, kc, 0:T], kc == 0, kc == KC - 1,
                   [sn_m, hn], [nma])
            for kc in range(KC):
                mm(bmb[:, 0:T], sl_m[:, kc, 256 + ol * 128:256 + (ol + 1) * 128], hTb[:, kc, 0:T], kc == 0,
                   kc == KC - 1, [sn_m, hn], [nmb])
            for c in range(KC):
                mm(bpa[:, 0:T], sl_o[:, c, ol * 128:(ol + 1) * 128], gu[:, c, 0:T], c == 0, c == KC - 1,
                   [sn_o, "gu%d" % c], [npa])
            act(tha[:, o % 4, 0:T], bma[:, 0:T], AF.Tanh, [nma, "bTh"], ["tha%d" % (o % 4)],
                bias=bTh[:, 48 + o:49 + o], scale=0.5)
            act(thb[:, o % 4, 0:T], bmb[:, 0:T], AF.Tanh, [nmb, "bTh"], ["thb%d" % (o % 4)],
                bias=bTh[:, 56 + o:57 + o], scale=0.5)

        def p7_part2(o):
            S.tag = "P7"
            i, ol = o // 2, o % 2
            (sl_m, sn_m), (sl_o, sn_o) = p7_slots(i)
            bpa, npa = p7[("pa", o)]
            bpb, npb_ = newbank()
            for c in range(KC):
                mm(bpb[:, 0:T], sl_o[:, c, 256 + ol * 128:256 + (ol + 1) * 128], sgb[:, c, 0:T], c == 0, c == KC - 1,
                   [sn_o, "sgb%d" % c], [npb_])
            stt(nt1[:, 0:T], tha[:, o % 4, 0:T], 1.0, bpa[:, 0:T], ALU.add, ALU.mult, ["tha%d" % (o % 4), npa], ["nt1"])
            stt(nt2[:, 0:T], thb[:, o % 4, 0:T], 1.0, bpb[:, 0:T], ALU.add, ALU.mult, ["thb%d" % (o % 4), npb_], ["nt2"])
            tt("dve", mT[:, o, HIST:HIST + T], nt1[:, 0:T], nt2[:, 0:T], ALU.add, ["nt1", "nt2"], ["gcat%d" % o])
            release(npa)

        p7_part1(0)
        p7_part1(1)
        p7_part1(2)
        S.tag = "LN_c"
        release(nsum)
        release(nsq)
        ts("dve", mu_bc[:, 0:T], bsum[:, 0:T], 1.0 / D, None, ALU.mult, None, [nsum], ["mu_bc"])
        tt("dve", nt1[:, 0:T], mu_bc[:, 0:T], mu_bc[:, 0:T], ALU.mult, ["mu_bc"], ["nt1"])
        stt(nt2[:, 0:T], bsq[:, 0:T], 1.0 / D, nt1[:, 0:T], ALU.mult, ALU.subtract, [nsq, "nt1"], ["nt2"])
        ts("dve", nt2[:, 0:T], nt2[:, 0:T], EPS, None, ALU.add, None, ["nt2"], ["nt2"])
        I32 = mybir.dt.int32
        rsi = rs_bc[:, 0:T].bitcast(I32)
        S.op("dve", lambda e: e.tensor_single_scalar(out=rsi, in_=nt2[:, 0:T].bitcast(I32), scalar=1,
                                                     op=ALU.arith_shift_right), reads=["nt2"], writes=["rs_bc"])
        ts("dve", rsi, rsi, -1, 0x5f3759df, ALU.mult, ALU.add, ["rs_bc"], ["rs_bc"])
        for it in range(3):
            tt("dve", nt1[:, 0:T], rs_bc[:, 0:T], rs_bc[:, 0:T], ALU.mult, ["rs_bc"], ["nt1"])
            tt("dve", nt1[:, 0:T], nt1[:, 0:T], nt2[:, 0:T], ALU.mult, ["nt1", "nt2"], ["nt1"])
            ts("dve", nt1[:, 0:T], nt1[:, 0:T], -0.5, 1.5, ALU.mult, ALU.add, ["nt1"], ["nt1"])
            tt("dve", rs_bc[:, 0:T], rs_bc[:, 0:T], nt1[:, 0:T], ALU.mult, ["rs_bc", "nt1"], ["rs_bc"])
        S.tag = "P6"
        for c in range(KC):
            tt("dve", nt1[:, 0:T], dwb[:, c, 0:T], mu_bc[:, 0:T], ALU.subtract, ["dwb%d" % c, "mu_bc"], ["nt1"])
            tt("dve", nt1[:, 0:T], nt1[:, 0:T], rs_bc[:, 0:T], ALU.mult, ["nt1", "rs_bc"], ["nt1"])
            act(nb[:, c % 2, 0:T], nt1[:, 0:T], AF.Silu, ["nt1", "RT0", "RT1"], ["nb%d" % (c % 2)], bias=lncbT(c),
                scale=lncgT(c))
            tt("dve", sgb[:, c, 0:T], nb[:, c % 2, 0:T], sgb[:, c, 0:T], ALU.mult, ["nb%d" % (c % 2), "sgb%d" % c],
               ["sgb%d" % c])
        for o in range(KC):
            p7_part2(o)
            if o + 3 < KC:
                p7_part1(o + 3)
            if nxt is not None and hoist:
                if o < 4:
                    if o == 0:
                        emit_P0_A(nxt, 0)
                    if o + 1 < nxt.NTT:
                        emit_P0_A(nxt, o + 1)
                    if o < nxt.NTT:
                        emit_P0_B(nxt, o)
                    if o == 3:
                        emit_P1_pre(nxt)
                else:
                    if o - 4 < nxt.NTT:
                        emit_P1_tile(nxt, o - 4)
        S.tag = "P8"
        slots_o = [load_slot(20), load_slot(21)]
        xrb = [(xr, "xr"), (gv, "gv")]

        def p8_A(t_):
            xb, xn_ = xrb[t_ % 2]
            dma("act", xb[0:TR, :], xsrc[t0 + t_ * 128:t0 + t_ * 128 + TR, :], [], [xn_], "xr_ld%d" % (t_ % 2))
            for half in range(2):
                sl, sn = slots_o[half]
                bb, nn = newbank()
                for kc in range(KC):
                    mm(bb[0:TR, :], mT[:, kc, HIST + t_ * 128:HIST + t_ * 128 + TR], sl[:, kc, :], kc == 0, kc == KC - 1,
                       ["gcat%d" % kc, sn], [nn])
                tmp, tmpn = (nt1, "nt1") if half == 0 else (nt2, "nt2")
                tt("dve", tmp[0:TR, :], bb[0:TR, :], gate_bc[0:TR, half * 512:(half + 1) * 512],
                   ALU.mult, [nn, "gate_bc"], [tmpn])
                tt("dve", xb[0:TR, half * 512:(half + 1) * 512], xb[0:TR, half * 512:(half + 1) * 512], tmp[0:TR, :],
                   ALU.add, [xn_, tmpn], [xn_])
            c8 = 8 + (t_ % 2)
            v8 = 8 + 2 * (t_ % 2)
            sn8, vn8 = "st_s8_%d" % (t_ % 2), "st_v8_%d" % (t_ % 2)
            act(junk[0:TR, :], xb[0:TR, :], AF.Square, [xn_], ["junk", sn8], accum_out=st_s[0:TR, c8:c8 + 1])

        def p8_A2(t_):
            c8 = 8 + (t_ % 2)
            v8 = 8 + 2 * (t_ % 2)
            sn8, vn8 = "st_s8_%d" % (t_ % 2), "st_v8_%d" % (t_ % 2)
            ts("dve", st_v[0:TR, v8:v8 + 1], st_s[0:TR, c8:c8 + 1], 1.0 / D, EPS, ALU.mult, ALU.add, [sn8], [vn8])
            tt("pool", st_v[0:TR, v8 + 1:v8 + 2], st_v[0:TR, v8:v8 + 1], negh[0:TR, 0:1], ALU.pow, [vn8, "negh"], [vn8])

        def p8_B(t_):
            xb, xn_ = xrb[t_ % 2]
            v8 = 8 + 2 * (t_ % 2)
            vn8 = "st_v8_%d" % (t_ % 2)
            stt(ybuf[0:TR, :], xb[0:TR, :], st_v[0:TR, v8 + 1:v8 + 2], fg_bc[0:TR, :], ALU.mult, ALU.mult,
                [xn_, vn8, "fg_bc"], ["ybuf"])
            dma("act", ydst[t0 + t_ * 128:t0 + t_ * 128 + TR, :], ybuf[0:TR, :], ["ybuf"], [], "ybuf_st")

        for t_ in range(NTT):
            p8_A(t_)
            if t_ >= 1:
                p8_B(t_ - 1)
            p8_A2(t_)
        p8_B(NTT - 1)
        if nxt is not None and not hoist:
            first_pass[0] = False
            for t_ in range(nxt.NTT):
                emit_P0_tile(nxt, t_)
            emit_P1_pre(nxt)
            for t_ in range(nxt.NTT):
                emit_P1_tile(nxt, t_)

    def emit_gate(seq):
        S.tag = "gate"
        for half in range(2):
            bb, nn = newbank()
            for k4 in range(4):
                kc = half * 4 + k4
                mm(bb[:, k4 * 128:(k4 + 1) * 128], gateh[:, kc, seq:seq + 1].to_broadcast([128, 128]), ident_f[:],
                   True, True, ["gateh", "ident_f"], [nn])
            act(gate_bc[:, half * 512:(half + 1) * 512], bb[:, :], AF.Identity, [nn], ["gate_bc"])

    emit_gate(blocks[0].seq)
    emit_front(blocks[0])
    cur_seq = blocks[0].seq
    for i, k in enumerate(blocks):
        nxt = blocks[i + 1] if i + 1 < len(blocks) else None
        if k.seq != cur_seq:
            emit_gate(k.seq)
            cur_seq = k.seq
        emit_body(k, nxt, hoist=(i > 0))
        first_pass[0] = False

    S.emit_all()
    print("sched stats:", S.stats)
    import os
    if os.environ.get("KDBG_TAGS"):
        import pickle
        tg = {}
        for o in S.ops:
            tg.setdefault(o.eng, []).append((o.tag, o.is_dma, o.ticket, o.w, o.r, o.idx, sorted(o.deps)))
        pickle.dump(tg, open(os.environ["KDBG_TAGS"], "wb"))
    st.close()
    return nc


_CACHE = {}


def _get_program(npb):
    if npb not in _CACHE:
        _CACHE[npb] = build_program(npb)
    return _CACHE[npb]


def kernel(x_prompt, x_sample, state_conv, c_prompt, c_sample, w_ada, b_ada, norm_g,
           w_in, b_in, ln_v_g, ln_v_b, w_spatial, b_spatial, conv_w, conv_b,
           ln_c_g, ln_c_b, w_o_a, w_o_b, w_out, final_g):
    f = lambda a: np.ascontiguousarray(np.asarray(a, dtype=np.float32))
    x_prompt = f(x_prompt)
    seqlen = x_prompt.shape[1]
    npb = seqlen // TB
    nc = _get_program(npb)
    ncores = 8
    shared = {
        "w_ada": f(w_ada)[0], "b_ada": f(b_ada)[0], "norm_g": f(norm_g)[0], "w_in": f(w_in)[0],
        "b_in": f(b_in)[0], "ln_v_g": f(ln_v_g)[0], "ln_v_b": f(ln_v_b)[0], "w_sp": f(w_spatial)[0],
        "b_sp": f(b_spatial)[0].reshape(-1), "conv_w": f(conv_w)[0], "conv_b": f(conv_b)[0],
        "ln_c_g": f(ln_c_g)[0], "ln_c_b": f(ln_c_b)[0], "w_o_a": f(w_o_a)[0], "w_o_b": f(w_o_b)[0],
        "w_out": f(w_out)[0], "final_g": f(final_g),
    }
    x_sample = f(x_sample)
    state_conv = f(state_conv)
    c_prompt = f(c_prompt)
    c_sample = f(c_sample)
    in_maps = []
    for i in range(ncores):
        m = dict(shared)
        m["xp"] = np.ascontiguousarray(x_prompt[2 * i:2 * i + 2])
        m["xs"] = np.ascontiguousarray(x_sample[i])
        m["sc"] = np.ascontiguousarray(state_conv[0, i])
        m["cc"] = np.ascontiguousarray(np.concatenate([c_prompt[2 * i:2 * i + 2], c_sample[i:i + 1]], axis=0))
        in_maps.append(m)
    res = run_bass_kernel_spmd(nc, in_maps, core_ids=list(range(ncores)))
    r = res.results
    y_prompt = np.concatenate([r[i]["yp"] for i in range(ncores)], axis=0)
    y_sample = np.stack([r[i]["ys"] for i in range(ncores)], axis=0)
    ncp = np.concatenate([r[i]["ncp"] for i in range(ncores)], axis=0)[None]
    ncs = np.stack([r[i]["ncs"] for i in range(ncores)], axis=0)[None]
    nvs = np.stack([r[i]["nvs"] for i in range(ncores)], axis=0)[None]
    return (y_prompt.astype(np.float32), y_sample.astype(np.float32), ncp.astype(np.float32),
            ncs.astype(np.float32), nvs.astype(np.float32))
```

```python
import contextlib
import numpy as np
import concourse.bass as bass
import concourse.mybir as mybir
from concourse.bass_utils import run_bass_kernel_spmd

F32 = mybir.dt.float32
BF16 = mybir.dt.bfloat16
AF = mybir.ActivationFunctionType
ALU = mybir.AluOpType

D = 1024
KC = 8
SEQ = 2048
TB = 512
DEC = 16
HIST = 30
NTAP = 31
EPS = 1e-6
NSLOT = 22
SLOT_ELEMS = KC * 512

COMPUTE = ("pe", "act", "dve", "pool")


class _Tile:
    __slots__ = ("name", "last_w", "readers")

    def __init__(self, name):
        self.name = name
        self.last_w = None
        self.readers = []


class _Op:
    __slots__ = ("idx", "eng", "emit", "deps", "is_dma", "sem_key", "dma_val",
                 "needs_inc", "ticket", "n_dma", "tag", "w", "r", "nowaw")

    def __init__(self):
        self.deps = set()
        self.needs_inc = False
        self.ticket = None
        self.is_dma = False
        self.nowaw = False
        self.sem_key = None
        self.dma_val = None
        self.n_dma = 1


class Sched:
    def __init__(self, nc):
        self.nc = nc
        self.ops = []
        self.tiles = {}
        self.wait_all_keys = set()
        self.tag = ""

    def _tile(self, name):
        t = self.tiles.get(name)
        if t is None:
            t = _Tile(name)
            self.tiles[name] = t
        return t

    def _track(self, op, reads, writes, nowaw=False):
        for n in reads:
            t = self._tile(n)
            if t.last_w is not None:
                op.deps.add(t.last_w)
            t.readers.append(op.idx)
        for n in writes:
            t = self._tile(n)
            if t.last_w is not None:
                lw = self.ops[t.last_w]
                if not (nowaw and lw.eng == op.eng and getattr(lw, "nowaw", False) and not lw.is_dma):
                    op.deps.add(t.last_w)
            for r in t.readers:
                if r != op.idx:
                    op.deps.add(r)
            t.last_w = op.idx
            t.readers = []
        op.deps.discard(op.idx)

    def op(self, eng, emit, reads=(), writes=(), nowaw=False):
        o = _Op()
        o.idx = len(self.ops)
        o.eng = eng
        o.emit = emit
        o.tag = self.tag
        o.nowaw = nowaw
        o.w = tuple(writes); o.r = tuple(reads)
        self.ops.append(o)
        self._track(o, reads, writes, nowaw)
        return o

    def dma(self, queue, emit, reads=(), writes=(), sem_key=None, n_dma=1):
        o = _Op()
        o.idx = len(self.ops)
        o.eng = queue
        o.emit = emit
        o.is_dma = True
        o.sem_key = sem_key
        o.n_dma = n_dma
        o.tag = self.tag
        o.w = tuple(writes); o.r = tuple(reads)
        self.ops.append(o)
        self._track(o, reads, writes)
        return o

    def emit_all(self):
        nc = self.nc
        ops = self.ops
        for o in ops:
            latest = {}
            keep = set()
            for d in o.deps:
                p = ops[d]
                if p.is_dma:
                    keep.add(d)
                elif latest.get(p.eng, -1) < d:
                    latest[p.eng] = d
            keep.update(latest.values())
            o.deps = keep
        for o in ops:
            for d in o.deps:
                p = ops[d]
                if p.is_dma:
                    continue
                if p.eng == "pe" and o.eng == "pe" and not o.is_dma:
                    continue
                p.needs_inc = True
        cnt = {e: 0 for e in COMPUTE}
        for o in ops:
            if o.is_dma:
                continue
            if o.needs_inc:
                cnt[o.eng] += 1
                o.ticket = cnt[o.eng]
        dkeys = {}
        for o in ops:
            if o.is_dma:
                k = o.sem_key
                dkeys[k] = dkeys.get(k, 0) + 16 * o.n_dma
                o.dma_val = dkeys[k]
        for o in ops:
            if o.is_dma and o.sem_key in self.wait_all_keys:
                o.dma_val = dkeys[o.sem_key]
        self.stats = dict(cnt)
        self.stats["n_ops"] = len(ops)
        self.stats["dma_keys"] = len(dkeys)
        with contextlib.ExitStack() as st:
            esem = {e: st.enter_context(nc.semaphore("s_" + e)) for e in COMPUTE}
            dsem = {k: st.enter_context(nc.semaphore("d_%d" % i)) for i, k in enumerate(dkeys)}
            block = st.enter_context(nc.Block())
            streams = {}
            for o in ops:
                streams.setdefault(o.eng, []).append(o)

            def run(engname, e):
                waited = {}
                for o in streams.get(engname, []):
                    need = {}
                    for d in o.deps:
                        p = ops[d]
                        if p.is_dma:
                            if o.is_dma and o.sem_key == p.sem_key and p.sem_key in self.wait_all_keys:
                                continue
                            s = dsem[p.sem_key]
                            v = p.dma_val
                        else:
                            if p.eng == "pe" and engname == "pe" and not o.is_dma:
                                continue
                            s = esem[p.eng]
                            v = p.ticket
                        key = id(s)
                        if need.get(key, (None, 0))[1] < v:
                            need[key] = (s, v)
                    for key, (s, v) in need.items():
                        if waited.get(key, 0) >= v:
                            continue
                        e.wait_ge(s, v)
                        waited[key] = v
                    r = o.emit(e)
                    if o.is_dma:
                        s = dsem[o.sem_key]
                        assert len(r) == o.n_dma, (len(r), o.n_dma)
                        for ins in r:
                            ins.then_inc(s, 16)
                    elif o.needs_inc:
                        r.then_inc(esem[engname], 1)
                last = {}
                for o in streams.get(engname, []):
                    if o.is_dma:
                        last[o.sem_key] = max(last.get(o.sem_key, 0), o.dma_val)
                for k, v in last.items():
                    e.wait_ge(dsem[k], v)

            if "pe" in streams:
                @block.tensor
                def _(e):
                    run("pe", e)
            if "act" in streams:
                @block.scalar
                def _(e):
                    run("act", e)
            if "dve" in streams:
                @block.vector
                def _(e):
                    run("dve", e)
            if "pool" in streams:
                @block.gpsimd
                def _(e):
                    run("pool", e)
            if "sp" in streams:
                @block.sync
                def _(e):
                    run("sp", e)


def build_program(npb=4, nring=5, diag_act_every=2):
    seqlen = npb * TB
    nc = bass.Bass("TRN2", target_bir_lowering=False)

    def din(name, shape):
        return nc.dram_tensor(name, shape, F32, kind="ExternalInput").ap()

    def dout(name, shape):
        return nc.dram_tensor(name, shape, F32, kind="ExternalOutput").ap()

    xp = din("xp", [2, seqlen, D])
    xs = din("xs", [DEC, D])
    scv = din("sc", [HIST, D])
    cc = din("cc", [3, D])
    w_ada = din("w_ada", [D, 3 * D])
    b_ada = din("b_ada", [3 * D])
    norm_g = din("norm_g", [D])
    w_in = din("w_in", [D, 8 * D])
    b_in = din("b_in", [8 * D])
    ln_v_g = din("ln_v_g", [D])
    ln_v_b = din("ln_v_b", [D])
    w_sp = din("w_sp", [8, 128, 128])
    b_sp = din("b_sp", [8 * 128])
    conv_w = din("conv_w", [NTAP, D])
    conv_b = din("conv_b", [D])
    ln_c_g = din("ln_c_g", [D])
    ln_c_b = din("ln_c_b", [D])
    w_o_a = din("w_o_a", [D, D])
    w_o_b = din("w_o_b", [D, D])
    w_out = din("w_out", [D, D])
    final_g = din("final_g", [D])
    yp = dout("yp", [2, seqlen, D])
    ys = dout("ys", [DEC, D])
    ncp = dout("ncp", [2, HIST, D])
    ncs = dout("ncs", [HIST, D])
    nvs = dout("nvs", [DEC, D])
    wscr = nc.dram_tensor("wscr", [NSLOT, 128, SLOT_ELEMS], BF16).ap()

    S = Sched(nc)
    S.wait_all_keys.add("setup")
    st = contextlib.ExitStack()

    def sb(name, shape, dt):
        return st.enter_context(nc.sbuf_tensor(name, shape, dt))

    ring = sb("ring", [128, nring, KC, 512], BF16)
    xin = sb("xin", [128, 4, D], F32)
    junk = sb("junk", [128, D], BF16)
    xn = sb("xn", [128, 2, D], BF16)
    hT = sb("hT", [128, 2, KC, TB], BF16)
    gv = sb("gv", [128, D], F32)
    vhat = sb("vhat", [128, 4, D], BF16)
    gu = sb("gu", [128, KC, TB], BF16)
    gcat = sb("gcat", [128, KC, HIST + TB], BF16)
    ah = sb("ah", [128, 2, TB], F32)
    th = sb("th", [128, 2, TB], F32)
    gtl = sb("gtl", [128, KC, 32], F32)
    diag = sb("diag", [128, 2, NTAP, 128], BF16)
    dwb = sb("dwb", [128, KC, TB], BF16)
    dwsq = sb("dwsq", [128, 2, TB], BF16)
    mu_bc = sb("mu_bc", [128, TB], F32)
    rs_bc = sb("rs_bc", [128, TB], F32)
    nt1 = sb("nt1", [128, TB], F32)
    nt2 = sb("nt2", [128, TB], F32)
    sga = sb("sga", [128, 2, TB], BF16)
    pbuf = sb("pbuf", [128, 2, TB], BF16)
    sgb = sb("sgb", [128, KC, TB], BF16)
    nb = sb("nb", [128, 2, TB], BF16)
    tha = sb("tha", [128, 4, TB], BF16)
    thb = sb("thb", [128, 4, TB], BF16)
    xr = sb("xr", [128, D], F32)
    ybuf = sb("ybuf", [128, D], F32)
    ident_f = sb("ident_f", [128, 128], F32)
    ident_b = sb("ident_b", [128, 128], BF16)
    ones_f = sb("ones_f", [128, 128], F32)
    ones_b = sb("ones_b", [128, 128], BF16)
    negh = sb("negh", [128, 8], F32)
    R0 = sb("R0", [128, 128], F32)
    R1 = sb("R1", [128, 128], F32)
    RT0 = sb("RT0", [128, 128], F32)
    RT1 = sb("RT1", [128, 32], F32)
    bTh = sb("bTh", [128, 64], F32)
    cwT = sb("cwT", [128, NTAP * 8], F32)
    scT = sb("scT", [128, HIST * 8], F32)
    wmT = sb("wmT", [128, KC, 128], BF16)
    wmf = sb("wmf", [128, 128], F32)
    Bh = sb("Bh", [128, KC, 128], F32)
    siluT = sb("siluT", [128, 24], F32)
    modT = sb("modT", [128, 24, 3], F32)
    aT = sb("aT", [128, KC, 3], F32)
    gateh = sb("gateh", [128, KC, 3], F32)
    glh = sb("glh", [128, 128], F32)
    gate_bc = sb("gate_bc", [128, D], F32)
    fg_bc = sb("fg_bc", [128, D], F32)
    bv2 = sb("bv2", [2, D], BF16)
    lo_row = sb("lo_row", [1, D], BF16)
    ones2 = sb("ones2", [2, 128], BF16)
    st_s = sb("st_s", [128, 16], F32)
    st_v = sb("st_v", [128, 16], F32)
    print("sbuf bytes remaining:", nc.sbuf_bytes_remaining)

    banks = [st.enter_context(nc.psum_tensor("bank%d" % i, [128, 512], F32)) for i in range(8)]
    bank_ctr = [0]
    held = set()

    def newbank(hold=False):
        while True:
            i = bank_ctr[0] % 8
            bank_ctr[0] += 1
            if i not in held:
                break
        if hold:
            held.add(i)
        return banks[i], "bank%d" % i

    def release(name):
        held.discard(int(name[4:]))

    def mm(out, lhsT, rhs, start, stop, reads, writes):
        S.op("pe", lambda e: e.matmul(out, lhsT=lhsT, rhs=rhs, start=start, stop=stop),
             reads=reads, writes=writes)

    def tr(out, in_, ident, reads, writes):
        S.op("pe", lambda e: e.transpose(out, in_, ident), reads=reads, writes=writes)

    def act(out, in_, func, reads, writes, bias=None, scale=None, accum_out=None, nowaw=False):
        kw = {}
        if bias is not None:
            kw["bias"] = bias
        if scale is not None:
            kw["scale"] = scale
        if accum_out is not None:
            kw["accum_out"] = accum_out
        S.op("act", lambda e: e.activation(out=out, in_=in_, func=func, **kw), reads=reads, writes=writes, nowaw=nowaw)

    def ts(eng, out, in0, s1, s2, op0, op1, reads, writes, nowaw=False):
        if op1 is None:
            S.op(eng, lambda e: e.tensor_scalar(out=out, in0=in0, scalar1=s1, scalar2=None, op0=op0),
                 reads=reads, writes=writes, nowaw=nowaw)
        else:
            S.op(eng, lambda e: e.tensor_scalar(out=out, in0=in0, scalar1=s1, scalar2=s2, op0=op0, op1=op1),
                 reads=reads, writes=writes)

    def tt(eng, out, in0, in1, op, reads, writes):
        S.op(eng, lambda e: e.tensor_tensor(out=out, in0=in0, in1=in1, op=op), reads=reads, writes=writes)

    def stt(out, in0, scalar, in1, op0, op1, reads, writes):
        S.op("dve", lambda e: e.scalar_tensor_tensor(out=out, in0=in0, scalar=scalar, in1=in1, op0=op0, op1=op1),
             reads=reads, writes=writes)

    def cp(eng, out, in_, reads, writes):
        S.op(eng, lambda e: e.tensor_copy(out, in_), reads=reads, writes=writes)

    def dma(q, out, in_, reads, writes, key):
        S.dma(q, lambda e: [e.dma_start(out=out, in_=in_)], reads=reads, writes=writes, sem_key=key)

    def rows128(v):
        return v.rearrange("(r k) -> r k", k=128)

    S.op("pool", lambda e: e.memset(ident_f[:], 1.0), writes=["ident_f"])
    S.op("pool", lambda e: e.affine_select(out=ident_f[:], in_=ident_f[:], pattern=[[-1, 128]],
                                           compare_op=ALU.is_equal, fill=0.0, base=0, channel_multiplier=1),
         reads=["ident_f"], writes=["ident_f"])
    S.op("pool", lambda e: e.memset(ones_f[:], 1.0), writes=["ones_f"])
    S.op("pool", lambda e: e.memset(negh[:], -0.5), writes=["negh"])
    cp("dve", ident_b[:], ident_f[:], ["ident_f"], ["ident_b"])
    cp("dve", ones_b[:], ones_f[:], ["ones_f"], ["ones_b"])
    cp("dve", ones2[:], ones_f[0:2, :], ["ones_f"], ["ones2"])

    dma("sp", R0[0:64, :], rows128(b_in), [], ["R0"], "setup")
    dma("sp", R0[64:88, :], rows128(b_ada), [], ["R0"], "setup")
    dma("sp", R0[88:96, :], rows128(norm_g), [], ["R0"], "setup")
    dma("sp", R0[96:104, :], rows128(ln_v_g), [], ["R0"], "setup")
    dma("sp", R0[104:112, :], rows128(ln_v_b), [], ["R0"], "setup")
    dma("sp", R0[112:120, :], rows128(conv_b), [], ["R0"], "setup")
    dma("sp", R0[120:128, :], rows128(ln_c_g), [], ["R0"], "setup")
    dma("sp", R1[0:8, :], rows128(ln_c_b), [], ["R1"], "setup")
    dma("sp", R1[8:32, :], cc.rearrange("s (c k) -> (s c) k", k=128), [], ["R1"], "setup")
    dma("sp", fg_bc[:], final_g.rearrange("(o n) -> o n", o=1).partition_broadcast(128), [], ["fg_bc"], "setup")
    vh_f = vhat[:].rearrange("p a b -> p (a b)").bitcast(F32)
    gu_f = gu[:].rearrange("p a b -> p (a b)").bitcast(F32)
    lnvb_row = vh_f[0:1, 0:D]
    bs_row = vh_f[0:1, D:2 * D]
    rs_row = gu_f[0:1, 0:D]
    bv_row = gu_f[0:1, D:2 * D]
    dma("sp", lnvb_row, ln_v_b.rearrange("(o n) -> o n", o=1), [], ["vhat"], "setup")
    dma("sp", bs_row, b_sp.rearrange("(o n) -> o n", o=1), [], ["vhat"], "setup")
    dma("sp", bv_row, b_in[D:2 * D].rearrange("(o n) -> o n", o=1), [], ["gu"], "setup")

    def wsrc(slot):
        def cols(w, c0, n):
            return w[:, c0:c0 + n].rearrange("(k p) n -> p k n", p=128)
        if slot in (0, 1):
            return [(0, 512, cols(w_in, D + slot * 512, 512))]
        if slot in (2, 3):
            return [(0, 512, cols(w_in, (slot - 2) * 512, 512))]
        if 4 <= slot <= 7:
            i = slot - 4
            return [(0, 256, cols(w_in, 3 * D + i * 256, 256)), (256, 256, cols(w_in, 4 * D + i * 256, 256))]
        if slot in (8, 9):
            return [(0, 512, cols(w_in, 2 * D + (slot - 8) * 512, 512))]
        if slot in (10, 11):
            return [(0, 512, cols(w_in, 5 * D + (slot - 10) * 512, 512))]
        if 12 <= slot <= 15:
            i = slot - 12
            return [(0, 256, cols(w_in, 6 * D + i * 256, 256)), (256, 256, cols(w_in, 7 * D + i * 256, 256))]
        if 16 <= slot <= 19:
            i = slot - 16
            return [(0, 256, cols(w_o_a, i * 256, 256)), (256, 256, cols(w_o_b, i * 256, 256))]
        return [(0, 512, cols(w_out, (slot - 20) * 512, 512))]

    def emit_conversion():
        S.tag = "conv"
        stg = []
        for h in range(2):
            v = ring[:, nring - 2 + h].rearrange("p k n -> p (k n)").bitcast(F32)
            stg.append((v, "ring%d" % (nring - 2 + h)))
        cobs = [(ring[:, nring - 3], "ring%d" % (nring - 3)), (hT[:, 1], "hT1")]
        for slot in range(NSLOT):
            srcs = wsrc(slot)
            cob, cn = cobs[slot % 2]
            for h in range(2):
                sv, sn = stg[h]
                if len(srcs) == 2:
                    (c0, n, src) = srcs[h]
                    view = sv.rearrange("p (k n) -> p k n", k=KC)
                    dstv = cob[:, :, c0:c0 + n]
                    dma("sp", view, src, [], [sn], "stg%d" % h)
                else:
                    (c0, n, src) = srcs[0]
                    view = sv.rearrange("p (k n) -> p k n", k=KC // 2)
                    dstv = cob[:, h * 4:(h + 1) * 4, :]
                    dma("sp", view, src[:, h * 4:(h + 1) * 4, :], [], [sn], "stg%d" % h)
                if h == 0:
                    act(dstv, view, AF.Identity, [sn], [cn])
                else:
                    cp("dve", dstv, view, [sn], [cn])
            dma("act", wscr[slot], cob.rearrange("p k n -> p (k n)"), [cn], ["wscr%d" % slot], "wscr_w%d" % slot)

    b0, n0 = newbank()
    tr(b0[:, 0:128], R0[:], ident_f[:], ["R0", "ident_f"], [n0])
    act(RT0[:], b0[:, 0:128], AF.Identity, [n0], ["RT0"])
    b1, n1 = newbank()
    tr(b1[:, 0:32], R1[0:32, :], ident_f[0:32, 0:32], ["R1", "ident_f"], [n1])
    act(RT1[:], b1[:, 0:32], AF.Identity, [n1], ["RT1"])
    ts("dve", bTh[:], RT0[:, 0:64], 0.5, None, ALU.mult, None, ["RT0"], ["bTh"])
    bT = RT0
    b_adaT = lambda jc: RT0[:, 64 + jc:65 + jc]
    norm_gT = lambda c: RT0[:, 88 + c:89 + c]
    lnvgT = lambda c: RT0[:, 96 + c:97 + c]
    convbT = lambda c: RT0[:, 112 + c:113 + c]
    lncgT = lambda c: RT0[:, 120 + c:121 + c]
    lncbT = lambda c: RT1[:, c:c + 1]
    act(siluT[:], RT1[:, 8:32], AF.Silu, ["RT1"], ["siluT"])

    cw_rows = conv_w.rearrange("j (c k) -> (j c) k", k=128)
    dma("act", R0[:], cw_rows[0:128, :], ["RT0"], ["R0"], "cw0")
    dma("act", R1[0:120, :], cw_rows[128:248, :], ["RT1"], ["R1"], "cw1")
    b2, n2 = newbank()
    tr(b2[:, 0:128], R0[:], ident_f[:], ["R0", "ident_f"], [n2])
    tr(b2[:, 128:248], R1[0:120, :], ident_f[0:120, 0:120], ["R1", "ident_f"], [n2])
    act(cwT[:], b2[:, 0:248], AF.Identity, [n2], ["cwT"])
    sc_rows = scv.rearrange("r (c k) -> (r c) k", k=128)
    dma("act", R0[:], sc_rows[0:128, :], [], ["R0"], "sc0")
    dma("act", R1[0:112, :], sc_rows[128:240, :], [], ["R1"], "sc1")
    b3, n3 = newbank()
    tr(b3[:, 0:128], R0[:], ident_f[:], ["R0", "ident_f"], [n3])
    tr(b3[:, 128:240], R1[0:112, :], ident_f[0:112, 0:112], ["R1", "ident_f"], [n3])
    act(scT[:], b3[:, 0:240], AF.Identity, [n3], ["scT"])

    def wsp_head(h):
        Rb, Rn = (R0, "R0") if h % 2 == 0 else (R1, "R1")
        dma("sp", Rb[:], w_sp[h], [], [Rn], "wsp%d" % (h % 2))
        bb, nn = newbank()
        tr(bb[:, 0:128], Rb[:], ident_f[:], [Rn, "ident_f"], [nn])
        act(wmf[:], bb[:, 0:128], AF.Identity, [nn], ["wmf"])
        S.op("pool", lambda e: e.affine_select(out=wmf[:], in_=wmf[:], pattern=[[1, 128]],
                                               compare_op=ALU.is_ge, fill=0.0, base=0, channel_multiplier=-1),
             reads=["wmf"], writes=["wmf"])
        cp("dve", wmT[:, h, :], wmf[:], ["wmf"], ["wmT"])
        bb2, nn2 = newbank()
        mm(bb2[0:1, 0:128], ones_f[:, 0:1], wmf[:], True, True, ["ones_f", "wmf"], [nn2])
        act(rs_row[:, h * 128:(h + 1) * 128], bb2[0:1, 0:128], AF.Identity, [nn2], ["gu"])
        bb3, nn3 = newbank()
        mm(bb3[:, 0:128], lnvb_row[:, h * 128:(h + 1) * 128], rs_row[:, h * 128:(h + 1) * 128], True, False,
           ["vhat", "gu"], [nn3])
        mm(bb3[:, 0:128], ones_f[0:1, :], bs_row[:, h * 128:(h + 1) * 128], False, True,
           ["ones_f", "vhat"], [nn3])
        act(Bh[:, h, :], bb3[:, 0:128], AF.Identity, [nn3], ["Bh"])

    sgb_names = ["sgb%d" % c for c in range(KC)]
    dwb_names = ["dwb%d" % c for c in range(KC)]
    ada_bufs = [
        (sgb[:].rearrange("p a b -> p (a b)").bitcast(F32).rearrange("p (k n) -> p k n", k=KC), sgb_names, "ada0"),
        (dwb[:].rearrange("p a b -> p (a b)").bitcast(F32).rearrange("p (k n) -> p k n", k=KC), dwb_names, "ada1"),
    ]
    for pc in range(12):
        ab, an, ak = ada_bufs[pc % 2]
        dma("sp", ab, w_ada[:, pc * 256:(pc + 1) * 256].rearrange("(k p) n -> p k n", p=128), [], an, ak)
        for jl in range(2):
            jc = pc * 2 + jl
            bb, nn = newbank()
            for kc in range(KC):
                mm(bb[:, 0:3], ab[:, kc, jl * 128:(jl + 1) * 128], siluT[:, kc:24:8], kc == 0, kc == KC - 1,
                   an + ["siluT"], [nn])
            act(modT[:, jc, :], bb[:, 0:3], AF.Identity, [nn], ["modT"], bias=b_adaT(jc))
        if pc < 8:
            wsp_head(pc)
    cp("dve", bv2[0:1, :], bv_row, ["gu"], ["bv2"])
    tt("dve", nt1[0:1, 0:512], bv_row[:, 0:512], bv2[0:1, 0:512], ALU.subtract, ["gu", "bv2"], ["nt1"])
    tt("dve", nt2[0:1, 0:512], bv_row[:, 512:1024], bv2[0:1, 512:1024], ALU.subtract, ["gu", "bv2"], ["nt2"])
    cp("dve", lo_row[:, 0:512], nt1[0:1, 0:512], ["nt1"], ["lo_row"])
    cp("dve", lo_row[:, 512:1024], nt2[0:1, 0:512], ["nt2"], ["lo_row"])
    dma("act", bv2[1:2, :], lo_row[:], ["lo_row"], ["bv2"], "bv2lo")

    for kc in range(KC):
        ts("dve", aT[:, kc, :], modT[:, 8 + kc, :], 1.0, norm_gT(kc), ALU.add, ALU.mult, ["modT", "RT0"], ["aT"])
        ts("dve", gateh[:, kc, :], modT[:, 16 + kc, :], 0.5, None, ALU.mult, None, ["modT"], ["gateh"])
    shT = lambda kc, s: modT[:, kc, s:s + 1]

    slot_ctr = [0]
    first_pass = [True]
    stg = []
    for h_ in range(2):
        v_ = ring[:, nring - 2 + h_].rearrange("p k n -> p (k n)").bitcast(F32)
        stg.append((v_, "ring%d" % (nring - 2 + h_), "stg%d" % h_))
    stg.append((hT[:, 1].rearrange("p k n -> p (k n)").bitcast(F32), "hT1", "stg2"))
    stg_ctr = [0]

    def load_slot(slot):
        nr = (nring - 2) if first_pass[0] else nring
        r = slot_ctr[0] % nr
        slot_ctr[0] += 1
        rn = "ring%d" % r
        dst = ring[:, r]
        if first_pass[0]:
            srcs = wsrc(slot)
            for h in range(2):
                sv, sn, sk = stg[stg_ctr[0] % len(stg)]
                stg_ctr[0] += 1
                if len(srcs) == 2:
                    (c0, n, src) = srcs[h]
                    view = sv.rearrange("p (k n) -> p k n", k=KC)
                    dstv = dst[:, :, c0:c0 + n]
                    dma("sp", view, src, [], [sn], sk)
                else:
                    (c0, n, src) = srcs[0]
                    view = sv.rearrange("p (k n) -> p k n", k=KC // 2)
                    dstv = dst[:, h * 4:(h + 1) * 4, :]
                    dma("sp", view, src[:, h * 4:(h + 1) * 4, :], [], [sn], sk)
                if h == 0:
                    act(dstv, view, AF.Identity, [sn], [rn])
                else:
                    cp("dve", dstv, view, [sn], [rn])
            dma("act", wscr[slot], dst.rearrange("p k n -> p (k n)"), [rn], ["wscr%d" % slot], "wscr_w%d" % slot)
        else:
            dma("sp", dst.rearrange("p k n -> p (k n)"), wscr[slot], ["wscr%d" % slot], [rn], "ringld%d" % r)
        return dst, rn

    class Blk:
        pass

    blocks = []
    for sq in range(2):
        for b in range(npb):
            k = Blk()
            k.seq, k.xsrc, k.ydst, k.t0, k.T = sq, xp[sq], yp[sq], b * TB, TB
            k.first, k.last, k.is_sample = (b == 0), (b == npb - 1), False
            blocks.append(k)
    k = Blk()
    k.seq, k.xsrc, k.ydst, k.t0, k.T = 2, xs, ys, 0, DEC
    k.first, k.last, k.is_sample = True, True, True
    blocks.append(k)
    for i, k in enumerate(blocks):
        k.idx = i
        k.TR = min(k.T, 128)
        k.NTT = max(1, k.T // 128)
        k.h = i % 2
        k.hn = "hT%d" % k.h

    def emit_xload(k):
        for t_ in range(k.NTT):
            dma("sp", xin[0:k.TR, t_, :], k.xsrc[k.t0 + t_ * 128:k.t0 + t_ * 128 + k.TR, :], [], ["xin%d" % t_],
                "xin%d" % t_)

    def emit_P0_A(k, t_):
        S.tag = "P0"
        TR = k.TR
        act(junk[0:TR, :], xin[0:TR, t_, :], AF.Square, ["xin%d" % t_], ["junk", "st_s0"], accum_out=st_s[0:TR, 0:1])
        ts("dve", st_v[0:TR, 0:1], st_s[0:TR, 0:1], 1.0 / D, EPS, ALU.mult, ALU.add, ["st_s0"], ["st_v0"])
        tt("pool", st_v[0:TR, 1:2], st_v[0:TR, 0:1], negh[0:TR, 0:1], ALU.pow, ["st_v0", "negh"], ["st_v0"])
        ts("dve", xn[0:TR, t_ % 2, :], xin[0:TR, t_, :], st_v[0:TR, 1:2], None, ALU.mult, None,
           ["xin%d" % t_, "st_v0"], ["xn%d" % (t_ % 2)])

    def emit_P0_B(k, t_):
        S.tag = "P0"
        TR, s = k.TR, k.seq
        hTb = hT[:, k.h]
        bb, nn = newbank()
        bbf = bb[:].bitcast(BF16)
        for kc in range(KC):
            tr(bbf[:, kc * 128:kc * 128 + TR], xn[0:TR, t_ % 2, kc * 128:(kc + 1) * 128], ident_b[0:TR, 0:TR],
               ["xn%d" % (t_ % 2), "ident_b"], [nn])
        for kc in range(KC):
            act(hTb[:, kc, t_ * 128:t_ * 128 + TR], bbf[:, kc * 128:kc * 128 + TR], AF.Identity, [nn, "aT", "modT"],
                [k.hn], bias=shT(kc, s), scale=aT[:, kc, s:s + 1], nowaw=True)

    def emit_P0_tile(k, t_):
        emit_P0_A(k, t_)
        emit_P0_B(k, t_)

    def emit_P1_pre(k):
        S.tag = "P1"
        k.slots_v = [load_slot(0), load_slot(1)]
        if k.is_sample:
            dma("sp", xr[0:DEC, :], ln_v_g.rearrange("(o n) -> o n", o=1).partition_broadcast(DEC), [], ["xr"], "xr")
            dma("sp", ybuf[0:DEC, :], ln_v_b.rearrange("(o n) -> o n", o=1).partition_broadcast(DEC), [], ["ybuf"],
                "ybuf_ld")

    def emit_P1_tile(k, t_):
        S.tag = "P1"
        TR = k.TR
        hTb = hT[:, k.h]
        for half in range(2):
            sl, sn = k.slots_v[half]
            bb, nn = newbank()
            for kc in range(KC):
                mm(bb[0:TR, :], hTb[:, kc, t_ * 128:t_ * 128 + TR], sl[:, kc, :], kc == 0, False, [k.hn, sn], [nn])
            mm(bb[0:TR, :], ones2[0:2, 0:TR], bv2[0:2, half * 512:(half + 1) * 512], False, True,
               ["ones2", "bv2"], [nn])
            act(gv[0:TR, half * 512:(half + 1) * 512], bb[0:TR, :], AF.Gelu_apprx_tanh, [nn], ["gv", "st_s1"],
                accum_out=st_s[0:TR, 2 + half:3 + half])
            act(junk[0:TR, 0:512], gv[0:TR, half * 512:(half + 1) * 512], AF.Square, ["gv"], ["junk", "st_s1"],
                accum_out=st_s[0:TR, 4 + half:5 + half])
        sv = ["st_v1"]
        tt("dve", st_v[0:TR, 2:3], st_s[0:TR, 2:3], st_s[0:TR, 3:4], ALU.add, ["st_s1"], sv)
        tt("dve", st_v[0:TR, 3:4], st_s[0:TR, 4:5], st_s[0:TR, 5:6], ALU.add, ["st_s1"], sv)
        ts("dve", st_v[0:TR, 2:4], st_v[0:TR, 2:4], 1.0 / D, None, ALU.mult, None, sv, sv)
        tt("dve", st_v[0:TR, 4:5], st_v[0:TR, 2:3], st_v[0:TR, 2:3], ALU.mult, sv, sv)
        tt("dve", st_v[0:TR, 5:6], st_v[0:TR, 3:4], st_v[0:TR, 4:5], ALU.subtract, sv, sv)
        ts("dve", st_v[0:TR, 5:6], st_v[0:TR, 5:6], EPS, None, ALU.add, None, sv, sv)
        tt("pool", st_v[0:TR, 6:7], st_v[0:TR, 5:6], negh[0:TR, 0:1], ALU.pow, sv + ["negh"], sv)
        ts("dve", st_v[0:TR, 7:8], st_v[0:TR, 2:3], st_v[0:TR, 6:7], -1.0, ALU.mult, ALU.mult, sv, sv)
        ts("dve", vhat[0:TR, t_, :], gv[0:TR, :], st_v[0:TR, 6:7], st_v[0:TR, 7:8], ALU.mult, ALU.add,
           ["gv"] + sv, ["vhat"])
        if k.is_sample:
            ts("dve", gv[0:TR, :], gv[0:TR, :], st_v[0:TR, 6:7], st_v[0:TR, 7:8], ALU.mult, ALU.add,
               ["gv"] + sv, ["gv"])
            tt("dve", gv[0:TR, :], gv[0:TR, :], xr[0:TR, :], ALU.mult, ["gv", "xr"], ["gv"])
            tt("dve", gv[0:TR, :], gv[0:TR, :], ybuf[0:TR, :], ALU.add, ["gv", "ybuf"], ["gv"])
            dma("act", nvs, gv[0:DEC, :], ["gv"], [], "nvs")

    def emit_front(k):
        emit_xload(k)
        for t_ in range(k.NTT):
            emit_P0_tile(k, t_)
        emit_P1_pre(k)
        for t_ in range(k.NTT):
            emit_P1_tile(k, t_)

    def emit_body(k, nxt, hoist):
        seq, xsrc, ydst, t0, T, first, last, is_sample = k.seq, k.xsrc, k.ydst, k.t0, k.T, k.first, k.last, k.is_sample
        TR, NTT, s = k.TR, k.NTT, k.seq
        hTb = hT[:, k.h]
        hn = k.hn
        S.tag = "P2"
        for sidx in (2, 3):
            sl, sn = load_slot(sidx)
            for cl in range(4):
                c = (sidx - 2) * 4 + cl
                bb, nn = newbank()
                for kc in range(KC):
                    mm(bb[:, 0:T], sl[:, kc, cl * 128:(cl + 1) * 128], hTb[:, kc, 0:T], kc == 0, kc == KC - 1,
                       [sn, hn], [nn])
                act(gu[:, c, 0:T], bb[:, 0:T], AF.Gelu_apprx_tanh, [nn], ["gu%d" % c], bias=bT[:, c:c + 1])

        def build_diag(c):
            for j in range(NTAP):
                if diag_act_every and (j % diag_act_every == diag_act_every - 1):
                    act(diag[:, c % 2, j, :], ident_b[:], AF.Identity, ["ident_b", "cwT"], ["diagA%d" % (c % 2)],
                        scale=cwT[:, j * 8 + c:j * 8 + c + 1], nowaw=True)
                else:
                    ts("dve", diag[:, c % 2, j, :], ident_b[:], cwT[:, j * 8 + c:j * 8 + c + 1], None, ALU.mult, None,
                       ["ident_b", "cwT"], ["diagD%d" % (c % 2)], nowaw=True)
        S.tag = "P3"
        build_diag(0)
        gnames = ["gcat%d" % c for c in range(KC)]
        if first and not is_sample:
            S.op("dve", lambda e: e.memset(gcat[:, :, 0:HIST], 0.0), writes=gnames)
        elif is_sample:
            cp("dve", gcat[:, :, 0:HIST], scT[:].rearrange("p (r c) -> p c r", c=8), ["scT"], gnames)
        NTL = min(HIST, T)
        for i in range(4):
            sl, sn = load_slot(4 + i)
            for cl in range(2):
                c = i * 2 + cl
                ba, na = newbank()
                bbk, nbk = newbank()
                for kc in range(KC):
                    mm(ba[:, 0:T], sl[:, kc, cl * 128:(cl + 1) * 128], hTb[:, kc, 0:T], kc == 0, kc == KC - 1,
                       [sn, hn], [na])
                for kc in range(KC):
                    mm(bbk[:, 0:T], sl[:, kc, 256 + cl * 128:256 + (cl + 1) * 128], hTb[:, kc, 0:T], kc == 0,
                       kc == KC - 1, [sn, hn], [nbk])
                act(ah[:, c % 2, 0:T], ba[:, 0:T], AF.Identity, [na, "bTh"], ["ah%d" % (c % 2)],
                    bias=bTh[:, 24 + c:25 + c], scale=0.5)
                act(th[:, c % 2, 0:T], bbk[:, 0:T], AF.Tanh, [nbk, "bTh"], ["th%d" % (c % 2)],
                    bias=bTh[:, 32 + c:33 + c], scale=0.5)
                stt(gcat[:, c, HIST:HIST + T], th[:, c % 2, 0:T], 1.0, ah[:, c % 2, 0:T], ALU.add, ALU.mult,
                    ["ah%d" % (c % 2), "th%d" % (c % 2)], ["gcat%d" % c])
                if last:
                    stt(gtl[:, c, 0:NTL], th[:, c % 2, T - NTL:T], 1.0, ah[:, c % 2, T - NTL:T], ALU.add, ALU.mult,
                        ["ah%d" % (c % 2), "th%d" % (c % 2)], ["gtl"])
        if last:
            bb, nn = newbank()
            bb2, nn2 = newbank()
            for c in range(KC):
                tgt = bb if c < 4 else bb2
                tr(tgt[0:NTL, (c % 4) * 128:(c % 4 + 1) * 128], gtl[:, c, 0:NTL], ident_f[:], ["gtl", "ident_f"],
                   [nn if c < 4 else nn2])
            act(ybuf[0:NTL, 0:512], bb[0:NTL, :], AF.Identity, [nn], ["ybuf"])
            act(ybuf[0:NTL, 512:1024], bb2[0:NTL, :], AF.Identity, [nn2], ["ybuf"])
            if is_sample:
                dma("act", ncs[HIST - NTL:HIST, :], ybuf[0:NTL, :], ["ybuf"], [], "ybuf_st")
                dma("sp", ncs[0:HIST - NTL, :], scv[NTL:HIST, :], [], [], "ncs_cp")
            else:
                dma("act", ncp[s], ybuf[0:NTL, :], ["ybuf"], [], "ybuf_st")
        if nxt is not None:
            emit_xload(nxt)
        S.tag = "P4P5"
        sl_ga = [None, None]
        sl_gb = [None, None]
        bsum, nsum = newbank(hold=True)
        bsq, nsq = newbank(hold=True)

        def stats_mm(c):
            mm(bsum[:, 0:T], ones_b[:], dwb[:, c, 0:T], c == 0, c == KC - 1, ["ones_b", "dwb%d" % c], [nsum])
            mm(bsq[:, 0:T], ones_b[:], dwsq[:, c % 2, 0:T], c == 0, c == KC - 1, ["ones_b", "dwsq%d" % (c % 2)], [nsq])

        for c in range(KC):
            if c + 1 < KC:
                build_diag(c + 1)
            bb, nn = newbank()
            for j in range(NTAP):
                mm(bb[:, 0:T], diag[:, c % 2, j, :], gcat[:, c, j:j + T], j == 0, j == NTAP - 1,
                   ["diagA%d" % (c % 2), "diagD%d" % (c % 2), "gcat%d" % c], [nn])
            act(dwb[:, c, 0:T], bb[:, 0:T], AF.Identity, [nn], ["dwb%d" % c], bias=convbT(c))
            act(dwsq[:, c % 2, 0:T], bb[:, 0:T], AF.Square, [nn], ["dwsq%d" % (c % 2)], bias=convbT(c))
            if c % 4 == 0:
                sl_ga[0], sl_ga[1] = load_slot(8 + c // 4)
                sl_gb[0], sl_gb[1] = load_slot(10 + c // 4)
            cl = c % 4
            bg, ng = newbank()
            for kc in range(KC):
                mm(bg[:, 0:T], sl_ga[0][:, kc, cl * 128:(cl + 1) * 128], hTb[:, kc, 0:T], kc == 0, kc == KC - 1,
                   [sl_ga[1], hn], [ng])
            bs_, ns_ = newbank()
            for n_ in range(NTT):
                mm(bs_[:, n_ * 128:n_ * 128 + TR], vhat[0:TR, n_, c * 128:(c + 1) * 128], wmT[0:TR, c, 0:TR], True, True,
                   ["vhat", "wmT"], [ns_])
            act(sga[:, c % 2, 0:T], bg[:, 0:T], AF.Silu, [ng], ["sga%d" % (c % 2)], bias=bT[:, 16 + c:17 + c])
            tt("dve", pbuf[:, c % 2, 0:T], gu[:, c, 0:T], sga[:, c % 2, 0:T], ALU.mult, ["gu%d" % c, "sga%d" % (c % 2)],
               ["pbuf%d" % (c % 2)])
            if T == TB:
                in1 = Bh[:, c, :].unsqueeze(1).to_broadcast([128, 4, 128])
                in0 = bs_[:, :].rearrange("p (a b) -> p a b", a=4)
                o_ = nt1[:, :].rearrange("p (a b) -> p a b", a=4)
            else:
                in1 = Bh[:, c, 0:T]
                in0 = bs_[:, 0:T]
                o_ = nt1[:, 0:T]
            stt(o_, in0, lnvgT(c), in1, ALU.mult, ALU.add, [ns_, "Bh", "RT0"], ["nt1"])
            tt("dve", gu[:, c, 0:T], nt1[:, 0:T], pbuf[:, c % 2, 0:T], ALU.mult, ["nt1", "pbuf%d" % (c % 2)],
               ["gu%d" % c])
            bgb, ngb = newbank()
            for kc in range(KC):
                mm(bgb[:, 0:T], sl_gb[0][:, kc, cl * 128:(cl + 1) * 128], hTb[:, kc, 0:T], kc == 0, kc == KC - 1,
                   [sl_gb[1], hn], [ngb])
            act(sgb[:, c, 0:T], bgb[:, 0:T], AF.Silu, [ngb], ["sgb%d" % c], bias=bT[:, 40 + c:41 + c])
            if c >= 1:
                stats_mm(c - 1)
        stats_mm(KC - 1)
        if not last:
            cp("dve", gcat[:, :, 0:HIST], gcat[:, :, T:T + HIST], gnames, gnames)
        S.tag = "P7"
        mT = gcat
        p7 = {}

        def p7_slots(i):
            if ("m", i) not in p7:
                p7[("m", i)] = load_slot(12 + i)
                p7[("o", i)] = load_slot(16 + i)
            return p7[("m", i)], p7[("o", i)]

        def p7_part1(o):
            S.tag = "P7"
            i, ol = o // 2, o % 2
            (sl_m, sn_m), (sl_o, sn_o) = p7_slots(i)
            bma, nma = newbank()
            bmb, nmb = newbank()
            bpa, npa = newbank(hold=True)
            p7[("pa", o)] = (bpa, npa)
            for kc in range(KC):
                mm(bma[:, 0:T], sl_m[:, kc, ol * 128:(ol + 1) * 128], hTb[:, kc, 0:T], kc == 0, kc == KC - 1,
                   [sn_m, hn], [nma])
            for kc in range(KC):
                mm(bmb[:, 0:T], sl_m[:, kc, 256 + ol * 128:256 + (ol + 1) * 128], hTb[:, kc, 0:T], kc == 0,
                   kc == KC - 1, [sn_m, hn], [nmb])
            for c in range(KC):
                mm(bpa[:, 0:T], sl_o[:, c, ol * 128:(ol + 1) * 128], gu[:, c, 0:T], c == 0, c == KC - 1,
                   [sn_o, "gu%d" % c], [npa])
            act(tha[:, o % 4, 0:T], bma[:, 0:T], AF.Tanh, [nma, "bTh"], ["tha%d" % (o % 4)],
                bias=bTh[:, 48 + o:49 + o], scale=0.5)
            act(thb[:, o % 4, 0:T], bmb[:, 0:T], AF.Tanh, [nmb, "bTh"], ["thb%d" % (o % 4)],
                bias=bTh[:, 56 + o:57 + o], scale=0.5)

        def p7_part2(o):
            S.tag = "P7"
            i, ol = o // 2, o % 2
            (sl_m, sn_m), (sl_o, sn_o) = p7_slots(i)
            bpa, npa = p7[("pa", o)]
            bpb, npb_ = newbank()
            for c in range(KC):
                mm(bpb[:, 0:T], sl_o[:, c, 256 + ol * 128:256 + (ol + 1) * 128], sgb[:, c, 0:T], c == 0, c == KC - 1,
                   [sn_o, "sgb%d" % c], [npb_])
            stt(nt1[:, 0:T], tha[:, o % 4, 0:T], 1.0, bpa[:, 0:T], ALU.add, ALU.mult, ["tha%d" % (o % 4), npa], ["nt1"])
            stt(nt2[:, 0:T], thb[:, o % 4, 0:T], 1.0, bpb[:, 0:T], ALU.add, ALU.mult, ["thb%d" % (o % 4), npb_], ["nt2"])
            tt("dve", mT[:, o, HIST:HIST + T], nt1[:, 0:T], nt2[:, 0:T], ALU.add, ["nt1", "nt2"], ["gcat%d" % o])
            release(npa)

        p7_part1(0)
        p7_part1(1)
        p7_part1(2)
        S.tag = "LN_c"
        release(nsum)
        release(nsq)
        ts("dve", mu_bc[:, 0:T], bsum[:, 0:T], 1.0 / D, None, ALU.mult, None, [nsum], ["mu_bc"])
        tt("dve", nt1[:, 0:T], mu_bc[:, 0:T], mu_bc[:, 0:T], ALU.mult, ["mu_bc"], ["nt1"])
        stt(nt2[:, 0:T], bsq[:, 0:T], 1.0 / D, nt1[:, 0:T], ALU.mult, ALU.subtract, [nsq, "nt1"], ["nt2"])
        ts("dve", nt2[:, 0:T], nt2[:, 0:T], EPS, None, ALU.add, None, ["nt2"], ["nt2"])
        I32 = mybir.dt.int32
        rsi = rs_bc[:, 0:T].bitcast(I32)
        S.op("dve", lambda e: e.tensor_single_scalar(out=rsi, in_=nt2[:, 0:T].bitcast(I32), scalar=1,
                                                     op=ALU.arith_shift_right), reads=["nt2"], writes=["rs_bc"])
        ts("dve", rsi, rsi, -1, 0x5f3759df, ALU.mult, ALU.add, ["rs_bc"], ["rs_bc"])
        for it in range(3):
            tt("dve", nt1[:, 0:T], rs_bc[:, 0:T], rs_bc[:, 0:T], ALU.mult, ["rs_bc"], ["nt1"])
            tt("dve", nt1[:, 0:T], nt1[:, 0:T], nt2[:, 0:T], ALU.mult, ["nt1", "nt2"], ["nt1"])
            ts("dve", nt1[:, 0:T], nt1[:, 0:T], -0.5, 1.5, ALU.mult, ALU.add, ["nt1"], ["nt1"])
            tt("dve", rs_bc[:, 0:T], rs_bc[:, 0:T], nt1[:, 0:T], ALU.mult, ["rs_bc", "nt1"], ["rs_bc"])
        S.tag = "P6"
        def p6_pre(c):
            tmp, tmpn = (nt1, "nt1") if c % 2 == 0 else (nt2, "nt2")
            tt("dve", tmp[:, 0:T], dwb[:, c, 0:T], mu_bc[:, 0:T], ALU.subtract, ["dwb%d" % c, "mu_bc"], [tmpn])
            tt("dve", tmp[:, 0:T], tmp[:, 0:T], rs_bc[:, 0:T], ALU.mult, [tmpn, "rs_bc"], [tmpn])
            act(nb[:, c % 2, 0:T], tmp[:, 0:T], AF.Silu, [tmpn, "RT0", "RT1"], ["nb%d" % (c % 2)], bias=lncbT(c),
                scale=lncgT(c))

        def p6_fin(c):
            tt("dve", sgb[:, c, 0:T], nb[:, c % 2, 0:T], sgb[:, c, 0:T], ALU.mult, ["nb%d" % (c % 2), "sgb%d" % c],
               ["sgb%d" % c])

        p6_pre(0)
        for c in range(1, KC):
            p6_pre(c)
            p6_fin(c - 1)
        p6_fin(KC - 1)
        for o in range(KC):
            p7_part2(o)
            if o + 3 < KC:
                p7_part1(o + 3)
            if nxt is not None and hoist:
                if o < 4:
                    if o == 0:
                        emit_P0_A(nxt, 0)
                    if o + 1 < nxt.NTT:
                        emit_P0_A(nxt, o + 1)
                    if o < nxt.NTT:
                        emit_P0_B(nxt, o)
                    if o == 3:
                        emit_P1_pre(nxt)
                else:
                    if o - 4 < nxt.NTT:
                        emit_P1_tile(nxt, o - 4)
        S.tag = "P8"
        slots_o = [load_slot(20), load_slot(21)]
        xrb = [(xr, "xr"), (gv, "gv")]

        def p8_A(t_):
            xb, xn_ = xrb[t_ % 2]
            dma("act", xb[0:TR, :], xsrc[t0 + t_ * 128:t0 + t_ * 128 + TR, :], [], [xn_], "xr_ld%d" % (t_ % 2))
            for half in range(2):
                sl, sn = slots_o[half]
                bb, nn = newbank()
                for kc in range(KC):
                    mm(bb[0:TR, :], mT[:, kc, HIST + t_ * 128:HIST + t_ * 128 + TR], sl[:, kc, :], kc == 0, kc == KC - 1,
                       ["gcat%d" % kc, sn], [nn])
                tmp, tmpn = (nt1, "nt1") if half == 0 else (nt2, "nt2")
                tt("dve", tmp[0:TR, :], bb[0:TR, :], gate_bc[0:TR, half * 512:(half + 1) * 512],
                   ALU.mult, [nn, "gate_bc"], [tmpn])
                tt("dve", xb[0:TR, half * 512:(half + 1) * 512], xb[0:TR, half * 512:(half + 1) * 512], tmp[0:TR, :],
                   ALU.add, [xn_, tmpn], [xn_])
            c8 = 8 + (t_ % 2)
            v8 = 8 + 2 * (t_ % 2)
            sn8, vn8 = "st_s8_%d" % (t_ % 2), "st_v8_%d" % (t_ % 2)
            act(junk[0:TR, :], xb[0:TR, :], AF.Square, [xn_], ["junk", sn8], accum_out=st_s[0:TR, c8:c8 + 1])

        def p8_A2(t_):
            c8 = 8 + (t_ % 2)
            v8 = 8 + 2 * (t_ % 2)
            sn8, vn8 = "st_s8_%d" % (t_ % 2), "st_v8_%d" % (t_ % 2)
            ts("dve", st_v[0:TR, v8:v8 + 1], st_s[0:TR, c8:c8 + 1], 1.0 / D, EPS, ALU.mult, ALU.add, [sn8], [vn8])
            tt("pool", st_v[0:TR, v8 + 1:v8 + 2], st_v[0:TR, v8:v8 + 1], negh[0:TR, 0:1], ALU.pow, [vn8, "negh"], [vn8])

        def p8_B(t_):
            xb, xn_ = xrb[t_ % 2]
            v8 = 8 + 2 * (t_ % 2)
            vn8 = "st_v8_%d" % (t_ % 2)
            stt(ybuf[0:TR, :], xb[0:TR, :], st_v[0:TR, v8 + 1:v8 + 2], fg_bc[0:TR, :], ALU.mult, ALU.mult,
                [xn_, vn8, "fg_bc"], ["ybuf"])
            dma("act", ydst[t0 + t_ * 128:t0 + t_ * 128 + TR, :], ybuf[0:TR, :], ["ybuf"], [], "ybuf_st")

        for t_ in range(NTT):
            p8_A(t_)
            if t_ >= 1:
                p8_B(t_ - 1)
            p8_A2(t_)
        p8_B(NTT - 1)
        if nxt is not None and not hoist:
            first_pass[0] = False
            for t_ in range(nxt.NTT):
                emit_P0_tile(nxt, t_)
            emit_P1_pre(nxt)
            for t_ in range(nxt.NTT):
                emit_P1_tile(nxt, t_)

    def emit_gate(seq):
        S.tag = "gate"
        for half in range(2):
            bb, nn = newbank()
            for k4 in range(4):
                kc = half * 4 + k4
                mm(bb[:, k4 * 128:(k4 + 1) * 128], gateh[:, kc, seq:seq + 1].to_broadcast([128, 128]), ident_f[:],
                   True, True, ["gateh", "ident_f"], [nn])
            act(gate_bc[:, half * 512:(half + 1) * 512], bb[:, :], AF.Identity, [nn], ["gate_bc"])

    emit_gate(blocks[0].seq)
    emit_front(blocks[0])
    cur_seq = blocks[0].seq
    for i, k in enumerate(blocks):
        nxt = blocks[i + 1] if i + 1 < len(blocks) else None
        if k.seq != cur_seq:
            emit_gate(k.seq)
            cur_seq = k.seq
        emit_body(k, nxt, hoist=(i > 0))
        first_pass[0] = False

    S.emit_all()
    print("sched stats:", S.stats)
    import os
    if os.environ.get("KDBG_TAGS"):
        import pickle
        tg = {}
        for o in S.ops:
            tg.setdefault(o.eng, []).append((o.tag, o.is_dma, o.ticket, o.w, o.r, o.idx, sorted(o.deps)))
        pickle.dump(tg, open(os.environ["KDBG_TAGS"], "wb"))
    st.close()
    return nc


_CACHE = {}


def _get_program(npb):
    if npb not in _CACHE:
        _CACHE[npb] = build_program(npb)
    return _CACHE[npb]


def kernel(x_prompt, x_sample, state_conv, c_prompt, c_sample, w_ada, b_ada, norm_g,
           w_in, b_in, ln_v_g, ln_v_b, w_spatial, b_spatial, conv_w, conv_b,
           ln_c_g, ln_c_b, w_o_a, w_o_b, w_out, final_g):
    f = lambda a: np.ascontiguousarray(np.asarray(a, dtype=np.float32))
    x_prompt = f(x_prompt)
    seqlen = x_prompt.shape[1]
    npb = seqlen // TB
    nc = _get_program(npb)
    ncores = 8
    shared = {
        "w_ada": f(w_ada)[0], "b_ada": f(b_ada)[0], "norm_g": f(norm_g)[0], "w_in": f(w_in)[0],
        "b_in": f(b_in)[0], "ln_v_g": f(ln_v_g)[0], "ln_v_b": f(ln_v_b)[0], "w_sp": f(w_spatial)[0],
        "b_sp": f(b_spatial)[0].reshape(-1), "conv_w": f(conv_w)[0], "conv_b": f(conv_b)[0],
        "ln_c_g": f(ln_c_g)[0], "ln_c_b": f(ln_c_b)[0], "w_o_a": f(w_o_a)[0], "w_o_b": f(w_o_b)[0],
        "w_out": f(w_out)[0], "final_g": f(final_g),
    }
    x_sample = f(x_sample)
    state_conv = f(state_conv)
    c_prompt = f(c_prompt)
    c_sample = f(c_sample)
    in_maps = []
    for i in range(ncores):
        m = dict(shared)
        m["xp"] = np.ascontiguousarray(x_prompt[2 * i:2 * i + 2])
        m["xs"] = np.ascontiguousarray(x_sample[i])
        m["sc"] = np.ascontiguousarray(state_conv[0, i])
        m["cc"] = np.ascontiguousarray(np.concatenate([c_prompt[2 * i:2 * i + 2], c_sample[i:i + 1]], axis=0))
        in_maps.append(m)
    res = run_bass_kernel_spmd(nc, in_maps, core_ids=list(range(ncores)))
    r = res.results
    y_prompt = np.concatenate([r[i]["yp"] for i in range(ncores)], axis=0)
    y_sample = np.stack([r[i]["ys"] for i in range(ncores)], axis=0)
    ncp = np.concatenate([r[i]["ncp"] for i in range(ncores)], axis=0)[None]
    ncs = np.stack([r[i]["ncs"] for i in range(ncores)], axis=0)[None]
    nvs = np.stack([r[i]["nvs"] for i in range(ncores)], axis=0)[None]
    return (y_prompt.astype(np.float32), y_sample.astype(np.float32), ncp.astype(np.float32),
            ncs.astype(np.float32), nvs.astype(np.float32))
```

```python
import contextlib
import numpy as np
import concourse.bass as bass
import concourse.mybir as mybir
from concourse.bass_utils import run_bass_kernel_spmd

F32 = mybir.dt.float32
BF16 = mybir.dt.bfloat16
AF = mybir.ActivationFunctionType
ALU = mybir.AluOpType

D = 1024
KC = 8
SEQ = 2048
TB = 512
DEC = 16
HIST = 30
NTAP = 31
EPS = 1e-6
NSLOT = 22
SLOT_ELEMS = KC * 512

COMPUTE = ("pe", "act", "dve", "pool")


class _Tile:
    __slots__ = ("name", "last_w", "readers")

    def __init__(self, name):
        self.name = name
        self.last_w = None
        self.readers = []


class _Op:
    __slots__ = ("idx", "eng", "emit", "deps", "is_dma", "sem_key", "dma_val",
                 "needs_inc", "ticket", "n_dma", "tag", "w", "r", "nowaw")

    def __init__(self):
        self.deps = set()
        self.needs_inc = False
        self.ticket = None
        self.is_dma = False
        self.nowaw = False
        self.sem_key = None
        self.dma_val = None
        self.n_dma = 1


class Sched:
    def __init__(self, nc):
        self.nc = nc
        self.ops = []
        self.tiles = {}
        self.wait_all_keys = set()
        self.tag = ""

    def _tile(self, name):
        t = self.tiles.get(name)
        if t is None:
            t = _Tile(name)
            self.tiles[name] = t
        return t

    def _track(self, op, reads, writes, nowaw=False):
        for n in reads:
            t = self._tile(n)
            if t.last_w is not None:
                op.deps.add(t.last_w)
            t.readers.append(op.idx)
        for n in writes:
            t = self._tile(n)
            if t.last_w is not None:
                lw = self.ops[t.last_w]
                if not (nowaw and lw.eng == op.eng and getattr(lw, "nowaw", False) and not lw.is_dma):
                    op.deps.add(t.last_w)
            for r in t.readers:
                if r != op.idx:
                    op.deps.add(r)
            t.last_w = op.idx
            t.readers = []
        op.deps.discard(op.idx)

    def op(self, eng, emit, reads=(), writes=(), nowaw=False):
        o = _Op()
        o.idx = len(self.ops)
        o.eng = eng
        o.emit = emit
        o.tag = self.tag
        o.nowaw = nowaw
        o.w = tuple(writes); o.r = tuple(reads)
        self.ops.append(o)
        self._track(o, reads, writes, nowaw)
        return o

    def dma(self, queue, emit, reads=(), writes=(), sem_key=None, n_dma=1):
        o = _Op()
        o.idx = len(self.ops)
        o.eng = queue
        o.emit = emit
        o.is_dma = True
        o.sem_key = sem_key
        o.n_dma = n_dma
        o.tag = self.tag
        o.w = tuple(writes); o.r = tuple(reads)
        self.ops.append(o)
        self._track(o, reads, writes)
        return o

    def emit_all(self):
        nc = self.nc
        ops = self.ops
        for o in ops:
            latest = {}
            keep = set()
            for d in o.deps:
                p = ops[d]
                if p.is_dma:
                    keep.add(d)
                elif latest.get(p.eng, -1) < d:
                    latest[p.eng] = d
            keep.update(latest.values())
            o.deps = keep
        for o in ops:
            for d in o.deps:
                p = ops[d]
                if p.is_dma:
                    continue
                if p.eng == "pe" and o.eng == "pe" and not o.is_dma:
                    continue
                p.needs_inc = True
        cnt = {e: 0 for e in COMPUTE}
        for o in ops:
            if o.is_dma:
                continue
            if o.needs_inc:
                cnt[o.eng] += 1
                o.ticket = cnt[o.eng]
        dkeys = {}
        for o in ops:
            if o.is_dma:
                k = o.sem_key
                dkeys[k] = dkeys.get(k, 0) + 16 * o.n_dma
                o.dma_val = dkeys[k]
        for o in ops:
            if o.is_dma and o.sem_key in self.wait_all_keys:
                o.dma_val = dkeys[o.sem_key]
        self.stats = dict(cnt)
        self.stats["n_ops"] = len(ops)
        self.stats["dma_keys"] = len(dkeys)
        with contextlib.ExitStack() as st:
            esem = {e: st.enter_context(nc.semaphore("s_" + e)) for e in COMPUTE}
            dsem = {k: st.enter_context(nc.semaphore("d_%d" % i)) for i, k in enumerate(dkeys)}
            block = st.enter_context(nc.Block())
            streams = {}
            for o in ops:
                streams.setdefault(o.eng, []).append(o)

            def run(engname, e):
                waited = {}
                for o in streams.get(engname, []):
                    need = {}
                    for d in o.deps:
                        p = ops[d]
                        if p.is_dma:
                            if o.is_dma and o.sem_key == p.sem_key and p.sem_key in self.wait_all_keys:
                                continue
                            s = dsem[p.sem_key]
                            v = p.dma_val
                        else:
                            if p.eng == "pe" and engname == "pe" and not o.is_dma:
                                continue
                            s = esem[p.eng]
                            v = p.ticket
                        key = id(s)
                        if need.get(key, (None, 0))[1] < v:
                            need[key] = (s, v)
                    for key, (s, v) in need.items():
                        if waited.get(key, 0) >= v:
                            continue
                        e.wait_ge(s, v)
                        waited[key] = v
                    r = o.emit(e)
                    if o.is_dma:
                        s = dsem[o.sem_key]
                        assert len(r) == o.n_dma, (len(r), o.n_dma)
                        for ins in r:
                            ins.then_inc(s, 16)
                    elif o.needs_inc:
                        r.then_inc(esem[engname], 1)
                last = {}
                for o in streams.get(engname, []):
                    if o.is_dma:
                        last[o.sem_key] = max(last.get(o.sem_key, 0), o.dma_val)
                for k, v in last.items():
                    e.wait_ge(dsem[k], v)

            if "pe" in streams:
                @block.tensor
                def _(e):
                    run("pe", e)
            if "act" in streams:
                @block.scalar
                def _(e):
                    run("act", e)
            if "dve" in streams:
                @block.vector
                def _(e):
                    run("dve", e)
            if "pool" in streams:
                @block.gpsimd
                def _(e):
                    run("pool", e)
            if "sp" in streams:
                @block.sync
                def _(e):
                    run("sp", e)


def build_program(npb=4, nring=5, diag_act_every=2):
    seqlen = npb * TB
    nc = bass.Bass("TRN2", target_bir_lowering=False)

    def din(name, shape):
        return nc.dram_tensor(name, shape, F32, kind="ExternalInput").ap()

    def dout(name, shape):
        return nc.dram_tensor(name, shape, F32, kind="ExternalOutput").ap()

    xp = din("xp", [2, seqlen, D])
    xs = din("xs", [DEC, D])
    scv = din("sc", [HIST, D])
    cc = din("cc", [3, D])
    w_ada = din("w_ada", [D, 3 * D])
    b_ada = din("b_ada", [3 * D])
    norm_g = din("norm_g", [D])
    w_in = din("w_in", [D, 8 * D])
    b_in = din("b_in", [8 * D])
    ln_v_g = din("ln_v_g", [D])
    ln_v_b = din("ln_v_b", [D])
    w_sp = din("w_sp", [8, 128, 128])
    b_sp = din("b_sp", [8 * 128])
    conv_w = din("conv_w", [NTAP, D])
    conv_b = din("conv_b", [D])
    ln_c_g = din("ln_c_g", [D])
    ln_c_b = din("ln_c_b", [D])
    w_o_a = din("w_o_a", [D, D])
    w_o_b = din("w_o_b", [D, D])
    w_out = din("w_out", [D, D])
    final_g = din("final_g", [D])
    yp = dout("yp", [2, seqlen, D])
    ys = dout("ys", [DEC, D])
    ncp = dout("ncp", [2, HIST, D])
    ncs = dout("ncs", [HIST, D])
    nvs = dout("nvs", [DEC, D])
    wscr = nc.dram_tensor("wscr", [NSLOT, 128, SLOT_ELEMS], BF16).ap()

    S = Sched(nc)
    S.wait_all_keys.add("setup")
    st = contextlib.ExitStack()

    def sb(name, shape, dt):
        return st.enter_context(nc.sbuf_tensor(name, shape, dt))

    ring = sb("ring", [128, nring, KC, 512], BF16)
    xin = sb("xin", [128, 4, D], F32)
    junk = sb("junk", [128, D], BF16)
    xn = sb("xn", [128, 2, D], BF16)
    hT = sb("hT", [128, 2, KC, TB], BF16)
    gv = sb("gv", [128, D], F32)
    vhat = sb("vhat", [128, 4, D], BF16)
    gu = sb("gu", [128, KC, TB], BF16)
    gcat = sb("gcat", [128, KC, HIST + TB], BF16)
    ah = sb("ah", [128, 2, TB], F32)
    th = sb("th", [128, 2, TB], F32)
    gtl = sb("gtl", [128, KC, 32], F32)
    diag = sb("diag", [128, 2, NTAP, 128], BF16)
    dwb = sb("dwb", [128, KC, TB], BF16)
    dwsq = sb("dwsq", [128, 2, TB], BF16)
    mu_bc = sb("mu_bc", [128, TB], F32)
    rs_bc = sb("rs_bc", [128, TB], F32)
    nt1 = sb("nt1", [128, TB], F32)
    nt2 = sb("nt2", [128, TB], F32)
    sga = sb("sga", [128, 2, TB], BF16)
    pbuf = sb("pbuf", [128, 2, TB], BF16)
    sgb = sb("sgb", [128, KC, TB], BF16)
    nb = sb("nb", [128, 2, TB], BF16)
    tha = sb("tha", [128, 4, TB], BF16)
    thb = sb("thb", [128, 4, TB], BF16)
    xr = sb("xr", [128, D], F32)
    ybuf = sb("ybuf", [128, D], F32)
    ident_f = sb("ident_f", [128, 128], F32)
    ident_b = sb("ident_b", [128, 128], BF16)
    ones_f = sb("ones_f", [128, 128], F32)
    ones_b = sb("ones_b", [128, 128], BF16)
    negh = sb("negh", [128, 8], F32)
    R0 = sb("R0", [128, 128], F32)
    R1 = sb("R1", [128, 128], F32)
    RT0 = sb("RT0", [128, 128], F32)
    RT1 = sb("RT1", [128, 32], F32)
    bTh = sb("bTh", [128, 64], F32)
    cwT = sb("cwT", [128, NTAP * 8], F32)
    scT = sb("scT", [128, HIST * 8], F32)
    wmT = sb("wmT", [128, KC, 128], BF16)
    wmf = sb("wmf", [128, 128], F32)
    Bh = sb("Bh", [128, KC, 128], F32)
    siluT = sb("siluT", [128, 24], F32)
    modT = sb("modT", [128, 24, 3], F32)
    aT = sb("aT", [128, KC, 3], F32)
    gateh = sb("gateh", [128, KC, 3], F32)
    glh = sb("glh", [128, 128], F32)
    gate_bc = sb("gate_bc", [128, D], F32)
    fg_bc = sb("fg_bc", [128, D], F32)
    bv2 = sb("bv2", [2, D], BF16)
    lo_row = sb("lo_row", [1, D], BF16)
    ones2 = sb("ones2", [2, 128], BF16)
    st_s = sb("st_s", [128, 16], F32)
    st_v = sb("st_v", [128, 16], F32)
    print("sbuf bytes remaining:", nc.sbuf_bytes_remaining)

    banks = [st.enter_context(nc.psum_tensor("bank%d" % i, [128, 512], F32)) for i in range(8)]
    bank_ctr = [0]
    held = set()

    def newbank(hold=False):
        while True:
            i = bank_ctr[0] % 8
            bank_ctr[0] += 1
            if i not in held:
                break
        if hold:
            held.add(i)
        return banks[i], "bank%d" % i

    def release(name):
        held.discard(int(name[4:]))

    def mm(out, lhsT, rhs, start, stop, reads, writes):
        S.op("pe", lambda e: e.matmul(out, lhsT=lhsT, rhs=rhs, start=start, stop=stop),
             reads=reads, writes=writes)

    def tr(out, in_, ident, reads, writes):
        S.op("pe", lambda e: e.transpose(out, in_, ident), reads=reads, writes=writes)

    def act(out, in_, func, reads, writes, bias=None, scale=None, accum_out=None, nowaw=False):
        kw = {}
        if bias is not None:
            kw["bias"] = bias
        if scale is not None:
            kw["scale"] = scale
        if accum_out is not None:
            kw["accum_out"] = accum_out
        S.op("act", lambda e: e.activation(out=out, in_=in_, func=func, **kw), reads=reads, writes=writes, nowaw=nowaw)

    def ts(eng, out, in0, s1, s2, op0, op1, reads, writes, nowaw=False):
        if op1 is None:
            S.op(eng, lambda e: e.tensor_scalar(out=out, in0=in0, scalar1=s1, scalar2=None, op0=op0),
                 reads=reads, writes=writes, nowaw=nowaw)
        else:
            S.op(eng, lambda e: e.tensor_scalar(out=out, in0=in0, scalar1=s1, scalar2=s2, op0=op0, op1=op1),
                 reads=reads, writes=writes)

    def tt(eng, out, in0, in1, op, reads, writes):
        S.op(eng, lambda e: e.tensor_tensor(out=out, in0=in0, in1=in1, op=op), reads=reads, writes=writes)

    def stt(out, in0, scalar, in1, op0, op1, reads, writes):
        S.op("dve", lambda e: e.scalar_tensor_tensor(out=out, in0=in0, scalar=scalar, in1=in1, op0=op0, op1=op1),
             reads=reads, writes=writes)

    def cp(eng, out, in_, reads, writes):
        S.op(eng, lambda e: e.tensor_copy(out, in_), reads=reads, writes=writes)

    def dma(q, out, in_, reads, writes, key):
        S.dma(q, lambda e: [e.dma_start(out=out, in_=in_)], reads=reads, writes=writes, sem_key=key)

    def rows128(v):
        return v.rearrange("(r k) -> r k", k=128)

    S.op("pool", lambda e: e.memset(ident_f[:], 1.0), writes=["ident_f"])
    S.op("pool", lambda e: e.affine_select(out=ident_f[:], in_=ident_f[:], pattern=[[-1, 128]],
                                           compare_op=ALU.is_equal, fill=0.0, base=0, channel_multiplier=1),
         reads=["ident_f"], writes=["ident_f"])
    S.op("pool", lambda e: e.memset(ones_f[:], 1.0), writes=["ones_f"])
    S.op("pool", lambda e: e.memset(negh[:], -0.5), writes=["negh"])
    cp("dve", ident_b[:], ident_f[:], ["ident_f"], ["ident_b"])
    cp("dve", ones_b[:], ones_f[:], ["ones_f"], ["ones_b"])
    cp("dve", ones2[:], ones_f[0:2, :], ["ones_f"], ["ones2"])

    dma("sp", R0[0:64, :], rows128(b_in), [], ["R0"], "setup")
    dma("sp", R0[64:88, :], rows128(b_ada), [], ["R0"], "setup")
    dma("sp", R0[88:96, :], rows128(norm_g), [], ["R0"], "setup")
    dma("sp", R0[96:104, :], rows128(ln_v_g), [], ["R0"], "setup")
    dma("sp", R0[104:112, :], rows128(ln_v_b), [], ["R0"], "setup")
    dma("sp", R0[112:120, :], rows128(conv_b), [], ["R0"], "setup")
    dma("sp", R0[120:128, :], rows128(ln_c_g), [], ["R0"], "setup")
    dma("sp", R1[0:8, :], rows128(ln_c_b), [], ["R1"], "setup")
    dma("sp", R1[8:32, :], cc.rearrange("s (c k) -> (s c) k", k=128), [], ["R1"], "setup")
    dma("sp", fg_bc[:], final_g.rearrange("(o n) -> o n", o=1).partition_broadcast(128), [], ["fg_bc"], "setup")
    vh_f = vhat[:].rearrange("p a b -> p (a b)").bitcast(F32)
    gu_f = gu[:].rearrange("p a b -> p (a b)").bitcast(F32)
    lnvb_row = vh_f[0:1, 0:D]
    bs_row = vh_f[0:1, D:2 * D]
    rs_row = gu_f[0:1, 0:D]
    bv_row = gu_f[0:1, D:2 * D]
    dma("sp", lnvb_row, ln_v_b.rearrange("(o n) -> o n", o=1), [], ["vhat"], "setup")
    dma("sp", bs_row, b_sp.rearrange("(o n) -> o n", o=1), [], ["vhat"], "setup")
    dma("sp", bv_row, b_in[D:2 * D].rearrange("(o n) -> o n", o=1), [], ["gu"], "setup")

    def wsrc(slot):
        def cols(w, c0, n):
            return w[:, c0:c0 + n].rearrange("(k p) n -> p k n", p=128)
        if slot in (0, 1):
            return [(0, 512, cols(w_in, D + slot * 512, 512))]
        if slot in (2, 3):
            return [(0, 512, cols(w_in, (slot - 2) * 512, 512))]
        if 4 <= slot <= 7:
            i = slot - 4
            return [(0, 256, cols(w_in, 3 * D + i * 256, 256)), (256, 256, cols(w_in, 4 * D + i * 256, 256))]
        if slot in (8, 9):
            return [(0, 512, cols(w_in, 2 * D + (slot - 8) * 512, 512))]
        if slot in (10, 11):
            return [(0, 512, cols(w_in, 5 * D + (slot - 10) * 512, 512))]
        if 12 <= slot <= 15:
            i = slot - 12
            return [(0, 256, cols(w_in, 6 * D + i * 256, 256)), (256, 256, cols(w_in, 7 * D + i * 256, 256))]
        if 16 <= slot <= 19:
            i = slot - 16
            return [(0, 256, cols(w_o_a, i * 256, 256)), (256, 256, cols(w_o_b, i * 256, 256))]
        return [(0, 512, cols(w_out, (slot - 20) * 512, 512))]

    def emit_conversion():
        S.tag = "conv"
        stg = []
        for h in range(2):
            v = ring[:, nring - 2 + h].rearrange("p k n -> p (k n)").bitcast(F32)
            stg.append((v, "ring%d" % (nring - 2 + h)))
        cobs = [(ring[:, nring - 3], "ring%d" % (nring - 3)), (hT[:, 1], "hT1")]
        for slot in range(NSLOT):
            srcs = wsrc(slot)
            cob, cn = cobs[slot % 2]
            for h in range(2):
                sv, sn = stg[h]
                if len(srcs) == 2:
                    (c0, n, src) = srcs[h]
                    view = sv.rearrange("p (k n) -> p k n", k=KC)
                    dstv = cob[:, :, c0:c0 + n]
                    dma("sp", view, src, [], [sn], "stg%d" % h)
                else:
                    (c0, n, src) = srcs[0]
                    view = sv.rearrange("p (k n) -> p k n", k=KC // 2)
                    dstv = cob[:, h * 4:(h + 1) * 4, :]
                    dma("sp", view, src[:, h * 4:(h + 1) * 4, :], [], [sn], "stg%d" % h)
                if h == 0:
                    act(dstv, view, AF.Identity, [sn], [cn])
                else:
                    cp("dve", dstv, view, [sn], [cn])
            dma("act", wscr[slot], cob.rearrange("p k n -> p (k n)"), [cn], ["wscr%d" % slot], "wscr_w%d" % slot)

    b0, n0 = newbank()
    tr(b0[:, 0:128], R0[:], ident_f[:], ["R0", "ident_f"], [n0])
    act(RT0[:], b0[:, 0:128], AF.Identity, [n0], ["RT0"])
    b1, n1 = newbank()
    tr(b1[:, 0:32], R1[0:32, :], ident_f[0:32, 0:32], ["R1", "ident_f"], [n1])
    act(RT1[:], b1[:, 0:32], AF.Identity, [n1], ["RT1"])
    ts("dve", bTh[:], RT0[:, 0:64], 0.5, None, ALU.mult, None, ["RT0"], ["bTh"])
    bT = RT0
    b_adaT = lambda jc: RT0[:, 64 + jc:65 + jc]
    norm_gT = lambda c: RT0[:, 88 + c:89 + c]
    lnvgT = lambda c: RT0[:, 96 + c:97 + c]
    convbT = lambda c: RT0[:, 112 + c:113 + c]
    lncgT = lambda c: RT0[:, 120 + c:121 + c]
    lncbT = lambda c: RT1[:, c:c + 1]
    act(siluT[:], RT1[:, 8:32], AF.Silu, ["RT1"], ["siluT"])

    cw_rows = conv_w.rearrange("j (c k) -> (j c) k", k=128)
    dma("act", R0[:], cw_rows[0:128, :], ["RT0"], ["R0"], "cw0")
    dma("act", R1[0:120, :], cw_rows[128:248, :], ["RT1"], ["R1"], "cw1")
    b2, n2 = newbank()
    tr(b2[:, 0:128], R0[:], ident_f[:], ["R0", "ident_f"], [n2])
    tr(b2[:, 128:248], R1[0:120, :], ident_f[0:120, 0:120], ["R1", "ident_f"], [n2])
    act(cwT[:], b2[:, 0:248], AF.Identity, [n2], ["cwT"])
    sc_rows = scv.rearrange("r (c k) -> (r c) k", k=128)
    dma("act", R0[:], sc_rows[0:128, :], [], ["R0"], "sc0")
    dma("act", R1[0:112, :], sc_rows[128:240, :], [], ["R1"], "sc1")
    b3, n3 = newbank()
    tr(b3[:, 0:128], R0[:], ident_f[:], ["R0", "ident_f"], [n3])
    tr(b3[:, 128:240], R1[0:112, :], ident_f[0:112, 0:112], ["R1", "ident_f"], [n3])
    act(scT[:], b3[:, 0:240], AF.Identity, [n3], ["scT"])

    def wsp_head(h):
        Rb, Rn = (R0, "R0") if h % 2 == 0 else (R1, "R1")
        dma("sp", Rb[:], w_sp[h], [], [Rn], "wsp%d" % (h % 2))
        bb, nn = newbank()
        tr(bb[:, 0:128], Rb[:], ident_f[:], [Rn, "ident_f"], [nn])
        act(wmf[:], bb[:, 0:128], AF.Identity, [nn], ["wmf"])
        S.op("pool", lambda e: e.affine_select(out=wmf[:], in_=wmf[:], pattern=[[1, 128]],
                                               compare_op=ALU.is_ge, fill=0.0, base=0, channel_multiplier=-1),
             reads=["wmf"], writes=["wmf"])
        cp("dve", wmT[:, h, :], wmf[:], ["wmf"], ["wmT"])
        bb2, nn2 = newbank()
        mm(bb2[0:1, 0:128], ones_f[:, 0:1], wmf[:], True, True, ["ones_f", "wmf"], [nn2])
        act(rs_row[:, h * 128:(h + 1) * 128], bb2[0:1, 0:128], AF.Identity, [nn2], ["gu"])
        bb3, nn3 = newbank()
        mm(bb3[:, 0:128], lnvb_row[:, h * 128:(h + 1) * 128], rs_row[:, h * 128:(h + 1) * 128], True, False,
           ["vhat", "gu"], [nn3])
        mm(bb3[:, 0:128], ones_f[0:1, :], bs_row[:, h * 128:(h + 1) * 128], False, True,
           ["ones_f", "vhat"], [nn3])
        act(Bh[:, h, :], bb3[:, 0:128], AF.Identity, [nn3], ["Bh"])

    sgb_names = ["sgb%d" % c for c in range(KC)]
    dwb_names = ["dwb%d" % c for c in range(KC)]
    ada_bufs = [
        (sgb[:].rearrange("p a b -> p (a b)").bitcast(F32).rearrange("p (k n) -> p k n", k=KC), sgb_names, "ada0"),
        (dwb[:].rearrange("p a b -> p (a b)").bitcast(F32).rearrange("p (k n) -> p k n", k=KC), dwb_names, "ada1"),
    ]
    for pc in range(12):
        ab, an, ak = ada_bufs[pc % 2]
        dma("sp", ab, w_ada[:, pc * 256:(pc + 1) * 256].rearrange("(k p) n -> p k n", p=128), [], an, ak)
        for jl in range(2):
            jc = pc * 2 + jl
            bb, nn = newbank()
            for kc in range(KC):
                mm(bb[:, 0:3], ab[:, kc, jl * 128:(jl + 1) * 128], siluT[:, kc:24:8], kc == 0, kc == KC - 1,
                   an + ["siluT"], [nn])
            act(modT[:, jc, :], bb[:, 0:3], AF.Identity, [nn], ["modT"], bias=b_adaT(jc))
        if pc < 8:
            wsp_head(pc)
    cp("dve", bv2[0:1, :], bv_row, ["gu"], ["bv2"])
    tt("dve", nt1[0:1, 0:512], bv_row[:, 0:512], bv2[0:1, 0:512], ALU.subtract, ["gu", "bv2"], ["nt1"])
    tt("dve", nt2[0:1, 0:512], bv_row[:, 512:1024], bv2[0:1, 512:1024], ALU.subtract, ["gu", "bv2"], ["nt2"])
    cp("dve", lo_row[:, 0:512], nt1[0:1, 0:512], ["nt1"], ["lo_row"])
    cp("dve", lo_row[:, 512:1024], nt2[0:1, 0:512], ["nt2"], ["lo_row"])
    dma("act", bv2[1:2, :], lo_row[:], ["lo_row"], ["bv2"], "bv2lo")

    for kc in range(KC):
        ts("dve", aT[:, kc, :], modT[:, 8 + kc, :], 1.0, norm_gT(kc), ALU.add, ALU.mult, ["modT", "RT0"], ["aT"])
        ts("dve", gateh[:, kc, :], modT[:, 16 + kc, :], 0.5, None, ALU.mult, None, ["modT"], ["gateh"])
    shT = lambda kc, s: modT[:, kc, s:s + 1]

    slot_ctr = [0]
    first_pass = [True]
    stg = []
    for h_ in range(2):
        v_ = ring[:, nring - 2 + h_].rearrange("p k n -> p (k n)").bitcast(F32)
        stg.append((v_, "ring%d" % (nring - 2 + h_), "stg%d" % h_))
    stg.append((hT[:, 1].rearrange("p k n -> p (k n)").bitcast(F32), "hT1", "stg2"))
    stg_ctr = [0]

    def load_slot(slot):
        nr = (nring - 2) if first_pass[0] else nring
        r = slot_ctr[0] % nr
        slot_ctr[0] += 1
        rn = "ring%d" % r
        dst = ring[:, r]
        if first_pass[0]:
            srcs = wsrc(slot)
            for h in range(2):
                sv, sn, sk = stg[stg_ctr[0] % len(stg)]
                stg_ctr[0] += 1
                if len(srcs) == 2:
                    (c0, n, src) = srcs[h]
                    view = sv.rearrange("p (k n) -> p k n", k=KC)
                    dstv = dst[:, :, c0:c0 + n]
                    dma("sp", view, src, [], [sn], sk)
                else:
                    (c0, n, src) = srcs[0]
                    view = sv.rearrange("p (k n) -> p k n", k=KC // 2)
                    dstv = dst[:, h * 4:(h + 1) * 4, :]
                    dma("sp", view, src[:, h * 4:(h + 1) * 4, :], [], [sn], sk)
                if h == 0:
                    act(dstv, view, AF.Identity, [sn], [rn])
                else:
                    cp("dve", dstv, view, [sn], [rn])
            dma("act", wscr[slot], dst.rearrange("p k n -> p (k n)"), [rn], ["wscr%d" % slot], "wscr_w%d" % slot)
        else:
            dma("sp", dst.rearrange("p k n -> p (k n)"), wscr[slot], ["wscr%d" % slot], [rn], "ringld%d" % r)
        return dst, rn

    class Blk:
        pass

    blocks = []
    for sq in range(2):
        for b in range(npb):
            k = Blk()
            k.seq, k.xsrc, k.ydst, k.t0, k.T = sq, xp[sq], yp[sq], b * TB, TB
            k.first, k.last, k.is_sample = (b == 0), (b == npb - 1), False
            blocks.append(k)
    k = Blk()
    k.seq, k.xsrc, k.ydst, k.t0, k.T = 2, xs, ys, 0, DEC
    k.first, k.last, k.is_sample = True, True, True
    blocks.append(k)
    for i, k in enumerate(blocks):
        k.idx = i
        k.TR = min(k.T, 128)
        k.NTT = max(1, k.T // 128)
        k.h = i % 2
        k.hn = "hT%d" % k.h

    def emit_xload(k):
        for t_ in range(k.NTT):
            dma("sp", xin[0:k.TR, t_, :], k.xsrc[k.t0 + t_ * 128:k.t0 + t_ * 128 + k.TR, :], [], ["xin%d" % t_],
                "xin%d" % t_)

    def emit_P0_S1(k, t_):
        S.tag = "P0"
        TR = k.TR
        c0 = 10 + (t_ % 2)
        act(junk[0:TR, :], xin[0:TR, t_, :], AF.Square, ["xin%d" % t_], ["junk", "st_s0_%d" % (t_ % 2)],
            accum_out=st_s[0:TR, c0:c0 + 1])

    def emit_P0_S2(k, t_):
        S.tag = "P0"
        TR = k.TR
        c0 = 10 + (t_ % 2)
        v0 = 12 + 2 * (t_ % 2)
        sn0, vn0 = "st_s0_%d" % (t_ % 2), "st_v0_%d" % (t_ % 2)
        ts("dve", st_v[0:TR, v0:v0 + 1], st_s[0:TR, c0:c0 + 1], 1.0 / D, EPS, ALU.mult, ALU.add, [sn0], [vn0])
        tt("pool", st_v[0:TR, v0 + 1:v0 + 2], st_v[0:TR, v0:v0 + 1], negh[0:TR, 0:1], ALU.pow, [vn0, "negh"], [vn0])

    def emit_P0_S3(k, t_):
        S.tag = "P0"
        TR = k.TR
        v0 = 12 + 2 * (t_ % 2)
        vn0 = "st_v0_%d" % (t_ % 2)
        ts("dve", xn[0:TR, t_ % 2, :], xin[0:TR, t_, :], st_v[0:TR, v0 + 1:v0 + 2], None, ALU.mult, None,
           ["xin%d" % t_, vn0], ["xn%d" % (t_ % 2)])

    def emit_P0_A(k, t_):
        emit_P0_S1(k, t_)
        emit_P0_S2(k, t_)
        emit_P0_S3(k, t_)

    def emit_P0_B(k, t_):
        S.tag = "P0"
        TR, s = k.TR, k.seq
        hTb = hT[:, k.h]
        bb, nn = newbank()
        bbf = bb[:].bitcast(BF16)
        for kc in range(KC):
            tr(bbf[:, kc * 128:kc * 128 + TR], xn[0:TR, t_ % 2, kc * 128:(kc + 1) * 128], ident_b[0:TR, 0:TR],
               ["xn%d" % (t_ % 2), "ident_b"], [nn])
        for kc in range(KC):
            act(hTb[:, kc, t_ * 128:t_ * 128 + TR], bbf[:, kc * 128:kc * 128 + TR], AF.Identity, [nn, "aT", "modT"],
                [k.hn], bias=shT(kc, s), scale=aT[:, kc, s:s + 1], nowaw=True)

    def emit_P0_tile(k, t_):
        emit_P0_A(k, t_)
        emit_P0_B(k, t_)

    def emit_P1_pre(k):
        S.tag = "P1"
        k.slots_v = [load_slot(0), load_slot(1)]
        if k.is_sample:
            dma("sp", xr[0:DEC, :], ln_v_g.rearrange("(o n) -> o n", o=1).partition_broadcast(DEC), [], ["xr"], "xr")
            dma("sp", ybuf[0:DEC, :], ln_v_b.rearrange("(o n) -> o n", o=1).partition_broadcast(DEC), [], ["ybuf"],
                "ybuf_ld")

    def emit_P1_tile(k, t_):
        S.tag = "P1"
        TR = k.TR
        hTb = hT[:, k.h]
        for half in range(2):
            sl, sn = k.slots_v[half]
            bb, nn = newbank()
            for kc in range(KC):
                mm(bb[0:TR, :], hTb[:, kc, t_ * 128:t_ * 128 + TR], sl[:, kc, :], kc == 0, False, [k.hn, sn], [nn])
            mm(bb[0:TR, :], ones2[0:2, 0:TR], bv2[0:2, half * 512:(half + 1) * 512], False, True,
               ["ones2", "bv2"], [nn])
            act(gv[0:TR, half * 512:(half + 1) * 512], bb[0:TR, :], AF.Gelu_apprx_tanh, [nn], ["gv", "st_s1"],
                accum_out=st_s[0:TR, 2 + half:3 + half])
            act(junk[0:TR, 0:512], gv[0:TR, half * 512:(half + 1) * 512], AF.Square, ["gv"], ["junk", "st_s1"],
                accum_out=st_s[0:TR, 4 + half:5 + half])
        sv = ["st_v1"]
        tt("dve", st_v[0:TR, 2:3], st_s[0:TR, 2:3], st_s[0:TR, 3:4], ALU.add, ["st_s1"], sv)
        tt("dve", st_v[0:TR, 3:4], st_s[0:TR, 4:5], st_s[0:TR, 5:6], ALU.add, ["st_s1"], sv)
        ts("dve", st_v[0:TR, 2:4], st_v[0:TR, 2:4], 1.0 / D, None, ALU.mult, None, sv, sv)
        tt("dve", st_v[0:TR, 4:5], st_v[0:TR, 2:3], st_v[0:TR, 2:3], ALU.mult, sv, sv)
        tt("dve", st_v[0:TR, 5:6], st_v[0:TR, 3:4], st_v[0:TR, 4:5], ALU.subtract, sv, sv)
        ts("dve", st_v[0:TR, 5:6], st_v[0:TR, 5:6], EPS, None, ALU.add, None, sv, sv)
        tt("pool", st_v[0:TR, 6:7], st_v[0:TR, 5:6], negh[0:TR, 0:1], ALU.pow, sv + ["negh"], sv)
        ts("dve", st_v[0:TR, 7:8], st_v[0:TR, 2:3], st_v[0:TR, 6:7], -1.0, ALU.mult, ALU.mult, sv, sv)
        ts("dve", vhat[0:TR, t_, :], gv[0:TR, :], st_v[0:TR, 6:7], st_v[0:TR, 7:8], ALU.mult, ALU.add,
           ["gv"] + sv, ["vhat"])
        if k.is_sample:
            ts("dve", gv[0:TR, :], gv[0:TR, :], st_v[0:TR, 6:7], st_v[0:TR, 7:8], ALU.mult, ALU.add,
               ["gv"] + sv, ["gv"])
            tt("dve", gv[0:TR, :], gv[0:TR, :], xr[0:TR, :], ALU.mult, ["gv", "xr"], ["gv"])
            tt("dve", gv[0:TR, :], gv[0:TR, :], ybuf[0:TR, :], ALU.add, ["gv", "ybuf"], ["gv"])
            dma("act", nvs, gv[0:DEC, :], ["gv"], [], "nvs")

    def emit_front(k):
        emit_xload(k)
        for t_ in range(k.NTT):
            emit_P0_tile(k, t_)
        emit_P1_pre(k)
        for t_ in range(k.NTT):
            emit_P1_tile(k, t_)

    def emit_body(k, nxt, hoist):
        seq, xsrc, ydst, t0, T, first, last, is_sample = k.seq, k.xsrc, k.ydst, k.t0, k.T, k.first, k.last, k.is_sample
        TR, NTT, s = k.TR, k.NTT, k.seq
        hTb = hT[:, k.h]
        hn = k.hn
        S.tag = "P2"
        for sidx in (2, 3):
            sl, sn = load_slot(sidx)
            for cl in range(4):
                c = (sidx - 2) * 4 + cl
                bb, nn = newbank()
                for kc in range(KC):
                    mm(bb[:, 0:T], sl[:, kc, cl * 128:(cl + 1) * 128], hTb[:, kc, 0:T], kc == 0, kc == KC - 1,
                       [sn, hn], [nn])
                act(gu[:, c, 0:T], bb[:, 0:T], AF.Gelu_apprx_tanh, [nn], ["gu%d" % c], bias=bT[:, c:c + 1])

        def build_diag(c):
            for j in range(NTAP):
                if diag_act_every and (j % diag_act_every == diag_act_every - 1):
                    act(diag[:, c % 2, j, :], ident_b[:], AF.Identity, ["ident_b", "cwT"], ["diagA%d" % (c % 2)],
                        scale=cwT[:, j * 8 + c:j * 8 + c + 1], nowaw=True)
                else:
                    ts("dve", diag[:, c % 2, j, :], ident_b[:], cwT[:, j * 8 + c:j * 8 + c + 1], None, ALU.mult, None,
                       ["ident_b", "cwT"], ["diagD%d" % (c % 2)], nowaw=True)
        S.tag = "P3"
        build_diag(0)
        gnames = ["gcat%d" % c for c in range(KC)]
        if first and not is_sample:
            S.op("dve", lambda e: e.memset(gcat[:, :, 0:HIST], 0.0), writes=gnames)
        elif is_sample:
            cp("dve", gcat[:, :, 0:HIST], scT[:].rearrange("p (r c) -> p c r", c=8), ["scT"], gnames)
        NTL = min(HIST, T)
        for i in range(4):
            sl, sn = load_slot(4 + i)
            for cl in range(2):
                c = i * 2 + cl
                ba, na = newbank()
                bbk, nbk = newbank()
                for kc in range(KC):
                    mm(ba[:, 0:T], sl[:, kc, cl * 128:(cl + 1) * 128], hTb[:, kc, 0:T], kc == 0, kc == KC - 1,
                       [sn, hn], [na])
                for kc in range(KC):
                    mm(bbk[:, 0:T], sl[:, kc, 256 + cl * 128:256 + (cl + 1) * 128], hTb[:, kc, 0:T], kc == 0,
                       kc == KC - 1, [sn, hn], [nbk])
                act(ah[:, c % 2, 0:T], ba[:, 0:T], AF.Identity, [na, "bTh"], ["ah%d" % (c % 2)],
                    bias=bTh[:, 24 + c:25 + c], scale=0.5)
                act(th[:, c % 2, 0:T], bbk[:, 0:T], AF.Tanh, [nbk, "bTh"], ["th%d" % (c % 2)],
                    bias=bTh[:, 32 + c:33 + c], scale=0.5)
                stt(gcat[:, c, HIST:HIST + T], th[:, c % 2, 0:T], 1.0, ah[:, c % 2, 0:T], ALU.add, ALU.mult,
                    ["ah%d" % (c % 2), "th%d" % (c % 2)], ["gcat%d" % c])
                if last:
                    stt(gtl[:, c, 0:NTL], th[:, c % 2, T - NTL:T], 1.0, ah[:, c % 2, T - NTL:T], ALU.add, ALU.mult,
                        ["ah%d" % (c % 2), "th%d" % (c % 2)], ["gtl"])
        if last:
            bb, nn = newbank()
            bb2, nn2 = newbank()
            for c in range(KC):
                tgt = bb if c < 4 else bb2
                tr(tgt[0:NTL, (c % 4) * 128:(c % 4 + 1) * 128], gtl[:, c, 0:NTL], ident_f[:], ["gtl", "ident_f"],
                   [nn if c < 4 else nn2])
            act(ybuf[0:NTL, 0:512], bb[0:NTL, :], AF.Identity, [nn], ["ybuf"])
            act(ybuf[0:NTL, 512:1024], bb2[0:NTL, :], AF.Identity, [nn2], ["ybuf"])
            if is_sample:
                dma("act", ncs[HIST - NTL:HIST, :], ybuf[0:NTL, :], ["ybuf"], [], "ybuf_st")
                dma("sp", ncs[0:HIST - NTL, :], scv[NTL:HIST, :], [], [], "ncs_cp")
            else:
                dma("act", ncp[s], ybuf[0:NTL, :], ["ybuf"], [], "ybuf_st")
        if nxt is not None:
            emit_xload(nxt)
        S.tag = "P4P5"
        sl_ga = [None, None]
        sl_gb = [None, None]
        bsum, nsum = newbank(hold=True)
        bsq, nsq = newbank(hold=True)

        def stats_mm(c):
            mm(bsum[:, 0:T], ones_b[:], dwb[:, c, 0:T], c == 0, c == KC - 1, ["ones_b", "dwb%d" % c], [nsum])
            mm(bsq[:, 0:T], ones_b[:], dwsq[:, c % 2, 0:T], c == 0, c == KC - 1, ["ones_b", "dwsq%d" % (c % 2)], [nsq])

        for c in range(KC):
            if c + 1 < KC:
                build_diag(c + 1)
            bb, nn = newbank()
            for j in range(NTAP):
                mm(bb[:, 0:T], diag[:, c % 2, j, :], gcat[:, c, j:j + T], j == 0, j == NTAP - 1,
                   ["diagA%d" % (c % 2), "diagD%d" % (c % 2), "gcat%d" % c], [nn])
            act(dwb[:, c, 0:T], bb[:, 0:T], AF.Identity, [nn], ["dwb%d" % c], bias=convbT(c))
            act(dwsq[:, c % 2, 0:T], bb[:, 0:T], AF.Square, [nn], ["dwsq%d" % (c % 2)], bias=convbT(c))
            if c % 4 == 0:
                sl_ga[0], sl_ga[1] = load_slot(8 + c // 4)
                sl_gb[0], sl_gb[1] = load_slot(10 + c // 4)
            cl = c % 4
            bg, ng = newbank()
            for kc in range(KC):
                mm(bg[:, 0:T], sl_ga[0][:, kc, cl * 128:(cl + 1) * 128], hTb[:, kc, 0:T], kc == 0, kc == KC - 1,
                   [sl_ga[1], hn], [ng])
            bs_, ns_ = newbank()
            for n_ in range(NTT):
                mm(bs_[:, n_ * 128:n_ * 128 + TR], vhat[0:TR, n_, c * 128:(c + 1) * 128], wmT[0:TR, c, 0:TR], True, True,
                   ["vhat", "wmT"], [ns_])
            act(sga[:, c % 2, 0:T], bg[:, 0:T], AF.Silu, [ng], ["sga%d" % (c % 2)], bias=bT[:, 16 + c:17 + c])
            tt("dve", pbuf[:, c % 2, 0:T], gu[:, c, 0:T], sga[:, c % 2, 0:T], ALU.mult, ["gu%d" % c, "sga%d" % (c % 2)],
               ["pbuf%d" % (c % 2)])
            if T == TB:
                in1 = Bh[:, c, :].unsqueeze(1).to_broadcast([128, 4, 128])
                in0 = bs_[:, :].rearrange("p (a b) -> p a b", a=4)
                o_ = nt1[:, :].rearrange("p (a b) -> p a b", a=4)
            else:
                in1 = Bh[:, c, 0:T]
                in0 = bs_[:, 0:T]
                o_ = nt1[:, 0:T]
            stt(o_, in0, lnvgT(c), in1, ALU.mult, ALU.add, [ns_, "Bh", "RT0"], ["nt1"])
            tt("dve", gu[:, c, 0:T], nt1[:, 0:T], pbuf[:, c % 2, 0:T], ALU.mult, ["nt1", "pbuf%d" % (c % 2)],
               ["gu%d" % c])
            bgb, ngb = newbank()
            for kc in range(KC):
                mm(bgb[:, 0:T], sl_gb[0][:, kc, cl * 128:(cl + 1) * 128], hTb[:, kc, 0:T], kc == 0, kc == KC - 1,
                   [sl_gb[1], hn], [ngb])
            act(sgb[:, c, 0:T], bgb[:, 0:T], AF.Silu, [ngb], ["sgb%d" % c], bias=bT[:, 40 + c:41 + c])
            if c >= 1:
                stats_mm(c - 1)
        stats_mm(KC - 1)
        if not last:
            cp("dve", gcat[:, :, 0:HIST], gcat[:, :, T:T + HIST], gnames, gnames)
        S.tag = "P7"
        mT = gcat
        p7 = {}

        def p7_slots(i):
            if ("m", i) not in p7:
                p7[("m", i)] = load_slot(12 + i)
                p7[("o", i)] = load_slot(16 + i)
            return p7[("m", i)], p7[("o", i)]

        def p7_part1(o):
            S.tag = "P7"
            i, ol = o // 2, o % 2
            (sl_m, sn_m), (sl_o, sn_o) = p7_slots(i)
            bma, nma = newbank()
            bmb, nmb = newbank()
            bpa, npa = newbank(hold=True)
            p7[("pa", o)] = (bpa, npa)
            for kc in range(KC):
                mm(bma[:, 0:T], sl_m[:, kc, ol * 128:(ol + 1) * 128], hTb[:, kc, 0:T], kc == 0, kc == KC - 1,
                   [sn_m, hn], [nma])
            for kc in range(KC):
                mm(bmb[:, 0:T], sl_m[:, kc, 256 + ol * 128:256 + (ol + 1) * 128], hTb[:, kc, 0:T], kc == 0,
                   kc == KC - 1, [sn_m, hn], [nmb])
            for c in range(KC):
                mm(bpa[:, 0:T], sl_o[:, c, ol * 128:(ol + 1) * 128], gu[:, c, 0:T], c == 0, c == KC - 1,
                   [sn_o, "gu%d" % c], [npa])
            act(tha[:, o % 4, 0:T], bma[:, 0:T], AF.Tanh, [nma, "bTh"], ["tha%d" % (o % 4)],
                bias=bTh[:, 48 + o:49 + o], scale=0.5)
            act(thb[:, o % 4, 0:T], bmb[:, 0:T], AF.Tanh, [nmb, "bTh"], ["thb%d" % (o % 4)],
                bias=bTh[:, 56 + o:57 + o], scale=0.5)

        def p7_part2(o):
            S.tag = "P7"
            i, ol = o // 2, o % 2
            (sl_m, sn_m), (sl_o, sn_o) = p7_slots(i)
            bpa, npa = p7[("pa", o)]
            bpb, npb_ = newbank()
            for c in range(KC):
                mm(bpb[:, 0:T], sl_o[:, c, 256 + ol * 128:256 + (ol + 1) * 128], sgb[:, c, 0:T], c == 0, c == KC - 1,
                   [sn_o, "sgb%d" % c], [npb_])
            stt(nt1[:, 0:T], tha[:, o % 4, 0:T], 1.0, bpa[:, 0:T], ALU.add, ALU.mult, ["tha%d" % (o % 4), npa], ["nt1"])
            stt(nt2[:, 0:T], thb[:, o % 4, 0:T], 1.0, bpb[:, 0:T], ALU.add, ALU.mult, ["thb%d" % (o % 4), npb_], ["nt2"])
            tt("dve", mT[:, o, HIST:HIST + T], nt1[:, 0:T], nt2[:, 0:T], ALU.add, ["nt1", "nt2"], ["gcat%d" % o])
            release(npa)

        p7_part1(0)
        p7_part1(1)
        p7_part1(2)
        S.tag = "LN_c"
        release(nsum)
        release(nsq)
        ts("dve", mu_bc[:, 0:T], bsum[:, 0:T], 1.0 / D, None, ALU.mult, None, [nsum], ["mu_bc"])
        tt("dve", nt1[:, 0:T], mu_bc[:, 0:T], mu_bc[:, 0:T], ALU.mult, ["mu_bc"], ["nt1"])
        stt(nt2[:, 0:T], bsq[:, 0:T], 1.0 / D, nt1[:, 0:T], ALU.mult, ALU.subtract, [nsq, "nt1"], ["nt2"])
        ts("dve", nt2[:, 0:T], nt2[:, 0:T], EPS, None, ALU.add, None, ["nt2"], ["nt2"])
        I32 = mybir.dt.int32
        rsi = rs_bc[:, 0:T].bitcast(I32)
        S.op("dve", lambda e: e.tensor_single_scalar(out=rsi, in_=nt2[:, 0:T].bitcast(I32), scalar=1,
                                                     op=ALU.arith_shift_right), reads=["nt2"], writes=["rs_bc"])
        ts("dve", rsi, rsi, -1, 0x5f3759df, ALU.mult, ALU.add, ["rs_bc"], ["rs_bc"])
        for it in range(3):
            tt("dve", nt1[:, 0:T], rs_bc[:, 0:T], rs_bc[:, 0:T], ALU.mult, ["rs_bc"], ["nt1"])
            tt("dve", nt1[:, 0:T], nt1[:, 0:T], nt2[:, 0:T], ALU.mult, ["nt1", "nt2"], ["nt1"])
            ts("dve", nt1[:, 0:T], nt1[:, 0:T], -0.5, 1.5, ALU.mult, ALU.add, ["nt1"], ["nt1"])
            tt("dve", rs_bc[:, 0:T], rs_bc[:, 0:T], nt1[:, 0:T], ALU.mult, ["rs_bc", "nt1"], ["rs_bc"])
        S.tag = "P6"
        def p6_pre(c):
            tmp, tmpn = (nt1, "nt1") if c % 2 == 0 else (nt2, "nt2")
            tt("dve", tmp[:, 0:T], dwb[:, c, 0:T], mu_bc[:, 0:T], ALU.subtract, ["dwb%d" % c, "mu_bc"], [tmpn])
            tt("dve", tmp[:, 0:T], tmp[:, 0:T], rs_bc[:, 0:T], ALU.mult, [tmpn, "rs_bc"], [tmpn])
            act(nb[:, c % 2, 0:T], tmp[:, 0:T], AF.Silu, [tmpn, "RT0", "RT1"], ["nb%d" % (c % 2)], bias=lncbT(c),
                scale=lncgT(c))

        def p6_fin(c):
            tt("dve", sgb[:, c, 0:T], nb[:, c % 2, 0:T], sgb[:, c, 0:T], ALU.mult, ["nb%d" % (c % 2), "sgb%d" % c],
               ["sgb%d" % c])

        p6_pre(0)
        for c in range(1, KC):
            p6_pre(c)
            p6_fin(c - 1)
        p6_fin(KC - 1)
        for o in range(KC):
            p7_part2(o)
            if o + 3 < KC:
                p7_part1(o + 3)
            if nxt is not None and hoist:
                n_ = nxt.NTT
                if o <= 4:
                    if o == 0:
                        emit_P0_S1(nxt, 0)
                        if n_ > 1:
                            emit_P0_S1(nxt, 1)
                        emit_P0_S2(nxt, 0)
                    else:
                        if o - 1 < n_:
                            emit_P0_S3(nxt, o - 1)
                        if o < n_:
                            emit_P0_S2(nxt, o)
                        if o + 1 < n_:
                            emit_P0_S1(nxt, o + 1)
                        if o - 1 < n_:
                            emit_P0_B(nxt, o - 1)
                    if o == 3:
                        emit_P1_pre(nxt)
                if o >= 4:
                    if o - 4 < n_:
                        emit_P1_tile(nxt, o - 4)
        S.tag = "P8"
        slots_o = [load_slot(20), load_slot(21)]
        xrb = [(xr, "xr"), (gv, "gv")]

        def p8_A(t_):
            xb, xn_ = xrb[t_ % 2]
            dma("act", xb[0:TR, :], xsrc[t0 + t_ * 128:t0 + t_ * 128 + TR, :], [], [xn_], "xr_ld%d" % (t_ % 2))
            for half in range(2):
                sl, sn = slots_o[half]
                bb, nn = newbank()
                for kc in range(KC):
                    mm(bb[0:TR, :], mT[:, kc, HIST + t_ * 128:HIST + t_ * 128 + TR], sl[:, kc, :], kc == 0, kc == KC - 1,
                       ["gcat%d" % kc, sn], [nn])
                tmp, tmpn = (nt1, "nt1") if half == 0 else (nt2, "nt2")
                tt("dve", tmp[0:TR, :], bb[0:TR, :], gate_bc[0:TR, half * 512:(half + 1) * 512],
                   ALU.mult, [nn, "gate_bc"], [tmpn])
                tt("dve", xb[0:TR, half * 512:(half + 1) * 512], xb[0:TR, half * 512:(half + 1) * 512], tmp[0:TR, :],
                   ALU.add, [xn_, tmpn], [xn_])
            c8 = 8 + (t_ % 2)
            v8 = 8 + 2 * (t_ % 2)
            sn8, vn8 = "st_s8_%d" % (t_ % 2), "st_v8_%d" % (t_ % 2)
            act(junk[0:TR, :], xb[0:TR, :], AF.Square, [xn_], ["junk", sn8], accum_out=st_s[0:TR, c8:c8 + 1])

        def p8_A2(t_):
            c8 = 8 + (t_ % 2)
            v8 = 8 + 2 * (t_ % 2)
            sn8, vn8 = "st_s8_%d" % (t_ % 2), "st_v8_%d" % (t_ % 2)
            ts("dve", st_v[0:TR, v8:v8 + 1], st_s[0:TR, c8:c8 + 1], 1.0 / D, EPS, ALU.mult, ALU.add, [sn8], [vn8])
            tt("pool", st_v[0:TR, v8 + 1:v8 + 2], st_v[0:TR, v8:v8 + 1], negh[0:TR, 0:1], ALU.pow, [vn8, "negh"], [vn8])

        def p8_B(t_):
            xb, xn_ = xrb[t_ % 2]
            v8 = 8 + 2 * (t_ % 2)
            vn8 = "st_v8_%d" % (t_ % 2)
            stt(ybuf[0:TR, :], xb[0:TR, :], st_v[0:TR, v8 + 1:v8 + 2], fg_bc[0:TR, :], ALU.mult, ALU.mult,
                [xn_, vn8, "fg_bc"], ["ybuf"])
            dma("act", ydst[t0 + t_ * 128:t0 + t_ * 128 + TR, :], ybuf[0:TR, :], ["ybuf"], [], "ybuf_st")

        for t_ in range(NTT):
            p8_A(t_)
            if t_ >= 1:
                p8_B(t_ - 1)
            p8_A2(t_)
        p8_B(NTT - 1)
        if nxt is not None and not hoist:
            first_pass[0] = False
            for t_ in range(nxt.NTT):
                emit_P0_tile(nxt, t_)
            emit_P1_pre(nxt)
            for t_ in range(nxt.NTT):
                emit_P1_tile(nxt, t_)

    def emit_gate(seq):
        S.tag = "gate"
        for half in range(2):
            bb, nn = newbank()
            for k4 in range(4):
                kc = half * 4 + k4
                mm(bb[:, k4 * 128:(k4 + 1) * 128], gateh[:, kc, seq:seq + 1].to_broadcast([128, 128]), ident_f[:],
                   True, True, ["gateh", "ident_f"], [nn])
            act(gate_bc[:, half * 512:(half + 1) * 512], bb[:, :], AF.Identity, [nn], ["gate_bc"])

    emit_gate(blocks[0].seq)
    emit_front(blocks[0])
    cur_seq = blocks[0].seq
    for i, k in enumerate(blocks):
        nxt = blocks[i + 1] if i + 1 < len(blocks) else None
        if k.seq != cur_seq:
            emit_gate(k.seq)
            cur_seq = k.seq
        emit_body(k, nxt, hoist=(i > 0))
        first_pass[0] = False

    S.emit_all()
    print("sched stats:", S.stats)
    import os
    if os.environ.get("KDBG_TAGS"):
        import pickle
        tg = {}
        for o in S.ops:
            tg.setdefault(o.eng, []).append((o.tag, o.is_dma, o.ticket, o.w, o.r, o.idx, sorted(o.deps)))
        pickle.dump(tg, open(os.environ["KDBG_TAGS"], "wb"))
    st.close()
    return nc


_CACHE = {}


def _get_program(npb):
    if npb not in _CACHE:
        _CACHE[npb] = build_program(npb)
    return _CACHE[npb]


def kernel(x_prompt, x_sample, state_conv, c_prompt, c_sample, w_ada, b_ada, norm_g,
           w_in, b_in, ln_v_g, ln_v_b, w_spatial, b_spatial, conv_w, conv_b,
           ln_c_g, ln_c_b, w_o_a, w_o_b, w_out, final_g):
    f = lambda a: np.ascontiguousarray(np.asarray(a, dtype=np.float32))
    x_prompt = f(x_prompt)
    seqlen = x_prompt.shape[1]
    npb = seqlen // TB
    nc = _get_program(npb)
    ncores = 8
    shared = {
        "w_ada": f(w_ada)[0], "b_ada": f(b_ada)[0], "norm_g": f(norm_g)[0], "w_in": f(w_in)[0],
        "b_in": f(b_in)[0], "ln_v_g": f(ln_v_g)[0], "ln_v_b": f(ln_v_b)[0], "w_sp": f(w_spatial)[0],
        "b_sp": f(b_spatial)[0].reshape(-1), "conv_w": f(conv_w)[0], "conv_b": f(conv_b)[0],
        "ln_c_g": f(ln_c_g)[0], "ln_c_b": f(ln_c_b)[0], "w_o_a": f(w_o_a)[0], "w_o_b": f(w_o_b)[0],
        "w_out": f(w_out)[0], "final_g": f(final_g),
    }
    x_sample = f(x_sample)
    state_conv = f(state_conv)
    c_prompt = f(c_prompt)
    c_sample = f(c_sample)
    in_maps = []
    for i in range(ncores):
        m = dict(shared)
        m["xp"] = np.ascontiguousarray(x_prompt[2 * i:2 * i + 2])
        m["xs"] = np.ascontiguousarray(x_sample[i])
        m["sc"] = np.ascontiguousarray(state_conv[0, i])
        m["cc"] = np.ascontiguousarray(np.concatenate([c_prompt[2 * i:2 * i + 2], c_sample[i:i + 1]], axis=0))
        in_maps.append(m)
    res = run_bass_kernel_spmd(nc, in_maps, core_ids=list(range(ncores)))
    r = res.results
    y_prompt = np.concatenate([r[i]["yp"] for i in range(ncores)], axis=0)
    y_sample = np.stack([r[i]["ys"] for i in range(ncores)], axis=0)
    ncp = np.concatenate([r[i]["ncp"] for i in range(ncores)], axis=0)[None]
    ncs = np.stack([r[i]["ncs"] for i in range(ncores)], axis=0)[None]
    nvs = np.stack([r[i]["nvs"] for i in range(ncores)], axis=0)[None]
    return (y_prompt.astype(np.float32), y_sample.astype(np.float32), ncp.astype(np.float32),
            ncs.astype(np.float32), nvs.astype(np.float32))
```

```python
import contextlib
import numpy as np
import concourse.bass as bass
import concourse.mybir as mybir
from concourse.bass_utils import run_bass_kernel_spmd

F32 = mybir.dt.float32
BF16 = mybir.dt.bfloat16
AF = mybir.ActivationFunctionType
ALU = mybir.AluOpType

D = 1024
KC = 8
SEQ = 2048
TB = 512
DEC = 16
HIST = 30
NTAP = 31
EPS = 1e-6
NSLOT = 22
SLOT_ELEMS = KC * 512

COMPUTE = ("pe", "act", "dve", "pool")


class _Tile:
    __slots__ = ("name", "last_w", "readers")

    def __init__(self, name):
        self.name = name
        self.last_w = None
        self.readers = []


class _Op:
    __slots__ = ("idx", "eng", "emit", "deps", "is_dma", "sem_key", "dma_val",
                 "needs_inc", "ticket", "n_dma", "tag", "w", "r", "nowaw")

    def __init__(self):
        self.deps = set()
        self.needs_inc = False
        self.ticket = None
        self.is_dma = False
        self.nowaw = False
        self.sem_key = None
        self.dma_val = None
        self.n_dma = 1


class Sched:
    def __init__(self, nc):
        self.nc = nc
        self.ops = []
        self.tiles = {}
        self.wait_all_keys = set()
        self.tag = ""

    def _tile(self, name):
        t = self.tiles.get(name)
        if t is None:
            t = _Tile(name)
            self.tiles[name] = t
        return t

    def _track(self, op, reads, writes, nowaw=False):
        for n in reads:
            t = self._tile(n)
            if t.last_w is not None:
                op.deps.add(t.last_w)
            t.readers.append(op.idx)
        for n in writes:
            t = self._tile(n)
            if t.last_w is not None:
                lw = self.ops[t.last_w]
                if not (nowaw and lw.eng == op.eng and getattr(lw, "nowaw", False) and not lw.is_dma):
                    op.deps.add(t.last_w)
            for r in t.readers:
                if r != op.idx:
                    op.deps.add(r)
            t.last_w = op.idx
            t.readers = []
        op.deps.discard(op.idx)

    def op(self, eng, emit, reads=(), writes=(), nowaw=False):
        o = _Op()
        o.idx = len(self.ops)
        o.eng = eng
        o.emit = emit
        o.tag = self.tag
        o.nowaw = nowaw
        o.w = tuple(writes); o.r = tuple(reads)
        self.ops.append(o)
        self._track(o, reads, writes, nowaw)
        return o

    def dma(self, queue, emit, reads=(), writes=(), sem_key=None, n_dma=1):
        o = _Op()
        o.idx = len(self.ops)
        o.eng = queue
        o.emit = emit
        o.is_dma = True
        o.sem_key = sem_key
        o.n_dma = n_dma
        o.tag = self.tag
        o.w = tuple(writes); o.r = tuple(reads)
        self.ops.append(o)
        self._track(o, reads, writes)
        return o

    def emit_all(self):
        nc = self.nc
        ops = self.ops
        for o in ops:
            latest = {}
            keep = set()
            for d in o.deps:
                p = ops[d]
                if p.is_dma:
                    keep.add(d)
                elif latest.get(p.eng, -1) < d:
                    latest[p.eng] = d
            keep.update(latest.values())
            o.deps = keep
        for o in ops:
            for d in o.deps:
                p = ops[d]
                if p.is_dma:
                    continue
                if p.eng == "pe" and o.eng == "pe" and not o.is_dma:
                    continue
                p.needs_inc = True
        cnt = {e: 0 for e in COMPUTE}
        for o in ops:
            if o.is_dma:
                continue
            if o.needs_inc:
                cnt[o.eng] += 1
                o.ticket = cnt[o.eng]
        dkeys = {}
        for o in ops:
            if o.is_dma:
                k = o.sem_key
                dkeys[k] = dkeys.get(k, 0) + 16 * o.n_dma
                o.dma_val = dkeys[k]
        for o in ops:
            if o.is_dma and o.sem_key in self.wait_all_keys:
                o.dma_val = dkeys[o.sem_key]
        self.stats = dict(cnt)
        self.stats["n_ops"] = len(ops)
        self.stats["dma_keys"] = len(dkeys)
        with contextlib.ExitStack() as st:
            esem = {e: st.enter_context(nc.semaphore("s_" + e)) for e in COMPUTE}
            dsem = {k: st.enter_context(nc.semaphore("d_%d" % i)) for i, k in enumerate(dkeys)}
            block = st.enter_context(nc.Block())
            streams = {}
            for o in ops:
                streams.setdefault(o.eng, []).append(o)

            def run(engname, e):
                waited = {}
                for o in streams.get(engname, []):
                    need = {}
                    for d in o.deps:
                        p = ops[d]
                        if p.is_dma:
                            if o.is_dma and o.sem_key == p.sem_key and p.sem_key in self.wait_all_keys:
                                continue
                            s = dsem[p.sem_key]
                            v = p.dma_val
                        else:
                            if p.eng == "pe" and engname == "pe" and not o.is_dma:
                                continue
                            s = esem[p.eng]
                            v = p.ticket
                        key = id(s)
                        if need.get(key, (None, 0))[1] < v:
                            need[key] = (s, v)
                    for key, (s, v) in need.items():
                        if waited.get(key, 0) >= v:
                            continue
                        e.wait_ge(s, v)
                        waited[key] = v
                    r = o.emit(e)
                    if o.is_dma:
                        s = dsem[o.sem_key]
                        assert len(r) == o.n_dma, (len(r), o.n_dma)
                        for ins in r:
                            ins.then_inc(s, 16)
                    elif o.needs_inc:
                        r.then_inc(esem[engname], 1)
                last = {}
                for o in streams.get(engname, []):
                    if o.is_dma:
                        last[o.sem_key] = max(last.get(o.sem_key, 0), o.dma_val)
                for k, v in last.items():
                    e.wait_ge(dsem[k], v)

            if "pe" in streams:
                @block.tensor
                def _(e):
                    run("pe", e)
            if "act" in streams:
                @block.scalar
                def _(e):
                    run("act", e)
            if "dve" in streams:
                @block.vector
                def _(e):
                    run("dve", e)
            if "pool" in streams:
                @block.gpsimd
                def _(e):
                    run("pool", e)
            if "sp" in streams:
                @block.sync
                def _(e):
                    run("sp", e)


def build_program(npb=4, nring=5, diag_act_every=2):
    seqlen = npb * TB
    nc = bass.Bass("TRN2", target_bir_lowering=False)

    def din(name, shape):
        return nc.dram_tensor(name, shape, F32, kind="ExternalInput").ap()

    def dout(name, shape):
        return nc.dram_tensor(name, shape, F32, kind="ExternalOutput").ap()

    xp = din("xp", [2, seqlen, D])
    xs = din("xs", [DEC, D])
    scv = din("sc", [HIST, D])
    cc = din("cc", [3, D])
    w_ada = din("w_ada", [D, 3 * D])
    b_ada = din("b_ada", [3 * D])
    norm_g = din("norm_g", [D])
    w_in = din("w_in", [D, 8 * D])
    b_in = din("b_in", [8 * D])
    ln_v_g = din("ln_v_g", [D])
    ln_v_b = din("ln_v_b", [D])
    w_sp = din("w_sp", [8, 128, 128])
    b_sp = din("b_sp", [8 * 128])
    conv_w = din("conv_w", [NTAP, D])
    conv_b = din("conv_b", [D])
    ln_c_g = din("ln_c_g", [D])
    ln_c_b = din("ln_c_b", [D])
    w_o_a = din("w_o_a", [D, D])
    w_o_b = din("w_o_b", [D, D])
    w_out = din("w_out", [D, D])
    final_g = din("final_g", [D])
    yp = dout("yp", [2, seqlen, D])
    ys = dout("ys", [DEC, D])
    ncp = dout("ncp", [2, HIST, D])
    ncs = dout("ncs", [HIST, D])
    nvs = dout("nvs", [DEC, D])
    wscr = nc.dram_tensor("wscr", [NSLOT, 128, SLOT_ELEMS], BF16).ap()

    S = Sched(nc)
    S.wait_all_keys.add("setup")
    st = contextlib.ExitStack()

    def sb(name, shape, dt):
        return st.enter_context(nc.sbuf_tensor(name, shape, dt))

    ring = sb("ring", [128, nring, KC, 512], BF16)
    xin = sb("xin", [128, 4, D], F32)
    junk = sb("junk", [128, D], BF16)
    xn = sb("xn", [128, 2, D], BF16)
    hT = sb("hT", [128, 2, KC, TB], BF16)
    gv = sb("gv", [128, D], F32)
    vhat = sb("vhat", [128, 4, D], BF16)
    gu = sb("gu", [128, KC, TB], BF16)
    gcat = sb("gcat", [128, KC, HIST + TB], BF16)
    ah = sb("ah", [128, 2, TB], F32)
    th = sb("th", [128, 2, TB], F32)
    gtl = sb("gtl", [128, KC, 32], F32)
    diag = sb("diag", [128, 2, NTAP, 128], BF16)
    dwb = sb("dwb", [128, KC, TB], BF16)
    dwsq = sb("dwsq", [128, 2, TB], BF16)
    mu_bc = sb("mu_bc", [128, TB], F32)
    rs_bc = sb("rs_bc", [128, TB], F32)
    nt1 = sb("nt1", [128, TB], F32)
    nt2 = sb("nt2", [128, TB], F32)
    sga = sb("sga", [128, 2, TB], BF16)
    pbuf = sb("pbuf", [128, 2, TB], BF16)
    sgb = sb("sgb", [128, KC, TB], BF16)
    nb = sb("nb", [128, 2, TB], BF16)
    tha = sb("tha", [128, 4, TB], BF16)
    thb = sb("thb", [128, 4, TB], BF16)
    xr = sb("xr", [128, D], F32)
    ybuf = sb("ybuf", [128, D], F32)
    ident_f = sb("ident_f", [128, 128], F32)
    ident_b = sb("ident_b", [128, 128], BF16)
    ones_f = sb("ones_f", [128, 128], F32)
    ones_b = sb("ones_b", [128, 128], BF16)
    negh = sb("negh", [128, 8], F32)
    R0 = sb("R0", [128, 128], F32)
    R1 = sb("R1", [128, 128], F32)
    RT0 = sb("RT0", [128, 128], F32)
    RT1 = sb("RT1", [128, 32], F32)
    bTh = sb("bTh", [128, 64], F32)
    cwT = sb("cwT", [128, NTAP * 8], F32)
    scT = sb("scT", [128, HIST * 8], F32)
    wmT = sb("wmT", [128, KC, 128], BF16)
    wmf = sb("wmf", [128, 128], F32)
    Bh = sb("Bh", [128, KC, 128], F32)
    siluT = sb("siluT", [128, 24], F32)
    modT = sb("modT", [128, 24, 3], F32)
    aT = sb("aT", [128, KC, 3], F32)
    gateh = sb("gateh", [128, KC, 3], F32)
    glh = sb("glh", [128, 128], F32)
    gate_bc = sb("gate_bc", [128, D], F32)
    fg_bc = sb("fg_bc", [128, D], F32)
    bv2 = sb("bv2", [2, D], BF16)
    lo_row = sb("lo_row", [1, D], BF16)
    ones2 = sb("ones2", [2, 128], BF16)
    st_s = sb("st_s", [128, 16], F32)
    st_v = sb("st_v", [128, 32], F32)
    print("sbuf bytes remaining:", nc.sbuf_bytes_remaining)

    banks = [st.enter_context(nc.psum_tensor("bank%d" % i, [128, 512], F32)) for i in range(8)]
    bank_ctr = [0]
    held = set()

    def newbank(hold=False):
        while True:
            i = bank_ctr[0] % 8
            bank_ctr[0] += 1
            if i not in held:
                break
        if hold:
            held.add(i)
        return banks[i], "bank%d" % i

    def release(name):
        held.discard(int(name[4:]))

    def mm(out, lhsT, rhs, start, stop, reads, writes):
        S.op("pe", lambda e: e.matmul(out, lhsT=lhsT, rhs=rhs, start=start, stop=stop),
             reads=reads, writes=writes)

    def tr(out, in_, ident, reads, writes):
        S.op("pe", lambda e: e.transpose(out, in_, ident), reads=reads, writes=writes)

    def act(out, in_, func, reads, writes, bias=None, scale=None, accum_out=None, nowaw=False):
        kw = {}
        if bias is not None:
            kw["bias"] = bias
        if scale is not None:
            kw["scale"] = scale
        if accum_out is not None:
            kw["accum_out"] = accum_out
        S.op("act", lambda e: e.activation(out=out, in_=in_, func=func, **kw), reads=reads, writes=writes, nowaw=nowaw)

    def ts(eng, out, in0, s1, s2, op0, op1, reads, writes, nowaw=False):
        if op1 is None:
            S.op(eng, lambda e: e.tensor_scalar(out=out, in0=in0, scalar1=s1, scalar2=None, op0=op0),
                 reads=reads, writes=writes, nowaw=nowaw)
        else:
            S.op(eng, lambda e: e.tensor_scalar(out=out, in0=in0, scalar1=s1, scalar2=s2, op0=op0, op1=op1),
                 reads=reads, writes=writes)

    def tt(eng, out, in0, in1, op, reads, writes):
        S.op(eng, lambda e: e.tensor_tensor(out=out, in0=in0, in1=in1, op=op), reads=reads, writes=writes)

    def stt(out, in0, scalar, in1, op0, op1, reads, writes):
        S.op("dve", lambda e: e.scalar_tensor_tensor(out=out, in0=in0, scalar=scalar, in1=in1, op0=op0, op1=op1),
             reads=reads, writes=writes)

    def cp(eng, out, in_, reads, writes):
        S.op(eng, lambda e: e.tensor_copy(out, in_), reads=reads, writes=writes)

    def dma(q, out, in_, reads, writes, key):
        S.dma(q, lambda e: [e.dma_start(out=out, in_=in_)], reads=reads, writes=writes, sem_key=key)

    def rows128(v):
        return v.rearrange("(r k) -> r k", k=128)

    S.op("pool", lambda e: e.memset(ident_f[:], 1.0), writes=["ident_f"])
    S.op("pool", lambda e: e.affine_select(out=ident_f[:], in_=ident_f[:], pattern=[[-1, 128]],
                                           compare_op=ALU.is_equal, fill=0.0, base=0, channel_multiplier=1),
         reads=["ident_f"], writes=["ident_f"])
    S.op("pool", lambda e: e.memset(ones_f[:], 1.0), writes=["ones_f"])
    S.op("pool", lambda e: e.memset(negh[:], -0.5), writes=["negh"])
    cp("dve", ident_b[:], ident_f[:], ["ident_f"], ["ident_b"])
    cp("dve", ones_b[:], ones_f[:], ["ones_f"], ["ones_b"])
    cp("dve", ones2[:], ones_f[0:2, :], ["ones_f"], ["ones2"])

    dma("sp", R0[0:64, :], rows128(b_in), [], ["R0"], "setup")
    dma("sp", R0[64:88, :], rows128(b_ada), [], ["R0"], "setup")
    dma("sp", R0[88:96, :], rows128(norm_g), [], ["R0"], "setup")
    dma("sp", R0[96:104, :], rows128(ln_v_g), [], ["R0"], "setup")
    dma("sp", R0[104:112, :], rows128(ln_v_b), [], ["R0"], "setup")
    dma("sp", R0[112:120, :], rows128(conv_b), [], ["R0"], "setup")
    dma("sp", R0[120:128, :], rows128(ln_c_g), [], ["R0"], "setup")
    dma("sp", R1[0:8, :], rows128(ln_c_b), [], ["R1"], "setup")
    dma("sp", R1[8:32, :], cc.rearrange("s (c k) -> (s c) k", k=128), [], ["R1"], "setup")
    dma("sp", fg_bc[:], final_g.rearrange("(o n) -> o n", o=1).partition_broadcast(128), [], ["fg_bc"], "setup")
    vh_f = vhat[:].rearrange("p a b -> p (a b)").bitcast(F32)
    gu_f = gu[:].rearrange("p a b -> p (a b)").bitcast(F32)
    lnvb_row = vh_f[0:1, 0:D]
    bs_row = vh_f[0:1, D:2 * D]
    rs_row = gu_f[0:1, 0:D]
    bv_row = gu_f[0:1, D:2 * D]
    dma("sp", lnvb_row, ln_v_b.rearrange("(o n) -> o n", o=1), [], ["vhat"], "setup")
    dma("sp", bs_row, b_sp.rearrange("(o n) -> o n", o=1), [], ["vhat"], "setup")
    dma("sp", bv_row, b_in[D:2 * D].rearrange("(o n) -> o n", o=1), [], ["gu"], "setup")

    def wsrc(slot):
        def cols(w, c0, n):
            return w[:, c0:c0 + n].rearrange("(k p) n -> p k n", p=128)
        if slot in (0, 1):
            return [(0, 512, cols(w_in, D + slot * 512, 512))]
        if slot in (2, 3):
            return [(0, 512, cols(w_in, (slot - 2) * 512, 512))]
        if 4 <= slot <= 7:
            i = slot - 4
            return [(0, 256, cols(w_in, 3 * D + i * 256, 256)), (256, 256, cols(w_in, 4 * D + i * 256, 256))]
        if slot in (8, 9):
            return [(0, 512, cols(w_in, 2 * D + (slot - 8) * 512, 512))]
        if slot in (10, 11):
            return [(0, 512, cols(w_in, 5 * D + (slot - 10) * 512, 512))]
        if 12 <= slot <= 15:
            i = slot - 12
            return [(0, 256, cols(w_in, 6 * D + i * 256, 256)), (256, 256, cols(w_in, 7 * D + i * 256, 256))]
        if 16 <= slot <= 19:
            i = slot - 16
            return [(0, 256, cols(w_o_a, i * 256, 256)), (256, 256, cols(w_o_b, i * 256, 256))]
        return [(0, 512, cols(w_out, (slot - 20) * 512, 512))]

    def emit_conversion():
        S.tag = "conv"
        stg = []
        for h in range(2):
            v = ring[:, nring - 2 + h].rearrange("p k n -> p (k n)").bitcast(F32)
            stg.append((v, "ring%d" % (nring - 2 + h)))
        cobs = [(ring[:, nring - 3], "ring%d" % (nring - 3)), (hT[:, 1], "hT1")]
        for slot in range(NSLOT):
            srcs = wsrc(slot)
            cob, cn = cobs[slot % 2]
            for h in range(2):
                sv, sn = stg[h]
                if len(srcs) == 2:
                    (c0, n, src) = srcs[h]
                    view = sv.rearrange("p (k n) -> p k n", k=KC)
                    dstv = cob[:, :, c0:c0 + n]
                    dma("sp", view, src, [], [sn], "stg%d" % h)
                else:
                    (c0, n, src) = srcs[0]
                    view = sv.rearrange("p (k n) -> p k n", k=KC // 2)
                    dstv = cob[:, h * 4:(h + 1) * 4, :]
                    dma("sp", view, src[:, h * 4:(h + 1) * 4, :], [], [sn], "stg%d" % h)
                if h == 0:
                    act(dstv, view, AF.Identity, [sn], [cn])
                else:
                    cp("dve", dstv, view, [sn], [cn])
            dma("act", wscr[slot], cob.rearrange("p k n -> p (k n)"), [cn], ["wscr%d" % slot], "wscr_w%d" % slot)

    b0, n0 = newbank()
    tr(b0[:, 0:128], R0[:], ident_f[:], ["R0", "ident_f"], [n0])
    act(RT0[:], b0[:, 0:128], AF.Identity, [n0], ["RT0"])
    b1, n1 = newbank()
    tr(b1[:, 0:32], R1[0:32, :], ident_f[0:32, 0:32], ["R1", "ident_f"], [n1])
    act(RT1[:], b1[:, 0:32], AF.Identity, [n1], ["RT1"])
    ts("dve", bTh[:], RT0[:, 0:64], 0.5, None, ALU.mult, None, ["RT0"], ["bTh"])
    bT = RT0
    b_adaT = lambda jc: RT0[:, 64 + jc:65 + jc]
    norm_gT = lambda c: RT0[:, 88 + c:89 + c]
    lnvgT = lambda c: RT0[:, 96 + c:97 + c]
    convbT = lambda c: RT0[:, 112 + c:113 + c]
    lncgT = lambda c: RT0[:, 120 + c:121 + c]
    lncbT = lambda c: RT1[:, c:c + 1]
    act(siluT[:], RT1[:, 8:32], AF.Silu, ["RT1"], ["siluT"])

    cw_rows = conv_w.rearrange("j (c k) -> (j c) k", k=128)
    dma("act", R0[:], cw_rows[0:128, :], ["RT0"], ["R0"], "cw0")
    dma("act", R1[0:120, :], cw_rows[128:248, :], ["RT1"], ["R1"], "cw1")
    b2, n2 = newbank()
    tr(b2[:, 0:128], R0[:], ident_f[:], ["R0", "ident_f"], [n2])
    tr(b2[:, 128:248], R1[0:120, :], ident_f[0:120, 0:120], ["R1", "ident_f"], [n2])
    act(cwT[:], b2[:, 0:248], AF.Identity, [n2], ["cwT"])
    sc_rows = scv.rearrange("r (c k) -> (r c) k", k=128)
    dma("act", R0[:], sc_rows[0:128, :], [], ["R0"], "sc0")
    dma("act", R1[0:112, :], sc_rows[128:240, :], [], ["R1"], "sc1")
    b3, n3 = newbank()
    tr(b3[:, 0:128], R0[:], ident_f[:], ["R0", "ident_f"], [n3])
    tr(b3[:, 128:240], R1[0:112, :], ident_f[0:112, 0:112], ["R1", "ident_f"], [n3])
    act(scT[:], b3[:, 0:240], AF.Identity, [n3], ["scT"])

    def wsp_head(h):
        Rb, Rn = (R0, "R0") if h % 2 == 0 else (R1, "R1")
        dma("sp", Rb[:], w_sp[h], [], [Rn], "wsp%d" % (h % 2))
        bb, nn = newbank()
        tr(bb[:, 0:128], Rb[:], ident_f[:], [Rn, "ident_f"], [nn])
        act(wmf[:], bb[:, 0:128], AF.Identity, [nn], ["wmf"])
        S.op("pool", lambda e: e.affine_select(out=wmf[:], in_=wmf[:], pattern=[[1, 128]],
                                               compare_op=ALU.is_ge, fill=0.0, base=0, channel_multiplier=-1),
             reads=["wmf"], writes=["wmf"])
        cp("dve", wmT[:, h, :], wmf[:], ["wmf"], ["wmT"])
        bb2, nn2 = newbank()
        mm(bb2[0:1, 0:128], ones_f[:, 0:1], wmf[:], True, True, ["ones_f", "wmf"], [nn2])
        act(rs_row[:, h * 128:(h + 1) * 128], bb2[0:1, 0:128], AF.Identity, [nn2], ["gu"])
        bb3, nn3 = newbank()
        mm(bb3[:, 0:128], lnvb_row[:, h * 128:(h + 1) * 128], rs_row[:, h * 128:(h + 1) * 128], True, False,
           ["vhat", "gu"], [nn3])
        mm(bb3[:, 0:128], ones_f[0:1, :], bs_row[:, h * 128:(h + 1) * 128], False, True,
           ["ones_f", "vhat"], [nn3])
        act(Bh[:, h, :], bb3[:, 0:128], AF.Identity, [nn3], ["Bh"])

    sgb_names = ["sgb%d" % c for c in range(KC)]
    dwb_names = ["dwb%d" % c for c in range(KC)]
    ada_bufs = [
        (sgb[:].rearrange("p a b -> p (a b)").bitcast(F32).rearrange("p (k n) -> p k n", k=KC), sgb_names, "ada0"),
        (dwb[:].rearrange("p a b -> p (a b)").bitcast(F32).rearrange("p (k n) -> p k n", k=KC), dwb_names, "ada1"),
    ]
    for pc in range(12):
        ab, an, ak = ada_bufs[pc % 2]
        dma("sp", ab, w_ada[:, pc * 256:(pc + 1) * 256].rearrange("(k p) n -> p k n", p=128), [], an, ak)
        for jl in range(2):
            jc = pc * 2 + jl
            bb, nn = newbank()
            for kc in range(KC):
                mm(bb[:, 0:3], ab[:, kc, jl * 128:(jl + 1) * 128], siluT[:, kc:24:8], kc == 0, kc == KC - 1,
                   an + ["siluT"], [nn])
            act(modT[:, jc, :], bb[:, 0:3], AF.Identity, [nn], ["modT"], bias=b_adaT(jc))
        if pc < 8:
            wsp_head(pc)
    cp("dve", bv2[0:1, :], bv_row, ["gu"], ["bv2"])
    tt("dve", nt1[0:1, 0:512], bv_row[:, 0:512], bv2[0:1, 0:512], ALU.subtract, ["gu", "bv2"], ["nt1"])
    tt("dve", nt2[0:1, 0:512], bv_row[:, 512:1024], bv2[0:1, 512:1024], ALU.subtract, ["gu", "bv2"], ["nt2"])
    cp("dve", lo_row[:, 0:512], nt1[0:1, 0:512], ["nt1"], ["lo_row"])
    cp("dve", lo_row[:, 512:1024], nt2[0:1, 0:512], ["nt2"], ["lo_row"])
    dma("act", bv2[1:2, :], lo_row[:], ["lo_row"], ["bv2"], "bv2lo")

    for kc in range(KC):
        ts("dve", aT[:, kc, :], modT[:, 8 + kc, :], 1.0, norm_gT(kc), ALU.add, ALU.mult, ["modT", "RT0"], ["aT"])
        ts("dve", gateh[:, kc, :], modT[:, 16 + kc, :], 0.5, None, ALU.mult, None, ["modT"], ["gateh"])
    shT = lambda kc, s: modT[:, kc, s:s + 1]

    slot_ctr = [0]
    first_pass = [True]
    stg = []
    for h_ in range(2):
        v_ = ring[:, nring - 2 + h_].rearrange("p k n -> p (k n)").bitcast(F32)
        stg.append((v_, "ring%d" % (nring - 2 + h_), "stg%d" % h_))
    stg.append((hT[:, 1].rearrange("p k n -> p (k n)").bitcast(F32), "hT1", "stg2"))
    stg_ctr = [0]

    def load_slot(slot):
        nr = (nring - 2) if first_pass[0] else nring
        r = slot_ctr[0] % nr
        slot_ctr[0] += 1
        rn = "ring%d" % r
        dst = ring[:, r]
        if first_pass[0]:
            srcs = wsrc(slot)
            for h in range(2):
                sv, sn, sk = stg[stg_ctr[0] % len(stg)]
                stg_ctr[0] += 1
                if len(srcs) == 2:
                    (c0, n, src) = srcs[h]
                    view = sv.rearrange("p (k n) -> p k n", k=KC)
                    dstv = dst[:, :, c0:c0 + n]
                    dma("sp", view, src, [], [sn], sk)
                else:
                    (c0, n, src) = srcs[0]
                    view = sv.rearrange("p (k n) -> p k n", k=KC // 2)
                    dstv = dst[:, h * 4:(h + 1) * 4, :]
                    dma("sp", view, src[:, h * 4:(h + 1) * 4, :], [], [sn], sk)
                if h == 0:
                    act(dstv, view, AF.Identity, [sn], [rn])
                else:
                    cp("dve", dstv, view, [sn], [rn])
            dma("act", wscr[slot], dst.rearrange("p k n -> p (k n)"), [rn], ["wscr%d" % slot], "wscr_w%d" % slot)
        else:
            dma("sp", dst.rearrange("p k n -> p (k n)"), wscr[slot], ["wscr%d" % slot], [rn], "ringld%d" % r)
        return dst, rn

    class Blk:
        pass

    blocks = []
    for sq in range(2):
        for b in range(npb):
            k = Blk()
            k.seq, k.xsrc, k.ydst, k.t0, k.T = sq, xp[sq], yp[sq], b * TB, TB
            k.first, k.last, k.is_sample = (b == 0), (b == npb - 1), False
            blocks.append(k)
    k = Blk()
    k.seq, k.xsrc, k.ydst, k.t0, k.T = 2, xs, ys, 0, DEC
    k.first, k.last, k.is_sample = True, True, True
    blocks.append(k)
    for i, k in enumerate(blocks):
        k.idx = i
        k.TR = min(k.T, 128)
        k.NTT = max(1, k.T // 128)
        k.h = i % 2
        k.hn = "hT%d" % k.h

    def emit_xload(k):
        for t_ in range(k.NTT):
            dma("sp", xin[0:k.TR, t_, :], k.xsrc[k.t0 + t_ * 128:k.t0 + t_ * 128 + k.TR, :], [], ["xin%d" % t_],
                "xin%d" % t_)

    def emit_P0_S1(k, t_):
        S.tag = "P0"
        TR = k.TR
        c0 = 10 + (t_ % 2)
        act(junk[0:TR, :], xin[0:TR, t_, :], AF.Square, ["xin%d" % t_], ["junk", "st_s0_%d" % (t_ % 2)],
            accum_out=st_s[0:TR, c0:c0 + 1])

    def emit_P0_S2(k, t_):
        S.tag = "P0"
        TR = k.TR
        c0 = 10 + (t_ % 2)
        v0 = 12 + 2 * (t_ % 2)
        sn0, vn0 = "st_s0_%d" % (t_ % 2), "st_v0_%d" % (t_ % 2)
        ts("dve", st_v[0:TR, v0:v0 + 1], st_s[0:TR, c0:c0 + 1], 1.0 / D, EPS, ALU.mult, ALU.add, [sn0], [vn0])
        tt("pool", st_v[0:TR, v0 + 1:v0 + 2], st_v[0:TR, v0:v0 + 1], negh[0:TR, 0:1], ALU.pow, [vn0, "negh"], [vn0])

    def emit_P0_S3(k, t_):
        S.tag = "P0"
        TR = k.TR
        v0 = 12 + 2 * (t_ % 2)
        vn0 = "st_v0_%d" % (t_ % 2)
        ts("dve", xn[0:TR, t_ % 2, :], xin[0:TR, t_, :], st_v[0:TR, v0 + 1:v0 + 2], None, ALU.mult, None,
           ["xin%d" % t_, vn0], ["xn%d" % (t_ % 2)])

    def emit_P0_A(k, t_):
        emit_P0_S1(k, t_)
        emit_P0_S2(k, t_)
        emit_P0_S3(k, t_)

    def emit_P0_B(k, t_):
        S.tag = "P0"
        TR, s = k.TR, k.seq
        hTb = hT[:, k.h]
        bb, nn = newbank()
        bbf = bb[:].bitcast(BF16)
        for kc in range(KC):
            tr(bbf[:, kc * 128:kc * 128 + TR], xn[0:TR, t_ % 2, kc * 128:(kc + 1) * 128], ident_b[0:TR, 0:TR],
               ["xn%d" % (t_ % 2), "ident_b"], [nn])
        for kc in range(KC):
            act(hTb[:, kc, t_ * 128:t_ * 128 + TR], bbf[:, kc * 128:kc * 128 + TR], AF.Identity, [nn, "aT", "modT"],
                [k.hn], bias=shT(kc, s), scale=aT[:, kc, s:s + 1], nowaw=True)

    def emit_P0_tile(k, t_):
        emit_P0_A(k, t_)
        emit_P0_B(k, t_)

    def emit_P1_pre(k):
        S.tag = "P1"
        k.slots_v = [load_slot(0), load_slot(1)]
        if k.is_sample:
            dma("sp", xr[0:DEC, :], ln_v_g.rearrange("(o n) -> o n", o=1).partition_broadcast(DEC), [], ["xr"], "xr")
            dma("sp", ybuf[0:DEC, :], ln_v_b.rearrange("(o n) -> o n", o=1).partition_broadcast(DEC), [], ["ybuf"],
                "ybuf_ld")

    def p1_buf(k, t_):
        if k.is_sample or t_ % 2 == 0:
            return gv, "gv"
        return xr, "xr"

    def emit_P1_S1(k, t_):
        S.tag = "P1"
        TR = k.TR
        hTb = hT[:, k.h]
        gb_, gn_ = p1_buf(k, t_)
        par = t_ % 2
        sb_ = 2 if par == 0 else 12
        sn1 = "st_s1_%d" % par
        for half in range(2):
            sl, sn = k.slots_v[half]
            bb, nn = newbank()
            for kc in range(KC):
                mm(bb[0:TR, :], hTb[:, kc, t_ * 128:t_ * 128 + TR], sl[:, kc, :], kc == 0, False, [k.hn, sn], [nn])
            mm(bb[0:TR, :], ones2[0:2, 0:TR], bv2[0:2, half * 512:(half + 1) * 512], False, True,
               ["ones2", "bv2"], [nn])
            act(gb_[0:TR, half * 512:(half + 1) * 512], bb[0:TR, :], AF.Gelu_apprx_tanh, [nn], [gn_, sn1],
                accum_out=st_s[0:TR, sb_ + half:sb_ + half + 1])
            act(junk[0:TR, 0:512], gb_[0:TR, half * 512:(half + 1) * 512], AF.Square, [gn_], ["junk", sn1],
                accum_out=st_s[0:TR, sb_ + 2 + half:sb_ + 3 + half])

    def emit_P1_S2(k, t_):
        S.tag = "P1"
        TR = k.TR
        par = t_ % 2
        sb_ = 2 if par == 0 else 12
        vb = 2 if par == 0 else 16
        sn1 = "st_s1_%d" % par
        sv = ["st_v1_%d" % par]
        tt("dve", st_v[0:TR, vb:vb + 1], st_s[0:TR, sb_:sb_ + 1], st_s[0:TR, sb_ + 1:sb_ + 2], ALU.add, [sn1], sv)
        tt("dve", st_v[0:TR, vb + 1:vb + 2], st_s[0:TR, sb_ + 2:sb_ + 3], st_s[0:TR, sb_ + 3:sb_ + 4], ALU.add, [sn1], sv)
        ts("dve", st_v[0:TR, vb:vb + 2], st_v[0:TR, vb:vb + 2], 1.0 / D, None, ALU.mult, None, sv, sv)
        tt("dve", st_v[0:TR, vb + 2:vb + 3], st_v[0:TR, vb:vb + 1], st_v[0:TR, vb:vb + 1], ALU.mult, sv, sv)
        tt("dve", st_v[0:TR, vb + 3:vb + 4], st_v[0:TR, vb + 1:vb + 2], st_v[0:TR, vb + 2:vb + 3], ALU.subtract, sv, sv)
        ts("dve", st_v[0:TR, vb + 3:vb + 4], st_v[0:TR, vb + 3:vb + 4], EPS, None, ALU.add, None, sv, sv)
        tt("pool", st_v[0:TR, vb + 4:vb + 5], st_v[0:TR, vb + 3:vb + 4], negh[0:TR, 0:1], ALU.pow, sv + ["negh"], sv)

    def emit_P1_S3(k, t_):
        S.tag = "P1"
        TR = k.TR
        gb_, gn_ = p1_buf(k, t_)
        par = t_ % 2
        vb = 2 if par == 0 else 16
        sv = ["st_v1_%d" % par]
        ts("dve", st_v[0:TR, vb + 5:vb + 6], st_v[0:TR, vb:vb + 1], st_v[0:TR, vb + 4:vb + 5], -1.0, ALU.mult, ALU.mult, sv, sv)
        ts("dve", vhat[0:TR, t_, :], gb_[0:TR, :], st_v[0:TR, vb + 4:vb + 5], st_v[0:TR, vb + 5:vb + 6], ALU.mult, ALU.add,
           [gn_] + sv, ["vhat"])
        if k.is_sample:
            ts("dve", gv[0:TR, :], gv[0:TR, :], st_v[0:TR, vb + 4:vb + 5], st_v[0:TR, vb + 5:vb + 6], ALU.mult, ALU.add,
               ["gv"] + sv, ["gv"])
            tt("dve", gv[0:TR, :], gv[0:TR, :], xr[0:TR, :], ALU.mult, ["gv", "xr"], ["gv"])
            tt("dve", gv[0:TR, :], gv[0:TR, :], ybuf[0:TR, :], ALU.add, ["gv", "ybuf"], ["gv"])
            dma("act", nvs, gv[0:DEC, :], ["gv"], [], "nvs")

    def emit_P1_tile(k, t_):
        emit_P1_S1(k, t_)
        emit_P1_S2(k, t_)
        emit_P1_S3(k, t_)

    def emit_front(k):
        emit_xload(k)
        for t_ in range(k.NTT):
            emit_P0_tile(k, t_)
        emit_P1_pre(k)
        for t_ in range(k.NTT):
            emit_P1_tile(k, t_)

    def emit_body(k, nxt, hoist):
        seq, xsrc, ydst, t0, T, first, last, is_sample = k.seq, k.xsrc, k.ydst, k.t0, k.T, k.first, k.last, k.is_sample
        TR, NTT, s = k.TR, k.NTT, k.seq
        hTb = hT[:, k.h]
        hn = k.hn
        S.tag = "P2"
        for sidx in (2, 3):
            sl, sn = load_slot(sidx)
            for cl in range(4):
                c = (sidx - 2) * 4 + cl
                bb, nn = newbank()
                for kc in range(KC):
                    mm(bb[:, 0:T], sl[:, kc, cl * 128:(cl + 1) * 128], hTb[:, kc, 0:T], kc == 0, kc == KC - 1,
                       [sn, hn], [nn])
                act(gu[:, c, 0:T], bb[:, 0:T], AF.Gelu_apprx_tanh, [nn], ["gu%d" % c], bias=bT[:, c:c + 1])

        def build_diag(c):
            for j in range(NTAP):
                if diag_act_every and (j % diag_act_every == diag_act_every - 1):
                    act(diag[:, c % 2, j, :], ident_b[:], AF.Identity, ["ident_b", "cwT"], ["diagA%d" % (c % 2)],
                        scale=cwT[:, j * 8 + c:j * 8 + c + 1], nowaw=True)
                else:
                    ts("dve", diag[:, c % 2, j, :], ident_b[:], cwT[:, j * 8 + c:j * 8 + c + 1], None, ALU.mult, None,
                       ["ident_b", "cwT"], ["diagD%d" % (c % 2)], nowaw=True)
        S.tag = "P3"
        build_diag(0)
        gnames = ["gcat%d" % c for c in range(KC)]
        if first and not is_sample:
            S.op("dve", lambda e: e.memset(gcat[:, :, 0:HIST], 0.0), writes=gnames)
        elif is_sample:
            cp("dve", gcat[:, :, 0:HIST], scT[:].rearrange("p (r c) -> p c r", c=8), ["scT"], gnames)
        NTL = min(HIST, T)
        for i in range(4):
            sl, sn = load_slot(4 + i)
            for cl in range(2):
                c = i * 2 + cl
                ba, na = newbank()
                bbk, nbk = newbank()
                for kc in range(KC):
                    mm(ba[:, 0:T], sl[:, kc, cl * 128:(cl + 1) * 128], hTb[:, kc, 0:T], kc == 0, kc == KC - 1,
                       [sn, hn], [na])
                for kc in range(KC):
                    mm(bbk[:, 0:T], sl[:, kc, 256 + cl * 128:256 + (cl + 1) * 128], hTb[:, kc, 0:T], kc == 0,
                       kc == KC - 1, [sn, hn], [nbk])
                act(ah[:, c % 2, 0:T], ba[:, 0:T], AF.Identity, [na, "bTh"], ["ah%d" % (c % 2)],
                    bias=bTh[:, 24 + c:25 + c], scale=0.5)
                act(th[:, c % 2, 0:T], bbk[:, 0:T], AF.Tanh, [nbk, "bTh"], ["th%d" % (c % 2)],
                    bias=bTh[:, 32 + c:33 + c], scale=0.5)
                stt(gcat[:, c, HIST:HIST + T], th[:, c % 2, 0:T], 1.0, ah[:, c % 2, 0:T], ALU.add, ALU.mult,
                    ["ah%d" % (c % 2), "th%d" % (c % 2)], ["gcat%d" % c])
                if last:
                    stt(gtl[:, c, 0:NTL], th[:, c % 2, T - NTL:T], 1.0, ah[:, c % 2, T - NTL:T], ALU.add, ALU.mult,
                        ["ah%d" % (c % 2), "th%d" % (c % 2)], ["gtl"])
        if last:
            bb, nn = newbank()
            bb2, nn2 = newbank()
            for c in range(KC):
                tgt = bb if c < 4 else bb2
                tr(tgt[0:NTL, (c % 4) * 128:(c % 4 + 1) * 128], gtl[:, c, 0:NTL], ident_f[:], ["gtl", "ident_f"],
                   [nn if c < 4 else nn2])
            act(ybuf[0:NTL, 0:512], bb[0:NTL, :], AF.Identity, [nn], ["ybuf"])
            act(ybuf[0:NTL, 512:1024], bb2[0:NTL, :], AF.Identity, [nn2], ["ybuf"])
            if is_sample:
                dma("act", ncs[HIST - NTL:HIST, :], ybuf[0:NTL, :], ["ybuf"], [], "ybuf_st")
                dma("sp", ncs[0:HIST - NTL, :], scv[NTL:HIST, :], [], [], "ncs_cp")
            else:
                dma("act", ncp[s], ybuf[0:NTL, :], ["ybuf"], [], "ybuf_st")
        if nxt is not None:
            emit_xload(nxt)
        S.tag = "P4P5"
        sl_ga = [None, None]
        sl_gb = [None, None]
        bsum, nsum = newbank(hold=True)
        bsq, nsq = newbank(hold=True)

        def stats_mm(c):
            mm(bsum[:, 0:T], ones_b[:], dwb[:, c, 0:T], c == 0, c == KC - 1, ["ones_b", "dwb%d" % c], [nsum])
            mm(bsq[:, 0:T], ones_b[:], dwsq[:, c % 2, 0:T], c == 0, c == KC - 1, ["ones_b", "dwsq%d" % (c % 2)], [nsq])

        for c in range(KC):
            if c + 1 < KC:
                build_diag(c + 1)
            bb, nn = newbank()
            for j in range(NTAP):
                mm(bb[:, 0:T], diag[:, c % 2, j, :], gcat[:, c, j:j + T], j == 0, j == NTAP - 1,
                   ["diagA%d" % (c % 2), "diagD%d" % (c % 2), "gcat%d" % c], [nn])
            act(dwb[:, c, 0:T], bb[:, 0:T], AF.Identity, [nn], ["dwb%d" % c], bias=convbT(c))
            act(dwsq[:, c % 2, 0:T], bb[:, 0:T], AF.Square, [nn], ["dwsq%d" % (c % 2)], bias=convbT(c))
            if c % 4 == 0:
                sl_ga[0], sl_ga[1] = load_slot(8 + c // 4)
                sl_gb[0], sl_gb[1] = load_slot(10 + c // 4)
            cl = c % 4
            bg, ng = newbank()
            for kc in range(KC):
                mm(bg[:, 0:T], sl_ga[0][:, kc, cl * 128:(cl + 1) * 128], hTb[:, kc, 0:T], kc == 0, kc == KC - 1,
                   [sl_ga[1], hn], [ng])
            bs_, ns_ = newbank()
            for n_ in range(NTT):
                mm(bs_[:, n_ * 128:n_ * 128 + TR], vhat[0:TR, n_, c * 128:(c + 1) * 128], wmT[0:TR, c, 0:TR], True, True,
                   ["vhat", "wmT"], [ns_])
            act(sga[:, c % 2, 0:T], bg[:, 0:T], AF.Silu, [ng], ["sga%d" % (c % 2)], bias=bT[:, 16 + c:17 + c])
            tt("dve", pbuf[:, c % 2, 0:T], gu[:, c, 0:T], sga[:, c % 2, 0:T], ALU.mult, ["gu%d" % c, "sga%d" % (c % 2)],
               ["pbuf%d" % (c % 2)])
            if T == TB:
                in1 = Bh[:, c, :].unsqueeze(1).to_broadcast([128, 4, 128])
                in0 = bs_[:, :].rearrange("p (a b) -> p a b", a=4)
                o_ = nt1[:, :].rearrange("p (a b) -> p a b", a=4)
            else:
                in1 = Bh[:, c, 0:T]
                in0 = bs_[:, 0:T]
                o_ = nt1[:, 0:T]
            stt(o_, in0, lnvgT(c), in1, ALU.mult, ALU.add, [ns_, "Bh", "RT0"], ["nt1"])
            tt("dve", gu[:, c, 0:T], nt1[:, 0:T], pbuf[:, c % 2, 0:T], ALU.mult, ["nt1", "pbuf%d" % (c % 2)],
               ["gu%d" % c])
            bgb, ngb = newbank()
            for kc in range(KC):
                mm(bgb[:, 0:T], sl_gb[0][:, kc, cl * 128:(cl + 1) * 128], hTb[:, kc, 0:T], kc == 0, kc == KC - 1,
                   [sl_gb[1], hn], [ngb])
            act(sgb[:, c, 0:T], bgb[:, 0:T], AF.Silu, [ngb], ["sgb%d" % c], bias=bT[:, 40 + c:41 + c])
            if c >= 1:
                stats_mm(c - 1)
        stats_mm(KC - 1)
        if not last:
            cp("dve", gcat[:, :, 0:HIST], gcat[:, :, T:T + HIST], gnames, gnames)
        S.tag = "P7"
        mT = gcat
        p7 = {}

        def p7_slots(i):
            if ("m", i) not in p7:
                p7[("m", i)] = load_slot(12 + i)
                p7[("o", i)] = load_slot(16 + i)
            return p7[("m", i)], p7[("o", i)]

        def p7_part1(o):
            S.tag = "P7"
            i, ol = o // 2, o % 2
            (sl_m, sn_m), (sl_o, sn_o) = p7_slots(i)
            bma, nma = newbank()
            bmb, nmb = newbank()
            bpa, npa = newbank(hold=True)
            p7[("pa", o)] = (bpa, npa)
            for kc in range(KC):
                mm(bma[:, 0:T], sl_m[:, kc, ol * 128:(ol + 1) * 128], hTb[:, kc, 0:T], kc == 0, kc == KC - 1,
                   [sn_m, hn], [nma])
            for kc in range(KC):
                mm(bmb[:, 0:T], sl_m[:, kc, 256 + ol * 128:256 + (ol + 1) * 128], hTb[:, kc, 0:T], kc == 0,
                   kc == KC - 1, [sn_m, hn], [nmb])
            for c in range(KC):
                mm(bpa[:, 0:T], sl_o[:, c, ol * 128:(ol + 1) * 128], gu[:, c, 0:T], c == 0, c == KC - 1,
                   [sn_o, "gu%d" % c], [npa])
            act(tha[:, o % 4, 0:T], bma[:, 0:T], AF.Tanh, [nma, "bTh"], ["tha%d" % (o % 4)],
                bias=bTh[:, 48 + o:49 + o], scale=0.5)
            act(thb[:, o % 4, 0:T], bmb[:, 0:T], AF.Tanh, [nmb, "bTh"], ["thb%d" % (o % 4)],
                bias=bTh[:, 56 + o:57 + o], scale=0.5)

        def p7_part2(o):
            S.tag = "P7"
            i, ol = o // 2, o % 2
            (sl_m, sn_m), (sl_o, sn_o) = p7_slots(i)
            bpa, npa = p7[("pa", o)]
            bpb, npb_ = newbank()
            for c in range(KC):
                mm(bpb[:, 0:T], sl_o[:, c, 256 + ol * 128:256 + (ol + 1) * 128], sgb[:, c, 0:T], c == 0, c == KC - 1,
                   [sn_o, "sgb%d" % c], [npb_])
            stt(nt1[:, 0:T], tha[:, o % 4, 0:T], 1.0, bpa[:, 0:T], ALU.add, ALU.mult, ["tha%d" % (o % 4), npa], ["nt1"])
            stt(nt2[:, 0:T], thb[:, o % 4, 0:T], 1.0, bpb[:, 0:T], ALU.add, ALU.mult, ["thb%d" % (o % 4), npb_], ["nt2"])
            tt("dve", mT[:, o, HIST:HIST + T], nt1[:, 0:T], nt2[:, 0:T], ALU.add, ["nt1", "nt2"], ["gcat%d" % o])
            release(npa)

        p7_part1(0)
        p7_part1(1)
        p7_part1(2)
        S.tag = "LN_c"
        release(nsum)
        release(nsq)
        ts("dve", mu_bc[:, 0:T], bsum[:, 0:T], 1.0 / D, None, ALU.mult, None, [nsum], ["mu_bc"])
        tt("dve", nt1[:, 0:T], mu_bc[:, 0:T], mu_bc[:, 0:T], ALU.mult, ["mu_bc"], ["nt1"])
        stt(nt2[:, 0:T], bsq[:, 0:T], 1.0 / D, nt1[:, 0:T], ALU.mult, ALU.subtract, [nsq, "nt1"], ["nt2"])
        ts("dve", nt2[:, 0:T], nt2[:, 0:T], EPS, None, ALU.add, None, ["nt2"], ["nt2"])
        I32 = mybir.dt.int32
        rsi = rs_bc[:, 0:T].bitcast(I32)
        S.op("dve", lambda e: e.tensor_single_scalar(out=rsi, in_=nt2[:, 0:T].bitcast(I32), scalar=1,
                                                     op=ALU.arith_shift_right), reads=["nt2"], writes=["rs_bc"])
        ts("dve", rsi, rsi, -1, 0x5f3759df, ALU.mult, ALU.add, ["rs_bc"], ["rs_bc"])
        for it in range(3):
            tt("dve", nt1[:, 0:T], rs_bc[:, 0:T], rs_bc[:, 0:T], ALU.mult, ["rs_bc"], ["nt1"])
            tt("dve", nt1[:, 0:T], nt1[:, 0:T], nt2[:, 0:T], ALU.mult, ["nt1", "nt2"], ["nt1"])
            ts("dve", nt1[:, 0:T], nt1[:, 0:T], -0.5, 1.5, ALU.mult, ALU.add, ["nt1"], ["nt1"])
            tt("dve", rs_bc[:, 0:T], rs_bc[:, 0:T], nt1[:, 0:T], ALU.mult, ["rs_bc", "nt1"], ["rs_bc"])
        S.tag = "P6"
        def p6_pre(c):
            tmp, tmpn = (nt1, "nt1") if c % 2 == 0 else (nt2, "nt2")
            tt("dve", tmp[:, 0:T], dwb[:, c, 0:T], mu_bc[:, 0:T], ALU.subtract, ["dwb%d" % c, "mu_bc"], [tmpn])
            tt("dve", tmp[:, 0:T], tmp[:, 0:T], rs_bc[:, 0:T], ALU.mult, [tmpn, "rs_bc"], [tmpn])
            act(nb[:, c % 2, 0:T], tmp[:, 0:T], AF.Silu, [tmpn, "RT0", "RT1"], ["nb%d" % (c % 2)], bias=lncbT(c),
                scale=lncgT(c))

        def p6_fin(c):
            tt("dve", sgb[:, c, 0:T], nb[:, c % 2, 0:T], sgb[:, c, 0:T], ALU.mult, ["nb%d" % (c % 2), "sgb%d" % c],
               ["sgb%d" % c])

        p6_pre(0)
        for c in range(1, KC):
            p6_pre(c)
            p6_fin(c - 1)
        p6_fin(KC - 1)
        for o in range(KC):
            p7_part2(o)
            if o + 3 < KC:
                p7_part1(o + 3)
            if nxt is not None and hoist:
                n_ = nxt.NTT
                if o <= 4:
                    if o == 0:
                        emit_P0_S1(nxt, 0)
                        if n_ > 1:
                            emit_P0_S1(nxt, 1)
                        emit_P0_S2(nxt, 0)
                    else:
                        if o - 1 < n_:
                            emit_P0_S3(nxt, o - 1)
                        if o < n_:
                            emit_P0_S2(nxt, o)
                        if o + 1 < n_:
                            emit_P0_S1(nxt, o + 1)
                        if o - 1 < n_:
                            emit_P0_B(nxt, o - 1)
                    if o == 3:
                        emit_P1_pre(nxt)
                if o >= 4:
                    t1 = o - 4
                    if 0 <= t1 - 2 < n_:
                        emit_P1_S3(nxt, t1 - 2)
                    if 0 <= t1 - 1 < n_:
                        emit_P1_S2(nxt, t1 - 1)
                    if t1 < n_:
                        emit_P1_S1(nxt, t1)
        if nxt is not None and hoist:
            for t1 in (4, 5):
                if 0 <= t1 - 2 < nxt.NTT:
                    emit_P1_S3(nxt, t1 - 2)
                if 0 <= t1 - 1 < nxt.NTT:
                    emit_P1_S2(nxt, t1 - 1)
        S.tag = "P8"
        slots_o = [load_slot(20), load_slot(21)]
        xrb = [(xr, "xr"), (gv, "gv")]

        def p8_A(t_):
            xb, xn_ = xrb[t_ % 2]
            dma("act", xb[0:TR, :], xsrc[t0 + t_ * 128:t0 + t_ * 128 + TR, :], [], [xn_], "xr_ld%d" % (t_ % 2))
            for half in range(2):
                sl, sn = slots_o[half]
                bb, nn = newbank()
                for kc in range(KC):
                    mm(bb[0:TR, :], mT[:, kc, HIST + t_ * 128:HIST + t_ * 128 + TR], sl[:, kc, :], kc == 0, kc == KC - 1,
                       ["gcat%d" % kc, sn], [nn])
                tmp, tmpn = (nt1, "nt1") if half == 0 else (nt2, "nt2")
                tt("dve", tmp[0:TR, :], bb[0:TR, :], gate_bc[0:TR, half * 512:(half + 1) * 512],
                   ALU.mult, [nn, "gate_bc"], [tmpn])
                tt("dve", xb[0:TR, half * 512:(half + 1) * 512], xb[0:TR, half * 512:(half + 1) * 512], tmp[0:TR, :],
                   ALU.add, [xn_, tmpn], [xn_])
            c8 = 8 + (t_ % 2)
            v8 = 8 + 2 * (t_ % 2)
            sn8, vn8 = "st_s8_%d" % (t_ % 2), "st_v8_%d" % (t_ % 2)
            act(junk[0:TR, :], xb[0:TR, :], AF.Square, [xn_], ["junk", sn8], accum_out=st_s[0:TR, c8:c8 + 1])

        def p8_A2(t_):
            c8 = 8 + (t_ % 2)
            v8 = 8 + 2 * (t_ % 2)
            sn8, vn8 = "st_s8_%d" % (t_ % 2), "st_v8_%d" % (t_ % 2)
            ts("dve", st_v[0:TR, v8:v8 + 1], st_s[0:TR, c8:c8 + 1], 1.0 / D, EPS, ALU.mult, ALU.add, [sn8], [vn8])
            tt("pool", st_v[0:TR, v8 + 1:v8 + 2], st_v[0:TR, v8:v8 + 1], negh[0:TR, 0:1], ALU.pow, [vn8, "negh"], [vn8])

        def p8_B(t_):
            xb, xn_ = xrb[t_ % 2]
            v8 = 8 + 2 * (t_ % 2)
            vn8 = "st_v8_%d" % (t_ % 2)
            stt(ybuf[0:TR, :], xb[0:TR, :], st_v[0:TR, v8 + 1:v8 + 2], fg_bc[0:TR, :], ALU.mult, ALU.mult,
                [xn_, vn8, "fg_bc"], ["ybuf"])
            dma("act", ydst[t0 + t_ * 128:t0 + t_ * 128 + TR, :], ybuf[0:TR, :], ["ybuf"], [], "ybuf_st")

        for t_ in range(NTT):
            p8_A(t_)
            if t_ >= 1:
                p8_B(t_ - 1)
            p8_A2(t_)
        p8_B(NTT - 1)
        if nxt is not None and not hoist:
            first_pass[0] = False
            for t_ in range(nxt.NTT):
                emit_P0_tile(nxt, t_)
            emit_P1_pre(nxt)
            for t_ in range(nxt.NTT):
                emit_P1_tile(nxt, t_)

    def emit_gate(seq):
        S.tag = "gate"
        for half in range(2):
            bb, nn = newbank()
            for k4 in range(4):
                kc = half * 4 + k4
                mm(bb[:, k4 * 128:(k4 + 1) * 128], gateh[:, kc, seq:seq + 1].to_broadcast([128, 128]), ident_f[:],
                   True, True, ["gateh", "ident_f"], [nn])
            act(gate_bc[:, half * 512:(half + 1) * 512], bb[:, :], AF.Identity, [nn], ["gate_bc"])

    emit_gate(blocks[0].seq)
    emit_front(blocks[0])
    cur_seq = blocks[0].seq
    for i, k in enumerate(blocks):
        nxt = blocks[i + 1] if i + 1 < len(blocks) else None
        if k.seq != cur_seq:
            emit_gate(k.seq)
            cur_seq = k.seq
        emit_body(k, nxt, hoist=(i > 0))
        first_pass[0] = False

    S.emit_all()
    print("sched stats:", S.stats)
    import os
    if os.environ.get("KDBG_TAGS"):
        import pickle
        tg = {}
        for o in S.ops:
            tg.setdefault(o.eng, []).append((o.tag, o.is_dma, o.ticket, o.w, o.r, o.idx, sorted(o.deps)))
        pickle.dump(tg, open(os.environ["KDBG_TAGS"], "wb"))
    st.close()
    return nc


_CACHE = {}


def _get_program(npb):
    if npb not in _CACHE:
        _CACHE[npb] = build_program(npb)
    return _CACHE[npb]


def kernel(x_prompt, x_sample, state_conv, c_prompt, c_sample, w_ada, b_ada, norm_g,
           w_in, b_in, ln_v_g, ln_v_b, w_spatial, b_spatial, conv_w, conv_b,
           ln_c_g, ln_c_b, w_o_a, w_o_b, w_out, final_g):
    f = lambda a: np.ascontiguousarray(np.asarray(a, dtype=np.float32))
    x_prompt = f(x_prompt)
    seqlen = x_prompt.shape[1]
    npb = seqlen // TB
    nc = _get_program(npb)
    ncores = 8
    shared = {
        "w_ada": f(w_ada)[0], "b_ada": f(b_ada)[0], "norm_g": f(norm_g)[0], "w_in": f(w_in)[0],
        "b_in": f(b_in)[0], "ln_v_g": f(ln_v_g)[0], "ln_v_b": f(ln_v_b)[0], "w_sp": f(w_spatial)[0],
        "b_sp": f(b_spatial)[0].reshape(-1), "conv_w": f(conv_w)[0], "conv_b": f(conv_b)[0],
        "ln_c_g": f(ln_c_g)[0], "ln_c_b": f(ln_c_b)[0], "w_o_a": f(w_o_a)[0], "w_o_b": f(w_o_b)[0],
        "w_out": f(w_out)[0], "final_g": f(final_g),
    }
    x_sample = f(x_sample)
    state_conv = f(state_conv)
    c_prompt = f(c_prompt)
    c_sample = f(c_sample)
    in_maps = []
    for i in range(ncores):
        m = dict(shared)
        m["xp"] = np.ascontiguousarray(x_prompt[2 * i:2 * i + 2])
        m["xs"] = np.ascontiguousarray(x_sample[i])
        m["sc"] = np.ascontiguousarray(state_conv[0, i])
        m["cc"] = np.ascontiguousarray(np.concatenate([c_prompt[2 * i:2 * i + 2], c_sample[i:i + 1]], axis=0))
        in_maps.append(m)
    res = run_bass_kernel_spmd(nc, in_maps, core_ids=list(range(ncores)))
    r = res.results
    y_prompt = np.concatenate([r[i]["yp"] for i in range(ncores)], axis=0)
    y_sample = np.stack([r[i]["ys"] for i in range(ncores)], axis=0)
    ncp = np.concatenate([r[i]["ncp"] for i in range(ncores)], axis=0)[None]
    ncs = np.stack([r[i]["ncs"] for i in range(ncores)], axis=0)[None]
    nvs = np.stack([r[i]["nvs"] for i in range(ncores)], axis=0)[None]
    return (y_prompt.astype(np.float32), y_sample.astype(np.float32), ncp.astype(np.float32),
            ncs.astype(np.float32), nvs.astype(np.float32))
```

```python
import contextlib
import numpy as np
import concourse.bass as bass
import concourse.mybir as mybir
from concourse.bass_utils import run_bass_kernel_spmd

F32 = mybir.dt.float32
BF16 = mybir.dt.bfloat16
AF = mybir.ActivationFunctionType
ALU = mybir.AluOpType

D = 1024
KC = 8
SEQ = 2048
TB = 512
DEC = 16
HIST = 30
NTAP = 31
EPS = 1e-6
NSLOT = 22
SLOT_ELEMS = KC * 512

COMPUTE = ("pe", "act", "dve", "pool")


class _Tile:
    __slots__ = ("name", "last_w", "readers")

    def __init__(self, name):
        self.name = name
        self.last_w = None
        self.readers = []


class _Op:
    __slots__ = ("idx", "eng", "emit", "deps", "is_dma", "sem_key", "dma_val",
                 "needs_inc", "ticket", "n_dma", "tag", "w", "r", "nowaw")

    def __init__(self):
        self.deps = set()
        self.needs_inc = False
        self.ticket = None
        self.is_dma = False
        self.nowaw = False
        self.sem_key = None
        self.dma_val = None
        self.n_dma = 1


class Sched:
    def __init__(self, nc):
        self.nc = nc
        self.ops = []
        self.tiles = {}
        self.wait_all_keys = set()
        self.tag = ""

    def _tile(self, name):
        t = self.tiles.get(name)
        if t is None:
            t = _Tile(name)
            self.tiles[name] = t
        return t

    def _track(self, op, reads, writes, nowaw=False):
        for n in reads:
            t = self._tile(n)
            if t.last_w is not None:
                op.deps.add(t.last_w)
            t.readers.append(op.idx)
        for n in writes:
            t = self._tile(n)
            if t.last_w is not None:
                lw = self.ops[t.last_w]
                if not (nowaw and lw.eng == op.eng and getattr(lw, "nowaw", False) and not lw.is_dma):
                    op.deps.add(t.last_w)
            for r in t.readers:
                if r != op.idx:
                    op.deps.add(r)
            t.last_w = op.idx
            t.readers = []
        op.deps.discard(op.idx)

    def op(self, eng, emit, reads=(), writes=(), nowaw=False):
        o = _Op()
        o.idx = len(self.ops)
        o.eng = eng
        o.emit = emit
        o.tag = self.tag
        o.nowaw = nowaw
        o.w = tuple(writes); o.r = tuple(reads)
        self.ops.append(o)
        self._track(o, reads, writes, nowaw)
        return o

    def dma(self, queue, emit, reads=(), writes=(), sem_key=None, n_dma=1):
        o = _Op()
        o.idx = len(self.ops)
        o.eng = queue
        o.emit = emit
        o.is_dma = True
        o.sem_key = sem_key
        o.n_dma = n_dma
        o.tag = self.tag
        o.w = tuple(writes); o.r = tuple(reads)
        self.ops.append(o)
        self._track(o, reads, writes)
        return o

    def emit_all(self):
        nc = self.nc
        ops = self.ops
        for o in ops:
            latest = {}
            keep = set()
            for d in o.deps:
                p = ops[d]
                if p.is_dma:
                    keep.add(d)
                elif latest.get(p.eng, -1) < d:
                    latest[p.eng] = d
            keep.update(latest.values())
            o.deps = keep
        for o in ops:
            for d in o.deps:
                p = ops[d]
                if p.is_dma:
                    continue
                if p.eng == "pe" and o.eng == "pe" and not o.is_dma:
                    continue
                p.needs_inc = True
        cnt = {e: 0 for e in COMPUTE}
        for o in ops:
            if o.is_dma:
                continue
            if o.needs_inc:
                cnt[o.eng] += 1
                o.ticket = cnt[o.eng]
        dkeys = {}
        for o in ops:
            if o.is_dma:
                k = o.sem_key
                dkeys[k] = dkeys.get(k, 0) + 16 * o.n_dma
                o.dma_val = dkeys[k]
        for o in ops:
            if o.is_dma and o.sem_key in self.wait_all_keys:
                o.dma_val = dkeys[o.sem_key]
        self.stats = dict(cnt)
        self.stats["n_ops"] = len(ops)
        self.stats["dma_keys"] = len(dkeys)
        with contextlib.ExitStack() as st:
            esem = {e: st.enter_context(nc.semaphore("s_" + e)) for e in COMPUTE}
            dsem = {k: st.enter_context(nc.semaphore("d_%d" % i)) for i, k in enumerate(dkeys)}
            block = st.enter_context(nc.Block())
            streams = {}
            for o in ops:
                streams.setdefault(o.eng, []).append(o)

            def run(engname, e):
                waited = {}
                for o in streams.get(engname, []):
                    need = {}
                    for d in o.deps:
                        p = ops[d]
                        if p.is_dma:
                            if o.is_dma and o.sem_key == p.sem_key and p.sem_key in self.wait_all_keys:
                                continue
                            s = dsem[p.sem_key]
                            v = p.dma_val
                        else:
                            if p.eng == "pe" and engname == "pe" and not o.is_dma:
                                continue
                            s = esem[p.eng]
                            v = p.ticket
                        key = id(s)
                        if need.get(key, (None, 0))[1] < v:
                            need[key] = (s, v)
                    for key, (s, v) in need.items():
                        if waited.get(key, 0) >= v:
                            continue
                        e.wait_ge(s, v)
                        waited[key] = v
                    r = o.emit(e)
                    if o.is_dma:
                        s = dsem[o.sem_key]
                        assert len(r) == o.n_dma, (len(r), o.n_dma)
                        for ins in r:
                            ins.then_inc(s, 16)
                    elif o.needs_inc:
                        r.then_inc(esem[engname], 1)
                last = {}
                for o in streams.get(engname, []):
                    if o.is_dma:
                        last[o.sem_key] = max(last.get(o.sem_key, 0), o.dma_val)
                for k, v in last.items():
                    e.wait_ge(dsem[k], v)

            if "pe" in streams:
                @block.tensor
                def _(e):
                    run("pe", e)
            if "act" in streams:
                @block.scalar
                def _(e):
                    run("act", e)
            if "dve" in streams:
                @block.vector
                def _(e):
                    run("dve", e)
            if "pool" in streams:
                @block.gpsimd
                def _(e):
                    run("pool", e)
            if "sp" in streams:
                @block.sync
                def _(e):
                    run("sp", e)


def build_program(npb=4, nring=5, diag_act_every=2):
    seqlen = npb * TB
    nc = bass.Bass("TRN2", target_bir_lowering=False)

    def din(name, shape):
        return nc.dram_tensor(name, shape, F32, kind="ExternalInput").ap()

    def dout(name, shape):
        return nc.dram_tensor(name, shape, F32, kind="ExternalOutput").ap()

    xp = din("xp", [2, seqlen, D])
    xs = din("xs", [DEC, D])
    scv = din("sc", [HIST, D])
    cc = din("cc", [3, D])
    w_ada = din("w_ada", [D, 3 * D])
    b_ada = din("b_ada", [3 * D])
    norm_g = din("norm_g", [D])
    w_in = din("w_in", [D, 8 * D])
    b_in = din("b_in", [8 * D])
    ln_v_g = din("ln_v_g", [D])
    ln_v_b = din("ln_v_b", [D])
    w_sp = din("w_sp", [8, 128, 128])
    b_sp = din("b_sp", [8 * 128])
    conv_w = din("conv_w", [NTAP, D])
    conv_b = din("conv_b", [D])
    ln_c_g = din("ln_c_g", [D])
    ln_c_b = din("ln_c_b", [D])
    w_o_a = din("w_o_a", [D, D])
    w_o_b = din("w_o_b", [D, D])
    w_out = din("w_out", [D, D])
    final_g = din("final_g", [D])
    yp = dout("yp", [2, seqlen, D])
    ys = dout("ys", [DEC, D])
    ncp = dout("ncp", [2, HIST, D])
    ncs = dout("ncs", [HIST, D])
    nvs = dout("nvs", [DEC, D])
    wscr = nc.dram_tensor("wscr", [NSLOT, 128, SLOT_ELEMS], BF16).ap()

    S = Sched(nc)
    S.wait_all_keys.add("setup")
    st = contextlib.ExitStack()

    def sb(name, shape, dt):
        return st.enter_context(nc.sbuf_tensor(name, shape, dt))

    ring = sb("ring", [128, nring, KC, 512], BF16)
    xin = sb("xin", [128, 4, D], F32)
    junk = sb("junk", [128, D], BF16)
    xn = sb("xn", [128, 2, D], BF16)
    hT = sb("hT", [128, 2, KC, TB], BF16)
    gv = sb("gv", [128, D], F32)
    vhat = sb("vhat", [128, 4, D], BF16)
    gu = sb("gu", [128, KC, TB], BF16)
    gcat = sb("gcat", [128, KC, HIST + TB], BF16)
    ah = sb("ah", [128, 2, TB], F32)
    th = sb("th", [128, 2, TB], F32)
    gtl = sb("gtl", [128, KC, 32], F32)
    diag = sb("diag", [128, 2, NTAP, 128], BF16)
    dwb = sb("dwb", [128, KC, TB], BF16)
    dwsq = sb("dwsq", [128, 2, TB], BF16)
    mu_bc = sb("mu_bc", [128, TB], F32)
    rs_bc = sb("rs_bc", [128, TB], F32)
    nt1 = sb("nt1", [128, TB], F32)
    nt2 = sb("nt2", [128, TB], F32)
    sga = sb("sga", [128, 2, TB], BF16)
    pbuf = sb("pbuf", [128, 2, TB], BF16)
    sgb = sb("sgb", [128, KC, TB], BF16)
    nb = sb("nb", [128, 2, TB], BF16)
    tha = sb("tha", [128, 4, TB], BF16)
    thb = sb("thb", [128, 4, TB], BF16)
    xr = sb("xr", [128, D], F32)
    ybuf = sb("ybuf", [128, D], F32)
    ident_f = sb("ident_f", [128, 128], F32)
    ident_b = sb("ident_b", [128, 128], BF16)
    ones_f = sb("ones_f", [128, 128], F32)
    ones_b = sb("ones_b", [128, 128], BF16)
    negh = sb("negh", [128, 8], F32)
    R0 = sb("R0", [128, 128], F32)
    R1 = sb("R1", [128, 128], F32)
    RT0 = sb("RT0", [128, 128], F32)
    RT1 = sb("RT1", [128, 32], F32)
    bTh = sb("bTh", [128, 64], F32)
    cwT = sb("cwT", [128, NTAP * 8], F32)
    scT = sb("scT", [128, HIST * 8], F32)
    wmT = sb("wmT", [128, KC, 128], BF16)
    wmf = sb("wmf", [128, 128], F32)
    Bh = sb("Bh", [128, KC, 128], F32)
    siluT = sb("siluT", [128, 24], F32)
    modT = sb("modT", [128, 24, 3], F32)
    aT = sb("aT", [128, KC, 3], F32)
    gateh = sb("gateh", [128, KC, 3], F32)
    glh = sb("glh", [128, 128], F32)
    gate_bc = sb("gate_bc", [128, D], F32)
    fg_bc = sb("fg_bc", [128, D], F32)
    bv2 = sb("bv2", [2, D], BF16)
    lo_row = sb("lo_row", [1, D], BF16)
    ones2 = sb("ones2", [2, 128], BF16)
    st_s = sb("st_s", [128, 16], F32)
    st_v = sb("st_v", [128, 32], F32)
    print("sbuf bytes remaining:", nc.sbuf_bytes_remaining)

    banks = [st.enter_context(nc.psum_tensor("bank%d" % i, [128, 512], F32)) for i in range(8)]
    bank_ctr = [0]
    held = set()

    def newbank(hold=False):
        while True:
            i = bank_ctr[0] % 8
            bank_ctr[0] += 1
            if i not in held:
                break
        if hold:
            held.add(i)
        return banks[i], "bank%d" % i

    def release(name):
        held.discard(int(name[4:]))

    def mm(out, lhsT, rhs, start, stop, reads, writes):
        S.op("pe", lambda e: e.matmul(out, lhsT=lhsT, rhs=rhs, start=start, stop=stop),
             reads=reads, writes=writes)

    def tr(out, in_, ident, reads, writes):
        S.op("pe", lambda e: e.transpose(out, in_, ident), reads=reads, writes=writes)

    def act(out, in_, func, reads, writes, bias=None, scale=None, accum_out=None, nowaw=False):
        kw = {}
        if bias is not None:
            kw["bias"] = bias
        if scale is not None:
            kw["scale"] = scale
        if accum_out is not None:
            kw["accum_out"] = accum_out
        S.op("act", lambda e: e.activation(out=out, in_=in_, func=func, **kw), reads=reads, writes=writes, nowaw=nowaw)

    def ts(eng, out, in0, s1, s2, op0, op1, reads, writes, nowaw=False):
        if op1 is None:
            S.op(eng, lambda e: e.tensor_scalar(out=out, in0=in0, scalar1=s1, scalar2=None, op0=op0),
                 reads=reads, writes=writes, nowaw=nowaw)
        else:
            S.op(eng, lambda e: e.tensor_scalar(out=out, in0=in0, scalar1=s1, scalar2=s2, op0=op0, op1=op1),
                 reads=reads, writes=writes)

    def tt(eng, out, in0, in1, op, reads, writes):
        S.op(eng, lambda e: e.tensor_tensor(out=out, in0=in0, in1=in1, op=op), reads=reads, writes=writes)

    def stt(out, in0, scalar, in1, op0, op1, reads, writes):
        S.op("dve", lambda e: e.scalar_tensor_tensor(out=out, in0=in0, scalar=scalar, in1=in1, op0=op0, op1=op1),
             reads=reads, writes=writes)

    def cp(eng, out, in_, reads, writes):
        S.op(eng, lambda e: e.tensor_copy(out, in_), reads=reads, writes=writes)

    def dma(q, out, in_, reads, writes, key):
        S.dma(q, lambda e: [e.dma_start(out=out, in_=in_)], reads=reads, writes=writes, sem_key=key)

    def rows128(v):
        return v.rearrange("(r k) -> r k", k=128)

    S.op("pool", lambda e: e.memset(ident_f[:], 1.0), writes=["ident_f"])
    S.op("pool", lambda e: e.affine_select(out=ident_f[:], in_=ident_f[:], pattern=[[-1, 128]],
                                           compare_op=ALU.is_equal, fill=0.0, base=0, channel_multiplier=1),
         reads=["ident_f"], writes=["ident_f"])
    S.op("pool", lambda e: e.memset(ones_f[:], 1.0), writes=["ones_f"])
    S.op("pool", lambda e: e.memset(negh[:], -0.5), writes=["negh"])
    cp("dve", ident_b[:], ident_f[:], ["ident_f"], ["ident_b"])
    cp("dve", ones_b[:], ones_f[:], ["ones_f"], ["ones_b"])
    cp("dve", ones2[:], ones_f[0:2, :], ["ones_f"], ["ones2"])

    dma("sp", R0[0:64, :], rows128(b_in), [], ["R0"], "setup")
    dma("sp", R0[64:88, :], rows128(b_ada), [], ["R0"], "setup")
    dma("sp", R0[88:96, :], rows128(norm_g), [], ["R0"], "setup")
    dma("sp", R0[96:104, :], rows128(ln_v_g), [], ["R0"], "setup")
    dma("sp", R0[104:112, :], rows128(ln_v_b), [], ["R0"], "setup")
    dma("sp", R0[112:120, :], rows128(conv_b), [], ["R0"], "setup")
    dma("sp", R0[120:128, :], rows128(ln_c_g), [], ["R0"], "setup")
    dma("sp", R1[0:8, :], rows128(ln_c_b), [], ["R1"], "setup")
    dma("sp", R1[8:32, :], cc.rearrange("s (c k) -> (s c) k", k=128), [], ["R1"], "setup")
    dma("sp", fg_bc[:], final_g.rearrange("(o n) -> o n", o=1).partition_broadcast(128), [], ["fg_bc"], "setup")
    vh_f = vhat[:].rearrange("p a b -> p (a b)").bitcast(F32)
    gu_f = gu[:].rearrange("p a b -> p (a b)").bitcast(F32)
    lnvb_row = vh_f[0:1, 0:D]
    bs_row = vh_f[0:1, D:2 * D]
    rs_row = gu_f[0:1, 0:D]
    bv_row = gu_f[0:1, D:2 * D]
    dma("sp", lnvb_row, ln_v_b.rearrange("(o n) -> o n", o=1), [], ["vhat"], "setup")
    dma("sp", bs_row, b_sp.rearrange("(o n) -> o n", o=1), [], ["vhat"], "setup")
    dma("sp", bv_row, b_in[D:2 * D].rearrange("(o n) -> o n", o=1), [], ["gu"], "setup")

    def wsrc(slot):
        def cols(w, c0, n):
            return w[:, c0:c0 + n].rearrange("(k p) n -> p k n", p=128)
        if slot in (0, 1):
            return [(0, 512, cols(w_in, D + slot * 512, 512))]
        if slot in (2, 3):
            return [(0, 512, cols(w_in, (slot - 2) * 512, 512))]
        if 4 <= slot <= 7:
            i = slot - 4
            return [(0, 256, cols(w_in, 3 * D + i * 256, 256)), (256, 256, cols(w_in, 4 * D + i * 256, 256))]
        if slot in (8, 9):
            return [(0, 512, cols(w_in, 2 * D + (slot - 8) * 512, 512))]
        if slot in (10, 11):
            return [(0, 512, cols(w_in, 5 * D + (slot - 10) * 512, 512))]
        if 12 <= slot <= 15:
            i = slot - 12
            return [(0, 256, cols(w_in, 6 * D + i * 256, 256)), (256, 256, cols(w_in, 7 * D + i * 256, 256))]
        if 16 <= slot <= 19:
            i = slot - 16
            return [(0, 256, cols(w_o_a, i * 256, 256)), (256, 256, cols(w_o_b, i * 256, 256))]
        return [(0, 512, cols(w_out, (slot - 20) * 512, 512))]

    def emit_conversion():
        S.tag = "conv"
        stg = []
        for h in range(2):
            v = ring[:, nring - 2 + h].rearrange("p k n -> p (k n)").bitcast(F32)
            stg.append((v, "ring%d" % (nring - 2 + h)))
        cobs = [(ring[:, nring - 3], "ring%d" % (nring - 3)), (hT[:, 1], "hT1")]
        for slot in range(NSLOT):
            srcs = wsrc(slot)
            cob, cn = cobs[slot % 2]
            for h in range(2):
                sv, sn = stg[h]
                if len(srcs) == 2:
                    (c0, n, src) = srcs[h]
                    view = sv.rearrange("p (k n) -> p k n", k=KC)
                    dstv = cob[:, :, c0:c0 + n]
                    dma("sp", view, src, [], [sn], "stg%d" % h)
                else:
                    (c0, n, src) = srcs[0]
                    view = sv.rearrange("p (k n) -> p k n", k=KC // 2)
                    dstv = cob[:, h * 4:(h + 1) * 4, :]
                    dma("sp", view, src[:, h * 4:(h + 1) * 4, :], [], [sn], "stg%d" % h)
                if h == 0:
                    act(dstv, view, AF.Identity, [sn], [cn])
                else:
                    cp("dve", dstv, view, [sn], [cn])
            dma("act", wscr[slot], cob.rearrange("p k n -> p (k n)"), [cn], ["wscr%d" % slot], "wscr_w%d" % slot)

    b0, n0 = newbank()
    tr(b0[:, 0:128], R0[:], ident_f[:], ["R0", "ident_f"], [n0])
    act(RT0[:], b0[:, 0:128], AF.Identity, [n0], ["RT0"])
    b1, n1 = newbank()
    tr(b1[:, 0:32], R1[0:32, :], ident_f[0:32, 0:32], ["R1", "ident_f"], [n1])
    act(RT1[:], b1[:, 0:32], AF.Identity, [n1], ["RT1"])
    ts("dve", bTh[:], RT0[:, 0:64], 0.5, None, ALU.mult, None, ["RT0"], ["bTh"])
    bT = RT0
    b_adaT = lambda jc: RT0[:, 64 + jc:65 + jc]
    norm_gT = lambda c: RT0[:, 88 + c:89 + c]
    lnvgT = lambda c: RT0[:, 96 + c:97 + c]
    convbT = lambda c: RT0[:, 112 + c:113 + c]
    lncgT = lambda c: RT0[:, 120 + c:121 + c]
    lncbT = lambda c: RT1[:, c:c + 1]
    act(siluT[:], RT1[:, 8:32], AF.Silu, ["RT1"], ["siluT"])

    cw_rows = conv_w.rearrange("j (c k) -> (j c) k", k=128)
    dma("act", R0[:], cw_rows[0:128, :], ["RT0"], ["R0"], "cw0")
    dma("act", R1[0:120, :], cw_rows[128:248, :], ["RT1"], ["R1"], "cw1")
    b2, n2 = newbank()
    tr(b2[:, 0:128], R0[:], ident_f[:], ["R0", "ident_f"], [n2])
    tr(b2[:, 128:248], R1[0:120, :], ident_f[0:120, 0:120], ["R1", "ident_f"], [n2])
    act(cwT[:], b2[:, 0:248], AF.Identity, [n2], ["cwT"])
    sc_rows = scv.rearrange("r (c k) -> (r c) k", k=128)
    dma("act", R0[:], sc_rows[0:128, :], [], ["R0"], "sc0")
    dma("act", R1[0:112, :], sc_rows[128:240, :], [], ["R1"], "sc1")
    b3, n3 = newbank()
    tr(b3[:, 0:128], R0[:], ident_f[:], ["R0", "ident_f"], [n3])
    tr(b3[:, 128:240], R1[0:112, :], ident_f[0:112, 0:112], ["R1", "ident_f"], [n3])
    act(scT[:], b3[:, 0:240], AF.Identity, [n3], ["scT"])

    def wsp_head(h):
        Rb, Rn = (R0, "R0") if h % 2 == 0 else (R1, "R1")
        dma("sp", Rb[:], w_sp[h], [], [Rn], "wsp%d" % (h % 2))
        bb, nn = newbank()
        tr(bb[:, 0:128], Rb[:], ident_f[:], [Rn, "ident_f"], [nn])
        act(wmf[:], bb[:, 0:128], AF.Identity, [nn], ["wmf"])
        S.op("pool", lambda e: e.affine_select(out=wmf[:], in_=wmf[:], pattern=[[1, 128]],
                                               compare_op=ALU.is_ge, fill=0.0, base=0, channel_multiplier=-1),
             reads=["wmf"], writes=["wmf"])
        cp("dve", wmT[:, h, :], wmf[:], ["wmf"], ["wmT"])
        bb2, nn2 = newbank()
        mm(bb2[0:1, 0:128], ones_f[:, 0:1], wmf[:], True, True, ["ones_f", "wmf"], [nn2])
        act(rs_row[:, h * 128:(h + 1) * 128], bb2[0:1, 0:128], AF.Identity, [nn2], ["gu"])
        bb3, nn3 = newbank()
        mm(bb3[:, 0:128], lnvb_row[:, h * 128:(h + 1) * 128], rs_row[:, h * 128:(h + 1) * 128], True, False,
           ["vhat", "gu"], [nn3])
        mm(bb3[:, 0:128], ones_f[0:1, :], bs_row[:, h * 128:(h + 1) * 128], False, True,
           ["ones_f", "vhat"], [nn3])
        act(Bh[:, h, :], bb3[:, 0:128], AF.Identity, [nn3], ["Bh"])

    sgb_names = ["sgb%d" % c for c in range(KC)]
    dwb_names = ["dwb%d" % c for c in range(KC)]
    ada_bufs = [
        (sgb[:].rearrange("p a b -> p (a b)").bitcast(F32).rearrange("p (k n) -> p k n", k=KC), sgb_names, "ada0"),
        (dwb[:].rearrange("p a b -> p (a b)").bitcast(F32).rearrange("p (k n) -> p k n", k=KC), dwb_names, "ada1"),
    ]
    for pc in range(12):
        ab, an, ak = ada_bufs[pc % 2]
        dma("sp", ab, w_ada[:, pc * 256:(pc + 1) * 256].rearrange("(k p) n -> p k n", p=128), [], an, ak)
        for jl in range(2):
            jc = pc * 2 + jl
            bb, nn = newbank()
            for kc in range(KC):
                mm(bb[:, 0:3], ab[:, kc, jl * 128:(jl + 1) * 128], siluT[:, kc:24:8], kc == 0, kc == KC - 1,
                   an + ["siluT"], [nn])
            act(modT[:, jc, :], bb[:, 0:3], AF.Identity, [nn], ["modT"], bias=b_adaT(jc))
        if pc < 8:
            wsp_head(pc)
    cp("dve", bv2[0:1, :], bv_row, ["gu"], ["bv2"])
    tt("dve", nt1[0:1, 0:512], bv_row[:, 0:512], bv2[0:1, 0:512], ALU.subtract, ["gu", "bv2"], ["nt1"])
    tt("dve", nt2[0:1, 0:512], bv_row[:, 512:1024], bv2[0:1, 512:1024], ALU.subtract, ["gu", "bv2"], ["nt2"])
    cp("dve", lo_row[:, 0:512], nt1[0:1, 0:512], ["nt1"], ["lo_row"])
    cp("dve", lo_row[:, 512:1024], nt2[0:1, 0:512], ["nt2"], ["lo_row"])
    dma("act", bv2[1:2, :], lo_row[:], ["lo_row"], ["bv2"], "bv2lo")

    for kc in range(KC):
        ts("dve", aT[:, kc, :], modT[:, 8 + kc, :], 1.0, norm_gT(kc), ALU.add, ALU.mult, ["modT", "RT0"], ["aT"])
        ts("dve", gateh[:, kc, :], modT[:, 16 + kc, :], 0.5, None, ALU.mult, None, ["modT"], ["gateh"])
    shT = lambda kc, s: modT[:, kc, s:s + 1]

    slot_ctr = [0]
    first_pass = [True]
    stg = []
    for h_ in range(2):
        v_ = ring[:, nring - 2 + h_].rearrange("p k n -> p (k n)").bitcast(F32)
        stg.append((v_, "ring%d" % (nring - 2 + h_), "stg%d" % h_))
    stg.append((hT[:, 1].rearrange("p k n -> p (k n)").bitcast(F32), "hT1", "stg2"))
    stg_ctr = [0]

    def load_slot(slot):
        nr = (nring - 2) if first_pass[0] else nring
        r = slot_ctr[0] % nr
        slot_ctr[0] += 1
        rn = "ring%d" % r
        dst = ring[:, r]
        if first_pass[0]:
            srcs = wsrc(slot)
            for h in range(2):
                sv, sn, sk = stg[stg_ctr[0] % len(stg)]
                stg_ctr[0] += 1
                if len(srcs) == 2:
                    (c0, n, src) = srcs[h]
                    view = sv.rearrange("p (k n) -> p k n", k=KC)
                    dstv = dst[:, :, c0:c0 + n]
                    dma("sp", view, src, [], [sn], sk)
                else:
                    (c0, n, src) = srcs[0]
                    view = sv.rearrange("p (k n) -> p k n", k=KC // 2)
                    dstv = dst[:, h * 4:(h + 1) * 4, :]
                    dma("sp", view, src[:, h * 4:(h + 1) * 4, :], [], [sn], sk)
                if h == 0:
                    act(dstv, view, AF.Identity, [sn], [rn])
                else:
                    cp("dve", dstv, view, [sn], [rn])
            dma("act", wscr[slot], dst.rearrange("p k n -> p (k n)"), [rn], ["wscr%d" % slot], "wscr_w%d" % slot)
        else:
            dma("sp", dst.rearrange("p k n -> p (k n)"), wscr[slot], ["wscr%d" % slot], [rn], "ringld%d" % r)
        return dst, rn

    class Blk:
        pass

    blocks = []
    for sq in range(2):
        for b in range(npb):
            k = Blk()
            k.seq, k.xsrc, k.ydst, k.t0, k.T = sq, xp[sq], yp[sq], b * TB, TB
            k.first, k.last, k.is_sample = (b == 0), (b == npb - 1), False
            blocks.append(k)
    k = Blk()
    k.seq, k.xsrc, k.ydst, k.t0, k.T = 2, xs, ys, 0, DEC
    k.first, k.last, k.is_sample = True, True, True
    blocks.append(k)
    for i, k in enumerate(blocks):
        k.idx = i
        k.TR = min(k.T, 128)
        k.NTT = max(1, k.T // 128)
        k.h = i % 2
        k.hn = "hT%d" % k.h

    def emit_xload(k):
        for t_ in range(k.NTT):
            dma("sp", xin[0:k.TR, t_, :], k.xsrc[k.t0 + t_ * 128:k.t0 + t_ * 128 + k.TR, :], [], ["xin%d" % t_],
                "xin%d" % t_)

    def emit_P0_S1(k, t_):
        S.tag = "P0"
        TR = k.TR
        c0 = 10 + (t_ % 2)
        act(junk[0:TR, :], xin[0:TR, t_, :], AF.Square, ["xin%d" % t_], ["junk", "st_s0_%d" % (t_ % 2)],
            accum_out=st_s[0:TR, c0:c0 + 1])

    def emit_P0_S2(k, t_):
        S.tag = "P0"
        TR = k.TR
        c0 = 10 + (t_ % 2)
        v0 = 12 + 2 * (t_ % 2)
        sn0, vn0 = "st_s0_%d" % (t_ % 2), "st_v0_%d" % (t_ % 2)
        ts("dve", st_v[0:TR, v0:v0 + 1], st_s[0:TR, c0:c0 + 1], 1.0 / D, EPS, ALU.mult, ALU.add, [sn0], [vn0])
        tt("pool", st_v[0:TR, v0 + 1:v0 + 2], st_v[0:TR, v0:v0 + 1], negh[0:TR, 0:1], ALU.pow, [vn0, "negh"], [vn0])

    def emit_P0_S3(k, t_):
        S.tag = "P0"
        TR = k.TR
        v0 = 12 + 2 * (t_ % 2)
        vn0 = "st_v0_%d" % (t_ % 2)
        ts("dve", xn[0:TR, t_ % 2, :], xin[0:TR, t_, :], st_v[0:TR, v0 + 1:v0 + 2], None, ALU.mult, None,
           ["xin%d" % t_, vn0], ["xn%d" % (t_ % 2)])

    def emit_P0_A(k, t_):
        emit_P0_S1(k, t_)
        emit_P0_S2(k, t_)
        emit_P0_S3(k, t_)

    def emit_P0_B(k, t_):
        S.tag = "P0"
        TR, s = k.TR, k.seq
        hTb = hT[:, k.h]
        bb, nn = newbank()
        bbf = bb[:].bitcast(BF16)
        for kc in range(KC):
            tr(bbf[:, kc * 128:kc * 128 + TR], xn[0:TR, t_ % 2, kc * 128:(kc + 1) * 128], ident_b[0:TR, 0:TR],
               ["xn%d" % (t_ % 2), "ident_b"], [nn])
        for kc in range(KC):
            act(hTb[:, kc, t_ * 128:t_ * 128 + TR], bbf[:, kc * 128:kc * 128 + TR], AF.Identity, [nn, "aT", "modT"],
                [k.hn], bias=shT(kc, s), scale=aT[:, kc, s:s + 1], nowaw=True)

    def emit_P0_tile(k, t_):
        emit_P0_A(k, t_)
        emit_P0_B(k, t_)

    def emit_P1_pre(k):
        S.tag = "P1"
        k.slots_v = [load_slot(0), load_slot(1)]
        if k.is_sample:
            dma("sp", xr[0:DEC, :], ln_v_g.rearrange("(o n) -> o n", o=1).partition_broadcast(DEC), [], ["xr"], "xr")
            dma("sp", ybuf[0:DEC, :], ln_v_b.rearrange("(o n) -> o n", o=1).partition_broadcast(DEC), [], ["ybuf"],
                "ybuf_ld")

    def p1_buf(k, t_):
        if k.is_sample or t_ % 2 == 0:
            return gv, "gv"
        return xr, "xr"

    def emit_P1_S1(k, t_):
        S.tag = "P1"
        TR = k.TR
        hTb = hT[:, k.h]
        gb_, gn_ = p1_buf(k, t_)
        par = t_ % 2
        sb_ = 2 if par == 0 else 12
        sn1 = "st_s1_%d" % par
        for half in range(2):
            sl, sn = k.slots_v[half]
            bb, nn = newbank()
            for kc in range(KC):
                mm(bb[0:TR, :], hTb[:, kc, t_ * 128:t_ * 128 + TR], sl[:, kc, :], kc == 0, False, [k.hn, sn], [nn])
            mm(bb[0:TR, :], ones2[0:2, 0:TR], bv2[0:2, half * 512:(half + 1) * 512], False, True,
               ["ones2", "bv2"], [nn])
            act(gb_[0:TR, half * 512:(half + 1) * 512], bb[0:TR, :], AF.Gelu_apprx_tanh, [nn], [gn_, sn1],
                accum_out=st_s[0:TR, sb_ + half:sb_ + half + 1])
            act(junk[0:TR, 0:512], gb_[0:TR, half * 512:(half + 1) * 512], AF.Square, [gn_], ["junk", sn1],
                accum_out=st_s[0:TR, sb_ + 2 + half:sb_ + 3 + half])

    def emit_P1_S2(k, t_):
        S.tag = "P1"
        TR = k.TR
        par = t_ % 2
        sb_ = 2 if par == 0 else 12
        vb = 2 if par == 0 else 16
        sn1 = "st_s1_%d" % par
        sv = ["st_v1_%d" % par]
        tt("dve", st_v[0:TR, vb:vb + 1], st_s[0:TR, sb_:sb_ + 1], st_s[0:TR, sb_ + 1:sb_ + 2], ALU.add, [sn1], sv)
        tt("dve", st_v[0:TR, vb + 1:vb + 2], st_s[0:TR, sb_ + 2:sb_ + 3], st_s[0:TR, sb_ + 3:sb_ + 4], ALU.add, [sn1], sv)
        ts("dve", st_v[0:TR, vb:vb + 2], st_v[0:TR, vb:vb + 2], 1.0 / D, None, ALU.mult, None, sv, sv)
        tt("dve", st_v[0:TR, vb + 2:vb + 3], st_v[0:TR, vb:vb + 1], st_v[0:TR, vb:vb + 1], ALU.mult, sv, sv)
        tt("dve", st_v[0:TR, vb + 3:vb + 4], st_v[0:TR, vb + 1:vb + 2], st_v[0:TR, vb + 2:vb + 3], ALU.subtract, sv, sv)
        ts("dve", st_v[0:TR, vb + 3:vb + 4], st_v[0:TR, vb + 3:vb + 4], EPS, None, ALU.add, None, sv, sv)
        tt("pool", st_v[0:TR, vb + 4:vb + 5], st_v[0:TR, vb + 3:vb + 4], negh[0:TR, 0:1], ALU.pow, sv + ["negh"], sv)

    def emit_P1_S3(k, t_):
        S.tag = "P1"
        TR = k.TR
        gb_, gn_ = p1_buf(k, t_)
        par = t_ % 2
        vb = 2 if par == 0 else 16
        sv = ["st_v1_%d" % par]
        ts("dve", st_v[0:TR, vb + 5:vb + 6], st_v[0:TR, vb:vb + 1], st_v[0:TR, vb + 4:vb + 5], -1.0, ALU.mult, ALU.mult, sv, sv)
        ts("dve", vhat[0:TR, t_, :], gb_[0:TR, :], st_v[0:TR, vb + 4:vb + 5], st_v[0:TR, vb + 5:vb + 6], ALU.mult, ALU.add,
           [gn_] + sv, ["vhat"])
        if k.is_sample:
            ts("dve", gv[0:TR, :], gv[0:TR, :], st_v[0:TR, vb + 4:vb + 5], st_v[0:TR, vb + 5:vb + 6], ALU.mult, ALU.add,
               ["gv"] + sv, ["gv"])
            tt("dve", gv[0:TR, :], gv[0:TR, :], xr[0:TR, :], ALU.mult, ["gv", "xr"], ["gv"])
            tt("dve", gv[0:TR, :], gv[0:TR, :], ybuf[0:TR, :], ALU.add, ["gv", "ybuf"], ["gv"])
            dma("act", nvs, gv[0:DEC, :], ["gv"], [], "nvs")

    def emit_P1_tile(k, t_):
        emit_P1_S1(k, t_)
        emit_P1_S2(k, t_)
        emit_P1_S3(k, t_)

    def emit_front_staged(k):
        n_ = k.NTT
        emit_P0_S1(k, 0)
        if n_ > 1:
            emit_P0_S1(k, 1)
        emit_P0_S2(k, 0)
        for o in range(1, n_ + 1):
            emit_P0_S3(k, o - 1)
            if o < n_:
                emit_P0_S2(k, o)
            if o + 1 < n_:
                emit_P0_S1(k, o + 1)
            emit_P0_B(k, o - 1)
        emit_P1_pre(k)
        for t1 in range(n_ + 2):
            if 0 <= t1 - 2 < n_:
                emit_P1_S3(k, t1 - 2)
            if 0 <= t1 - 1 < n_:
                emit_P1_S2(k, t1 - 1)
            if t1 < n_:
                emit_P1_S1(k, t1)

    def emit_front(k):
        emit_xload(k)
        emit_front_staged(k)

    def emit_body(k, nxt, hoist):
        seq, xsrc, ydst, t0, T, first, last, is_sample = k.seq, k.xsrc, k.ydst, k.t0, k.T, k.first, k.last, k.is_sample
        TR, NTT, s = k.TR, k.NTT, k.seq
        hTb = hT[:, k.h]
        hn = k.hn
        S.tag = "P2"
        for sidx in (2, 3):
            sl, sn = load_slot(sidx)
            for cl in range(4):
                c = (sidx - 2) * 4 + cl
                bb, nn = newbank()
                for kc in range(KC):
                    mm(bb[:, 0:T], sl[:, kc, cl * 128:(cl + 1) * 128], hTb[:, kc, 0:T], kc == 0, kc == KC - 1,
                       [sn, hn], [nn])
                act(gu[:, c, 0:T], bb[:, 0:T], AF.Gelu_apprx_tanh, [nn], ["gu%d" % c], bias=bT[:, c:c + 1])

        def build_diag(c):
            for j in range(NTAP):
                if diag_act_every and (j % diag_act_every == diag_act_every - 1):
                    act(diag[:, c % 2, j, :], ident_b[:], AF.Identity, ["ident_b", "cwT"], ["diagA%d" % (c % 2)],
                        scale=cwT[:, j * 8 + c:j * 8 + c + 1], nowaw=True)
                else:
                    ts("dve", diag[:, c % 2, j, :], ident_b[:], cwT[:, j * 8 + c:j * 8 + c + 1], None, ALU.mult, None,
                       ["ident_b", "cwT"], ["diagD%d" % (c % 2)], nowaw=True)
        S.tag = "P3"
        build_diag(0)
        gnames = ["gcat%d" % c for c in range(KC)]
        if first and not is_sample:
            S.op("dve", lambda e: e.memset(gcat[:, :, 0:HIST], 0.0), writes=gnames)
        elif is_sample:
            cp("dve", gcat[:, :, 0:HIST], scT[:].rearrange("p (r c) -> p c r", c=8), ["scT"], gnames)
        NTL = min(HIST, T)
        for i in range(4):
            sl, sn = load_slot(4 + i)
            for cl in range(2):
                c = i * 2 + cl
                ba, na = newbank()
                bbk, nbk = newbank()
                for kc in range(KC):
                    mm(ba[:, 0:T], sl[:, kc, cl * 128:(cl + 1) * 128], hTb[:, kc, 0:T], kc == 0, kc == KC - 1,
                       [sn, hn], [na])
                for kc in range(KC):
                    mm(bbk[:, 0:T], sl[:, kc, 256 + cl * 128:256 + (cl + 1) * 128], hTb[:, kc, 0:T], kc == 0,
                       kc == KC - 1, [sn, hn], [nbk])
                act(ah[:, c % 2, 0:T], ba[:, 0:T], AF.Identity, [na, "bTh"], ["ah%d" % (c % 2)],
                    bias=bTh[:, 24 + c:25 + c], scale=0.5)
                act(th[:, c % 2, 0:T], bbk[:, 0:T], AF.Tanh, [nbk, "bTh"], ["th%d" % (c % 2)],
                    bias=bTh[:, 32 + c:33 + c], scale=0.5)
                stt(gcat[:, c, HIST:HIST + T], th[:, c % 2, 0:T], 1.0, ah[:, c % 2, 0:T], ALU.add, ALU.mult,
                    ["ah%d" % (c % 2), "th%d" % (c % 2)], ["gcat%d" % c])
                if last:
                    stt(gtl[:, c, 0:NTL], th[:, c % 2, T - NTL:T], 1.0, ah[:, c % 2, T - NTL:T], ALU.add, ALU.mult,
                        ["ah%d" % (c % 2), "th%d" % (c % 2)], ["gtl"])
        if last:
            bb, nn = newbank()
            bb2, nn2 = newbank()
            for c in range(KC):
                tgt = bb if c < 4 else bb2
                tr(tgt[0:NTL, (c % 4) * 128:(c % 4 + 1) * 128], gtl[:, c, 0:NTL], ident_f[:], ["gtl", "ident_f"],
                   [nn if c < 4 else nn2])
            act(ybuf[0:NTL, 0:512], bb[0:NTL, :], AF.Identity, [nn], ["ybuf"])
            act(ybuf[0:NTL, 512:1024], bb2[0:NTL, :], AF.Identity, [nn2], ["ybuf"])
            if is_sample:
                dma("act", ncs[HIST - NTL:HIST, :], ybuf[0:NTL, :], ["ybuf"], [], "ybuf_st")
                dma("sp", ncs[0:HIST - NTL, :], scv[NTL:HIST, :], [], [], "ncs_cp")
            else:
                dma("act", ncp[s], ybuf[0:NTL, :], ["ybuf"], [], "ybuf_st")
        if nxt is not None:
            emit_xload(nxt)
        S.tag = "P4P5"
        sl_ga = [None, None]
        sl_gb = [None, None]
        bsum, nsum = newbank(hold=True)
        bsq, nsq = newbank(hold=True)

        def stats_mm(c):
            mm(bsum[:, 0:T], ones_b[:], dwb[:, c, 0:T], c == 0, c == KC - 1, ["ones_b", "dwb%d" % c], [nsum])
            mm(bsq[:, 0:T], ones_b[:], dwsq[:, c % 2, 0:T], c == 0, c == KC - 1, ["ones_b", "dwsq%d" % (c % 2)], [nsq])

        for c in range(KC):
            if c + 1 < KC:
                build_diag(c + 1)
            bb, nn = newbank()
            for j in range(NTAP):
                mm(bb[:, 0:T], diag[:, c % 2, j, :], gcat[:, c, j:j + T], j == 0, j == NTAP - 1,
                   ["diagA%d" % (c % 2), "diagD%d" % (c % 2), "gcat%d" % c], [nn])
            act(dwb[:, c, 0:T], bb[:, 0:T], AF.Identity, [nn], ["dwb%d" % c], bias=convbT(c))
            act(dwsq[:, c % 2, 0:T], bb[:, 0:T], AF.Square, [nn], ["dwsq%d" % (c % 2)], bias=convbT(c))
            if c % 4 == 0:
                sl_ga[0], sl_ga[1] = load_slot(8 + c // 4)
                sl_gb[0], sl_gb[1] = load_slot(10 + c // 4)
            cl = c % 4
            bg, ng = newbank()
            for kc in range(KC):
                mm(bg[:, 0:T], sl_ga[0][:, kc, cl * 128:(cl + 1) * 128], hTb[:, kc, 0:T], kc == 0, kc == KC - 1,
                   [sl_ga[1], hn], [ng])
            bs_, ns_ = newbank()
            for n_ in range(NTT):
                mm(bs_[:, n_ * 128:n_ * 128 + TR], vhat[0:TR, n_, c * 128:(c + 1) * 128], wmT[0:TR, c, 0:TR], True, True,
                   ["vhat", "wmT"], [ns_])
            act(sga[:, c % 2, 0:T], bg[:, 0:T], AF.Silu, [ng], ["sga%d" % (c % 2)], bias=bT[:, 16 + c:17 + c])
            tt("dve", pbuf[:, c % 2, 0:T], gu[:, c, 0:T], sga[:, c % 2, 0:T], ALU.mult, ["gu%d" % c, "sga%d" % (c % 2)],
               ["pbuf%d" % (c % 2)])
            if T == TB:
                in1 = Bh[:, c, :].unsqueeze(1).to_broadcast([128, 4, 128])
                in0 = bs_[:, :].rearrange("p (a b) -> p a b", a=4)
                o_ = nt1[:, :].rearrange("p (a b) -> p a b", a=4)
            else:
                in1 = Bh[:, c, 0:T]
                in0 = bs_[:, 0:T]
                o_ = nt1[:, 0:T]
            stt(o_, in0, lnvgT(c), in1, ALU.mult, ALU.add, [ns_, "Bh", "RT0"], ["nt1"])
            tt("dve", gu[:, c, 0:T], nt1[:, 0:T], pbuf[:, c % 2, 0:T], ALU.mult, ["nt1", "pbuf%d" % (c % 2)],
               ["gu%d" % c])
            bgb, ngb = newbank()
            for kc in range(KC):
                mm(bgb[:, 0:T], sl_gb[0][:, kc, cl * 128:(cl + 1) * 128], hTb[:, kc, 0:T], kc == 0, kc == KC - 1,
                   [sl_gb[1], hn], [ngb])
            act(sgb[:, c, 0:T], bgb[:, 0:T], AF.Silu, [ngb], ["sgb%d" % c], bias=bT[:, 40 + c:41 + c])
            if c >= 1:
                stats_mm(c - 1)
        stats_mm(KC - 1)
        if not last:
            cp("dve", gcat[:, :, 0:HIST], gcat[:, :, T:T + HIST], gnames, gnames)
        S.tag = "P7"
        mT = gcat
        p7 = {}

        def p7_slots(i):
            if ("m", i) not in p7:
                p7[("m", i)] = load_slot(12 + i)
                p7[("o", i)] = load_slot(16 + i)
            return p7[("m", i)], p7[("o", i)]

        def p7_part1(o):
            S.tag = "P7"
            i, ol = o // 2, o % 2
            (sl_m, sn_m), (sl_o, sn_o) = p7_slots(i)
            bma, nma = newbank()
            bmb, nmb = newbank()
            bpa, npa = newbank(hold=True)
            p7[("pa", o)] = (bpa, npa)
            for kc in range(KC):
                mm(bma[:, 0:T], sl_m[:, kc, ol * 128:(ol + 1) * 128], hTb[:, kc, 0:T], kc == 0, kc == KC - 1,
                   [sn_m, hn], [nma])
            for kc in range(KC):
                mm(bmb[:, 0:T], sl_m[:, kc, 256 + ol * 128:256 + (ol + 1) * 128], hTb[:, kc, 0:T], kc == 0,
                   kc == KC - 1, [sn_m, hn], [nmb])
            for c in range(KC):
                mm(bpa[:, 0:T], sl_o[:, c, ol * 128:(ol + 1) * 128], gu[:, c, 0:T], c == 0, c == KC - 1,
                   [sn_o, "gu%d" % c], [npa])
            act(tha[:, o % 4, 0:T], bma[:, 0:T], AF.Tanh, [nma, "bTh"], ["tha%d" % (o % 4)],
                bias=bTh[:, 48 + o:49 + o], scale=0.5)
            act(thb[:, o % 4, 0:T], bmb[:, 0:T], AF.Tanh, [nmb, "bTh"], ["thb%d" % (o % 4)],
                bias=bTh[:, 56 + o:57 + o], scale=0.5)

        def p7_part2(o):
            S.tag = "P7"
            i, ol = o // 2, o % 2
            (sl_m, sn_m), (sl_o, sn_o) = p7_slots(i)
            bpa, npa = p7[("pa", o)]
            bpb, npb_ = newbank()
            for c in range(KC):
                mm(bpb[:, 0:T], sl_o[:, c, 256 + ol * 128:256 + (ol + 1) * 128], sgb[:, c, 0:T], c == 0, c == KC - 1,
                   [sn_o, "sgb%d" % c], [npb_])
            stt(nt1[:, 0:T], tha[:, o % 4, 0:T], 1.0, bpa[:, 0:T], ALU.add, ALU.mult, ["tha%d" % (o % 4), npa], ["nt1"])
            stt(nt2[:, 0:T], thb[:, o % 4, 0:T], 1.0, bpb[:, 0:T], ALU.add, ALU.mult, ["thb%d" % (o % 4), npb_], ["nt2"])
            tt("dve", mT[:, o, HIST:HIST + T], nt1[:, 0:T], nt2[:, 0:T], ALU.add, ["nt1", "nt2"], ["gcat%d" % o])
            release(npa)

        p7_part1(0)
        p7_part1(1)
        p7_part1(2)
        S.tag = "LN_c"
        release(nsum)
        release(nsq)
        ts("dve", mu_bc[:, 0:T], bsum[:, 0:T], 1.0 / D, None, ALU.mult, None, [nsum], ["mu_bc"])
        tt("dve", nt1[:, 0:T], mu_bc[:, 0:T], mu_bc[:, 0:T], ALU.mult, ["mu_bc"], ["nt1"])
        stt(nt2[:, 0:T], bsq[:, 0:T], 1.0 / D, nt1[:, 0:T], ALU.mult, ALU.subtract, [nsq, "nt1"], ["nt2"])
        ts("dve", nt2[:, 0:T], nt2[:, 0:T], EPS, None, ALU.add, None, ["nt2"], ["nt2"])
        I32 = mybir.dt.int32
        rsi = rs_bc[:, 0:T].bitcast(I32)
        S.op("dve", lambda e: e.tensor_single_scalar(out=rsi, in_=nt2[:, 0:T].bitcast(I32), scalar=1,
                                                     op=ALU.arith_shift_right), reads=["nt2"], writes=["rs_bc"])
        ts("dve", rsi, rsi, -1, 0x5f3759df, ALU.mult, ALU.add, ["rs_bc"], ["rs_bc"])
        for it in range(3):
            tt("dve", nt1[:, 0:T], rs_bc[:, 0:T], rs_bc[:, 0:T], ALU.mult, ["rs_bc"], ["nt1"])
            tt("dve", nt1[:, 0:T], nt1[:, 0:T], nt2[:, 0:T], ALU.mult, ["nt1", "nt2"], ["nt1"])
            ts("dve", nt1[:, 0:T], nt1[:, 0:T], -0.5, 1.5, ALU.mult, ALU.add, ["nt1"], ["nt1"])
            tt("dve", rs_bc[:, 0:T], rs_bc[:, 0:T], nt1[:, 0:T], ALU.mult, ["rs_bc", "nt1"], ["rs_bc"])
        S.tag = "P6"
        def p6_pre(c):
            tmp, tmpn = (nt1, "nt1") if c % 2 == 0 else (nt2, "nt2")
            tt("dve", tmp[:, 0:T], dwb[:, c, 0:T], mu_bc[:, 0:T], ALU.subtract, ["dwb%d" % c, "mu_bc"], [tmpn])
            tt("dve", tmp[:, 0:T], tmp[:, 0:T], rs_bc[:, 0:T], ALU.mult, [tmpn, "rs_bc"], [tmpn])
            act(nb[:, c % 2, 0:T], tmp[:, 0:T], AF.Silu, [tmpn, "RT0", "RT1"], ["nb%d" % (c % 2)], bias=lncbT(c),
                scale=lncgT(c))

        def p6_fin(c):
            tt("dve", sgb[:, c, 0:T], nb[:, c % 2, 0:T], sgb[:, c, 0:T], ALU.mult, ["nb%d" % (c % 2), "sgb%d" % c],
               ["sgb%d" % c])

        p6_pre(0)
        for c in range(1, KC):
            p6_pre(c)
            p6_fin(c - 1)
        p6_fin(KC - 1)
        for o in range(KC):
            p7_part2(o)
            if o + 3 < KC:
                p7_part1(o + 3)
            if nxt is not None and hoist:
                n_ = nxt.NTT
                if o <= 4:
                    if o == 0:
                        emit_P0_S1(nxt, 0)
                        if n_ > 1:
                            emit_P0_S1(nxt, 1)
                        emit_P0_S2(nxt, 0)
                    else:
                        if o - 1 < n_:
                            emit_P0_S3(nxt, o - 1)
                        if o < n_:
                            emit_P0_S2(nxt, o)
                        if o + 1 < n_:
                            emit_P0_S1(nxt, o + 1)
                        if o - 1 < n_:
                            emit_P0_B(nxt, o - 1)
                    if o == 3:
                        emit_P1_pre(nxt)
                if o >= 4:
                    t1 = o - 4
                    if 0 <= t1 - 2 < n_:
                        emit_P1_S3(nxt, t1 - 2)
                    if 0 <= t1 - 1 < n_:
                        emit_P1_S2(nxt, t1 - 1)
                    if t1 < n_:
                        emit_P1_S1(nxt, t1)
        if nxt is not None and hoist:
            for t1 in (4, 5):
                if 0 <= t1 - 2 < nxt.NTT:
                    emit_P1_S3(nxt, t1 - 2)
                if 0 <= t1 - 1 < nxt.NTT:
                    emit_P1_S2(nxt, t1 - 1)
        S.tag = "P8"
        slots_o = [load_slot(20), load_slot(21)]
        xrb = [(xr, "xr"), (gv, "gv")]

        def p8_A(t_):
            xb, xn_ = xrb[t_ % 2]
            dma("act", xb[0:TR, :], xsrc[t0 + t_ * 128:t0 + t_ * 128 + TR, :], [], [xn_], "xr_ld%d" % (t_ % 2))
            for half in range(2):
                sl, sn = slots_o[half]
                bb, nn = newbank()
                for kc in range(KC):
                    mm(bb[0:TR, :], mT[:, kc, HIST + t_ * 128:HIST + t_ * 128 + TR], sl[:, kc, :], kc == 0, kc == KC - 1,
                       ["gcat%d" % kc, sn], [nn])
                tmp, tmpn = (nt1, "nt1") if half == 0 else (nt2, "nt2")
                tt("dve", tmp[0:TR, :], bb[0:TR, :], gate_bc[0:TR, half * 512:(half + 1) * 512],
                   ALU.mult, [nn, "gate_bc"], [tmpn])
                tt("dve", xb[0:TR, half * 512:(half + 1) * 512], xb[0:TR, half * 512:(half + 1) * 512], tmp[0:TR, :],
                   ALU.add, [xn_, tmpn], [xn_])
            c8 = 8 + (t_ % 2)
            v8 = 8 + 2 * (t_ % 2)
            sn8, vn8 = "st_s8_%d" % (t_ % 2), "st_v8_%d" % (t_ % 2)
            act(junk[0:TR, :], xb[0:TR, :], AF.Square, [xn_], ["junk", sn8], accum_out=st_s[0:TR, c8:c8 + 1])

        def p8_A2(t_):
            c8 = 8 + (t_ % 2)
            v8 = 8 + 2 * (t_ % 2)
            sn8, vn8 = "st_s8_%d" % (t_ % 2), "st_v8_%d" % (t_ % 2)
            ts("dve", st_v[0:TR, v8:v8 + 1], st_s[0:TR, c8:c8 + 1], 1.0 / D, EPS, ALU.mult, ALU.add, [sn8], [vn8])
            tt("pool", st_v[0:TR, v8 + 1:v8 + 2], st_v[0:TR, v8:v8 + 1], negh[0:TR, 0:1], ALU.pow, [vn8, "negh"], [vn8])

        def p8_B(t_):
            xb, xn_ = xrb[t_ % 2]
            v8 = 8 + 2 * (t_ % 2)
            vn8 = "st_v8_%d" % (t_ % 2)
            stt(ybuf[0:TR, :], xb[0:TR, :], st_v[0:TR, v8 + 1:v8 + 2], fg_bc[0:TR, :], ALU.mult, ALU.mult,
                [xn_, vn8, "fg_bc"], ["ybuf"])
            dma("act", ydst[t0 + t_ * 128:t0 + t_ * 128 + TR, :], ybuf[0:TR, :], ["ybuf"], [], "ybuf_st")

        for t_ in range(NTT):
            p8_A(t_)
            if t_ >= 1:
                p8_B(t_ - 1)
            p8_A2(t_)
        p8_B(NTT - 1)
        if nxt is not None and not hoist:
            first_pass[0] = False
            emit_front_staged(nxt)

    def emit_gate(seq):
        S.tag = "gate"
        for half in range(2):
            bb, nn = newbank()
            for k4 in range(4):
                kc = half * 4 + k4
                mm(bb[:, k4 * 128:(k4 + 1) * 128], gateh[:, kc, seq:seq + 1].to_broadcast([128, 128]), ident_f[:],
                   True, True, ["gateh", "ident_f"], [nn])
            act(gate_bc[:, half * 512:(half + 1) * 512], bb[:, :], AF.Identity, [nn], ["gate_bc"])

    emit_gate(blocks[0].seq)
    emit_front(blocks[0])
    cur_seq = blocks[0].seq
    for i, k in enumerate(blocks):
        nxt = blocks[i + 1] if i + 1 < len(blocks) else None
        if k.seq != cur_seq:
            emit_gate(k.seq)
            cur_seq = k.seq
        emit_body(k, nxt, hoist=(i > 0))
        first_pass[0] = False

    S.emit_all()
    print("sched stats:", S.stats)
    import os
    if os.environ.get("KDBG_TAGS"):
        import pickle
        tg = {}
        for o in S.ops:
            tg.setdefault(o.eng, []).append((o.tag, o.is_dma, o.ticket, o.w, o.r, o.idx, sorted(o.deps)))
        pickle.dump(tg, open(os.environ["KDBG_TAGS"], "wb"))
    st.close()
    return nc


_CACHE = {}


def _get_program(npb):
    if npb not in _CACHE:
        _CACHE[npb] = build_program(npb)
    return _CACHE[npb]


def kernel(x_prompt, x_sample, state_conv, c_prompt, c_sample, w_ada, b_ada, norm_g,
           w_in, b_in, ln_v_g, ln_v_b, w_spatial, b_spatial, conv_w, conv_b,
           ln_c_g, ln_c_b, w_o_a, w_o_b, w_out, final_g):
    f = lambda a: np.ascontiguousarray(np.asarray(a, dtype=np.float32))
    x_prompt = f(x_prompt)
    seqlen = x_prompt.shape[1]
    npb = seqlen // TB
    nc = _get_program(npb)
    ncores = 8
    shared = {
        "w_ada": f(w_ada)[0], "b_ada": f(b_ada)[0], "norm_g": f(norm_g)[0], "w_in": f(w_in)[0],
        "b_in": f(b_in)[0], "ln_v_g": f(ln_v_g)[0], "ln_v_b": f(ln_v_b)[0], "w_sp": f(w_spatial)[0],
        "b_sp": f(b_spatial)[0].reshape(-1), "conv_w": f(conv_w)[0], "conv_b": f(conv_b)[0],
        "ln_c_g": f(ln_c_g)[0], "ln_c_b": f(ln_c_b)[0], "w_o_a": f(w_o_a)[0], "w_o_b": f(w_o_b)[0],
        "w_out": f(w_out)[0], "final_g": f(final_g),
    }
    x_sample = f(x_sample)
    state_conv = f(state_conv)
    c_prompt = f(c_prompt)
    c_sample = f(c_sample)
    in_maps = []
    for i in range(ncores):
        m = dict(shared)
        m["xp"] = np.ascontiguousarray(x_prompt[2 * i:2 * i + 2])
        m["xs"] = np.ascontiguousarray(x_sample[i])
        m["sc"] = np.ascontiguousarray(state_conv[0, i])
        m["cc"] = np.ascontiguousarray(np.concatenate([c_prompt[2 * i:2 * i + 2], c_sample[i:i + 1]], axis=0))
        in_maps.append(m)
    res = run_bass_kernel_spmd(nc, in_maps, core_ids=list(range(ncores)))
    r = res.results
    y_prompt = np.concatenate([r[i]["yp"] for i in range(ncores)], axis=0)
    y_sample = np.stack([r[i]["ys"] for i in range(ncores)], axis=0)
    ncp = np.concatenate([r[i]["ncp"] for i in range(ncores)], axis=0)[None]
    ncs = np.stack([r[i]["ncs"] for i in range(ncores)], axis=0)[None]
    nvs = np.stack([r[i]["nvs"] for i in range(ncores)], axis=0)[None]
    return (y_prompt.astype(np.float32), y_sample.astype(np.float32), ncp.astype(np.float32),
            ncs.astype(np.float32), nvs.astype(np.float32))
```
